# Optimizing a Trainium2 kernel written in Bass

```python
import jax
import jax.numpy as jnp
from jax import lax
import numpy as np

D_MODEL = 2048
BATCH = 8
SEQ = 2048
DEPTH = 2

GRID_W = 64
CTX_LEN = 256
HEAD_DIM = 128
N_Q_HEADS = 16
N_KV_HEADS = 4
Q_BLOCK = 128
ROPE_THETA = 10000.0
POOL_GROUPS = 4
POOL_GROUP_DIM = 256
POOL_WINDOWS = (2, 4, 8, 16)
POOL_OUT_DIM = D_MODEL // POOL_GROUPS
PEER_HEADS = 8
PEER_N_KEYS = 128
PEER_N_EXPERTS = PEER_N_KEYS * PEER_N_KEYS
PEER_QUERY_DIM = 256
PEER_TOPK = 16
PEER_TOKEN_BLOCK = 128
EPS = 1e-6

Q_W = N_Q_HEADS * HEAD_DIM
KV_W = N_KV_HEADS * HEAD_DIM
POOL_W = POOL_GROUPS * POOL_GROUP_DIM
KV_OFF = Q_W
POOL_OFF = Q_W + 2 * KV_W
GA_OFF = POOL_OFF + POOL_W
GB_OFF = GA_OFF + D_MODEL
IN_W = GB_OFF + D_MODEL

kernel_name = 'hybrid_gqa_pool_peer_dit_block'


def rms(x):
    xf = x.astype(jnp.float32)
    return (xf * lax.rsqrt(jnp.mean(xf * xf, axis=-1, keepdims=True) + EPS)).astype(x.dtype)


def modulate(x, shift, scale):
    return rms(x) * (1 + scale) + shift


def rope_tables(L, dtype):
    rows = L // GRID_W
    row = jnp.repeat(jnp.arange(rows), GRID_W)
    col = jnp.tile(jnp.arange(GRID_W), rows)
    n_freq = HEAD_DIM // 4
    inv = ROPE_THETA ** (-jnp.arange(n_freq, dtype=jnp.float32) / n_freq)
    ang_r = row.astype(jnp.float32)[:, None, None] * inv
    ang_c = col.astype(jnp.float32)[:, None, None] * inv
    return (jnp.cos(ang_r).astype(dtype), jnp.sin(ang_r).astype(dtype),
            jnp.cos(ang_c).astype(dtype), jnp.sin(ang_c).astype(dtype))


def rotate(x, cos, sin):
    half = x.shape[-1] // 2
    x1, x2 = x[..., :half], x[..., half:]
    return jnp.concatenate([x1 * cos - x2 * sin, x2 * cos + x1 * sin], axis=-1)


def rope_2d(x, tabs):
    cos_r, sin_r, cos_c, sin_c = tabs
    half = HEAD_DIM // 2
    return jnp.concatenate([rotate(x[..., :half], cos_r, sin_r),
                            rotate(x[..., half:], cos_c, sin_c)], axis=-1)


def split_kv(kv, k_gain):
    B, L, _ = kv.shape
    k = rms(kv[..., :KV_W].reshape(B, L, N_KV_HEADS, HEAD_DIM)) * k_gain
    v = kv[..., KV_W:].reshape(B, L, N_KV_HEADS, HEAD_DIM)
    return k, v


def attn_block(qb, k, v):
    s = jnp.einsum('bqhgd,bkhd->bhgqk', qb, k).astype(jnp.float32) * (HEAD_DIM ** -0.5)
    p = jax.nn.softmax(s, axis=-1).astype(v.dtype)
    return jnp.einsum('bhgqk,bkhd->bqhgd', p, v)


def blocked_attention(q, k, v):
    B, L, Hq, Dh = q.shape
    G = Hq // N_KV_HEADS
    nb = L // Q_BLOCK
    qb = q.reshape(B, nb, Q_BLOCK, N_KV_HEADS, G, Dh).transpose(1, 0, 2, 3, 4, 5)
    o = lax.map(lambda blk: attn_block(blk, k, v), qb)
    return o.transpose(1, 0, 2, 3, 4, 5).reshape(B, L, Hq * Dh)


def multiscale_pool(u):
    B, L, _ = u.shape
    ug = u.reshape(B, L, POOL_GROUPS, POOL_GROUP_DIM)
    csum = jnp.concatenate([jnp.zeros((B, 1, POOL_GROUPS, POOL_GROUP_DIM), jnp.float32),
                            jnp.cumsum(ug.astype(jnp.float32), axis=1)], axis=1)
    t = jnp.arange(L)[:, None]
    w = jnp.array(POOL_WINDOWS)[None, :]
    lo = jnp.clip(t - w // 2, 0, L)
    hi = jnp.clip(t + (w - w // 2), 0, L)
    g_idx = jnp.arange(POOL_GROUPS)[None, :]
    sums = csum[:, hi, g_idx, :] - csum[:, lo, g_idx, :]
    count = (hi - lo).astype(jnp.float32)[None, :, :, None]
    return (sums / count).astype(u.dtype) - ug


def mix_tokens(p, k, v, q_gain, w_br_attn, w_pool, pool_scale, w_out, tabs):
    B, L, _ = p.shape
    q = rms(p[..., :Q_W].reshape(B, L, N_Q_HEADS, HEAD_DIM)) * q_gain
    if tabs is not None:
        q = rope_2d(q, tabs)
    y_a = blocked_attention(q, k, v) @ w_br_attn
    pooled = multiscale_pool(p[..., POOL_OFF:GA_OFF])
    y_b = jnp.einsum('blgc,gcd->blgd', pooled, w_pool).reshape(B, L, D_MODEL) * pool_scale
    merged = (jax.nn.sigmoid(p[..., GA_OFF:GB_OFF]) * y_a
              + jax.nn.sigmoid(p[..., GB_OFF:]) * y_b)
    return merged @ w_out


def peer_ffn(h, w_q_peer, peer_keys, peer_u, peer_v):
    B, L, D = h.shape
    q = (h @ w_q_peer).reshape(B, L, PEER_HEADS, 2, PEER_QUERY_DIM // 2)
    s = jnp.einsum('blhpd,hpkd->blhpk', q, peer_keys).astype(jnp.float32)
    sv, si = lax.top_k(s, PEER_TOPK)
    cand_s = (sv[..., 0, :, None] + sv[..., 1, None, :]).reshape(B, L, PEER_HEADS, PEER_TOPK * PEER_TOPK)
    cand_i = (si[..., 0, :, None] * PEER_N_KEYS + si[..., 1, None, :]).reshape(B, L, PEER_HEADS, PEER_TOPK * PEER_TOPK)
    top_s, pos = lax.top_k(cand_s, PEER_TOPK)
    idx = jnp.take_along_axis(cand_i, pos, axis=-1)
    g = jax.nn.softmax(top_s, axis=-1).astype(h.dtype)
    n_tok = B * L
    nb = n_tok // PEER_TOKEN_BLOCK
    n_sel = PEER_HEADS * PEER_TOPK
    hb = h.reshape(nb, PEER_TOKEN_BLOCK, D)
    ib = idx.reshape(nb, PEER_TOKEN_BLOCK, n_sel)
    gb = g.reshape(nb, PEER_TOKEN_BLOCK, n_sel)

    def token_block(args):
        hc, ic, gc = args
        act = jax.nn.gelu(jnp.einsum('td,ted->te', hc, peer_u[ic]), approximate=False)
        return jnp.einsum('te,ted->td', gc * act, peer_v[ic])

    out = lax.map(token_block, (hb, ib, gb))
    return out.reshape(B, L, D)


def setup_inputs(seed: int = 0) -> dict:
    key = jax.random.key(seed)
    ks = jax.random.split(key, 18)

    def nrm(k, shape, scale):
        return jax.random.normal(k, shape, jnp.float32) * scale

    return {
        'x': nrm(ks[0], (BATCH, SEQ, D_MODEL), 1.0),
        'c': nrm(ks[1], (BATCH, D_MODEL), 1.0),
        'ctx': nrm(ks[2], (BATCH, CTX_LEN, D_MODEL), 1.0),
        'c_ctx': nrm(ks[3], (D_MODEL,), 1.0),
        'w_ada': nrm(ks[4], (DEPTH, D_MODEL, 6 * D_MODEL), 0.5 * D_MODEL ** -0.5),
        'b_ada': nrm(ks[5], (DEPTH, 6 * D_MODEL), 0.01),
        'w_in': nrm(ks[6], (DEPTH, D_MODEL, IN_W), D_MODEL ** -0.5),
        'q_gain': 1.0 + nrm(ks[7], (DEPTH, HEAD_DIM), 0.1),
        'k_gain': 1.0 + nrm(ks[8], (DEPTH, HEAD_DIM), 0.1),
        'w_br_attn': nrm(ks[9], (DEPTH, Q_W, D_MODEL), Q_W ** -0.5),
        'w_pool': nrm(ks[10], (DEPTH, POOL_GROUPS, POOL_GROUP_DIM, POOL_OUT_DIM), POOL_GROUP_DIM ** -0.5),
        'pool_scale': 1.0 + nrm(ks[11], (DEPTH, D_MODEL), 0.1),
        'w_out': nrm(ks[12], (DEPTH, D_MODEL, D_MODEL), D_MODEL ** -0.5),
        'w_q_peer': nrm(ks[13], (DEPTH, D_MODEL, PEER_HEADS * PEER_QUERY_DIM), D_MODEL ** -0.5),
        'peer_keys': nrm(ks[14], (DEPTH, PEER_HEADS, 2, PEER_N_KEYS, PEER_QUERY_DIM // 2), (PEER_QUERY_DIM // 2) ** -0.5),
        'peer_u': nrm(ks[15], (DEPTH, PEER_N_EXPERTS, D_MODEL), D_MODEL ** -0.5),
        'peer_v': nrm(ks[16], (DEPTH, PEER_N_EXPERTS, D_MODEL), 2.0 * (PEER_HEADS * PEER_TOPK) ** -0.5),
        'final_gain': 1.0 + nrm(ks[17], (D_MODEL,), 0.1),
    }


def reference(x, c, ctx, c_ctx, w_ada, b_ada, w_in, q_gain, k_gain, w_br_attn, w_pool,
              pool_scale, w_out, w_q_peer, peer_keys, peer_u, peer_v, final_gain):
    B, L, D = x.shape
    tabs = rope_tables(L, x.dtype)
    for i in range(DEPTH):
        last = i == DEPTH - 1
        mod = jax.nn.silu(c) @ w_ada[i] + b_ada[i]
        sh_a, sc_a, g_a, sh_f, sc_f, g_f = [m[:, None, :] for m in jnp.split(mod, 6, axis=-1)]
        mod_c = jax.nn.silu(c_ctx) @ w_ada[i] + b_ada[i]
        csh_a, csc_a, cg_a, csh_f, csc_f, cg_f = jnp.split(mod_c, 6, axis=-1)

        hc = modulate(ctx, csh_a, csc_a)
        if last:
            pc = None
            kv_c = hc @ w_in[i][:, KV_OFF:POOL_OFF]
        else:
            pc = hc @ w_in[i]
            kv_c = pc[..., KV_OFF:POOL_OFF]
        k_c, v_c = split_kv(kv_c, k_gain[i])

        h = modulate(x, sh_a, sc_a)
        p = h @ w_in[i]
        k_x, v_x = split_kv(p[..., KV_OFF:POOL_OFF], k_gain[i])
        k_all = jnp.concatenate([rope_2d(k_x, tabs), k_c], axis=1)
        v_all = jnp.concatenate([v_x, v_c], axis=1)
        x = x + g_a * mix_tokens(p, k_all, v_all, q_gain[i], w_br_attn[i], w_pool[i],
                                 pool_scale[i], w_out[i], tabs)
        x = x + g_f * peer_ffn(modulate(x, sh_f, sc_f), w_q_peer[i], peer_keys[i],
                               peer_u[i], peer_v[i])

        if not last:
            ctx = ctx + cg_a * mix_tokens(pc, k_c, v_c, q_gain[i], w_br_attn[i], w_pool[i],
                                          pool_scale[i], w_out[i], None)
            ctx = ctx + cg_f * peer_ffn(modulate(ctx, csh_f, csc_f), w_q_peer[i], peer_keys[i],
                                        peer_u[i], peer_v[i])
    return rms(x) * final_gain
```

```python
from contextlib import ExitStack, contextmanager
import numpy as np
import concourse.bass as bass
import concourse.mybir as mybir
from concourse.bass_utils import run_bass_kernel_spmd

F32 = mybir.dt.float32
BF16 = mybir.dt.bfloat16
U32 = mybir.dt.uint32
AF = mybir.ActivationFunctionType
ALU = mybir.AluOpType
AX = mybir.AxisListType

ENGS = ["pe", "act", "dve", "pool", "sp"]

D = 2048
TL = 2048
TC = 256
T = TL + TC
NEXP = 16384
EPS = 1e-6
SH_A, SC_A, G_A, SH_F, SC_F, G_F = range(6)


class Prog:
    def __init__(self, nc, same_engine_sync=True):
        self.nc = nc
        self.stack = ExitStack()
        self.pstack = None
        self.streams = {e: [] for e in ENGS}
        self.sems = {}
        self.count = {}
        self.waited = {e: {} for e in ENGS}
        self.last_write = {}
        self.reads = {}
        self.same_engine_sync = same_engine_sync
        self.n_ops = 0
        self.phase_sems = {}
        self.persist = set()
        self.uid = 0
        for e in ENGS:
            self._sem("e_" + e)

    def _sem(self, name):
        if name not in self.sems:
            self.sems[name] = self.stack.enter_context(self.nc.semaphore(name))
            self.count[name] = 0
        return self.sems[name]

    def sbuf(self, name, shape, dtype):
        self.uid += 1
        return self.pstack.enter_context(self.nc.sbuf_tensor(f"{name}_s{self.uid}", list(shape), dtype))

    def psum(self, name, shape, dtype):
        self.uid += 1
        return self.pstack.enter_context(self.nc.psum_tensor(f"{name}_p{self.uid}", list(shape), dtype))

    def _wait(self, eng, sem, val):
        if sem == "e_pe" and eng == "pe":
            return
        if sem == "e_" + eng and not self.same_engine_sync:
            return
        if self.waited[eng].get(sem, 0) >= val:
            return
        self.waited[eng][sem] = val
        self.streams[eng].append(("wait", sem, val))

    def _deps(self, eng, reads, writes):
        deps = {}
        for k in reads:
            lw = self.last_write.get(k)
            if lw:
                deps[lw[0]] = max(deps.get(lw[0], 0), lw[1])
        for k in writes:
            lw = self.last_write.get(k)
            if lw:
                deps[lw[0]] = max(deps.get(lw[0], 0), lw[1])
            for s, v in self.reads.get(k, {}).items():
                deps[s] = max(deps.get(s, 0), v)
        for s, v in deps.items():
            self._wait(eng, s, v)

    def _record(self, ev, reads, writes):
        for k in reads:
            d = self.reads.setdefault(k, {})
            d[ev[0]] = max(d.get(ev[0], 0), ev[1])
        for k in writes:
            self.last_write[k] = ev
            self.reads[k] = {}

    def op(self, eng, fn, reads=(), writes=()):
        self._deps(eng, reads, writes)
        sem = "e_" + eng
        self.count[sem] += 1
        ev = (sem, self.count[sem])
        self.streams[eng].append(("op", fn, sem, 1))
        self._record(ev, reads, writes)
        self.n_ops += 1

    def dma(self, queue, out, in_, reads=(), writes=(), sem=None, **kw):
        self.dma_fn(queue, lambda e, o=out, i=in_, kw=kw: e.dma_start(out=o, in_=i, **kw), reads, writes, sem)

    def dma_fn(self, queue, fn, reads=(), writes=(), sem=None):
        self._deps(queue, reads, writes)
        sem = sem or "default"
        if sem.startswith("x_"):
            self.persist.add(sem)
        else:
            if sem not in self.phase_sems:
                self.phase_sems[sem] = "d_%d" % len(self.phase_sems)
            sem = self.phase_sems[sem]
        self._sem(sem)
        self.count[sem] += 16
        ev = (sem, self.count[sem])
        self.streams[queue].append(("op", fn, sem, 16))
        self._record(ev, reads, writes)
        self.n_ops += 1

    def dma_split(self, queue, out, in_, n, reads=(), writes=(), sem=None):
        a = out.shape[1]
        step = (a + n - 1) // n
        for k in range(0, a, step):
            self.dma(queue, out[:, k:min(a, k + step), :], in_[:, k:min(a, k + step), :], reads=reads, writes=writes, sem=sem)

    def wait_persistent(self):
        for e in ENGS:
            for s in sorted(self.persist):
                self._wait(e, s, self.count[s])
        self.persist = set()

    def barrier(self):
        for e in ENGS:
            for s, c in self.count.items():
                if c > 0 and s not in self.persist:
                    self._wait(e, s, c)
        self.last_write = {}
        self.reads = {}

    def emit_block(self):
        nc = self.nc
        streams = self.streams
        self.streams = {e: [] for e in ENGS}
        with nc.Block() as block:
            def replay(name):
                def f(engine):
                    for rec in streams[name]:
                        if rec[0] == "wait":
                            engine.wait_ge(self.sems[rec[1]], rec[2])
                        else:
                            rec[1](engine).then_inc(self.sems[rec[2]], rec[3])
                return f
            block.tensor(replay("pe"))
            block.scalar(replay("act"))
            block.vector(replay("dve"))
            block.gpsimd(replay("pool"))
            block.sync(replay("sp"))

    @contextmanager
    def phase(self, name=""):
        self.pstack = ExitStack()
        self.phase_sems = {}
        try:
            yield
            self.barrier()
            self.emit_block()
        finally:
            self.pstack.close()
            self.pstack = None

    def close(self):
        self.stack.close()


ALL_PHASES = ["mod", "modA", "inproj", "pool", "attn", "mix", "wout", "modF", "qpeer", "topk", "gather"]


def build_program(dbg=(), nlayers=2, stop_after=None, same_engine_sync=True, peer_mode="gather"):
    nc = bass.Bass("TRN2", target_bir_lowering=False)

    def inp(name, shape, dt=F32):
        return nc.dram_tensor(name, list(shape), dt, kind="ExternalInput").ap()

    def scratch(name, shape, dt):
        kind = "ExternalOutput" if name in dbg else "Internal"
        return nc.dram_tensor(name, list(shape), dt, kind=kind).ap()

    x_in = inp("x", [TL, D])
    ctx_in = inp("ctx", [TC, D])
    cvec = inp("cvec", [2, D])
    w_ada = inp("w_ada", [2, D, 6 * D])
    b_ada = inp("b_ada", [2, 6 * D])
    w_in = inp("w_in", [2, D, 8192])
    q_gain = inp("q_gain", [2, 128])
    k_gain = inp("k_gain", [2, 128])
    w_br = inp("w_br_attn", [2, D, D])
    w_pool = inp("w_pool", [2, 4, 256, 512])
    pool_scale = inp("pool_scale", [2, D])
    w_out = inp("w_out", [2, D, D])
    w_qp = inp("w_q_peer", [2, D, D])
    peer_keys = inp("peer_keys", [2, 8, 2, 128, 128])
    peer_u = inp("peer_u", [2, NEXP, D])
    peer_v = inp("peer_v", [2, NEXP, D])
    final_gain = inp("final_gain", [D])
    ropeC = inp("ropeC", [TL, 128])
    ropeS = inp("ropeS", [TL, 128])
    rcnt = inp("rcnt", [4, T])
    identf_d = inp("identf", [128, 128])
    iota16_d = inp("iota16", [128, 16])
    iota128_d = inp("iota128", [128, 128])
    out_d = nc.dram_tensor("out", [TL, D], F32, kind="ExternalOutput").ap()

    MODROW = scratch("MODROW", [2, 2, 6 * D], F32)
    X = scratch("X", [T, D], F32)
    HT = scratch("HT", [128, 16, T], BF16)
    QT = scratch("QT", [16, 128, T], BF16)
    KT = scratch("KT", [4, 128, T], BF16)
    V = scratch("V", [T, 512], BF16)
    PL = scratch("PL", [8, 128, T], F32)
    PD = scratch("PD", [8, 128, T], BF16)
    GAB = scratch("GAB", [32, 128, T], BF16)
    AT = scratch("AT", [16, 128, T], BF16)
    MG = scratch("MG", [16, 128, T], BF16)
    H2 = scratch("H2", [T, D], F32)
    QPT = scratch("QPT", [16, 128, T], F32)
    EIDX = scratch("EIDX", [T, 128], U32)
    GATE = scratch("GATE", [T, 128], F32)
    GTd = scratch("GTd", [128, 128, T], BF16)
    UV = scratch("UV", [2 * NEXP, 2 * D], BF16)

    P = Prog(nc, same_engine_sync=same_engine_sync)

    BLOCKS_ALL = [(0, 512), (512, 512), (1024, 512), (1536, 512), (2048, 256)]
    BLOCKS_LAT = BLOCKS_ALL[:4]

    def xsrc(l, r0, nr, c0=0, ncol=D, after_attn=False):
        if l == 0 and not after_attn:
            if r0 < TL:
                return x_in[r0:r0 + nr, c0:c0 + ncol]
            return ctx_in[r0 - TL:r0 - TL + nr, c0:c0 + ncol]
        return X[r0:r0 + nr, c0:c0 + ncol]

    def load_consts(need_bf=False):
        identf = P.sbuf("identf", [128, 128], F32)
        P.dma("sp", identf[:], identf_d, writes=["identf"], sem="const")
        identb = None
        if need_bf:
            identb = P.sbuf("identb", [128, 128], BF16)
            P.op("dve", lambda e: e.tensor_copy(out=identb[:], in_=identf[:]), reads=["identf"], writes=["identb"])
        return identf, identb

    def emit_convert(k0, k1):
        Uf = peer_u.rearrange("l e d -> (l e) d")
        Vf = peer_v.rearrange("l e d -> (l e) d")
        RB = 1024
        k = 0
        for r0 in range(0, 2 * NEXP, RB):
            for (c0, src) in ((0, Uf), (D, Vf)):
                if k0 <= k < k1:
                    P.dma("pool", UV[r0:r0 + RB, c0:c0 + D], src[r0:r0 + RB, :], sem=f"x_cv{k % 4}")
                k += 1

    def phase_mod(l):
        with P.phase("mod"):
            if l == 0:
                emit_convert(0, 22)
            craw = P.sbuf("craw", [128, 2, 16], F32)
            sc = P.sbuf("sc", [128, 16, 2], F32)
            bb = P.sbuf("bb", [2, 6 * D], F32)
            wts = [P.sbuf(f"wt{i}", [128, 16, 512], F32) for i in range(2)]
            mrow = [P.sbuf(f"mrow{i}", [2, 512], F32) for i in range(2)]
            pm = [P.psum(f"pm{i}", [128, 512], F32) for i in range(2)]
            P.dma("sp", craw[:], cvec.rearrange("s (p j) -> p s j", j=16), writes=["craw"], sem="const")
            P.dma("sp", bb[:], b_ada[l].partition_broadcast(2), writes=["bb"], sem="const2")
            P.op("act", lambda e: e.activation(out=sc[:].rearrange("p j s -> p s j"), in_=craw[:], func=AF.Silu),
                 reads=["craw"], writes=["sc"])
            wtb = [P.sbuf(f"wtb{i}", [128, 16, 512], BF16) for i in range(2)]
            scb = P.sbuf("scb", [128, 16, 2], BF16)
            P.op("dve", lambda e: e.tensor_copy(out=scb[:], in_=sc[:]), reads=["sc"], writes=["scb"])
            wv = w_ada[l].rearrange("(p j) n -> p j n", j=16)
            for nb in range(24):
                wt = wts[nb % 2]
                P.dma_split("sp", wt[:], wv[:, :, nb * 512:(nb + 1) * 512], 2, writes=[f"wt{nb%2}"], sem=f"wt{nb%2}")
                ps = pm[nb % 2]
                wb = wtb[nb % 2]
                P.op("act", lambda e, wb=wb, wt=wt: e.copy(out=wb[:, 0:6, :], in_=wt[:, 0:6, :]), reads=[f"wt{nb%2}"], writes=[f"wtb{nb%2}a"])
                P.op("dve", lambda e, wb=wb, wt=wt: e.tensor_copy(out=wb[:, 6:13, :], in_=wt[:, 6:13, :]), reads=[f"wt{nb%2}"], writes=[f"wtb{nb%2}b"])
                P.op("pool" if l > 0 else "dve", lambda e, wb=wb, wt=wt: e.tensor_copy(out=wb[:, 13:16, :], in_=wt[:, 13:16, :]), reads=[f"wt{nb%2}"], writes=[f"wtb{nb%2}c"])
                for j in range(16):
                    P.op("pe", lambda e, ps=ps, wb=wb, j=j: e.matmul(ps[0:2, :], lhsT=scb[:, j, :], rhs=wb[:, j, :],
                                                                      start=(j == 0), stop=(j == 15)),
                         reads=["scb", f"wtb{nb%2}a", f"wtb{nb%2}b", f"wtb{nb%2}c"], writes=[f"pm{nb%2}"])
                addc = 1.0 if (4 <= nb < 8 or 16 <= nb < 20) else 0.0
                mr = mrow[nb % 2]
                P.op("dve", lambda e, ps=ps, mr=mr, nb=nb, addc=addc: e.scalar_tensor_tensor(
                    out=mr[:], in0=ps[0:2, :], scalar=addc, in1=bb[:, nb * 512:(nb + 1) * 512], op0=ALU.add, op1=ALU.add),
                    reads=[f"pm{nb%2}", "bb"], writes=[f"mrow{nb%2}"])
                P.dma("sp", MODROW[l, :, nb * 512:(nb + 1) * 512], mr[:], reads=[f"mrow{nb%2}"], writes=[], sem=f"mrow{nb%2}")

    def load_bc(tile, key, l, s, which, sem):
        P.dma("sp", tile[:], MODROW[l, s, which * D:(which + 1) * D].partition_broadcast(128), writes=[key], sem=sem)

    def phase_modulate(l, which_sh, which_sc, after_attn, blocks, write_h2):
        with P.phase("modulate"):
            identf, identb = load_consts(need_bf=True)
            A = [P.sbuf(f"A{s}", [128, D], F32) for s in range(2)]
            B = [P.sbuf(f"B{s}", [128, D], F32) for s in range(2)]
            classes = sorted({0 if t0 < TL else 1 for t0, _ in blocks})
            for s in classes:
                load_bc(A[s], f"A{s}", l, s, which_sc, f"bcA{s}")
                load_bc(B[s], f"B{s}", l, s, which_sh, f"bcB{s}")
            xt = [P.sbuf(f"xt{i}", [128, D], F32) for i in range(2)]
            hb = [P.sbuf(f"hb{i}", [128, D], BF16) for i in range(2)]
            junk = P.sbuf("junk", [128, D], F32)
            st = [P.sbuf(f"st{i}", [128, 4], F32) for i in range(2)]
            hT = [P.sbuf(f"hT{i}", [128, 16, 512], BF16) for i in range(2)]
            pT = [P.psum(f"pT{i}", [128, 16, 128], BF16) for i in range(2)]
            k = 0
            for bi, (t0, nt) in enumerate(blocks):
                s = 0 if t0 < TL else 1
                hTb = hT[bi % 2]
                for ti in range(nt // 128):
                    r0 = t0 + ti * 128
                    i = k % 2
                    k += 1
                    x_t, h_b, s_t, p_t = xt[i], hb[i], st[i], pT[i]
                    P.dma("sp", x_t[:], xsrc(l, r0, 128, after_attn=after_attn), writes=[f"xt{i}"], sem=f"xt{i}")
                    P.op("act", lambda e, x_t=x_t, s_t=s_t: e.activation(out=junk[:], in_=x_t[:], func=AF.Square, accum_out=s_t[:, 0:1]),
                         reads=[f"xt{i}"], writes=["junk", f"st{i}"])
                    P.op("dve", lambda e, s_t=s_t: e.tensor_scalar(out=s_t[:, 1:2], in0=s_t[:, 0:1], scalar1=1.0 / D, scalar2=EPS, op0=ALU.mult, op1=ALU.add),
                         reads=[f"st{i}"], writes=[f"st{i}"])
                    P.op("act", lambda e, s_t=s_t: e.sqrt(out=s_t[:, 2:3], in_=s_t[:, 1:2]), reads=[f"st{i}"], writes=[f"st{i}"])
                    P.op("dve", lambda e, s_t=s_t: e.reciprocal(out=s_t[:, 3:4], in_=s_t[:, 2:3]), reads=[f"st{i}"], writes=[f"st{i}"])
                    P.op("dve", lambda e, x_t=x_t, s_t=s_t, s=s: e.scalar_tensor_tensor(out=x_t[:], in0=x_t[:], scalar=s_t[:, 3:4], in1=A[s][:], op0=ALU.mult, op1=ALU.mult),
                         reads=[f"xt{i}", f"st{i}", f"A{s}"], writes=[f"xt{i}"])
                    P.op("pool", lambda e, x_t=x_t, s=s: e.tensor_tensor(out=x_t[:], in0=x_t[:], in1=B[s][:], op=ALU.add),
                         reads=[f"xt{i}", f"B{s}"], writes=[f"xt{i}"])
                    if write_h2:
                        P.dma("sp", H2[r0:r0 + 128, :], x_t[:], reads=[f"xt{i}"], writes=[], sem=f"h2st{i}")
                    P.op("act", lambda e, x_t=x_t, h_b=h_b: e.copy(out=h_b[:], in_=x_t[:]), reads=[f"xt{i}"], writes=[f"hb{i}"])
                    for j in range(16):
                        P.op("pe", lambda e, h_b=h_b, p_t=p_t, j=j: e.transpose(out=p_t[:, j, :], in_=h_b[:, j * 128:(j + 1) * 128], identity=identb[:]),
                             reads=[f"hb{i}", "identb"], writes=[f"pT{i}"])
                    P.op("dve", lambda e, p_t=p_t, hTb=hTb, ti=ti: e.tensor_copy(out=hTb[:, :, ti * 128:(ti + 1) * 128], in_=p_t[:]),
                         reads=[f"pT{i}"], writes=[f"hT{bi%2}"])
                P.dma_split("sp", HT[:, :, t0:t0 + nt], hTb[:, :, 0:nt], 2, reads=[f"hT{bi%2}"], writes=[], sem=f"hTst{bi%2}")

    def proj(W, col_blocks, act_src, blocks_for, mode_for, evac, per_block=None, end_block=None, npp=3):
        wst = [P.sbuf(f"wst{i}", [128, 16, 256], F32) for i in range(2)]
        wbf = [P.sbuf(f"wbf{i}", [128, 16, 512], BF16) for i in range(2)]
        ablk = [P.sbuf(f"ablk{i}", [128, 16, 512], BF16) for i in range(2)]
        pp = [P.psum(f"pp{i}", [128, 512], F32) for i in range(npp)]
        Wv = W.rearrange("(j p) n -> p j n", p=128)
        items = []
        for ci, cb in enumerate(col_blocks):
            for (t0, nt) in blocks_for(cb):
                items.append((ci, cb, t0, nt))

        def load_w(ci, cb):
            for hf in range(2):
                P.dma_split("sp", wst[hf][:], Wv[:, :, cb * 512 + hf * 256:cb * 512 + (hf + 1) * 256], 2, writes=[f"wst{hf}"], sem=f"wst{hf}")

        def load_a(n):
            ci, cb, t0, nt = items[n]
            i = n % 2
            P.dma_split("sp", ablk[i][:, :, 0:nt], act_src[:, :, t0:t0 + nt], 2, writes=[f"ablk{i}"], sem=f"ablk{i}")

        load_w(0, col_blocks[0])
        load_a(0)
        q = 0
        last_ci = -1
        for n, (ci, cb, t0, nt) in enumerate(items):
            if ci != last_ci:
                i = ci % 2
                P.op("act", lambda e, i=i: e.copy(out=wbf[i][:, :, 0:256], in_=wst[0][:]), reads=["wst0"], writes=[f"wbf{i}"])
                P.op("pool", lambda e, i=i: e.tensor_copy(out=wbf[i][:, :, 256:512], in_=wst[1][:]), reads=["wst1"], writes=[f"wbf{i}"])
                if ci + 1 < len(col_blocks):
                    load_w(ci + 1, col_blocks[ci + 1])
                last_ci = ci
            if n + 1 < len(items):
                load_a(n + 1)
            wb = wbf[ci % 2]
            ab = ablk[n % 2]
            if per_block:
                per_block(cb, t0, nt)
            if mode_for(cb) == "tok":
                for ti in range(nt // 128):
                    ps = pp[q % npp]
                    pk = f"pp{q % npp}"
                    q += 1
                    for j in range(16):
                        P.op("pe", lambda e, ps=ps, ab=ab, wb=wb, j=j, ti=ti: e.matmul(ps[:], lhsT=ab[:, j, ti * 128:(ti + 1) * 128], rhs=wb[:, j, :],
                                                                                      start=(j == 0), stop=(j == 15)),
                             reads=[f"ablk{n%2}", f"wbf{ci%2}"], writes=[pk])
                    evac(cb, t0, ti, nt, ps, pk)
            else:
                for cc in range(4):
                    ps = pp[q % npp]
                    pk = f"pp{q % npp}"
                    q += 1
                    for j in range(16):
                        P.op("pe", lambda e, ps=ps, ab=ab, wb=wb, j=j, cc=cc, nt=nt: e.matmul(ps[:, 0:nt], lhsT=wb[:, j, cc * 128:(cc + 1) * 128], rhs=ab[:, j, 0:nt],
                                                                                             start=(j == 0), stop=(j == 15)),
                             reads=[f"ablk{n%2}", f"wbf{ci%2}"], writes=[pk])
                    evac(cb, t0, cc, nt, ps, pk)
            if end_block:
                end_block(cb, t0, nt)

    def phase_inproj(l):
        with P.phase("inproj"):
            identf, identb = load_consts(need_bf=True)
            rC = P.sbuf("rC", [128, 16, 128], F32)
            rS = P.sbuf("rS", [128, 16, 128], F32)
            P.dma_split("sp", rC[:], ropeC.rearrange("(t p) d -> p t d", p=128), 2, writes=["rC"], sem="const")
            P.dma_split("sp", rS[:], ropeS.rearrange("(t p) d -> p t d", p=128), 2, writes=["rS"], sem="const2")
            gq = P.sbuf("gq", [128, 128], F32)
            gk = P.sbuf("gk", [128, 128], F32)
            P.dma("sp", gq[:], q_gain[l].partition_broadcast(128), writes=["gq"], sem="const3")
            P.dma("sp", gk[:], k_gain[l].partition_broadcast(128), writes=["gk"], sem="const4")
            NB = 2
            qf = [P.sbuf(f"qf{i}", [128, 512], F32) for i in range(NB)]
            sq = [P.sbuf(f"sq{i}", [128, 512], F32) for i in range(NB)]
            t1 = [P.sbuf(f"t1{i}", [128, 512], F32) for i in range(NB)]
            t2 = [P.sbuf(f"t2{i}", [128, 512], F32) for i in range(NB)]
            qb = [P.sbuf(f"qb{i}", [128, 512], BF16) for i in range(NB)]
            sst = [P.sbuf(f"sst{i}", [128, 16], F32) for i in range(NB)]
            stage = [P.sbuf(f"stage{i}", [128, 4, 512], BF16) for i in range(2)]
            ev = [P.sbuf(f"ev{i}", [128, 512], F32) for i in range(3)]
            evb = [P.sbuf(f"evb{i}", [128, 512], BF16) for i in range(3)]
            pq = [P.psum(f"pq{i}", [128, 4, 128], BF16) for i in range(2)]
            cnt = {"qk": 0, "ev": 0, "blk": 0}

            def blocks_for(cb):
                if l == 1 and cb not in (4, 5):
                    return BLOCKS_LAT
                return BLOCKS_ALL

            def mode_for(cb):
                return "tok" if cb < 6 else "feat"

            pending = []

            def flush():
                while pending:
                    pending.pop(0)()

            def evac(cb, t0, idx, nt, ps, pk):
                flush()
                if cb < 5:
                    ti = idx
                    r0 = t0 + ti * 128
                    latent = r0 < TL
                    i = cnt["qk"] % NB
                    cnt["qk"] += 1
                    gain = gq if cb < 4 else gk
                    gkey = "gq" if cb < 4 else "gk"
                    q_f, s_q, t_1, t_2, q_b, s_t = qf[i], sq[i], t1[i], t2[i], qb[i], sst[i]
                    P.op("act", lambda e: e.copy(out=q_f[:], in_=ps[:]), reads=[pk], writes=[f"qf{i}"])
                    P.op("dve", lambda e: e.tensor_tensor(out=s_q[:], in0=q_f[:], in1=q_f[:], op=ALU.mult), reads=[f"qf{i}"], writes=[f"sq{i}"])
                    P.op("dve", lambda e: e.tensor_reduce(out=s_t[:, 0:4], in_=s_q[:].rearrange("p (h d) -> p h d", h=4), axis=AX.X, op=ALU.add),
                         reads=[f"sq{i}"], writes=[f"sst{i}"])
                    P.op("dve", lambda e: e.tensor_scalar(out=s_t[:, 4:8], in0=s_t[:, 0:4], scalar1=1.0 / 128, scalar2=EPS, op0=ALU.mult, op1=ALU.add),
                         reads=[f"sst{i}"], writes=[f"sst{i}"])
                    P.op("act", lambda e: e.sqrt(out=s_t[:, 8:12], in_=s_t[:, 4:8]), reads=[f"sst{i}"], writes=[f"sst{i}"])
                    P.op("dve", lambda e: e.reciprocal(out=s_t[:, 12:16], in_=s_t[:, 8:12]), reads=[f"sst{i}"], writes=[f"sst{i}"])
                    P.op("dve", lambda e: e.tensor_tensor(out=s_q[:].rearrange("p (h d) -> p h d", h=4), in0=q_f[:].rearrange("p (h d) -> p h d", h=4),
                                                          in1=s_t[:, 12:16].unsqueeze(2).to_broadcast([128, 4, 128]), op=ALU.mult),
                         reads=[f"qf{i}", f"sst{i}"], writes=[f"sq{i}"])
                    P.op("pool", lambda e: e.tensor_tensor(out=q_f[:].rearrange("p (h d) -> p h d", h=4), in0=s_q[:].rearrange("p (h d) -> p h d", h=4),
                                                           in1=gain[:].unsqueeze(1).to_broadcast([128, 4, 128]), op=ALU.mult),
                         reads=[f"sq{i}", gkey], writes=[f"qf{i}"])
                    if latent:
                        tt = r0 // 128
                        P.op("pool", lambda e: e.tensor_tensor(out=t_1[:].rearrange("p (h d) -> p h d", h=4), in0=q_f[:].rearrange("p (h d) -> p h d", h=4),
                                                               in1=rC[:, tt, :].unsqueeze(1).to_broadcast([128, 4, 128]), op=ALU.mult),
                             reads=[f"qf{i}", "rC"], writes=[f"t1{i}"])
                        qv = q_f[:].rearrange("p (h a two d) -> p h a two d", h=4, a=2, two=2)
                        tv = t_2[:].rearrange("p (h a two d) -> p h a two d", h=4, a=2, two=2)
                        sv = rS[:, tt, :].rearrange("p (a two d) -> p a two d", a=2, two=2)
                        for pr in range(2):
                            P.op("dve", lambda e, pr=pr: e.tensor_tensor(out=tv[:, :, :, pr, :], in0=qv[:, :, :, 1 - pr, :],
                                                                         in1=sv[:, :, pr, :].unsqueeze(1).to_broadcast([128, 4, 2, 32]), op=ALU.mult),
                                 reads=[f"qf{i}", "rS"], writes=[f"t2{i}"])
                        P.op("dve", lambda e: e.tensor_tensor(out=q_b[:], in0=t_1[:], in1=t_2[:], op=ALU.add), reads=[f"t1{i}", f"t2{i}"], writes=[f"qb{i}"])
                    else:
                        P.op("act", lambda e: e.copy(out=q_b[:], in_=q_f[:]), reads=[f"qf{i}"], writes=[f"qb{i}"])
                    p_q = pq[i % 2]
                    sg = cnt["blk"] % 2

                    def later():
                        for hh in range(4):
                            P.op("pe", lambda e, hh=hh: e.transpose(out=p_q[:, hh, :], in_=q_b[:, hh * 128:(hh + 1) * 128], identity=identb[:]),
                                 reads=[f"qb{i}", "identb"], writes=[f"pq{i%2}"])
                        P.op("act", lambda e: e.copy(out=stage[sg][:, :, ti * 128:(ti + 1) * 128], in_=p_q[:]), reads=[f"pq{i%2}"], writes=[f"stage{sg}"])
                    pending.append(later)
                elif cb == 5:
                    ti = idx
                    r0 = t0 + ti * 128
                    i = cnt["ev"] % 3
                    cnt["ev"] += 1
                    P.op("act", lambda e: e.copy(out=evb[i][:], in_=ps[:]), reads=[pk], writes=[f"evb{i}"])
                    P.dma("sp", V[r0:r0 + 128, :], evb[i][:], reads=[f"evb{i}"], writes=[], sem=f"evb{i}")
                elif cb < 8:
                    cc = idx
                    i = cnt["ev"] % 3
                    cnt["ev"] += 1
                    P.op("act", lambda e: e.copy(out=ev[i][:, 0:nt], in_=ps[:, 0:nt]), reads=[pk], writes=[f"ev{i}"])
                    P.dma("sp", PL[(cb - 6) * 4 + cc][:, t0:t0 + nt], ev[i][:, 0:nt], reads=[f"ev{i}"], writes=[], sem=f"ev{i}")
                else:
                    cc = idx
                    i = cnt["ev"] % 3
                    cnt["ev"] += 1
                    P.op("act", lambda e: e.activation(out=evb[i][:, 0:nt], in_=ps[:, 0:nt], func=AF.Sigmoid), reads=[pk], writes=[f"evb{i}"])
                    P.dma("sp", GAB[(cb - 8) * 4 + cc][:, t0:t0 + nt], evb[i][:, 0:nt], reads=[f"evb{i}"], writes=[], sem=f"evb{i}")

            def end_block(cb, t0, nt):
                if cb < 5:
                    flush()
                    sg = cnt["blk"] % 2
                    cnt["blk"] += 1
                    dst = QT[cb * 4:(cb + 1) * 4] if cb < 4 else KT[0:4]
                    P.dma("sp", dst.rearrange("h p t -> p h t")[:, :, t0:t0 + nt], stage[sg][:, :, 0:nt], reads=[f"stage{sg}"], writes=[], sem=f"stage{sg}")

            proj(w_in[l], list(range(16)), HT, blocks_for, mode_for, evac, end_block=end_block)

    def phase_pool(l):
        with P.phase("pool"):
            classes = [(0, TL)] + ([(TL, TC)] if l == 0 else [])

            def do_class(off, L):
                W = L + 32
                tag = "L" if off == 0 else "C"
                rc = P.sbuf(f"rc{tag}", [128, 4, L], F32)
                for g in range(4):
                    P.dma("sp", rc[:, g, :], rcnt[g, off:off + L].partition_broadcast(128), writes=[f"rc{tag}"], sem=f"rc{tag}")
                u = [P.sbuf(f"u{tag}{i}", [128, W], F32) for i in range(2)]
                sa = P.sbuf(f"sa{tag}", [128, W], F32)
                sb = P.sbuf(f"sb{tag}", [128, W], F32)
                tmp = P.sbuf(f"tmp{tag}", [128, L], F32)
                pd = [P.sbuf(f"pd{tag}{i}", [128, L], BF16) for i in range(2)]
                for i in range(2):
                    P.op("pool", lambda e, i=i: e.memset(u[i][:], 0.0), writes=[f"u{tag}{i}"])
                P.op("pool", lambda e: e.memset(sa[:], 0.0), writes=[f"sa{tag}"])
                P.op("pool", lambda e: e.memset(sb[:], 0.0), writes=[f"sb{tag}"])
                for c in range(8):
                    g = c // 2
                    i = c % 2
                    uu = u[i]
                    uk = f"u{tag}{i}"
                    P.dma("sp", uu[:, 16:16 + L], PL[c][:, off:off + L], writes=[uk], sem=uk)
                    P.op("dve", lambda e, uu=uu: e.tensor_tensor(out=sa[:, 1:W], in0=uu[:, 1:W], in1=uu[:, 0:W - 1], op=ALU.add), reads=[uk], writes=[f"sa{tag}"])
                    cur, curk = sa, f"sa{tag}"
                    if g >= 1:
                        P.op("pool", lambda e: e.tensor_tensor(out=sb[:, 2:W - 1], in0=sa[:, 3:W], in1=sa[:, 1:W - 2], op=ALU.add), reads=[f"sa{tag}"], writes=[f"sb{tag}"])
                        cur, curk = sb, f"sb{tag}"
                    if g >= 2:
                        P.op("dve", lambda e: e.tensor_tensor(out=sa[:, 4:W - 3], in0=sb[:, 6:W - 1], in1=sb[:, 2:W - 5], op=ALU.add), reads=[f"sb{tag}"], writes=[f"sa{tag}"])
                        cur, curk = sa, f"sa{tag}"
                    if g >= 3:
                        P.op("pool", lambda e: e.tensor_tensor(out=sb[:, 8:W - 7], in0=sa[:, 12:W - 3], in1=sa[:, 4:W - 11], op=ALU.add), reads=[f"sa{tag}"], writes=[f"sb{tag}"])
                        cur, curk = sb, f"sb{tag}"
                    P.op("dve", lambda e, cur=cur, g=g: e.tensor_tensor(out=tmp[:], in0=cur[:, 16:16 + L], in1=rc[:, g, :], op=ALU.mult),
                         reads=[curk, f"rc{tag}"], writes=[f"tmp{tag}"])
                    P.op("pool", lambda e, uu=uu, i=i: e.tensor_tensor(out=pd[i][:], in0=tmp[:], in1=uu[:, 16:16 + L], op=ALU.subtract),
                         reads=[f"tmp{tag}", uk], writes=[f"pd{tag}{i}"])
                    P.dma("sp", PD[c][:, off:off + L], pd[i][:], reads=[f"pd{tag}{i}"], writes=[], sem=f"pd{tag}{i}")

            for (off, L) in classes:
                do_class(off, L)

    def phase_attn(l):
        with P.phase("attn"):
            if l == 0:
                emit_convert(22, 64)
            ones = P.sbuf("ones", [128, 128], BF16)
            P.op("dve", lambda e: e.memset(ones[:], 1.0), writes=["ones"])
            kT = [P.sbuf(f"kT{i}", [128, T], BF16) for i in range(2)]
            Vg = [P.sbuf(f"Vg{i}", [128, 18, 128], BF16) for i in range(2)]
            qT = [P.sbuf(f"qT{i}", [128, T], BF16) for i in range(2)]
            pt = [P.sbuf(f"pt{i}", [128, 512], BF16) for i in range(4)]
            rden = [P.sbuf(f"rden{i}", [128, 512], F32) for i in range(2)]
            ob = [P.sbuf(f"ob{i}", [128, 512], BF16) for i in range(2)]
            sps = [P.psum(f"sps{i}", [128, 512], F32) for i in range(2)]
            ops_ = [P.psum(f"ops{i}", [128, 512], F32) for i in range(2)]
            dps = [P.psum(f"dps{i}", [128, 512], F32) for i in range(2)]
            scale = 128.0 ** -0.5
            st = {"nq": 0, "npt": 0, "nsp": 0}

            def do_qblock(g, gi, h, qi, c0, nqc, kts, st):
                oi = st["nq"] % 2
                st["nq"] += 1
                o_ps, d_ps = ops_[oi], dps[oi]
                nk = len(kts)

                def S(kt, si):
                    P.op("pe", lambda e: e.matmul(sps[si][:, 0:nqc], lhsT=kT[gi][:, kt * 128:(kt + 1) * 128], rhs=qT[qi][:, c0:c0 + nqc],
                                                  start=True, stop=True),
                         reads=[f"kT{gi}", f"qT{qi}"], writes=[f"sps{si}"])

                def step(ii, kt, si, pi):
                    P.op("act", lambda e: e.activation(out=pt[pi][:, 0:nqc], in_=sps[si][:, 0:nqc], func=AF.Exp, scale=scale),
                         reads=[f"sps{si}"], writes=[f"pt{pi}"])
                    P.op("pe", lambda e: e.matmul(o_ps[:, 0:nqc], lhsT=Vg[gi][:, kt, :], rhs=pt[pi][:, 0:nqc], start=(ii == 0), stop=(ii == nk - 1)),
                         reads=[f"Vg{gi}", f"pt{pi}"], writes=[f"ops{oi}"])
                    P.op("pe", lambda e: e.matmul(d_ps[:, 0:nqc], lhsT=ones[:], rhs=pt[pi][:, 0:nqc], start=(ii == 0), stop=(ii == nk - 1)),
                         reads=["ones", f"pt{pi}"], writes=[f"dps{oi}"])

                S(kts[0], st["nsp"] % 2)
                for ii, kt in enumerate(kts):
                    si = st["nsp"] % 2
                    st["nsp"] += 1
                    if ii + 1 < nk:
                        S(kts[ii + 1], st["nsp"] % 2)
                    pi = st["npt"] % 4
                    st["npt"] += 1
                    step(ii, kt, si, pi)
                P.op("dve", lambda e: e.reciprocal(out=rden[oi][:, 0:nqc], in_=d_ps[:, 0:nqc]), reads=[f"dps{oi}"], writes=[f"rden{oi}"])
                P.op("dve", lambda e: e.tensor_tensor(out=ob[oi][:, 0:nqc], in0=o_ps[:, 0:nqc], in1=rden[oi][:, 0:nqc], op=ALU.mult),
                     reads=[f"ops{oi}", f"rden{oi}"], writes=[f"ob{oi}"])
                P.dma("sp", AT[h][:, c0:c0 + nqc], ob[oi][:, 0:nqc], reads=[f"ob{oi}"], writes=[], sem=f"ob{oi}")

            for g in range(4):
                gi = g % 2
                P.dma("sp", kT[gi][:], KT[g], writes=[f"kT{gi}"], sem=f"kT{gi}")
                P.dma_split("sp", Vg[gi][:], V.rearrange("(kt p) c -> p kt c", p=128)[:, :, g * 128:(g + 1) * 128], 3, writes=[f"Vg{gi}"], sem=f"Vg{gi}")
                for hh in range(4):
                    h = g * 4 + hh
                    qi = h % 2
                    ncol = T if l == 0 else TL
                    P.dma("sp", qT[qi][:, 0:ncol], QT[h][:, 0:ncol], writes=[f"qT{qi}"], sem=f"qT{qi}")
                    qblocks = [(c0, 512, list(range(18))) for c0 in range(0, TL, 512)]
                    if l == 0:
                        qblocks.append((TL, TC, [16, 17]))
                    for (c0, nqc, kts) in qblocks:
                        do_qblock(g, gi, h, qi, c0, nqc, kts, st)

    def phase_mix(l):
        with P.phase("mix"):
            identf, _ = load_consts()
            blocks = BLOCKS_ALL if l == 0 else BLOCKS_LAT
            wpf = P.sbuf("wpf", [128, 8, 512], F32)
            wpb = P.sbuf("wpb", [128, 8, 512], BF16)
            P.dma("sp", wpf[:], w_pool[l].rearrange("g (kc p) d -> p (g kc) d", p=128), writes=["wpf"], sem="const2")
            P.op("dve", lambda e: e.tensor_copy(out=wpb[:], in_=wpf[:]), reads=["wpf"], writes=["wpb"])
            psr = P.sbuf("psr", [16, 128], F32)
            pscT = P.sbuf("pscT", [128, 16], F32)
            P.dma("sp", psr[:], pool_scale[l].rearrange("(j p) -> j p", p=128), writes=["psr"], sem="const3")
            ptp = P.psum("ptp", [128, 16], F32)
            P.op("pe", lambda e: e.transpose(out=ptp[:], in_=psr[:], identity=identf[0:16, 0:16]), reads=["psr", "identf"], writes=["ptp"])
            P.op("dve", lambda e: e.tensor_copy(out=pscT[:], in_=ptp[:]), reads=["ptp"], writes=["pscT"])
            pdb = [P.sbuf(f"pdb{i}", [128, 2, 512], BF16) for i in range(2)]
            gab = [P.sbuf(f"gab{i}", [128, 4, 512], BF16) for i in range(2)]
            gbb = [P.sbuf(f"gbb{i}", [128, 4, 512], BF16) for i in range(2)]
            m1 = [P.sbuf(f"m1{i}", [128, 512], F32) for i in range(2)]
            m2 = [P.sbuf(f"m2{i}", [128, 512], F32) for i in range(2)]
            mg = [P.sbuf(f"mg{i}", [128, 512], BF16) for i in range(3)]
            pb = [P.psum(f"pb{i}", [128, 512], F32) for i in range(2)]
            cnt = {"blk": 0, "ev": 0}
            cur = {}

            def per_block(cb, t0, nt):
                i = cnt["blk"] % 2
                cnt["blk"] += 1
                cur["i"] = i
                P.dma("sp", pdb[i][:, :, 0:nt], PD[2 * cb:2 * cb + 2].rearrange("c p t -> p c t")[:, :, t0:t0 + nt], writes=[f"pdb{i}"], sem=f"pdb{i}")
                P.dma("sp", gab[i][:, :, 0:nt], GAB[4 * cb:4 * cb + 4].rearrange("c p t -> p c t")[:, :, t0:t0 + nt], writes=[f"gab{i}"], sem=f"gab{i}")
                P.dma("sp", gbb[i][:, :, 0:nt], GAB[16 + 4 * cb:16 + 4 * cb + 4].rearrange("c p t -> p c t")[:, :, t0:t0 + nt], writes=[f"gbb{i}"], sem=f"gbb{i}")

            def evac(cb, t0, cc, nt, ps, pk):
                i = cur["i"]
                dc = cb * 4 + cc
                e2 = cnt["ev"] % 2
                e3 = cnt["ev"] % 3
                cnt["ev"] += 1
                p_b = pb[e2]
                for kc in range(2):
                    P.op("pe", lambda e, kc=kc: e.matmul(p_b[:, 0:nt], lhsT=wpb[:, cb * 2 + kc, cc * 128:(cc + 1) * 128], rhs=pdb[i][:, kc, 0:nt],
                                                         start=(kc == 0), stop=(kc == 1)),
                         reads=["wpb", f"pdb{i}"], writes=[f"pb{e2}"])
                P.op("dve", lambda e: e.tensor_tensor(out=m1[e2][:, 0:nt], in0=ps[:, 0:nt], in1=gab[i][:, cc, 0:nt], op=ALU.mult),
                     reads=[pk, f"gab{i}"], writes=[f"m1{e2}"])
                P.op("dve", lambda e: e.scalar_tensor_tensor(out=m2[e2][:, 0:nt], in0=p_b[:, 0:nt], scalar=pscT[:, dc:dc + 1], in1=gbb[i][:, cc, 0:nt],
                                                             op0=ALU.mult, op1=ALU.mult),
                     reads=[f"pb{e2}", "pscT", f"gbb{i}"], writes=[f"m2{e2}"])
                P.op("pool", lambda e: e.tensor_tensor(out=mg[e3][:, 0:nt], in0=m1[e2][:, 0:nt], in1=m2[e2][:, 0:nt], op=ALU.add),
                     reads=[f"m1{e2}", f"m2{e2}"], writes=[f"mg{e3}"])
                P.dma("sp", MG[dc][:, t0:t0 + nt], mg[e3][:, 0:nt], reads=[f"mg{e3}"], writes=[], sem=f"mg{e3}")

            proj(w_br[l], list(range(4)), AT.rearrange("h p t -> p h t"), lambda cb: blocks, lambda cb: "feat", evac, per_block=per_block, npp=2)

    def phase_wout(l):
        with P.phase("wout"):
            blocks = BLOCKS_ALL if l == 0 else BLOCKS_LAT
            G = [P.sbuf(f"G{s}", [128, D], F32) for s in range(2)]
            for s in ([0, 1] if l == 0 else [0]):
                load_bc(G[s], f"G{s}", l, s, G_A, f"bcG{s}")
            xt = [P.sbuf(f"xo{i}", [128, 512], F32) for i in range(3)]
            tt = [P.sbuf(f"to{i}", [128, 512], F32) for i in range(3)]
            cnt = {"ev": 0}

            def evac(cb, t0, ti, nt, ps, pk):
                r0 = t0 + ti * 128
                s = 0 if r0 < TL else 1
                i = cnt["ev"] % 3
                cnt["ev"] += 1
                P.dma("sp", xt[i][:], xsrc(l, r0, 128, cb * 512, 512), writes=[f"xo{i}"], sem=f"xo{i}")
                P.op("dve", lambda e: e.tensor_tensor(out=tt[i][:], in0=ps[:], in1=G[s][:, cb * 512:(cb + 1) * 512], op=ALU.mult),
                     reads=[pk, f"G{s}"], writes=[f"to{i}"])
                P.op("pool", lambda e: e.tensor_tensor(out=tt[i][:], in0=tt[i][:], in1=xt[i][:], op=ALU.add), reads=[f"to{i}", f"xo{i}"], writes=[f"to{i}"])
                P.dma("sp", X[r0:r0 + 128, cb * 512:(cb + 1) * 512], tt[i][:], reads=[f"to{i}"], writes=[], sem=f"to{i}")

            proj(w_out[l], list(range(4)), MG.rearrange("h p t -> p h t"), lambda cb: blocks, lambda cb: "tok", evac)

    def phase_qpeer(l):
        with P.phase("qpeer"):
            blocks = BLOCKS_ALL if l == 0 else BLOCKS_LAT
            ev = [P.sbuf(f"ev{i}", [128, 512], F32) for i in range(3)]
            cnt = {"ev": 0}

            def evac(cb, t0, cc, nt, ps, pk):
                i = cnt["ev"] % 3
                cnt["ev"] += 1
                P.op("act", lambda e: e.copy(out=ev[i][:, 0:nt], in_=ps[:, 0:nt]), reads=[pk], writes=[f"ev{i}"])
                P.dma("sp", QPT[cb * 4 + cc][:, t0:t0 + nt], ev[i][:, 0:nt], reads=[f"ev{i}"], writes=[], sem=f"ev{i}")

            proj(w_qp[l], list(range(4)), HT, lambda cb: blocks, lambda cb: "feat", evac)

    def phase_topk(l):
        with P.phase("topk"):
            identf, _ = load_consts()
            ntiles = (T if l == 0 else TL) // 128
            io16 = P.sbuf("io16", [128, 16], F32)
            P.dma("sp", io16[:], iota16_d, writes=["io16"], sem="const2")
            kraw = P.sbuf("kraw", [128, 16, 128], F32)
            keysT = P.sbuf("keysT", [128, 16, 128], F32)
            P.dma_split("sp", kraw[:], peer_keys[l].rearrange("h p k d -> k (h p) d"), 2, writes=["kraw"], sem="const3")
            pk4 = [P.psum(f"pk4{i}", [128, 4, 128], F32) for i in range(4)]
            for grp in range(4):
                for q in range(4):
                    hp = grp * 4 + q
                    P.op("pe", lambda e, hp=hp, q=q, grp=grp: e.transpose(out=pk4[grp][:, q, :], in_=kraw[:, hp, :], identity=identf[:]),
                         reads=["kraw", "identf"], writes=[f"pk4{grp}"])
                P.op("act", lambda e, grp=grp: e.copy(out=keysT[:, grp * 4:(grp + 1) * 4, :], in_=pk4[grp][:]), reads=[f"pk4{grp}"], writes=["keysT"])
            qt = [P.sbuf(f"qt{i}", [128, 16, 128], F32) for i in range(2)]
            S = P.sbuf("S", [128, 16, 128], F32)
            S2 = P.sbuf("S2", [128, 16, 128], F32)
            m = P.sbuf("m", [128, 16, 16], F32)
            ix = P.sbuf("ix", [128, 16, 16], U32)
            ixf = P.sbuf("ixf", [128, 16, 16], F32)
            i1s = P.sbuf("i1s", [128, 8, 16], F32)
            cand = P.sbuf("cand", [128, 8, 256], F32)
            cand2 = P.sbuf("cand2", [128, 8, 256], F32)
            ts = P.sbuf("ts", [128, 8, 16], F32)
            pos = P.sbuf("pos", [128, 8, 16], U32)
            au = P.sbuf("au", [128, 8, 16], U32)
            bu = P.sbuf("bu", [128, 8, 16], U32)
            af_ = P.sbuf("af", [128, 8, 16], F32)
            bf_ = P.sbuf("bf", [128, 8, 16], F32)
            oh = P.sbuf("oh", [128, 8, 16, 16], F32)
            isel = P.sbuf("isel", [128, 8, 16], F32)
            jsel = P.sbuf("jsel", [128, 8, 16], F32)
            ef = P.sbuf("ef", [128, 128], F32)
            eu = [P.sbuf(f"eu{i}", [128, 128], U32) for i in range(2)]
            dd = P.sbuf("dd", [128, 8, 16], F32)
            ee = P.sbuf("ee", [128, 8, 16], F32)
            zz = P.sbuf("zz", [128, 16], F32)
            gg = [P.sbuf(f"gg{i}", [128, 128], F32) for i in range(2)]
            NEG = -1e30
            dense = peer_mode == "dense"
            if dense:
                io128 = P.sbuf("io128", [128, 128], F32)
                P.dma("sp", io128[:], iota128_d, writes=["io128"], sem="const4")
                tp = P.psum("tp", [128, 3, 128], F32)
                ijg = P.sbuf("ijg", [128, 3, 128], F32)
                Aoh = [P.sbuf(f"Aoh{k}", [128, 16, 128], BF16) for k in range(2)]
                Boh = [P.sbuf(f"Boh{k}", [128, 128], BF16) for k in range(8)]
                gp = [P.psum(f"gp{k}", [128, 4, 128], F32) for k in range(2)]
                stg = [P.sbuf(f"stg{k}", [128, 128, 128], BF16) for k in range(2)]
                gst = {"b": 0, "g": 0}

            def gbuild(tt, i):
                r0 = tt * 128
                sg = tt % 2
                srcs = [isel[:].rearrange("p h k -> p (h k)"), jsel[:].rearrange("p h k -> p (h k)"), gg[i][:]]
                keys = ["sel0", "sel1", f"gg{i}"]
                for q in range(3):
                    P.op("pe", lambda e, q=q: e.transpose(out=tp[:, q, :], in_=srcs[q], identity=identf[:]), reads=[keys[q], "identf"], writes=["tp"])
                P.op("act", lambda e: e.copy(out=ijg[:], in_=tp[:]), reads=["tp"], writes=["ijg"])
                for grp in range(8):
                    a = grp % 2
                    for tq in range(16):
                        tl = grp * 16 + tq
                        P.op("pool", lambda e, a=a, tq=tq, tl=tl: e.tensor_scalar(out=Aoh[a][:, tq, :], in0=io128[:], scalar1=ijg[:, 0, tl:tl + 1], scalar2=None, op0=ALU.is_equal),
                             reads=["io128", "ijg"], writes=[f"Aoh{a}"])
                    for q4 in range(4):
                        gk = gst["g"] % 2
                        gst["g"] += 1
                        for q in range(4):
                            tl = grp * 16 + q4 * 4 + q
                            b = gst["b"] % 8
                            gst["b"] += 1
                            P.op("dve", lambda e, b=b, tl=tl: e.tensor_scalar(out=Boh[b][:], in0=io128[:], scalar1=ijg[:, 1, tl:tl + 1], scalar2=ijg[:, 2, tl:tl + 1],
                                                                              op0=ALU.is_equal, op1=ALU.mult),
                                 reads=["io128", "ijg"], writes=[f"Boh{b}"])
                            P.op("pe", lambda e, b=b, a=a, gk=gk, q=q, q4=q4: e.matmul(gp[gk][:, q, :], lhsT=Boh[b][:], rhs=Aoh[a][:, q4 * 4 + q, :], start=True, stop=True),
                                 reads=[f"Boh{b}", f"Aoh{a}"], writes=[f"gp{gk}"])
                        tl0 = grp * 16 + q4 * 4
                        P.op("act", lambda e, gk=gk, tl0=tl0, sg=sg: e.copy(out=stg[sg][:, :, tl0:tl0 + 4].rearrange("p i t -> p t i"), in_=gp[gk][:]),
                             reads=[f"gp{gk}"], writes=[f"stg{sg}"])
                dst = GTd.rearrange("i j t -> j i t")
                for k in range(16):
                    P.dma("sp", dst[:, k * 8:(k + 1) * 8, r0:r0 + 128], stg[sg][:, k * 8:(k + 1) * 8, :], reads=[f"stg{sg}"], writes=[], sem=f"stg{sg}")

            for tt in range(ntiles):
                r0 = tt * 128
                i = tt % 2
                P.dma_split("sp", qt[i][:], QPT.rearrange("c p t -> p c t")[:, :, r0:r0 + 128], 2, writes=[f"qt{i}"], sem=f"qt{i}")
                for grp in range(4):
                    for q in range(4):
                        hp = grp * 4 + q
                        P.op("pe", lambda e, hp=hp, q=q, grp=grp, i=i: e.matmul(pk4[grp][:, q, :], lhsT=qt[i][:, hp, :], rhs=keysT[:, hp, :], start=True, stop=True),
                             reads=[f"qt{i}", "keysT"], writes=[f"pk4{grp}"])
                    P.op("act", lambda e, grp=grp: e.copy(out=S[:, grp * 4:(grp + 1) * 4, :], in_=pk4[grp][:]), reads=[f"pk4{grp}"], writes=[f"S{grp}"])
                for hp in range(16):
                    sk = f"S{hp // 4}"
                    P.op("dve", lambda e, hp=hp: e.max(out=m[:, hp, 0:8], in_=S[:, hp, :]), reads=[sk], writes=[f"ma{hp}"])
                for hp in range(16):
                    sk = f"S{hp // 4}"
                    P.op("dve", lambda e, hp=hp: e.max_index(out=ix[:, hp, 0:8], in_max=m[:, hp, 0:8], in_values=S[:, hp, :]), reads=[sk, f"ma{hp}"], writes=[f"ixa{hp}"])
                for hp in range(16):
                    sk = f"S{hp // 4}"
                    P.op("dve", lambda e, hp=hp: e.match_replace(out=S2[:, hp, :], in_to_replace=m[:, hp, 0:8], in_values=S[:, hp, :], imm_value=NEG),
                         reads=[sk, f"ma{hp}"], writes=[f"S2_{hp}"])
                for hp in range(16):
                    P.op("dve", lambda e, hp=hp: e.max(out=m[:, hp, 8:16], in_=S2[:, hp, :]), reads=[f"S2_{hp}"], writes=[f"mb{hp}"])
                for hp in range(16):
                    P.op("dve", lambda e, hp=hp: e.max_index(out=ix[:, hp, 8:16], in_max=m[:, hp, 8:16], in_values=S2[:, hp, :]), reads=[f"S2_{hp}", f"mb{hp}"], writes=[f"ixb{hp}"])
                mkeys = [f"ma{hp}" for hp in range(16)] + [f"mb{hp}" for hp in range(16)]
                ixkeys = [f"ixa{hp}" for hp in range(16)] + [f"ixb{hp}" for hp in range(16)]
                P.op("dve", lambda e: e.tensor_copy(out=ixf[:], in_=ix[:]), reads=ixkeys, writes=["ixf"])
                mv = m[:].rearrange("p (h two) k -> p h two k", two=2)
                iv = ixf[:].rearrange("p (h two) k -> p h two k", two=2)
                cv = cand[:].rearrange("p h (a b) -> p h a b", a=16)
                P.op("dve", lambda e: e.tensor_tensor(out=cv, in0=mv[:, :, 0, :].unsqueeze(3).to_broadcast([128, 8, 16, 16]),
                                                      in1=mv[:, :, 1, :].unsqueeze(2).to_broadcast([128, 8, 16, 16]), op=ALU.add),
                     reads=mkeys, writes=["cand"])
                for h in range(8):
                    P.op("dve", lambda e, h=h: e.max(out=ts[:, h, 0:8], in_=cand[:, h, :]), reads=["cand"], writes=[f"tsa{h}"])
                for h in range(8):
                    P.op("dve", lambda e, h=h: e.max_index(out=pos[:, h, 0:8], in_max=ts[:, h, 0:8], in_values=cand[:, h, :]), reads=["cand", f"tsa{h}"], writes=[f"posa{h}"])
                for h in range(8):
                    P.op("dve", lambda e, h=h: e.match_replace(out=cand2[:, h, :], in_to_replace=ts[:, h, 0:8], in_values=cand[:, h, :], imm_value=NEG),
                         reads=["cand", f"tsa{h}"], writes=[f"c2_{h}"])
                for h in range(8):
                    P.op("dve", lambda e, h=h: e.max(out=ts[:, h, 8:16], in_=cand2[:, h, :]), reads=[f"c2_{h}"], writes=[f"tsb{h}"])
                for h in range(8):
                    P.op("dve", lambda e, h=h: e.max_index(out=pos[:, h, 8:16], in_max=ts[:, h, 8:16], in_values=cand2[:, h, :]), reads=[f"c2_{h}", f"tsb{h}"], writes=[f"posb{h}"])
                tskeys = [f"tsa{h}" for h in range(8)] + [f"tsb{h}" for h in range(8)]
                poskeys = [f"posa{h}" for h in range(8)] + [f"posb{h}" for h in range(8)]
                P.op("dve", lambda e: e.tensor_single_scalar(out=au[:], in_=pos[:], scalar=4, op=ALU.logical_shift_right), reads=poskeys, writes=["au"])
                P.op("dve", lambda e: e.tensor_single_scalar(out=bu[:], in_=pos[:], scalar=15, op=ALU.bitwise_and), reads=poskeys, writes=["bu"])
                P.op("dve", lambda e: e.tensor_copy(out=af_[:], in_=au[:]), reads=["au"], writes=["af"])
                P.op("dve", lambda e: e.tensor_copy(out=bf_[:], in_=bu[:]), reads=["bu"], writes=["bf"])
                for (sel, xf, which, key) in ((isel, af_, 0, "af"), (jsel, bf_, 1, "bf")):
                    P.op("dve", lambda e, xf=xf: e.tensor_tensor(out=oh[:], in0=io16[:].unsqueeze(1).unsqueeze(1).to_broadcast([128, 8, 16, 16]),
                                                                  in1=xf[:].unsqueeze(3).to_broadcast([128, 8, 16, 16]), op=ALU.is_equal),
                         reads=["io16", key], writes=["oh"])
                    P.op("dve", lambda e, which=which: e.tensor_tensor(out=oh[:], in0=oh[:], in1=iv[:, :, which, :].unsqueeze(2).to_broadcast([128, 8, 16, 16]), op=ALU.mult),
                         reads=["oh", "ixf"], writes=["oh"])
                    P.op("dve", lambda e, sel=sel: e.tensor_reduce(out=sel[:], in_=oh[:], axis=AX.X, op=ALU.add), reads=["oh"], writes=["sel%d" % which])
                P.op("dve", lambda e: e.scalar_tensor_tensor(out=ef[:], in0=isel[:].rearrange("p h k -> p (h k)"), scalar=128.0, in1=jsel[:].rearrange("p h k -> p (h k)"),
                                                             op0=ALU.mult, op1=ALU.add),
                     reads=["sel0", "sel1"], writes=["ef"])
                if l > 0:
                    P.op("dve", lambda e: e.tensor_scalar(out=ef[:], in0=ef[:], scalar1=float(l * NEXP), scalar2=None, op0=ALU.add), reads=["ef"], writes=["ef"])
                P.op("dve", lambda e, i=i: e.tensor_copy(out=eu[i][:], in_=ef[:]), reads=["ef"], writes=[f"eu{i}"])
                P.dma("sp", EIDX[r0:r0 + 128, :], eu[i][:], reads=[f"eu{i}"], writes=[], sem=f"eu{i}")
                P.op("dve", lambda e: e.tensor_tensor(out=dd[:], in0=ts[:], in1=ts[:, :, 0:1].to_broadcast([128, 8, 16]), op=ALU.subtract), reads=tskeys, writes=["dd"])
                P.op("act", lambda e: e.activation(out=ee[:], in_=dd[:], func=AF.Exp), reads=["dd"], writes=["ee"])
                P.op("dve", lambda e: e.tensor_reduce(out=zz[:, 0:8], in_=ee[:], axis=AX.X, op=ALU.add), reads=["ee"], writes=["zz"])
                P.op("dve", lambda e: e.reciprocal(out=zz[:, 8:16], in_=zz[:, 0:8]), reads=["zz"], writes=["zz"])
                P.op("dve", lambda e, i=i: e.tensor_tensor(out=gg[i][:].rearrange("p (h k) -> p h k", h=8), in0=ee[:], in1=zz[:, 8:16].unsqueeze(2).to_broadcast([128, 8, 16]), op=ALU.mult),
                     reads=["ee", "zz"], writes=[f"gg{i}"])
                P.dma("sp", GATE[r0:r0 + 128, :], gg[i][:], reads=[f"gg{i}"], writes=[], sem=f"gg{i}")
                if dense:
                    gbuild(tt, i)

    def phase_gather(l):
        with P.phase("gather"):
            P.wait_persistent()
            identf, identb = load_consts(need_bf=True)
            ntiles = (T if l == 0 else TL) // 128
            G = [P.sbuf(f"G{s}", [128, D], F32) for s in range(2)]
            for s in ([0, 1] if l == 0 else [0]):
                load_bc(G[s], f"G{s}", l, s, G_F, f"bcG{s}")
            NS = 8
            LOOK = 5
            uv = [P.sbuf(f"uv{i}", [128, 2 * D], BF16) for i in range(NS)]
            h2 = [P.sbuf(f"h2{i}", [128, D], F32) for i in range(2)]
            xt = [P.sbuf(f"xt{i}", [128, D], F32) for i in range(2)]
            junk = P.sbuf("junk", [128, D], F32)
            eix = [P.sbuf(f"eix{i}", [128, 128], U32) for i in range(2)]
            gat = [P.sbuf(f"gat{i}", [128, 128], F32) for i in range(2)]
            act = [P.sbuf(f"act{i}", [128, 128], F32) for i in range(2)]
            ge = [P.sbuf(f"ge{i}", [128, 128], F32) for i in range(2)]
            dg = [P.sbuf(f"dg{i}", [128, 128], BF16) for i in range(4)]
            tmp = [P.sbuf(f"tmp{i}", [128, 512], F32) for i in range(2)]
            acc = [[P.psum(f"acc{a}_{b}", [128, 512], F32) for b in range(4)] for a in range(2)]
            st = {"ntmp": 0}
            items = [(tt, sidx) for tt in range(ntiles) for sidx in range(128)]

            def loads(tt):
                r0 = tt * 128
                i = tt % 2
                P.dma("sp", h2[i][:], H2[r0:r0 + 128, :], writes=[f"h2{i}"], sem=f"h2{i}")
                P.dma("sp", xt[i][:], X[r0:r0 + 128, :], writes=[f"xt{i}"], sem=f"xt{i}")
                P.dma("sp", eix[i][:], EIDX[r0:r0 + 128, :], writes=[f"eix{i}"], sem=f"eix{i}")
                P.dma("sp", gat[i][:], GATE[r0:r0 + 128, :], writes=[f"gat{i}"], sem=f"gat{i}")

            def gather(n):
                tt, sidx = items[n]
                i = tt % 2
                u = n % NS
                if sidx == 0:
                    loads(tt)
                P.dma_fn("pool", lambda e: e.indirect_dma_start(
                    out=uv[u][:], out_offset=None, in_=UV, in_offset=bass.IndirectOffsetOnAxis(ap=eix[i][:, sidx:sidx + 1], axis=0)),
                    reads=[f"eix{i}"], writes=[f"uv{u}"], sem=f"uv{u}")

            def dot(n):
                tt, sidx = items[n]
                i = tt % 2
                u = n % NS
                P.op("dve", lambda e: e.scalar_tensor_tensor(out=junk[:], in0=h2[i][:], scalar=1.0, in1=uv[u][:, 0:D], op0=ALU.mult, op1=ALU.mult,
                                                             accum_out=act[i][:, sidx:sidx + 1]),
                     reads=[f"h2{i}", f"uv{u}"], writes=[f"a{i}_{sidx}"])
                P.op("act", lambda e: e.activation(out=ge[i][:, sidx:sidx + 1], in_=act[i][:, sidx:sidx + 1], func=AF.Gelu),
                     reads=[f"a{i}_{sidx}"], writes=[f"g{i}_{sidx}"])

            def combine(n):
                tt, sidx = items[n]
                i = tt % 2
                u = n % NS
                d = n % 4
                P.op("dve", lambda e: e.tensor_scalar(out=dg[d][:], in0=identb[:], scalar1=ge[i][:, sidx:sidx + 1], scalar2=gat[i][:, sidx:sidx + 1],
                                                      op0=ALU.mult, op1=ALU.mult),
                     reads=["identb", f"g{i}_{sidx}", f"gat{i}"], writes=[f"dg{d}"])
                for db in range(4):
                    P.op("pe", lambda e, db=db: e.matmul(acc[i][db][:], lhsT=dg[d][:], rhs=uv[u][:, D + db * 512:D + (db + 1) * 512],
                                                         start=(sidx == 0), stop=(sidx == 127)),
                         reads=[f"dg{d}", f"uv{u}"], writes=[f"acc{i}_{db}"])
                if sidx == 127:
                    finalize(tt)

            def finalize(tt):
                r0 = tt * 128
                i = tt % 2
                s = 0 if r0 < TL else 1
                for db in range(4):
                    tq = st["ntmp"] % 2
                    st["ntmp"] += 1
                    P.op("dve", lambda e, tq=tq, db=db: e.tensor_tensor(out=tmp[tq][:], in0=acc[i][db][:], in1=G[s][:, db * 512:(db + 1) * 512], op=ALU.mult),
                         reads=[f"acc{i}_{db}", f"G{s}"], writes=[f"tmp{tq}"])
                    P.op("dve", lambda e, tq=tq, db=db: e.tensor_tensor(out=xt[i][:, db * 512:(db + 1) * 512], in0=tmp[tq][:], in1=xt[i][:, db * 512:(db + 1) * 512], op=ALU.add),
                         reads=[f"tmp{tq}", f"xt{i}"], writes=[f"xt{i}"])
                P.dma("sp", X[r0:r0 + 128, :], xt[i][:], reads=[f"xt{i}"], writes=[], sem=f"xst{i}")

            N = len(items)
            for n in range(min(LOOK, N)):
                gather(n)
            for n in range(N):
                if n + LOOK < N:
                    gather(n + LOOK)
                dot(n)
                if n >= 1:
                    combine(n - 1)
            combine(N - 1)

    def phase_dense(l):
        with P.phase("dense"):
            _, identb = load_consts(need_bf=True)
            ntok = T if l == 0 else TL
            groups = [(0, 768), (768, 768), (1536, ntok - 1536)]
            G = [P.sbuf(f"G{s}", [128, D], F32) for s in range(2)]
            for s in ([0, 1] if l == 0 else [0]):
                load_bc(G[s], f"G{s}", l, s, G_F, f"bcG{s}")
            hT = P.sbuf("hT", [128, 16, 768], BF16)
            acc = P.sbuf("acc", [128, 6, D], F32)
            GA = P.sbuf("GA", [128, 8, 768], BF16)
            Vb = P.sbuf("Vb", [128, 8, D], BF16)
            Ub = [P.sbuf(f"Ub{k}", [128, D], BF16) for k in range(3)]
            UT = [P.sbuf(f"UT{k}", [128, 16, 128], BF16) for k in range(2)]
            gt = [P.sbuf(f"gt{k}", [128, 768], BF16) for k in range(3)]
            gl = [P.sbuf(f"gl{k}", [128, 512], BF16) for k in range(2)]
            xt = [P.sbuf(f"xt{k}", [128, D], F32) for k in range(2)]
            ptu = [P.psum(f"ptu{k}", [128, 16, 128], BF16) for k in range(2)]
            ps1 = [P.psum(f"ps1{k}", [128, 512], F32) for k in range(2)]
            ps2 = [P.psum(f"ps2{k}", [128, 512], F32) for k in range(2)]
            st = {"c": 0, "p1": 0, "p2": 0, "x": 0}

            def chunk(c, ci, g0, gn, nblocks):
                k3 = st["c"] % 3
                k2 = st["c"] % 2
                st["c"] += 1
                row0 = l * NEXP + c * 128
                P.dma("sp", Ub[k3][:], UV[row0:row0 + 128, 0:D], writes=[f"Ub{k3}"], sem=f"Ub{k3}")
                P.dma("sp", Vb[:, ci, :], UV[row0:row0 + 128, D:2 * D], writes=[f"Vb{ci}"], sem=f"Vb{ci}")
                P.dma("sp", gt[k3][:, 0:gn], GTd[c][:, g0:g0 + gn], writes=[f"gt{k3}"], sem=f"gt{k3}")
                for j in range(16):
                    P.op("pe", lambda e, j=j: e.transpose(out=ptu[k2][:, j, :], in_=Ub[k3][:, j * 128:(j + 1) * 128], identity=identb[:]),
                         reads=[f"Ub{k3}", "identb"], writes=[f"ptu{k2}"])
                P.op("pool" if False else "dve", lambda e: e.tensor_copy(out=UT[k2][:], in_=ptu[k2][:]), reads=[f"ptu{k2}"], writes=[f"UT{k2}"])
                for (b0, bn) in nblocks:
                    p1 = st["p1"] % 2
                    st["p1"] += 1
                    for j in range(16):
                        P.op("pe", lambda e, j=j, p1=p1: e.matmul(ps1[p1][:, 0:bn], lhsT=UT[k2][:, j, :], rhs=hT[:, j, b0:b0 + bn], start=(j == 0), stop=(j == 15)),
                             reads=[f"UT{k2}", "hT"], writes=[f"ps1{p1}"])
                    P.op("act", lambda e, p1=p1: e.activation(out=gl[p1][:, 0:bn], in_=ps1[p1][:, 0:bn], func=AF.Gelu), reads=[f"ps1{p1}"], writes=[f"gl{p1}"])
                    P.op("pool", lambda e, p1=p1: e.tensor_tensor(out=GA[:, ci, b0:b0 + bn], in0=gl[p1][:, 0:bn], in1=gt[k3][:, b0:b0 + bn], op=ALU.mult),
                         reads=[f"gl{p1}", f"gt{k3}"], writes=[f"GA{ci}"])

            def combine(cg, ntile):
                for ti in range(ntile):
                    for db in range(4):
                        p2 = st["p2"] % 2
                        st["p2"] += 1
                        for ci in range(8):
                            P.op("pe", lambda e, ci=ci, p2=p2: e.matmul(ps2[p2][:], lhsT=GA[:, ci, ti * 128:(ti + 1) * 128], rhs=Vb[:, ci, db * 512:(db + 1) * 512],
                                                                        start=(ci == 0), stop=(ci == 7)),
                                 reads=[f"GA{ci}", f"Vb{ci}"], writes=[f"ps2{p2}"])
                        if cg == 0:
                            P.op("dve", lambda e, p2=p2: e.tensor_copy(out=acc[:, ti, db * 512:(db + 1) * 512], in_=ps2[p2][:]), reads=[f"ps2{p2}"], writes=[f"acc{ti}_{db}"])
                        else:
                            P.op("dve", lambda e, p2=p2: e.tensor_tensor(out=acc[:, ti, db * 512:(db + 1) * 512], in0=ps2[p2][:], in1=acc[:, ti, db * 512:(db + 1) * 512], op=ALU.add),
                                 reads=[f"ps2{p2}", f"acc{ti}_{db}"], writes=[f"acc{ti}_{db}"])

            def finalize(g0, ti):
                r0 = g0 + ti * 128
                s = 0 if r0 < TL else 1
                k = st["x"] % 2
                st["x"] += 1
                P.dma("sp", xt[k][:], X[r0:r0 + 128, :], writes=[f"xt{k}"], sem=f"xt{k}")
                P.op("pool", lambda e: e.tensor_tensor(out=acc[:, ti, :], in0=acc[:, ti, :], in1=G[s][:], op=ALU.mult),
                     reads=[f"acc{ti}_{db}" for db in range(4)] + [f"G{s}"], writes=[f"acc{ti}_{db}" for db in range(4)])
                P.op("pool", lambda e: e.tensor_tensor(out=xt[k][:], in0=xt[k][:], in1=acc[:, ti, :], op=ALU.add),
                     reads=[f"acc{ti}_{db}" for db in range(4)] + [f"xt{k}"], writes=[f"xt{k}"])
                P.dma("sp", X[r0:r0 + 128, :], xt[k][:], reads=[f"xt{k}"], writes=[], sem=f"xst{k}")

            for (g0, gn) in groups:
                ntile = gn // 128
                nblocks = [(0, 384), (384, 384)] if gn == 768 else [(0, gn)]
                P.dma_split("sp", hT[:, :, 0:gn], HT[:, :, g0:g0 + gn], 2, writes=["hT"], sem="hT")
                for cg in range(16):
                    for ci in range(8):
                        chunk(cg * 8 + ci, ci, g0, gn, nblocks)
                    combine(cg, ntile)
                for ti in range(ntile):
                    finalize(g0, ti)

    def phase_final():
        with P.phase("final"):
            fg = P.sbuf("fg", [128, D], F32)
            P.dma("sp", fg[:], final_gain.partition_broadcast(128), writes=["fg"], sem="const")
            xt = [P.sbuf(f"xt{i}", [128, D], F32) for i in range(3)]
            junk = P.sbuf("junk", [128, D], F32)
            st = [P.sbuf(f"st{i}", [128, 4], F32) for i in range(3)]
            for tt in range(TL // 128):
                r0 = tt * 128
                i = tt % 3
                x_t, s_t = xt[i], st[i]
                P.dma("sp", x_t[:], X[r0:r0 + 128, :], writes=[f"xt{i}"], sem=f"xt{i}")
                P.op("act", lambda e, x_t=x_t, s_t=s_t: e.activation(out=junk[:], in_=x_t[:], func=AF.Square, accum_out=s_t[:, 0:1]),
                     reads=[f"xt{i}"], writes=["junk", f"st{i}"])
                P.op("dve", lambda e, s_t=s_t: e.tensor_scalar(out=s_t[:, 1:2], in0=s_t[:, 0:1], scalar1=1.0 / D, scalar2=EPS, op0=ALU.mult, op1=ALU.add),
                     reads=[f"st{i}"], writes=[f"st{i}"])
                P.op("act", lambda e, s_t=s_t: e.sqrt(out=s_t[:, 2:3], in_=s_t[:, 1:2]), reads=[f"st{i}"], writes=[f"st{i}"])
                P.op("dve", lambda e, s_t=s_t: e.reciprocal(out=s_t[:, 3:4], in_=s_t[:, 2:3]), reads=[f"st{i}"], writes=[f"st{i}"])
                P.op("dve", lambda e, x_t=x_t, s_t=s_t: e.scalar_tensor_tensor(out=x_t[:], in0=x_t[:], scalar=s_t[:, 3:4], in1=fg[:], op0=ALU.mult, op1=ALU.mult),
                     reads=[f"xt{i}", f"st{i}", "fg"], writes=[f"xt{i}"])
                P.dma("sp", out_d[r0:r0 + 128, :], x_t[:], reads=[f"xt{i}"], writes=[], sem=f"ost{i}")

    def stop(l, name):
        return stop_after is not None and stop_after == (l, name)

    done = False
    for l in range(nlayers):
        blocks = BLOCKS_ALL
        steps = [
            ("mod", lambda: phase_mod(l)),
            ("modA", lambda: phase_modulate(l, SH_A, SC_A, False, BLOCKS_ALL, False)),
            ("inproj", lambda: phase_inproj(l)),
            ("pool", lambda: phase_pool(l)),
            ("attn", lambda: phase_attn(l)),
            ("mix", lambda: phase_mix(l)),
            ("wout", lambda: phase_wout(l)),
            ("modF", lambda: phase_modulate(l, SH_F, SC_F, True, BLOCKS_ALL if l == 0 else BLOCKS_LAT, True)),
            ("qpeer", lambda: phase_qpeer(l)),
            ("topk", lambda: phase_topk(l)),
            ("gather", (lambda: phase_dense(l)) if peer_mode == "dense" else (lambda: phase_gather(l))),
        ]
        for name, fn in steps:
            fn()
            if stop(l, name):
                done = True
                break
        if done:
            break
    if not done:
        phase_final()
    P.close()
    return nc, P


def _consts():
    t = np.arange(TL)
    row = (t // 64).astype(np.float32)
    col = (t % 64).astype(np.float32)
    inv = (np.float32(10000.0) ** (-np.arange(32, dtype=np.float32) / np.float32(32))).astype(np.float32)
    ar = (row[:, None] * inv[None, :]).astype(np.float32)
    ac = (col[:, None] * inv[None, :]).astype(np.float32)
    cr, sr, cc, sc = np.cos(ar), np.sin(ar), np.cos(ac), np.sin(ac)
    ropeC = np.concatenate([cr, cr, cc, cc], axis=1).astype(np.float32)
    ropeS = np.concatenate([-sr, sr, -sc, sc], axis=1).astype(np.float32)
    rc = np.zeros((4, T), np.float32)
    for g, w in enumerate((2, 4, 8, 16)):
        for (off, L) in ((0, TL), (TL, TC)):
            tt = np.arange(L)
            lo = np.clip(tt - w // 2, 0, L)
            hi = np.clip(tt + (w - w // 2), 0, L)
            rc[g, off:off + L] = 1.0 / (hi - lo).astype(np.float32)
    identf = np.eye(128, dtype=np.float32)
    iota16 = np.tile(np.arange(16, dtype=np.float32)[None, :], (128, 1))
    iota128 = np.tile(np.arange(128, dtype=np.float32)[None, :], (128, 1))
    return dict(ropeC=ropeC, ropeS=ropeS, rcnt=rc, identf=identf, iota16=iota16, iota128=iota128)


def make_in_map(inputs, b):
    f = lambda a: np.ascontiguousarray(np.asarray(a, dtype=np.float32))
    m = dict(
        x=f(inputs["x"][b]), ctx=f(inputs["ctx"][b]),
        cvec=f(np.stack([np.asarray(inputs["c"][b]), np.asarray(inputs["c_ctx"])], axis=0)),
    )
    for k in ["w_ada", "b_ada", "w_in", "q_gain", "k_gain", "w_br_attn", "w_pool", "pool_scale", "w_out", "w_q_peer",
              "peer_keys", "peer_u", "peer_v", "final_gain"]:
        m[k] = f(inputs[k])
    m.update(_consts())
    return m


def kernel(**inputs):
    nc, _ = build_program()
    shared = None
    in_maps = []
    for b in range(8):
        m = make_in_map(inputs, b) if shared is None else dict(shared)
        if shared is None:
            shared = m
        else:
            m["x"] = np.ascontiguousarray(np.asarray(inputs["x"][b], dtype=np.float32))
            m["ctx"] = np.ascontiguousarray(np.asarray(inputs["ctx"][b], dtype=np.float32))
            m["cvec"] = np.ascontiguousarray(np.stack([np.asarray(inputs["c"][b]), np.asarray(inputs["c_ctx"])], axis=0).astype(np.float32))
        in_maps.append(m)
    res = run_bass_kernel_spmd(nc, in_maps, core_ids=list(range(8)))
    return np.stack([np.asarray(r["out"], dtype=np.float32) for r in res.results], axis=0)
```

```python
from contextlib import ExitStack, contextmanager
import numpy as np
import concourse.bass as bass
import concourse.mybir as mybir
from concourse.bass_utils import run_bass_kernel_spmd

F32 = mybir.dt.float32
BF16 = mybir.dt.bfloat16
U32 = mybir.dt.uint32
AF = mybir.ActivationFunctionType
ALU = mybir.AluOpType
AX = mybir.AxisListType

ENGS = ["pe", "act", "dve", "pool", "sp"]

D = 2048
TL = 2048
TC = 256
T = TL + TC
NEXP = 16384
EPS = 1e-6
SH_A, SC_A, G_A, SH_F, SC_F, G_F = range(6)


class Prog:
    def __init__(self, nc, same_engine_sync=True):
        self.nc = nc
        self.stack = ExitStack()
        self.pstack = None
        self.streams = {e: [] for e in ENGS}
        self.sems = {}
        self.count = {}
        self.waited = {e: {} for e in ENGS}
        self.last_write = {}
        self.reads = {}
        self.same_engine_sync = same_engine_sync
        self.n_ops = 0
        self.phase_sems = {}
        self.persist = set()
        self.uid = 0
        for e in ENGS:
            self._sem("e_" + e)

    def _sem(self, name):
        if name not in self.sems:
            self.sems[name] = self.stack.enter_context(self.nc.semaphore(name))
            self.count[name] = 0
        return self.sems[name]

    def sbuf(self, name, shape, dtype):
        self.uid += 1
        return self.pstack.enter_context(self.nc.sbuf_tensor(f"{name}_s{self.uid}", list(shape), dtype))

    def psum(self, name, shape, dtype):
        self.uid += 1
        return self.pstack.enter_context(self.nc.psum_tensor(f"{name}_p{self.uid}", list(shape), dtype))

    def _wait(self, eng, sem, val):
        if sem == "e_pe" and eng == "pe":
            return
        if sem == "e_" + eng and not self.same_engine_sync:
            return
        if self.waited[eng].get(sem, 0) >= val:
            return
        self.waited[eng][sem] = val
        self.streams[eng].append(("wait", sem, val))

    def _deps(self, eng, reads, writes):
        deps = {}
        for k in reads:
            lw = self.last_write.get(k)
            if lw:
                deps[lw[0]] = max(deps.get(lw[0], 0), lw[1])
        for k in writes:
            lw = self.last_write.get(k)
            if lw:
                deps[lw[0]] = max(deps.get(lw[0], 0), lw[1])
            for s, v in self.reads.get(k, {}).items():
                deps[s] = max(deps.get(s, 0), v)
        for s, v in deps.items():
            self._wait(eng, s, v)

    def _record(self, ev, reads, writes):
        for k in reads:
            d = self.reads.setdefault(k, {})
            d[ev[0]] = max(d.get(ev[0], 0), ev[1])
        for k in writes:
            self.last_write[k] = ev
            self.reads[k] = {}

    def op(self, eng, fn, reads=(), writes=()):
        self._deps(eng, reads, writes)
        sem = "e_" + eng
        self.count[sem] += 1
        ev = (sem, self.count[sem])
        self.streams[eng].append(("op", fn, sem, 1))
        self._record(ev, reads, writes)
        self.n_ops += 1

    def dma(self, queue, out, in_, reads=(), writes=(), sem=None, **kw):
        self.dma_fn(queue, lambda e, o=out, i=in_, kw=kw: e.dma_start(out=o, in_=i, **kw), reads, writes, sem)

    def dma_fn(self, queue, fn, reads=(), writes=(), sem=None):
        self._deps(queue, reads, writes)
        sem = sem or "default"
        if sem.startswith("x_"):
            self.persist.add(sem)
        else:
            if sem not in self.phase_sems:
                self.phase_sems[sem] = "d_%d" % len(self.phase_sems)
            sem = self.phase_sems[sem]
        self._sem(sem)
        self.count[sem] += 16
        ev = (sem, self.count[sem])
        self.streams[queue].append(("op", fn, sem, 16))
        self._record(ev, reads, writes)
        self.n_ops += 1

    def dma_split(self, queue, out, in_, n, reads=(), writes=(), sem=None):
        a = out.shape[1]
        step = (a + n - 1) // n
        for k in range(0, a, step):
            self.dma(queue, out[:, k:min(a, k + step), :], in_[:, k:min(a, k + step), :], reads=reads, writes=writes, sem=sem)

    def wait_persistent(self):
        for e in ENGS:
            for s in sorted(self.persist):
                self._wait(e, s, self.count[s])
        self.persist = set()

    def barrier(self):
        for e in ENGS:
            for s, c in self.count.items():
                if c > 0 and s not in self.persist:
                    self._wait(e, s, c)
        self.last_write = {}
        self.reads = {}

    def emit_block(self):
        nc = self.nc
        streams = self.streams
        self.streams = {e: [] for e in ENGS}
        with nc.Block() as block:
            def replay(name):
                def f(engine):
                    for rec in streams[name]:
                        if rec[0] == "wait":
                            engine.wait_ge(self.sems[rec[1]], rec[2])
                        else:
                            rec[1](engine).then_inc(self.sems[rec[2]], rec[3])
                return f
            block.tensor(replay("pe"))
            block.scalar(replay("act"))
            block.vector(replay("dve"))
            block.gpsimd(replay("pool"))
            block.sync(replay("sp"))

    @contextmanager
    def phase(self, name=""):
        self.pstack = ExitStack()
        self.phase_sems = {}
        try:
            yield
            self.barrier()
            self.emit_block()
        finally:
            self.pstack.close()
            self.pstack = None

    def close(self):
        self.stack.close()


ALL_PHASES = ["mod", "modA", "inproj", "pool", "attn", "mix", "wout", "modF", "qpeer", "topk", "gather"]


def build_program(dbg=(), nlayers=2, stop_after=None, same_engine_sync=True, peer_mode="gather"):
    nc = bass.Bass("TRN2", target_bir_lowering=False)

    def inp(name, shape, dt=F32):
        return nc.dram_tensor(name, list(shape), dt, kind="ExternalInput").ap()

    def scratch(name, shape, dt):
        kind = "ExternalOutput" if name in dbg else "Internal"
        return nc.dram_tensor(name, list(shape), dt, kind=kind).ap()

    x_in = inp("x", [TL, D])
    ctx_in = inp("ctx", [TC, D])
    cvec = inp("cvec", [2, D])
    w_ada = inp("w_ada", [2, D, 6 * D])
    b_ada = inp("b_ada", [2, 6 * D])
    w_in = inp("w_in", [2, D, 8192])
    q_gain = inp("q_gain", [2, 128])
    k_gain = inp("k_gain", [2, 128])
    w_br = inp("w_br_attn", [2, D, D])
    w_pool = inp("w_pool", [2, 4, 256, 512])
    pool_scale = inp("pool_scale", [2, D])
    w_out = inp("w_out", [2, D, D])
    w_qp = inp("w_q_peer", [2, D, D])
    peer_keys = inp("peer_keys", [2, 8, 2, 128, 128])
    peer_u = inp("peer_u", [2, NEXP, D])
    peer_v = inp("peer_v", [2, NEXP, D])
    final_gain = inp("final_gain", [D])
    ropeC = inp("ropeC", [TL, 128])
    ropeS = inp("ropeS", [TL, 128])
    rcnt = inp("rcnt", [4, T])
    identf_d = inp("identf", [128, 128])
    iota16_d = inp("iota16", [128, 16])
    iota128_d = inp("iota128", [128, 128])
    out_d = nc.dram_tensor("out", [TL, D], F32, kind="ExternalOutput").ap()

    MODROW = scratch("MODROW", [2, 2, 6 * D], F32)
    X = scratch("X", [T, D], F32)
    HT = scratch("HT", [128, 16, T], BF16)
    QT = scratch("QT", [16, 128, T], BF16)
    KT = scratch("KT", [4, 128, T], BF16)
    V = scratch("V", [T, 512], BF16)
    PL = scratch("PL", [8, 128, T], F32)
    PD = scratch("PD", [8, 128, T], BF16)
    GAB = scratch("GAB", [32, 128, T], BF16)
    AT = scratch("AT", [16, 128, T], BF16)
    MG = scratch("MG", [16, 128, T], BF16)
    H2 = scratch("H2", [T, D], F32)
    QPT = scratch("QPT", [16, 128, T], F32)
    EIDX = scratch("EIDX", [T, 128], U32)
    GATE = scratch("GATE", [T, 128], F32)
    GTd = scratch("GTd", [128, 128, T], BF16)
    UV = scratch("UV", [2 * NEXP, 2 * D], BF16)

    P = Prog(nc, same_engine_sync=same_engine_sync)

    BLOCKS_ALL = [(0, 512), (512, 512), (1024, 512), (1536, 512), (2048, 256)]
    BLOCKS_LAT = BLOCKS_ALL[:4]

    def xsrc(l, r0, nr, c0=0, ncol=D, after_attn=False):
        if l == 0 and not after_attn:
            if r0 < TL:
                return x_in[r0:r0 + nr, c0:c0 + ncol]
            return ctx_in[r0 - TL:r0 - TL + nr, c0:c0 + ncol]
        return X[r0:r0 + nr, c0:c0 + ncol]

    def load_consts(need_bf=False):
        identf = P.sbuf("identf", [128, 128], F32)
        P.dma("sp", identf[:], identf_d, writes=["identf"], sem="const")
        identb = None
        if need_bf:
            identb = P.sbuf("identb", [128, 128], BF16)
            P.op("dve", lambda e: e.tensor_copy(out=identb[:], in_=identf[:]), reads=["identf"], writes=["identb"])
        return identf, identb

    def emit_convert(k0, k1):
        Uf = peer_u.rearrange("l e d -> (l e) d")
        Vf = peer_v.rearrange("l e d -> (l e) d")
        RB = 1024
        k = 0
        for r0 in range(0, 2 * NEXP, RB):
            for (c0, src) in ((0, Uf), (D, Vf)):
                if k0 <= k < k1:
                    P.dma("pool", UV[r0:r0 + RB, c0:c0 + D], src[r0:r0 + RB, :], sem=f"x_cv{k % 4}")
                k += 1

    def phase_mod(l):
        with P.phase("mod"):
            if l == 0:
                emit_convert(0, 22)
            craw = P.sbuf("craw", [128, 2, 16], F32)
            sc = P.sbuf("sc", [128, 16, 2], F32)
            bb = P.sbuf("bb", [2, 6 * D], F32)
            wts = [P.sbuf(f"wt{i}", [128, 16, 512], F32) for i in range(2)]
            mrow = [P.sbuf(f"mrow{i}", [2, 512], F32) for i in range(2)]
            pm = [P.psum(f"pm{i}", [128, 512], F32) for i in range(2)]
            P.dma("sp", craw[:], cvec.rearrange("s (p j) -> p s j", j=16), writes=["craw"], sem="const")
            P.dma("sp", bb[:], b_ada[l].partition_broadcast(2), writes=["bb"], sem="const2")
            P.op("act", lambda e: e.activation(out=sc[:].rearrange("p j s -> p s j"), in_=craw[:], func=AF.Silu),
                 reads=["craw"], writes=["sc"])
            wtb = [P.sbuf(f"wtb{i}", [128, 16, 512], BF16) for i in range(2)]
            scb = P.sbuf("scb", [128, 16, 2], BF16)
            P.op("dve", lambda e: e.tensor_copy(out=scb[:], in_=sc[:]), reads=["sc"], writes=["scb"])
            wv = w_ada[l].rearrange("(p j) n -> p j n", j=16)
            for nb in range(24):
                wt = wts[nb % 2]
                P.dma_split("sp", wt[:], wv[:, :, nb * 512:(nb + 1) * 512], 2, writes=[f"wt{nb%2}"], sem=f"wt{nb%2}")
                ps = pm[nb % 2]
                wb = wtb[nb % 2]
                P.op("act", lambda e, wb=wb, wt=wt: e.copy(out=wb[:, 0:6, :], in_=wt[:, 0:6, :]), reads=[f"wt{nb%2}"], writes=[f"wtb{nb%2}a"])
                P.op("dve", lambda e, wb=wb, wt=wt: e.tensor_copy(out=wb[:, 6:13, :], in_=wt[:, 6:13, :]), reads=[f"wt{nb%2}"], writes=[f"wtb{nb%2}b"])
                P.op("pool" if l > 0 else "dve", lambda e, wb=wb, wt=wt: e.tensor_copy(out=wb[:, 13:16, :], in_=wt[:, 13:16, :]), reads=[f"wt{nb%2}"], writes=[f"wtb{nb%2}c"])
                for j in range(16):
                    P.op("pe", lambda e, ps=ps, wb=wb, j=j: e.matmul(ps[0:2, :], lhsT=scb[:, j, :], rhs=wb[:, j, :],
                                                                      start=(j == 0), stop=(j == 15)),
                         reads=["scb", f"wtb{nb%2}a", f"wtb{nb%2}b", f"wtb{nb%2}c"], writes=[f"pm{nb%2}"])
                addc = 1.0 if (4 <= nb < 8 or 16 <= nb < 20) else 0.0
                mr = mrow[nb % 2]
                P.op("dve", lambda e, ps=ps, mr=mr, nb=nb, addc=addc: e.scalar_tensor_tensor(
                    out=mr[:], in0=ps[0:2, :], scalar=addc, in1=bb[:, nb * 512:(nb + 1) * 512], op0=ALU.add, op1=ALU.add),
                    reads=[f"pm{nb%2}", "bb"], writes=[f"mrow{nb%2}"])
                P.dma("sp", MODROW[l, :, nb * 512:(nb + 1) * 512], mr[:], reads=[f"mrow{nb%2}"], writes=[], sem=f"mrow{nb%2}")

    def load_bc(tile, key, l, s, which, sem):
        P.dma("sp", tile[:], MODROW[l, s, which * D:(which + 1) * D].partition_broadcast(128), writes=[key], sem=sem)

    def phase_modulate(l, which_sh, which_sc, after_attn, blocks, write_h2):
        with P.phase("modulate"):
            identf, identb = load_consts(need_bf=True)
            A = [P.sbuf(f"A{s}", [128, D], F32) for s in range(2)]
            B = [P.sbuf(f"B{s}", [128, D], F32) for s in range(2)]
            classes = sorted({0 if t0 < TL else 1 for t0, _ in blocks})
            for s in classes:
                load_bc(A[s], f"A{s}", l, s, which_sc, f"bcA{s}")
                load_bc(B[s], f"B{s}", l, s, which_sh, f"bcB{s}")
            xt = [P.sbuf(f"xt{i}", [128, D], F32) for i in range(2)]
            hb = [P.sbuf(f"hb{i}", [128, D], BF16) for i in range(2)]
            junk = P.sbuf("junk", [128, D], F32)
            st = [P.sbuf(f"st{i}", [128, 4], F32) for i in range(2)]
            hT = [P.sbuf(f"hT{i}", [128, 16, 512], BF16) for i in range(2)]
            pT = [P.psum(f"pT{i}", [128, 16, 128], BF16) for i in range(2)]
            k = 0
            for bi, (t0, nt) in enumerate(blocks):
                s = 0 if t0 < TL else 1
                hTb = hT[bi % 2]
                for ti in range(nt // 128):
                    r0 = t0 + ti * 128
                    i = k % 2
                    k += 1
                    x_t, h_b, s_t, p_t = xt[i], hb[i], st[i], pT[i]
                    P.dma("sp", x_t[:], xsrc(l, r0, 128, after_attn=after_attn), writes=[f"xt{i}"], sem=f"xt{i}")
                    P.op("act", lambda e, x_t=x_t, s_t=s_t: e.activation(out=junk[:], in_=x_t[:], func=AF.Square, accum_out=s_t[:, 0:1]),
                         reads=[f"xt{i}"], writes=["junk", f"st{i}"])
                    P.op("dve", lambda e, s_t=s_t: e.tensor_scalar(out=s_t[:, 1:2], in0=s_t[:, 0:1], scalar1=1.0 / D, scalar2=EPS, op0=ALU.mult, op1=ALU.add),
                         reads=[f"st{i}"], writes=[f"st{i}"])
                    P.op("act", lambda e, s_t=s_t: e.sqrt(out=s_t[:, 2:3], in_=s_t[:, 1:2]), reads=[f"st{i}"], writes=[f"st{i}"])
                    P.op("dve", lambda e, s_t=s_t: e.reciprocal(out=s_t[:, 3:4], in_=s_t[:, 2:3]), reads=[f"st{i}"], writes=[f"st{i}"])
                    P.op("dve", lambda e, x_t=x_t, s_t=s_t, s=s: e.scalar_tensor_tensor(out=x_t[:], in0=x_t[:], scalar=s_t[:, 3:4], in1=A[s][:], op0=ALU.mult, op1=ALU.mult),
                         reads=[f"xt{i}", f"st{i}", f"A{s}"], writes=[f"xt{i}"])
                    P.op("pool", lambda e, x_t=x_t, s=s: e.tensor_tensor(out=x_t[:], in0=x_t[:], in1=B[s][:], op=ALU.add),
                         reads=[f"xt{i}", f"B{s}"], writes=[f"xt{i}"])
                    if write_h2:
                        P.dma("act", H2[r0:r0 + 128, :], x_t[:], reads=[f"xt{i}"], writes=[], sem=f"h2st{i}")
                    P.op("act", lambda e, x_t=x_t, h_b=h_b: e.copy(out=h_b[:], in_=x_t[:]), reads=[f"xt{i}"], writes=[f"hb{i}"])
                    for j in range(16):
                        P.op("pe", lambda e, h_b=h_b, p_t=p_t, j=j: e.transpose(out=p_t[:, j, :], in_=h_b[:, j * 128:(j + 1) * 128], identity=identb[:]),
                             reads=[f"hb{i}", "identb"], writes=[f"pT{i}"])
                    P.op("dve", lambda e, p_t=p_t, hTb=hTb, ti=ti: e.tensor_copy(out=hTb[:, :, ti * 128:(ti + 1) * 128], in_=p_t[:]),
                         reads=[f"pT{i}"], writes=[f"hT{bi%2}"])
                P.dma_split("act", HT[:, :, t0:t0 + nt], hTb[:, :, 0:nt], 2, reads=[f"hT{bi%2}"], writes=[], sem=f"hTst{bi%2}")

    def proj(W, col_blocks, act_src, blocks_for, mode_for, evac, per_block=None, end_block=None, npp=3):
        wst = [P.sbuf(f"wst{i}", [128, 16, 256], F32) for i in range(2)]
        wbf = [P.sbuf(f"wbf{i}", [128, 16, 512], BF16) for i in range(2)]
        ablk = [P.sbuf(f"ablk{i}", [128, 16, 512], BF16) for i in range(2)]
        pp = [P.psum(f"pp{i}", [128, 512], F32) for i in range(npp)]
        Wv = W.rearrange("(j p) n -> p j n", p=128)
        items = []
        for ci, cb in enumerate(col_blocks):
            for (t0, nt) in blocks_for(cb):
                items.append((ci, cb, t0, nt))

        def load_w(ci, cb):
            for hf in range(2):
                P.dma_split("sp", wst[hf][:], Wv[:, :, cb * 512 + hf * 256:cb * 512 + (hf + 1) * 256], 2, writes=[f"wst{hf}"], sem=f"wst{hf}")

        def load_a(n):
            ci, cb, t0, nt = items[n]
            i = n % 2
            P.dma_split("sp", ablk[i][:, :, 0:nt], act_src[:, :, t0:t0 + nt], 2, writes=[f"ablk{i}"], sem=f"ablk{i}")

        load_w(0, col_blocks[0])
        load_a(0)
        q = 0
        last_ci = -1
        for n, (ci, cb, t0, nt) in enumerate(items):
            if ci != last_ci:
                i = ci % 2
                P.op("act", lambda e, i=i: e.copy(out=wbf[i][:, :, 0:256], in_=wst[0][:]), reads=["wst0"], writes=[f"wbf{i}"])
                P.op("pool", lambda e, i=i: e.tensor_copy(out=wbf[i][:, :, 256:512], in_=wst[1][:]), reads=["wst1"], writes=[f"wbf{i}"])
                if ci + 1 < len(col_blocks):
                    load_w(ci + 1, col_blocks[ci + 1])
                last_ci = ci
            if n + 1 < len(items):
                load_a(n + 1)
            wb = wbf[ci % 2]
            ab = ablk[n % 2]
            if per_block:
                per_block(cb, t0, nt)
            if mode_for(cb) == "tok":
                for ti in range(nt // 128):
                    ps = pp[q % npp]
                    pk = f"pp{q % npp}"
                    q += 1
                    for j in range(16):
                        P.op("pe", lambda e, ps=ps, ab=ab, wb=wb, j=j, ti=ti: e.matmul(ps[:], lhsT=ab[:, j, ti * 128:(ti + 1) * 128], rhs=wb[:, j, :],
                                                                                      start=(j == 0), stop=(j == 15)),
                             reads=[f"ablk{n%2}", f"wbf{ci%2}"], writes=[pk])
                    evac(cb, t0, ti, nt, ps, pk)
            else:
                for cc in range(4):
                    ps = pp[q % npp]
                    pk = f"pp{q % npp}"
                    q += 1
                    for j in range(16):
                        P.op("pe", lambda e, ps=ps, ab=ab, wb=wb, j=j, cc=cc, nt=nt: e.matmul(ps[:, 0:nt], lhsT=wb[:, j, cc * 128:(cc + 1) * 128], rhs=ab[:, j, 0:nt],
                                                                                             start=(j == 0), stop=(j == 15)),
                             reads=[f"ablk{n%2}", f"wbf{ci%2}"], writes=[pk])
                    evac(cb, t0, cc, nt, ps, pk)
            if end_block:
                end_block(cb, t0, nt)

    def phase_inproj(l):
        with P.phase("inproj"):
            identf, identb = load_consts(need_bf=True)
            rC = P.sbuf("rC", [128, 16, 128], F32)
            rS = P.sbuf("rS", [128, 16, 128], F32)
            P.dma_split("sp", rC[:], ropeC.rearrange("(t p) d -> p t d", p=128), 2, writes=["rC"], sem="const")
            P.dma_split("sp", rS[:], ropeS.rearrange("(t p) d -> p t d", p=128), 2, writes=["rS"], sem="const2")
            gq = P.sbuf("gq", [128, 128], F32)
            gk = P.sbuf("gk", [128, 128], F32)
            P.dma("sp", gq[:], q_gain[l].partition_broadcast(128), writes=["gq"], sem="const3")
            P.dma("sp", gk[:], k_gain[l].partition_broadcast(128), writes=["gk"], sem="const4")
            NB = 2
            qf = [P.sbuf(f"qf{i}", [128, 512], F32) for i in range(NB)]
            sq = [P.sbuf(f"sq{i}", [128, 512], F32) for i in range(NB)]
            t1 = [P.sbuf(f"t1{i}", [128, 512], F32) for i in range(NB)]
            t2 = [P.sbuf(f"t2{i}", [128, 512], F32) for i in range(NB)]
            qb = [P.sbuf(f"qb{i}", [128, 512], BF16) for i in range(NB)]
            sst = [P.sbuf(f"sst{i}", [128, 16], F32) for i in range(NB)]
            stage = [P.sbuf(f"stage{i}", [128, 4, 512], BF16) for i in range(2)]
            ev = [P.sbuf(f"ev{i}", [128, 512], F32) for i in range(3)]
            evb = [P.sbuf(f"evb{i}", [128, 512], BF16) for i in range(3)]
            pq = [P.psum(f"pq{i}", [128, 4, 128], BF16) for i in range(2)]
            cnt = {"qk": 0, "ev": 0, "blk": 0}

            def blocks_for(cb):
                if l == 1 and cb not in (4, 5):
                    return BLOCKS_LAT
                return BLOCKS_ALL

            def mode_for(cb):
                return "tok" if cb < 6 else "feat"

            pending = []

            def flush():
                while pending:
                    pending.pop(0)()

            def evac(cb, t0, idx, nt, ps, pk):
                flush()
                if cb < 5:
                    ti = idx
                    r0 = t0 + ti * 128
                    latent = r0 < TL
                    i = cnt["qk"] % NB
                    cnt["qk"] += 1
                    gain = gq if cb < 4 else gk
                    gkey = "gq" if cb < 4 else "gk"
                    q_f, s_q, t_1, t_2, q_b, s_t = qf[i], sq[i], t1[i], t2[i], qb[i], sst[i]
                    P.op("act", lambda e: e.copy(out=q_f[:], in_=ps[:]), reads=[pk], writes=[f"qf{i}"])
                    P.op("dve", lambda e: e.tensor_tensor(out=s_q[:], in0=q_f[:], in1=q_f[:], op=ALU.mult), reads=[f"qf{i}"], writes=[f"sq{i}"])
                    P.op("dve", lambda e: e.tensor_reduce(out=s_t[:, 0:4], in_=s_q[:].rearrange("p (h d) -> p h d", h=4), axis=AX.X, op=ALU.add),
                         reads=[f"sq{i}"], writes=[f"sst{i}"])
                    P.op("dve", lambda e: e.tensor_scalar(out=s_t[:, 4:8], in0=s_t[:, 0:4], scalar1=1.0 / 128, scalar2=EPS, op0=ALU.mult, op1=ALU.add),
                         reads=[f"sst{i}"], writes=[f"sst{i}"])
                    P.op("act", lambda e: e.sqrt(out=s_t[:, 8:12], in_=s_t[:, 4:8]), reads=[f"sst{i}"], writes=[f"sst{i}"])
                    P.op("dve", lambda e: e.reciprocal(out=s_t[:, 12:16], in_=s_t[:, 8:12]), reads=[f"sst{i}"], writes=[f"sst{i}"])
                    P.op("dve", lambda e: e.tensor_tensor(out=s_q[:].rearrange("p (h d) -> p h d", h=4), in0=q_f[:].rearrange("p (h d) -> p h d", h=4),
                                                          in1=s_t[:, 12:16].unsqueeze(2).to_broadcast([128, 4, 128]), op=ALU.mult),
                         reads=[f"qf{i}", f"sst{i}"], writes=[f"sq{i}"])
                    P.op("pool", lambda e: e.tensor_tensor(out=q_f[:].rearrange("p (h d) -> p h d", h=4), in0=s_q[:].rearrange("p (h d) -> p h d", h=4),
                                                           in1=gain[:].unsqueeze(1).to_broadcast([128, 4, 128]), op=ALU.mult),
                         reads=[f"sq{i}", gkey], writes=[f"qf{i}"])
                    if latent:
                        tt = r0 // 128
                        P.op("pool", lambda e: e.tensor_tensor(out=t_1[:].rearrange("p (h d) -> p h d", h=4), in0=q_f[:].rearrange("p (h d) -> p h d", h=4),
                                                               in1=rC[:, tt, :].unsqueeze(1).to_broadcast([128, 4, 128]), op=ALU.mult),
                             reads=[f"qf{i}", "rC"], writes=[f"t1{i}"])
                        qv = q_f[:].rearrange("p (h a two d) -> p h a two d", h=4, a=2, two=2)
                        tv = t_2[:].rearrange("p (h a two d) -> p h a two d", h=4, a=2, two=2)
                        sv = rS[:, tt, :].rearrange("p (a two d) -> p a two d", a=2, two=2)
                        for pr in range(2):
                            P.op("dve", lambda e, pr=pr: e.tensor_tensor(out=tv[:, :, :, pr, :], in0=qv[:, :, :, 1 - pr, :],
                                                                         in1=sv[:, :, pr, :].unsqueeze(1).to_broadcast([128, 4, 2, 32]), op=ALU.mult),
                                 reads=[f"qf{i}", "rS"], writes=[f"t2{i}"])
                        P.op("dve", lambda e: e.tensor_tensor(out=q_b[:], in0=t_1[:], in1=t_2[:], op=ALU.add), reads=[f"t1{i}", f"t2{i}"], writes=[f"qb{i}"])
                    else:
                        P.op("act", lambda e: e.copy(out=q_b[:], in_=q_f[:]), reads=[f"qf{i}"], writes=[f"qb{i}"])
                    p_q = pq[i % 2]
                    sg = cnt["blk"] % 2

                    def later():
                        for hh in range(4):
                            P.op("pe", lambda e, hh=hh: e.transpose(out=p_q[:, hh, :], in_=q_b[:, hh * 128:(hh + 1) * 128], identity=identb[:]),
                                 reads=[f"qb{i}", "identb"], writes=[f"pq{i%2}"])
                        P.op("act", lambda e: e.copy(out=stage[sg][:, :, ti * 128:(ti + 1) * 128], in_=p_q[:]), reads=[f"pq{i%2}"], writes=[f"stage{sg}"])
                    pending.append(later)
                elif cb == 5:
                    ti = idx
                    r0 = t0 + ti * 128
                    i = cnt["ev"] % 3
                    cnt["ev"] += 1
                    P.op("act", lambda e: e.copy(out=evb[i][:], in_=ps[:]), reads=[pk], writes=[f"evb{i}"])
                    P.dma("act", V[r0:r0 + 128, :], evb[i][:], reads=[f"evb{i}"], writes=[], sem=f"evb{i}")
                elif cb < 8:
                    cc = idx
                    i = cnt["ev"] % 3
                    cnt["ev"] += 1
                    P.op("act", lambda e: e.copy(out=ev[i][:, 0:nt], in_=ps[:, 0:nt]), reads=[pk], writes=[f"ev{i}"])
                    P.dma("act", PL[(cb - 6) * 4 + cc][:, t0:t0 + nt], ev[i][:, 0:nt], reads=[f"ev{i}"], writes=[], sem=f"ev{i}")
                else:
                    cc = idx
                    i = cnt["ev"] % 3
                    cnt["ev"] += 1
                    P.op("act", lambda e: e.activation(out=evb[i][:, 0:nt], in_=ps[:, 0:nt], func=AF.Sigmoid), reads=[pk], writes=[f"evb{i}"])
                    P.dma("act", GAB[(cb - 8) * 4 + cc][:, t0:t0 + nt], evb[i][:, 0:nt], reads=[f"evb{i}"], writes=[], sem=f"evb{i}")

            def end_block(cb, t0, nt):
                if cb < 5:
                    flush()
                    sg = cnt["blk"] % 2
                    cnt["blk"] += 1
                    dst = QT[cb * 4:(cb + 1) * 4] if cb < 4 else KT[0:4]
                    P.dma("act", dst.rearrange("h p t -> p h t")[:, :, t0:t0 + nt], stage[sg][:, :, 0:nt], reads=[f"stage{sg}"], writes=[], sem=f"stage{sg}")

            proj(w_in[l], list(range(16)), HT, blocks_for, mode_for, evac, end_block=end_block)

    def phase_pool(l):
        with P.phase("pool"):
            classes = [(0, TL)] + ([(TL, TC)] if l == 0 else [])

            def do_class(off, L):
                W = L + 32
                tag = "L" if off == 0 else "C"
                rc = P.sbuf(f"rc{tag}", [128, 4, L], F32)
                for g in range(4):
                    P.dma("sp", rc[:, g, :], rcnt[g, off:off + L].partition_broadcast(128), writes=[f"rc{tag}"], sem=f"rc{tag}")
                u = [P.sbuf(f"u{tag}{i}", [128, W], F32) for i in range(2)]
                sa = P.sbuf(f"sa{tag}", [128, W], F32)
                sb = P.sbuf(f"sb{tag}", [128, W], F32)
                tmp = P.sbuf(f"tmp{tag}", [128, L], F32)
                pd = [P.sbuf(f"pd{tag}{i}", [128, L], BF16) for i in range(2)]
                for i in range(2):
                    P.op("pool", lambda e, i=i: e.memset(u[i][:], 0.0), writes=[f"u{tag}{i}"])
                P.op("pool", lambda e: e.memset(sa[:], 0.0), writes=[f"sa{tag}"])
                P.op("pool", lambda e: e.memset(sb[:], 0.0), writes=[f"sb{tag}"])
                for c in range(8):
                    g = c // 2
                    i = c % 2
                    uu = u[i]
                    uk = f"u{tag}{i}"
                    P.dma("sp", uu[:, 16:16 + L], PL[c][:, off:off + L], writes=[uk], sem=uk)
                    P.op("dve", lambda e, uu=uu: e.tensor_tensor(out=sa[:, 1:W], in0=uu[:, 1:W], in1=uu[:, 0:W - 1], op=ALU.add), reads=[uk], writes=[f"sa{tag}"])
                    cur, curk = sa, f"sa{tag}"
                    if g >= 1:
                        P.op("pool", lambda e: e.tensor_tensor(out=sb[:, 2:W - 1], in0=sa[:, 3:W], in1=sa[:, 1:W - 2], op=ALU.add), reads=[f"sa{tag}"], writes=[f"sb{tag}"])
                        cur, curk = sb, f"sb{tag}"
                    if g >= 2:
                        P.op("dve", lambda e: e.tensor_tensor(out=sa[:, 4:W - 3], in0=sb[:, 6:W - 1], in1=sb[:, 2:W - 5], op=ALU.add), reads=[f"sb{tag}"], writes=[f"sa{tag}"])
                        cur, curk = sa, f"sa{tag}"
                    if g >= 3:
                        P.op("pool", lambda e: e.tensor_tensor(out=sb[:, 8:W - 7], in0=sa[:, 12:W - 3], in1=sa[:, 4:W - 11], op=ALU.add), reads=[f"sa{tag}"], writes=[f"sb{tag}"])
                        cur, curk = sb, f"sb{tag}"
                    P.op("dve", lambda e, cur=cur, g=g: e.tensor_tensor(out=tmp[:], in0=cur[:, 16:16 + L], in1=rc[:, g, :], op=ALU.mult),
                         reads=[curk, f"rc{tag}"], writes=[f"tmp{tag}"])
                    P.op("pool", lambda e, uu=uu, i=i: e.tensor_tensor(out=pd[i][:], in0=tmp[:], in1=uu[:, 16:16 + L], op=ALU.subtract),
                         reads=[f"tmp{tag}", uk], writes=[f"pd{tag}{i}"])
                    P.dma("act", PD[c][:, off:off + L], pd[i][:], reads=[f"pd{tag}{i}"], writes=[], sem=f"pd{tag}{i}")

            for (off, L) in classes:
                do_class(off, L)

    def phase_attn(l):
        with P.phase("attn"):
            if l == 0:
                emit_convert(22, 64)
            ones = P.sbuf("ones", [128, 128], BF16)
            P.op("dve", lambda e: e.memset(ones[:], 1.0), writes=["ones"])
            kT = [P.sbuf(f"kT{i}", [128, T], BF16) for i in range(2)]
            Vg = [P.sbuf(f"Vg{i}", [128, 18, 128], BF16) for i in range(2)]
            qT = [P.sbuf(f"qT{i}", [128, T], BF16) for i in range(2)]
            pt = [P.sbuf(f"pt{i}", [128, 512], BF16) for i in range(4)]
            rden = [P.sbuf(f"rden{i}", [128, 512], F32) for i in range(2)]
            ob = [P.sbuf(f"ob{i}", [128, 512], BF16) for i in range(2)]
            sps = [P.psum(f"sps{i}", [128, 512], F32) for i in range(2)]
            ops_ = [P.psum(f"ops{i}", [128, 512], F32) for i in range(2)]
            dps = [P.psum(f"dps{i}", [128, 512], F32) for i in range(2)]
            scale = 128.0 ** -0.5
            st = {"nq": 0, "npt": 0, "nsp": 0}

            def do_qblock(g, gi, h, qi, c0, nqc, kts, st):
                oi = st["nq"] % 2
                st["nq"] += 1
                o_ps, d_ps = ops_[oi], dps[oi]
                nk = len(kts)

                def S(kt, si):
                    P.op("pe", lambda e: e.matmul(sps[si][:, 0:nqc], lhsT=kT[gi][:, kt * 128:(kt + 1) * 128], rhs=qT[qi][:, c0:c0 + nqc],
                                                  start=True, stop=True),
                         reads=[f"kT{gi}", f"qT{qi}"], writes=[f"sps{si}"])

                def step(ii, kt, si, pi):
                    P.op("act", lambda e: e.activation(out=pt[pi][:, 0:nqc], in_=sps[si][:, 0:nqc], func=AF.Exp, scale=scale),
                         reads=[f"sps{si}"], writes=[f"pt{pi}"])
                    P.op("pe", lambda e: e.matmul(o_ps[:, 0:nqc], lhsT=Vg[gi][:, kt, :], rhs=pt[pi][:, 0:nqc], start=(ii == 0), stop=(ii == nk - 1)),
                         reads=[f"Vg{gi}", f"pt{pi}"], writes=[f"ops{oi}"])
                    P.op("pe", lambda e: e.matmul(d_ps[:, 0:nqc], lhsT=ones[:], rhs=pt[pi][:, 0:nqc], start=(ii == 0), stop=(ii == nk - 1)),
                         reads=["ones", f"pt{pi}"], writes=[f"dps{oi}"])

                S(kts[0], st["nsp"] % 2)
                for ii, kt in enumerate(kts):
                    si = st["nsp"] % 2
                    st["nsp"] += 1
                    if ii + 1 < nk:
                        S(kts[ii + 1], st["nsp"] % 2)
                    pi = st["npt"] % 4
                    st["npt"] += 1
                    step(ii, kt, si, pi)
                P.op("dve", lambda e: e.reciprocal(out=rden[oi][:, 0:nqc], in_=d_ps[:, 0:nqc]), reads=[f"dps{oi}"], writes=[f"rden{oi}"])
                P.op("dve", lambda e: e.tensor_tensor(out=ob[oi][:, 0:nqc], in0=o_ps[:, 0:nqc], in1=rden[oi][:, 0:nqc], op=ALU.mult),
                     reads=[f"ops{oi}", f"rden{oi}"], writes=[f"ob{oi}"])
                P.dma("sp", AT[h][:, c0:c0 + nqc], ob[oi][:, 0:nqc], reads=[f"ob{oi}"], writes=[], sem=f"ob{oi}")

            for g in range(4):
                gi = g % 2
                P.dma("sp", kT[gi][:], KT[g], writes=[f"kT{gi}"], sem=f"kT{gi}")
                P.dma_split("sp", Vg[gi][:], V.rearrange("(kt p) c -> p kt c", p=128)[:, :, g * 128:(g + 1) * 128], 3, writes=[f"Vg{gi}"], sem=f"Vg{gi}")
                for hh in range(4):
                    h = g * 4 + hh
                    qi = h % 2
                    ncol = T if l == 0 else TL
                    P.dma("sp", qT[qi][:, 0:ncol], QT[h][:, 0:ncol], writes=[f"qT{qi}"], sem=f"qT{qi}")
                    qblocks = [(c0, 512, list(range(18))) for c0 in range(0, TL, 512)]
                    if l == 0:
                        qblocks.append((TL, TC, [16, 17]))
                    for (c0, nqc, kts) in qblocks:
                        do_qblock(g, gi, h, qi, c0, nqc, kts, st)

    def phase_mix(l):
        with P.phase("mix"):
            identf, _ = load_consts()
            blocks = BLOCKS_ALL if l == 0 else BLOCKS_LAT
            wpf = P.sbuf("wpf", [128, 8, 512], F32)
            wpb = P.sbuf("wpb", [128, 8, 512], BF16)
            P.dma("sp", wpf[:], w_pool[l].rearrange("g (kc p) d -> p (g kc) d", p=128), writes=["wpf"], sem="const2")
            P.op("dve", lambda e: e.tensor_copy(out=wpb[:], in_=wpf[:]), reads=["wpf"], writes=["wpb"])
            psr = P.sbuf("psr", [16, 128], F32)
            pscT = P.sbuf("pscT", [128, 16], F32)
            P.dma("sp", psr[:], pool_scale[l].rearrange("(j p) -> j p", p=128), writes=["psr"], sem="const3")
            ptp = P.psum("ptp", [128, 16], F32)
            P.op("pe", lambda e: e.transpose(out=ptp[:], in_=psr[:], identity=identf[0:16, 0:16]), reads=["psr", "identf"], writes=["ptp"])
            P.op("dve", lambda e: e.tensor_copy(out=pscT[:], in_=ptp[:]), reads=["ptp"], writes=["pscT"])
            pdb = [P.sbuf(f"pdb{i}", [128, 2, 512], BF16) for i in range(2)]
            gab = [P.sbuf(f"gab{i}", [128, 4, 512], BF16) for i in range(2)]
            gbb = [P.sbuf(f"gbb{i}", [128, 4, 512], BF16) for i in range(2)]
            m1 = [P.sbuf(f"m1{i}", [128, 512], F32) for i in range(2)]
            m2 = [P.sbuf(f"m2{i}", [128, 512], F32) for i in range(2)]
            mg = [P.sbuf(f"mg{i}", [128, 512], BF16) for i in range(3)]
            pb = [P.psum(f"pb{i}", [128, 512], F32) for i in range(2)]
            cnt = {"blk": 0, "ev": 0}
            cur = {}

            def per_block(cb, t0, nt):
                i = cnt["blk"] % 2
                cnt["blk"] += 1
                cur["i"] = i
                P.dma("sp", pdb[i][:, :, 0:nt], PD[2 * cb:2 * cb + 2].rearrange("c p t -> p c t")[:, :, t0:t0 + nt], writes=[f"pdb{i}"], sem=f"pdb{i}")
                P.dma("sp", gab[i][:, :, 0:nt], GAB[4 * cb:4 * cb + 4].rearrange("c p t -> p c t")[:, :, t0:t0 + nt], writes=[f"gab{i}"], sem=f"gab{i}")
                P.dma("sp", gbb[i][:, :, 0:nt], GAB[16 + 4 * cb:16 + 4 * cb + 4].rearrange("c p t -> p c t")[:, :, t0:t0 + nt], writes=[f"gbb{i}"], sem=f"gbb{i}")

            def evac(cb, t0, cc, nt, ps, pk):
                i = cur["i"]
                dc = cb * 4 + cc
                e2 = cnt["ev"] % 2
                e3 = cnt["ev"] % 3
                cnt["ev"] += 1
                p_b = pb[e2]
                for kc in range(2):
                    P.op("pe", lambda e, kc=kc: e.matmul(p_b[:, 0:nt], lhsT=wpb[:, cb * 2 + kc, cc * 128:(cc + 1) * 128], rhs=pdb[i][:, kc, 0:nt],
                                                         start=(kc == 0), stop=(kc == 1)),
                         reads=["wpb", f"pdb{i}"], writes=[f"pb{e2}"])
                P.op("dve", lambda e: e.tensor_tensor(out=m1[e2][:, 0:nt], in0=ps[:, 0:nt], in1=gab[i][:, cc, 0:nt], op=ALU.mult),
                     reads=[pk, f"gab{i}"], writes=[f"m1{e2}"])
                P.op("dve", lambda e: e.scalar_tensor_tensor(out=m2[e2][:, 0:nt], in0=p_b[:, 0:nt], scalar=pscT[:, dc:dc + 1], in1=gbb[i][:, cc, 0:nt],
                                                             op0=ALU.mult, op1=ALU.mult),
                     reads=[f"pb{e2}", "pscT", f"gbb{i}"], writes=[f"m2{e2}"])
                P.op("pool", lambda e: e.tensor_tensor(out=mg[e3][:, 0:nt], in0=m1[e2][:, 0:nt], in1=m2[e2][:, 0:nt], op=ALU.add),
                     reads=[f"m1{e2}", f"m2{e2}"], writes=[f"mg{e3}"])
                P.dma("act", MG[dc][:, t0:t0 + nt], mg[e3][:, 0:nt], reads=[f"mg{e3}"], writes=[], sem=f"mg{e3}")

            proj(w_br[l], list(range(4)), AT.rearrange("h p t -> p h t"), lambda cb: blocks, lambda cb: "feat", evac, per_block=per_block, npp=2)

    def phase_wout(l):
        with P.phase("wout"):
            blocks = BLOCKS_ALL if l == 0 else BLOCKS_LAT
            G = [P.sbuf(f"G{s}", [128, D], F32) for s in range(2)]
            for s in ([0, 1] if l == 0 else [0]):
                load_bc(G[s], f"G{s}", l, s, G_A, f"bcG{s}")
            xb = [P.sbuf(f"xo{i}", [128, 4, 512], F32) for i in range(2)]
            tt = [P.sbuf(f"to{i}", [128, 512], F32) for i in range(3)]
            cnt = {"ev": 0, "blk": 0}
            cur = {}

            def per_block(cb, t0, nt):
                b = cnt["blk"] % 2
                cnt["blk"] += 1
                cur["b"] = b
                P.dma("sp", xb[b][:, 0:nt // 128, :], xsrc(l, t0, nt, cb * 512, 512).rearrange("(t p) c -> p t c", p=128), writes=[f"xo{b}"], sem=f"xo{b}")

            def evac(cb, t0, ti, nt, ps, pk):
                r0 = t0 + ti * 128
                s = 0 if r0 < TL else 1
                i = cnt["ev"] % 3
                cnt["ev"] += 1
                b = cur["b"]
                P.op("dve", lambda e: e.tensor_tensor(out=tt[i][:], in0=ps[:], in1=G[s][:, cb * 512:(cb + 1) * 512], op=ALU.mult),
                     reads=[pk, f"G{s}"], writes=[f"to{i}"])
                P.op("pool", lambda e: e.tensor_tensor(out=tt[i][:], in0=tt[i][:], in1=xb[b][:, ti, :], op=ALU.add), reads=[f"to{i}", f"xo{b}"], writes=[f"to{i}"])
                P.dma("act", X[r0:r0 + 128, cb * 512:(cb + 1) * 512], tt[i][:], reads=[f"to{i}"], writes=[], sem=f"to{i}")

            proj(w_out[l], list(range(4)), MG.rearrange("h p t -> p h t"), lambda cb: blocks, lambda cb: "tok", evac, per_block=per_block)

    def phase_qpeer(l):
        with P.phase("qpeer"):
            blocks = BLOCKS_ALL if l == 0 else BLOCKS_LAT
            ev = [P.sbuf(f"ev{i}", [128, 512], F32) for i in range(3)]
            cnt = {"ev": 0}

            def evac(cb, t0, cc, nt, ps, pk):
                i = cnt["ev"] % 3
                cnt["ev"] += 1
                P.op("act", lambda e: e.copy(out=ev[i][:, 0:nt], in_=ps[:, 0:nt]), reads=[pk], writes=[f"ev{i}"])
                P.dma("act", QPT[cb * 4 + cc][:, t0:t0 + nt], ev[i][:, 0:nt], reads=[f"ev{i}"], writes=[], sem=f"ev{i}")

            proj(w_qp[l], list(range(4)), HT, lambda cb: blocks, lambda cb: "feat", evac)

    def phase_topk(l):
        with P.phase("topk"):
            identf, _ = load_consts()
            ntiles = (T if l == 0 else TL) // 128
            io16 = P.sbuf("io16", [128, 16], F32)
            P.dma("sp", io16[:], iota16_d, writes=["io16"], sem="const2")
            kraw = P.sbuf("kraw", [128, 16, 128], F32)
            keysT = P.sbuf("keysT", [128, 16, 128], F32)
            P.dma_split("sp", kraw[:], peer_keys[l].rearrange("h p k d -> k (h p) d"), 2, writes=["kraw"], sem="const3")
            pk4 = [P.psum(f"pk4{i}", [128, 4, 128], F32) for i in range(4)]
            for grp in range(4):
                for q in range(4):
                    hp = grp * 4 + q
                    P.op("pe", lambda e, hp=hp, q=q, grp=grp: e.transpose(out=pk4[grp][:, q, :], in_=kraw[:, hp, :], identity=identf[:]),
                         reads=["kraw", "identf"], writes=[f"pk4{grp}"])
                P.op("act", lambda e, grp=grp: e.copy(out=keysT[:, grp * 4:(grp + 1) * 4, :], in_=pk4[grp][:]), reads=[f"pk4{grp}"], writes=["keysT"])
            qt = [P.sbuf(f"qt{i}", [128, 16, 128], F32) for i in range(2)]
            S = P.sbuf("S", [128, 16, 128], F32)
            S2 = P.sbuf("S2", [128, 16, 128], F32)
            m = P.sbuf("m", [128, 16, 16], F32)
            ix = P.sbuf("ix", [128, 16, 16], U32)
            ixf = P.sbuf("ixf", [128, 16, 16], F32)
            i1s = P.sbuf("i1s", [128, 8, 16], F32)
            cand = P.sbuf("cand", [128, 8, 256], F32)
            cand2 = P.sbuf("cand2", [128, 8, 256], F32)
            ts = P.sbuf("ts", [128, 8, 16], F32)
            pos = P.sbuf("pos", [128, 8, 16], U32)
            au = P.sbuf("au", [128, 8, 16], U32)
            bu = P.sbuf("bu", [128, 8, 16], U32)
            af_ = P.sbuf("af", [128, 8, 16], F32)
            bf_ = P.sbuf("bf", [128, 8, 16], F32)
            oh = P.sbuf("oh", [128, 8, 16, 16], F32)
            isel = P.sbuf("isel", [128, 8, 16], F32)
            jsel = P.sbuf("jsel", [128, 8, 16], F32)
            ef = P.sbuf("ef", [128, 128], F32)
            eu = [P.sbuf(f"eu{i}", [128, 128], U32) for i in range(2)]
            dd = P.sbuf("dd", [128, 8, 16], F32)
            ee = P.sbuf("ee", [128, 8, 16], F32)
            zz = P.sbuf("zz", [128, 16], F32)
            gg = [P.sbuf(f"gg{i}", [128, 128], F32) for i in range(2)]
            NEG = -1e30
            dense = peer_mode == "dense"
            if dense:
                io128 = P.sbuf("io128", [128, 128], F32)
                P.dma("sp", io128[:], iota128_d, writes=["io128"], sem="const4")
                tp = P.psum("tp", [128, 3, 128], F32)
                ijg = P.sbuf("ijg", [128, 3, 128], F32)
                Aoh = [P.sbuf(f"Aoh{k}", [128, 16, 128], BF16) for k in range(2)]
                Boh = [P.sbuf(f"Boh{k}", [128, 128], BF16) for k in range(8)]
                gp = [P.psum(f"gp{k}", [128, 4, 128], F32) for k in range(2)]
                stg = [P.sbuf(f"stg{k}", [128, 128, 128], BF16) for k in range(2)]
                gst = {"b": 0, "g": 0}

            def gbuild(tt, i):
                r0 = tt * 128
                sg = tt % 2
                srcs = [isel[:].rearrange("p h k -> p (h k)"), jsel[:].rearrange("p h k -> p (h k)"), gg[i][:]]
                keys = ["sel0", "sel1", f"gg{i}"]
                for q in range(3):
                    P.op("pe", lambda e, q=q: e.transpose(out=tp[:, q, :], in_=srcs[q], identity=identf[:]), reads=[keys[q], "identf"], writes=["tp"])
                P.op("act", lambda e: e.copy(out=ijg[:], in_=tp[:]), reads=["tp"], writes=["ijg"])
                for grp in range(8):
                    a = grp % 2
                    for tq in range(16):
                        tl = grp * 16 + tq
                        P.op("pool", lambda e, a=a, tq=tq, tl=tl: e.tensor_scalar(out=Aoh[a][:, tq, :], in0=io128[:], scalar1=ijg[:, 0, tl:tl + 1], scalar2=None, op0=ALU.is_equal),
                             reads=["io128", "ijg"], writes=[f"Aoh{a}"])
                    for q4 in range(4):
                        gk = gst["g"] % 2
                        gst["g"] += 1
                        for q in range(4):
                            tl = grp * 16 + q4 * 4 + q
                            b = gst["b"] % 8
                            gst["b"] += 1
                            P.op("dve", lambda e, b=b, tl=tl: e.tensor_scalar(out=Boh[b][:], in0=io128[:], scalar1=ijg[:, 1, tl:tl + 1], scalar2=ijg[:, 2, tl:tl + 1],
                                                                              op0=ALU.is_equal, op1=ALU.mult),
                                 reads=["io128", "ijg"], writes=[f"Boh{b}"])
                            P.op("pe", lambda e, b=b, a=a, gk=gk, q=q, q4=q4: e.matmul(gp[gk][:, q, :], lhsT=Boh[b][:], rhs=Aoh[a][:, q4 * 4 + q, :], start=True, stop=True),
                                 reads=[f"Boh{b}", f"Aoh{a}"], writes=[f"gp{gk}"])
                        tl0 = grp * 16 + q4 * 4
                        P.op("act", lambda e, gk=gk, tl0=tl0, sg=sg: e.copy(out=stg[sg][:, :, tl0:tl0 + 4].rearrange("p i t -> p t i"), in_=gp[gk][:]),
                             reads=[f"gp{gk}"], writes=[f"stg{sg}"])
                dst = GTd.rearrange("i j t -> j i t")
                for k in range(16):
                    P.dma("sp", dst[:, k * 8:(k + 1) * 8, r0:r0 + 128], stg[sg][:, k * 8:(k + 1) * 8, :], reads=[f"stg{sg}"], writes=[], sem=f"stg{sg}")

            for tt in range(ntiles):
                r0 = tt * 128
                i = tt % 2
                P.dma_split("sp", qt[i][:], QPT.rearrange("c p t -> p c t")[:, :, r0:r0 + 128], 2, writes=[f"qt{i}"], sem=f"qt{i}")
                for grp in range(4):
                    for q in range(4):
                        hp = grp * 4 + q
                        P.op("pe", lambda e, hp=hp, q=q, grp=grp, i=i: e.matmul(pk4[grp][:, q, :], lhsT=qt[i][:, hp, :], rhs=keysT[:, hp, :], start=True, stop=True),
                             reads=[f"qt{i}", "keysT"], writes=[f"pk4{grp}"])
                    P.op("act", lambda e, grp=grp: e.copy(out=S[:, grp * 4:(grp + 1) * 4, :], in_=pk4[grp][:]), reads=[f"pk4{grp}"], writes=[f"S{grp}"])
                for hp in range(16):
                    sk = f"S{hp // 4}"
                    P.op("dve", lambda e, hp=hp: e.max(out=m[:, hp, 0:8], in_=S[:, hp, :]), reads=[sk], writes=[f"ma{hp}"])
                for hp in range(16):
                    sk = f"S{hp // 4}"
                    P.op("dve", lambda e, hp=hp: e.max_index(out=ix[:, hp, 0:8], in_max=m[:, hp, 0:8], in_values=S[:, hp, :]), reads=[sk, f"ma{hp}"], writes=[f"ixa{hp}"])
                for hp in range(16):
                    sk = f"S{hp // 4}"
                    P.op("dve", lambda e, hp=hp: e.match_replace(out=S2[:, hp, :], in_to_replace=m[:, hp, 0:8], in_values=S[:, hp, :], imm_value=NEG),
                         reads=[sk, f"ma{hp}"], writes=[f"S2_{hp}"])
                for hp in range(16):
                    P.op("dve", lambda e, hp=hp: e.max(out=m[:, hp, 8:16], in_=S2[:, hp, :]), reads=[f"S2_{hp}"], writes=[f"mb{hp}"])
                for hp in range(16):
                    P.op("dve", lambda e, hp=hp: e.max_index(out=ix[:, hp, 8:16], in_max=m[:, hp, 8:16], in_values=S2[:, hp, :]), reads=[f"S2_{hp}", f"mb{hp}"], writes=[f"ixb{hp}"])
                mkeys = [f"ma{hp}" for hp in range(16)] + [f"mb{hp}" for hp in range(16)]
                ixkeys = [f"ixa{hp}" for hp in range(16)] + [f"ixb{hp}" for hp in range(16)]
                P.op("dve", lambda e: e.tensor_copy(out=ixf[:], in_=ix[:]), reads=ixkeys, writes=["ixf"])
                mv = m[:].rearrange("p (h two) k -> p h two k", two=2)
                iv = ixf[:].rearrange("p (h two) k -> p h two k", two=2)
                cv = cand[:].rearrange("p h (a b) -> p h a b", a=16)
                P.op("dve", lambda e: e.tensor_tensor(out=cv, in0=mv[:, :, 0, :].unsqueeze(3).to_broadcast([128, 8, 16, 16]),
                                                      in1=mv[:, :, 1, :].unsqueeze(2).to_broadcast([128, 8, 16, 16]), op=ALU.add),
                     reads=mkeys, writes=["cand"])
                for h in range(8):
                    P.op("dve", lambda e, h=h: e.max(out=ts[:, h, 0:8], in_=cand[:, h, :]), reads=["cand"], writes=[f"tsa{h}"])
                for h in range(8):
                    P.op("dve", lambda e, h=h: e.max_index(out=pos[:, h, 0:8], in_max=ts[:, h, 0:8], in_values=cand[:, h, :]), reads=["cand", f"tsa{h}"], writes=[f"posa{h}"])
                for h in range(8):
                    P.op("dve", lambda e, h=h: e.match_replace(out=cand2[:, h, :], in_to_replace=ts[:, h, 0:8], in_values=cand[:, h, :], imm_value=NEG),
                         reads=["cand", f"tsa{h}"], writes=[f"c2_{h}"])
                for h in range(8):
                    P.op("dve", lambda e, h=h: e.max(out=ts[:, h, 8:16], in_=cand2[:, h, :]), reads=[f"c2_{h}"], writes=[f"tsb{h}"])
                for h in range(8):
                    P.op("dve", lambda e, h=h: e.max_index(out=pos[:, h, 8:16], in_max=ts[:, h, 8:16], in_values=cand2[:, h, :]), reads=[f"c2_{h}", f"tsb{h}"], writes=[f"posb{h}"])
                tskeys = [f"tsa{h}" for h in range(8)] + [f"tsb{h}" for h in range(8)]
                poskeys = [f"posa{h}" for h in range(8)] + [f"posb{h}" for h in range(8)]
                P.op("dve", lambda e: e.tensor_single_scalar(out=au[:], in_=pos[:], scalar=4, op=ALU.logical_shift_right), reads=poskeys, writes=["au"])
                P.op("dve", lambda e: e.tensor_single_scalar(out=bu[:], in_=pos[:], scalar=15, op=ALU.bitwise_and), reads=poskeys, writes=["bu"])
                P.op("dve", lambda e: e.tensor_copy(out=af_[:], in_=au[:]), reads=["au"], writes=["af"])
                P.op("dve", lambda e: e.tensor_copy(out=bf_[:], in_=bu[:]), reads=["bu"], writes=["bf"])
                for (sel, xf, which, key) in ((isel, af_, 0, "af"), (jsel, bf_, 1, "bf")):
                    P.op("dve", lambda e, xf=xf: e.tensor_tensor(out=oh[:], in0=io16[:].unsqueeze(1).unsqueeze(1).to_broadcast([128, 8, 16, 16]),
                                                                  in1=xf[:].unsqueeze(3).to_broadcast([128, 8, 16, 16]), op=ALU.is_equal),
                         reads=["io16", key], writes=["oh"])
                    P.op("dve", lambda e, which=which: e.tensor_tensor(out=oh[:], in0=oh[:], in1=iv[:, :, which, :].unsqueeze(2).to_broadcast([128, 8, 16, 16]), op=ALU.mult),
                         reads=["oh", "ixf"], writes=["oh"])
                    P.op("dve", lambda e, sel=sel: e.tensor_reduce(out=sel[:], in_=oh[:], axis=AX.X, op=ALU.add), reads=["oh"], writes=["sel%d" % which])
                P.op("dve", lambda e: e.scalar_tensor_tensor(out=ef[:], in0=isel[:].rearrange("p h k -> p (h k)"), scalar=128.0, in1=jsel[:].rearrange("p h k -> p (h k)"),
                                                             op0=ALU.mult, op1=ALU.add),
                     reads=["sel0", "sel1"], writes=["ef"])
                if l > 0:
                    P.op("dve", lambda e: e.tensor_scalar(out=ef[:], in0=ef[:], scalar1=float(l * NEXP), scalar2=None, op0=ALU.add), reads=["ef"], writes=["ef"])
                P.op("dve", lambda e, i=i: e.tensor_copy(out=eu[i][:], in_=ef[:]), reads=["ef"], writes=[f"eu{i}"])
                P.dma("sp", EIDX[r0:r0 + 128, :], eu[i][:], reads=[f"eu{i}"], writes=[], sem=f"eu{i}")
                P.op("dve", lambda e: e.tensor_tensor(out=dd[:], in0=ts[:], in1=ts[:, :, 0:1].to_broadcast([128, 8, 16]), op=ALU.subtract), reads=tskeys, writes=["dd"])
                P.op("act", lambda e: e.activation(out=ee[:], in_=dd[:], func=AF.Exp), reads=["dd"], writes=["ee"])
                P.op("dve", lambda e: e.tensor_reduce(out=zz[:, 0:8], in_=ee[:], axis=AX.X, op=ALU.add), reads=["ee"], writes=["zz"])
                P.op("dve", lambda e: e.reciprocal(out=zz[:, 8:16], in_=zz[:, 0:8]), reads=["zz"], writes=["zz"])
                P.op("dve", lambda e, i=i: e.tensor_tensor(out=gg[i][:].rearrange("p (h k) -> p h k", h=8), in0=ee[:], in1=zz[:, 8:16].unsqueeze(2).to_broadcast([128, 8, 16]), op=ALU.mult),
                     reads=["ee", "zz"], writes=[f"gg{i}"])
                P.dma("sp", GATE[r0:r0 + 128, :], gg[i][:], reads=[f"gg{i}"], writes=[], sem=f"gg{i}")
                if dense:
                    gbuild(tt, i)

    def phase_gather(l):
        with P.phase("gather"):
            P.wait_persistent()
            identf, identb = load_consts(need_bf=True)
            ntiles = (T if l == 0 else TL) // 128
            G = [P.sbuf(f"G{s}", [128, D], F32) for s in range(2)]
            for s in ([0, 1] if l == 0 else [0]):
                load_bc(G[s], f"G{s}", l, s, G_F, f"bcG{s}")
            NS = 8
            LOOK = 5
            uv = [P.sbuf(f"uv{i}", [128, 2 * D], BF16) for i in range(NS)]
            h2 = [P.sbuf(f"h2{i}", [128, D], F32) for i in range(2)]
            xt = [P.sbuf(f"xt{i}", [128, D], F32) for i in range(2)]
            junk = P.sbuf("junk", [128, D], F32)
            eix = [P.sbuf(f"eix{i}", [128, 128], U32) for i in range(2)]
            gat = [P.sbuf(f"gat{i}", [128, 128], F32) for i in range(2)]
            act = [P.sbuf(f"act{i}", [128, 128], F32) for i in range(2)]
            ge = [P.sbuf(f"ge{i}", [128, 128], F32) for i in range(2)]
            dg = [P.sbuf(f"dg{i}", [128, 128], BF16) for i in range(4)]
            tmp = [P.sbuf(f"tmp{i}", [128, 512], F32) for i in range(2)]
            acc = [[P.psum(f"acc{a}_{b}", [128, 512], F32) for b in range(4)] for a in range(2)]
            st = {"ntmp": 0}
            items = [(tt, sidx) for tt in range(ntiles) for sidx in range(128)]

            def loads(tt):
                r0 = tt * 128
                i = tt % 2
                P.dma("sp", h2[i][:], H2[r0:r0 + 128, :], writes=[f"h2{i}"], sem=f"h2{i}")
                P.dma("sp", xt[i][:], X[r0:r0 + 128, :], writes=[f"xt{i}"], sem=f"xt{i}")
                P.dma("sp", eix[i][:], EIDX[r0:r0 + 128, :], writes=[f"eix{i}"], sem=f"eix{i}")
                P.dma("sp", gat[i][:], GATE[r0:r0 + 128, :], writes=[f"gat{i}"], sem=f"gat{i}")

            def gather(n):
                tt, sidx = items[n]
                i = tt % 2
                u = n % NS
                if sidx == 0:
                    loads(tt)
                P.dma_fn("pool", lambda e: e.indirect_dma_start(
                    out=uv[u][:], out_offset=None, in_=UV, in_offset=bass.IndirectOffsetOnAxis(ap=eix[i][:, sidx:sidx + 1], axis=0)),
                    reads=[f"eix{i}"], writes=[f"uv{u}"], sem=f"uv{u}")

            def dot(n):
                tt, sidx = items[n]
                i = tt % 2
                u = n % NS
                P.op("dve", lambda e: e.scalar_tensor_tensor(out=junk[:], in0=h2[i][:], scalar=1.0, in1=uv[u][:, 0:D], op0=ALU.mult, op1=ALU.mult,
                                                             accum_out=act[i][:, sidx:sidx + 1]),
                     reads=[f"h2{i}", f"uv{u}"], writes=[f"a{i}_{sidx}"])
                P.op("act", lambda e: e.activation(out=ge[i][:, sidx:sidx + 1], in_=act[i][:, sidx:sidx + 1], func=AF.Gelu),
                     reads=[f"a{i}_{sidx}"], writes=[f"g{i}_{sidx}"])

            def combine(n):
                tt, sidx = items[n]
                i = tt % 2
                u = n % NS
                d = n % 4
                P.op("dve", lambda e: e.tensor_scalar(out=dg[d][:], in0=identb[:], scalar1=ge[i][:, sidx:sidx + 1], scalar2=gat[i][:, sidx:sidx + 1],
                                                      op0=ALU.mult, op1=ALU.mult),
                     reads=["identb", f"g{i}_{sidx}", f"gat{i}"], writes=[f"dg{d}"])
                for db in range(4):
                    P.op("pe", lambda e, db=db: e.matmul(acc[i][db][:], lhsT=dg[d][:], rhs=uv[u][:, D + db * 512:D + (db + 1) * 512],
                                                         start=(sidx == 0), stop=(sidx == 127)),
                         reads=[f"dg{d}", f"uv{u}"], writes=[f"acc{i}_{db}"])
                if sidx == 127:
                    finalize(tt)

            def finalize(tt):
                r0 = tt * 128
                i = tt % 2
                s = 0 if r0 < TL else 1
                for db in range(4):
                    tq = st["ntmp"] % 2
                    st["ntmp"] += 1
                    P.op("dve", lambda e, tq=tq, db=db: e.tensor_tensor(out=tmp[tq][:], in0=acc[i][db][:], in1=G[s][:, db * 512:(db + 1) * 512], op=ALU.mult),
                         reads=[f"acc{i}_{db}", f"G{s}"], writes=[f"tmp{tq}"])
                    P.op("dve", lambda e, tq=tq, db=db: e.tensor_tensor(out=xt[i][:, db * 512:(db + 1) * 512], in0=tmp[tq][:], in1=xt[i][:, db * 512:(db + 1) * 512], op=ALU.add),
                         reads=[f"tmp{tq}", f"xt{i}"], writes=[f"xt{i}"])
                P.dma("sp", X[r0:r0 + 128, :], xt[i][:], reads=[f"xt{i}"], writes=[], sem=f"xst{i}")

            N = len(items)
            for n in range(min(LOOK, N)):
                gather(n)
            for n in range(N):
                if n + LOOK < N:
                    gather(n + LOOK)
                dot(n)
                if n >= 1:
                    combine(n - 1)
            combine(N - 1)

    def phase_dense(l):
        with P.phase("dense"):
            _, identb = load_consts(need_bf=True)
            ntok = T if l == 0 else TL
            groups = [(0, 768), (768, 768), (1536, ntok - 1536)]
            G = [P.sbuf(f"G{s}", [128, D], F32) for s in range(2)]
            for s in ([0, 1] if l == 0 else [0]):
                load_bc(G[s], f"G{s}", l, s, G_F, f"bcG{s}")
            hT = P.sbuf("hT", [128, 16, 768], BF16)
            acc = P.sbuf("acc", [128, 6, D], F32)
            GA = P.sbuf("GA", [128, 8, 768], BF16)
            Vb = P.sbuf("Vb", [128, 8, D], BF16)
            Ub = [P.sbuf(f"Ub{k}", [128, D], BF16) for k in range(3)]
            UT = [P.sbuf(f"UT{k}", [128, 16, 128], BF16) for k in range(2)]
            gt = [P.sbuf(f"gt{k}", [128, 768], BF16) for k in range(3)]
            gl = [P.sbuf(f"gl{k}", [128, 512], BF16) for k in range(2)]
            xt = [P.sbuf(f"xt{k}", [128, D], F32) for k in range(2)]
            ptu = [P.psum(f"ptu{k}", [128, 16, 128], BF16) for k in range(2)]
            ps1 = [P.psum(f"ps1{k}", [128, 512], F32) for k in range(2)]
            ps2 = [P.psum(f"ps2{k}", [128, 512], F32) for k in range(2)]
            st = {"c": 0, "p1": 0, "p2": 0, "x": 0}

            def chunk(c, ci, g0, gn, nblocks):
                k3 = st["c"] % 3
                k2 = st["c"] % 2
                st["c"] += 1
                row0 = l * NEXP + c * 128
                P.dma("sp", Ub[k3][:], UV[row0:row0 + 128, 0:D], writes=[f"Ub{k3}"], sem=f"Ub{k3}")
                P.dma("sp", Vb[:, ci, :], UV[row0:row0 + 128, D:2 * D], writes=[f"Vb{ci}"], sem=f"Vb{ci}")
                P.dma("sp", gt[k3][:, 0:gn], GTd[c][:, g0:g0 + gn], writes=[f"gt{k3}"], sem=f"gt{k3}")
                for j in range(16):
                    P.op("pe", lambda e, j=j: e.transpose(out=ptu[k2][:, j, :], in_=Ub[k3][:, j * 128:(j + 1) * 128], identity=identb[:]),
                         reads=[f"Ub{k3}", "identb"], writes=[f"ptu{k2}"])
                P.op("pool" if False else "dve", lambda e: e.tensor_copy(out=UT[k2][:], in_=ptu[k2][:]), reads=[f"ptu{k2}"], writes=[f"UT{k2}"])
                for (b0, bn) in nblocks:
                    p1 = st["p1"] % 2
                    st["p1"] += 1
                    for j in range(16):
                        P.op("pe", lambda e, j=j, p1=p1: e.matmul(ps1[p1][:, 0:bn], lhsT=UT[k2][:, j, :], rhs=hT[:, j, b0:b0 + bn], start=(j == 0), stop=(j == 15)),
                             reads=[f"UT{k2}", "hT"], writes=[f"ps1{p1}"])
                    P.op("act", lambda e, p1=p1: e.activation(out=gl[p1][:, 0:bn], in_=ps1[p1][:, 0:bn], func=AF.Gelu), reads=[f"ps1{p1}"], writes=[f"gl{p1}"])
                    P.op("pool", lambda e, p1=p1: e.tensor_tensor(out=GA[:, ci, b0:b0 + bn], in0=gl[p1][:, 0:bn], in1=gt[k3][:, b0:b0 + bn], op=ALU.mult),
                         reads=[f"gl{p1}", f"gt{k3}"], writes=[f"GA{ci}"])

            def combine(cg, ntile):
                for ti in range(ntile):
                    for db in range(4):
                        p2 = st["p2"] % 2
                        st["p2"] += 1
                        for ci in range(8):
                            P.op("pe", lambda e, ci=ci, p2=p2: e.matmul(ps2[p2][:], lhsT=GA[:, ci, ti * 128:(ti + 1) * 128], rhs=Vb[:, ci, db * 512:(db + 1) * 512],
                                                                        start=(ci == 0), stop=(ci == 7)),
                                 reads=[f"GA{ci}", f"Vb{ci}"], writes=[f"ps2{p2}"])
                        if cg == 0:
                            P.op("dve", lambda e, p2=p2: e.tensor_copy(out=acc[:, ti, db * 512:(db + 1) * 512], in_=ps2[p2][:]), reads=[f"ps2{p2}"], writes=[f"acc{ti}_{db}"])
                        else:
                            P.op("dve", lambda e, p2=p2: e.tensor_tensor(out=acc[:, ti, db * 512:(db + 1) * 512], in0=ps2[p2][:], in1=acc[:, ti, db * 512:(db + 1) * 512], op=ALU.add),
                                 reads=[f"ps2{p2}", f"acc{ti}_{db}"], writes=[f"acc{ti}_{db}"])

            def finalize(g0, ti):
                r0 = g0 + ti * 128
                s = 0 if r0 < TL else 1
                k = st["x"] % 2
                st["x"] += 1
                P.dma("sp", xt[k][:], X[r0:r0 + 128, :], writes=[f"xt{k}"], sem=f"xt{k}")
                P.op("pool", lambda e: e.tensor_tensor(out=acc[:, ti, :], in0=acc[:, ti, :], in1=G[s][:], op=ALU.mult),
                     reads=[f"acc{ti}_{db}" for db in range(4)] + [f"G{s}"], writes=[f"acc{ti}_{db}" for db in range(4)])
                P.op("pool", lambda e: e.tensor_tensor(out=xt[k][:], in0=xt[k][:], in1=acc[:, ti, :], op=ALU.add),
                     reads=[f"acc{ti}_{db}" for db in range(4)] + [f"xt{k}"], writes=[f"xt{k}"])
                P.dma("sp", X[r0:r0 + 128, :], xt[k][:], reads=[f"xt{k}"], writes=[], sem=f"xst{k}")

            for (g0, gn) in groups:
                ntile = gn // 128
                nblocks = [(0, 384), (384, 384)] if gn == 768 else [(0, gn)]
                P.dma_split("sp", hT[:, :, 0:gn], HT[:, :, g0:g0 + gn], 2, writes=["hT"], sem="hT")
                for cg in range(16):
                    for ci in range(8):
                        chunk(cg * 8 + ci, ci, g0, gn, nblocks)
                    combine(cg, ntile)
                for ti in range(ntile):
                    finalize(g0, ti)

    def phase_final():
        with P.phase("final"):
            fg = P.sbuf("fg", [128, D], F32)
            P.dma("sp", fg[:], final_gain.partition_broadcast(128), writes=["fg"], sem="const")
            xt = [P.sbuf(f"xt{i}", [128, D], F32) for i in range(3)]
            junk = P.sbuf("junk", [128, D], F32)
            st = [P.sbuf(f"st{i}", [128, 4], F32) for i in range(3)]
            for tt in range(TL // 128):
                r0 = tt * 128
                i = tt % 3
                x_t, s_t = xt[i], st[i]
                P.dma("sp", x_t[:], X[r0:r0 + 128, :], writes=[f"xt{i}"], sem=f"xt{i}")
                P.op("act", lambda e, x_t=x_t, s_t=s_t: e.activation(out=junk[:], in_=x_t[:], func=AF.Square, accum_out=s_t[:, 0:1]),
                     reads=[f"xt{i}"], writes=["junk", f"st{i}"])
                P.op("dve", lambda e, s_t=s_t: e.tensor_scalar(out=s_t[:, 1:2], in0=s_t[:, 0:1], scalar1=1.0 / D, scalar2=EPS, op0=ALU.mult, op1=ALU.add),
                     reads=[f"st{i}"], writes=[f"st{i}"])
                P.op("act", lambda e, s_t=s_t: e.sqrt(out=s_t[:, 2:3], in_=s_t[:, 1:2]), reads=[f"st{i}"], writes=[f"st{i}"])
                P.op("dve", lambda e, s_t=s_t: e.reciprocal(out=s_t[:, 3:4], in_=s_t[:, 2:3]), reads=[f"st{i}"], writes=[f"st{i}"])
                P.op("dve", lambda e, x_t=x_t, s_t=s_t: e.scalar_tensor_tensor(out=x_t[:], in0=x_t[:], scalar=s_t[:, 3:4], in1=fg[:], op0=ALU.mult, op1=ALU.mult),
                     reads=[f"xt{i}", f"st{i}", "fg"], writes=[f"xt{i}"])
                P.dma("pool", out_d[r0:r0 + 128, :], x_t[:], reads=[f"xt{i}"], writes=[], sem=f"ost{i}")

    def stop(l, name):
        return stop_after is not None and stop_after == (l, name)

    done = False
    for l in range(nlayers):
        blocks = BLOCKS_ALL
        steps = [
            ("mod", lambda: phase_mod(l)),
            ("modA", lambda: phase_modulate(l, SH_A, SC_A, False, BLOCKS_ALL, False)),
            ("inproj", lambda: phase_inproj(l)),
            ("pool", lambda: phase_pool(l)),
            ("attn", lambda: phase_attn(l)),
            ("mix", lambda: phase_mix(l)),
            ("wout", lambda: phase_wout(l)),
            ("modF", lambda: phase_modulate(l, SH_F, SC_F, True, BLOCKS_ALL if l == 0 else BLOCKS_LAT, True)),
            ("qpeer", lambda: phase_qpeer(l)),
            ("topk", lambda: phase_topk(l)),
            ("gather", (lambda: phase_dense(l)) if peer_mode == "dense" else (lambda: phase_gather(l))),
        ]
        for name, fn in steps:
            fn()
            if stop(l, name):
                done = True
                break
        if done:
            break
    if not done:
        phase_final()
    P.close()
    return nc, P


def _consts():
    t = np.arange(TL)
    row = (t // 64).astype(np.float32)
    col = (t % 64).astype(np.float32)
    inv = (np.float32(10000.0) ** (-np.arange(32, dtype=np.float32) / np.float32(32))).astype(np.float32)
    ar = (row[:, None] * inv[None, :]).astype(np.float32)
    ac = (col[:, None] * inv[None, :]).astype(np.float32)
    cr, sr, cc, sc = np.cos(ar), np.sin(ar), np.cos(ac), np.sin(ac)
    ropeC = np.concatenate([cr, cr, cc, cc], axis=1).astype(np.float32)
    ropeS = np.concatenate([-sr, sr, -sc, sc], axis=1).astype(np.float32)
    rc = np.zeros((4, T), np.float32)
    for g, w in enumerate((2, 4, 8, 16)):
        for (off, L) in ((0, TL), (TL, TC)):
            tt = np.arange(L)
            lo = np.clip(tt - w // 2, 0, L)
            hi = np.clip(tt + (w - w // 2), 0, L)
            rc[g, off:off + L] = 1.0 / (hi - lo).astype(np.float32)
    identf = np.eye(128, dtype=np.float32)
    iota16 = np.tile(np.arange(16, dtype=np.float32)[None, :], (128, 1))
    iota128 = np.tile(np.arange(128, dtype=np.float32)[None, :], (128, 1))
    return dict(ropeC=ropeC, ropeS=ropeS, rcnt=rc, identf=identf, iota16=iota16, iota128=iota128)


def make_in_map(inputs, b):
    f = lambda a: np.ascontiguousarray(np.asarray(a, dtype=np.float32))
    m = dict(
        x=f(inputs["x"][b]), ctx=f(inputs["ctx"][b]),
        cvec=f(np.stack([np.asarray(inputs["c"][b]), np.asarray(inputs["c_ctx"])], axis=0)),
    )
    for k in ["w_ada", "b_ada", "w_in", "q_gain", "k_gain", "w_br_attn", "w_pool", "pool_scale", "w_out", "w_q_peer",
              "peer_keys", "peer_u", "peer_v", "final_gain"]:
        m[k] = f(inputs[k])
    m.update(_consts())
    return m


def kernel(**inputs):
    nc, _ = build_program()
    shared = None
    in_maps = []
    for b in range(8):
        m = make_in_map(inputs, b) if shared is None else dict(shared)
        if shared is None:
            shared = m
        else:
            m["x"] = np.ascontiguousarray(np.asarray(inputs["x"][b], dtype=np.float32))
            m["ctx"] = np.ascontiguousarray(np.asarray(inputs["ctx"][b], dtype=np.float32))
            m["cvec"] = np.ascontiguousarray(np.stack([np.asarray(inputs["c"][b]), np.asarray(inputs["c_ctx"])], axis=0).astype(np.float32))
        in_maps.append(m)
    res = run_bass_kernel_spmd(nc, in_maps, core_ids=list(range(8)))
    return np.stack([np.asarray(r["out"], dtype=np.float32) for r in res.results], axis=0)
```

```python
from contextlib import ExitStack, contextmanager
import numpy as np
import concourse.bass as bass
import concourse.mybir as mybir
from concourse.bass_utils import run_bass_kernel_spmd

F32 = mybir.dt.float32
BF16 = mybir.dt.bfloat16
U32 = mybir.dt.uint32
AF = mybir.ActivationFunctionType
ALU = mybir.AluOpType
AX = mybir.AxisListType

ENGS = ["pe", "act", "dve", "pool", "sp"]

D = 2048
TL = 2048
TC = 256
T = TL + TC
NEXP = 16384
EPS = 1e-6
SH_A, SC_A, G_A, SH_F, SC_F, G_F = range(6)


class Prog:
    def __init__(self, nc, same_engine_sync=True):
        self.nc = nc
        self.stack = ExitStack()
        self.pstack = None
        self.streams = {e: [] for e in ENGS}
        self.sems = {}
        self.count = {}
        self.waited = {e: {} for e in ENGS}
        self.last_write = {}
        self.reads = {}
        self.same_engine_sync = same_engine_sync
        self.n_ops = 0
        self.phase_sems = {}
        self.persist = set()
        self.uid = 0
        for e in ENGS:
            self._sem("e_" + e)

    def _sem(self, name):
        if name not in self.sems:
            self.sems[name] = self.stack.enter_context(self.nc.semaphore(name))
            self.count[name] = 0
        return self.sems[name]

    def sbuf(self, name, shape, dtype):
        self.uid += 1
        return self.pstack.enter_context(self.nc.sbuf_tensor(f"{name}_s{self.uid}", list(shape), dtype))

    def psum(self, name, shape, dtype):
        self.uid += 1
        return self.pstack.enter_context(self.nc.psum_tensor(f"{name}_p{self.uid}", list(shape), dtype))

    def _wait(self, eng, sem, val):
        if sem == "e_pe" and eng == "pe":
            return
        if sem == "e_" + eng and not self.same_engine_sync:
            return
        if self.waited[eng].get(sem, 0) >= val:
            return
        self.waited[eng][sem] = val
        self.streams[eng].append(("wait", sem, val))

    def _deps(self, eng, reads, writes):
        deps = {}
        for k in reads:
            lw = self.last_write.get(k)
            if lw:
                deps[lw[0]] = max(deps.get(lw[0], 0), lw[1])
        for k in writes:
            lw = self.last_write.get(k)
            if lw:
                deps[lw[0]] = max(deps.get(lw[0], 0), lw[1])
            for s, v in self.reads.get(k, {}).items():
                deps[s] = max(deps.get(s, 0), v)
        for s, v in deps.items():
            self._wait(eng, s, v)

    def _record(self, ev, reads, writes):
        for k in reads:
            d = self.reads.setdefault(k, {})
            d[ev[0]] = max(d.get(ev[0], 0), ev[1])
        for k in writes:
            self.last_write[k] = ev
            self.reads[k] = {}

    def op(self, eng, fn, reads=(), writes=()):
        self._deps(eng, reads, writes)
        sem = "e_" + eng
        self.count[sem] += 1
        ev = (sem, self.count[sem])
        self.streams[eng].append(("op", fn, sem, 1))
        self._record(ev, reads, writes)
        self.n_ops += 1

    def dma(self, queue, out, in_, reads=(), writes=(), sem=None, **kw):
        self.dma_fn(queue, lambda e, o=out, i=in_, kw=kw: e.dma_start(out=o, in_=i, **kw), reads, writes, sem)

    def dma_fn(self, queue, fn, reads=(), writes=(), sem=None):
        self._deps(queue, reads, writes)
        sem = sem or "default"
        if sem.startswith("x_"):
            self.persist.add(sem)
        else:
            if sem not in self.phase_sems:
                self.phase_sems[sem] = "d_%d" % len(self.phase_sems)
            sem = self.phase_sems[sem]
        self._sem(sem)
        self.count[sem] += 16
        ev = (sem, self.count[sem])
        self.streams[queue].append(("op", fn, sem, 16))
        self._record(ev, reads, writes)
        self.n_ops += 1

    def dma_split(self, queue, out, in_, n, reads=(), writes=(), sem=None):
        a = out.shape[1]
        step = (a + n - 1) // n
        for k in range(0, a, step):
            self.dma(queue, out[:, k:min(a, k + step), :], in_[:, k:min(a, k + step), :], reads=reads, writes=writes, sem=sem)

    def wait_persistent(self):
        for e in ENGS:
            for s in sorted(self.persist):
                self._wait(e, s, self.count[s])
        self.persist = set()

    def barrier(self):
        for e in ENGS:
            for s, c in self.count.items():
                if c > 0 and s not in self.persist:
                    self._wait(e, s, c)
        self.last_write = {}
        self.reads = {}

    def emit_block(self):
        nc = self.nc
        streams = self.streams
        self.streams = {e: [] for e in ENGS}
        with nc.Block() as block:
            def replay(name):
                def f(engine):
                    for rec in streams[name]:
                        if rec[0] == "wait":
                            engine.wait_ge(self.sems[rec[1]], rec[2])
                        else:
                            rec[1](engine).then_inc(self.sems[rec[2]], rec[3])
                return f
            block.tensor(replay("pe"))
            block.scalar(replay("act"))
            block.vector(replay("dve"))
            block.gpsimd(replay("pool"))
            block.sync(replay("sp"))

    @contextmanager
    def phase(self, name=""):
        self.pstack = ExitStack()
        self.phase_sems = {}
        try:
            yield
            self.barrier()
            self.emit_block()
        finally:
            self.pstack.close()
            self.pstack = None

    def close(self):
        self.stack.close()


ALL_PHASES = ["mod", "modA", "inproj", "pool", "attn", "mix", "wout", "modF", "qpeer", "topk", "gather"]


def build_program(dbg=(), nlayers=2, stop_after=None, same_engine_sync=True, peer_mode="gather"):
    nc = bass.Bass("TRN2", target_bir_lowering=False)

    def inp(name, shape, dt=F32):
        return nc.dram_tensor(name, list(shape), dt, kind="ExternalInput").ap()

    def scratch(name, shape, dt):
        kind = "ExternalOutput" if name in dbg else "Internal"
        return nc.dram_tensor(name, list(shape), dt, kind=kind).ap()

    x_in = inp("x", [TL, D])
    ctx_in = inp("ctx", [TC, D])
    cvec = inp("cvec", [2, D])
    w_ada = inp("w_ada", [2, D, 6 * D])
    b_ada = inp("b_ada", [2, 6 * D])
    w_in = inp("w_in", [2, D, 8192])
    q_gain = inp("q_gain", [2, 128])
    k_gain = inp("k_gain", [2, 128])
    w_br = inp("w_br_attn", [2, D, D])
    w_pool = inp("w_pool", [2, 4, 256, 512])
    pool_scale = inp("pool_scale", [2, D])
    w_out = inp("w_out", [2, D, D])
    w_qp = inp("w_q_peer", [2, D, D])
    peer_keys = inp("peer_keys", [2, 8, 2, 128, 128])
    peer_u = inp("peer_u", [2, NEXP, D])
    peer_v = inp("peer_v", [2, NEXP, D])
    final_gain = inp("final_gain", [D])
    ropeC = inp("ropeC", [TL, 128])
    ropeS = inp("ropeS", [TL, 128])
    rcnt = inp("rcnt", [4, T])
    identf_d = inp("identf", [128, 128])
    iota16_d = inp("iota16", [128, 16])
    iota128_d = inp("iota128", [128, 128])
    out_d = nc.dram_tensor("out", [TL, D], F32, kind="ExternalOutput").ap()

    MODROW = scratch("MODROW", [2, 2, 6 * D], F32)
    X = scratch("X", [T, D], F32)
    HT = scratch("HT", [128, 16, T], BF16)
    QT = scratch("QT", [16, 128, T], BF16)
    KT = scratch("KT", [4, 128, T], BF16)
    V = scratch("V", [T, 512], BF16)
    PL = scratch("PL", [8, 128, T], F32)
    PD = scratch("PD", [8, 128, T], BF16)
    GAB = scratch("GAB", [32, 128, T], BF16)
    AT = scratch("AT", [16, 128, T], BF16)
    MG = scratch("MG", [16, 128, T], BF16)
    H2 = scratch("H2", [T, D], F32)
    QPT = scratch("QPT", [16, 128, T], F32)
    EIDX = scratch("EIDX", [T, 128], U32)
    GATE = scratch("GATE", [T, 128], F32)
    GTd = scratch("GTd", [128, 128, T], BF16)
    UV = scratch("UV", [2 * NEXP, 2 * D], BF16)

    P = Prog(nc, same_engine_sync=same_engine_sync)

    BLOCKS_ALL = [(0, 512), (512, 512), (1024, 512), (1536, 512), (2048, 256)]
    BLOCKS_LAT = BLOCKS_ALL[:4]

    def xsrc(l, r0, nr, c0=0, ncol=D, after_attn=False):
        if l == 0 and not after_attn:
            if r0 < TL:
                return x_in[r0:r0 + nr, c0:c0 + ncol]
            return ctx_in[r0 - TL:r0 - TL + nr, c0:c0 + ncol]
        return X[r0:r0 + nr, c0:c0 + ncol]

    def load_consts(need_bf=False):
        identf = P.sbuf("identf", [128, 128], F32)
        P.dma("sp", identf[:], identf_d, writes=["identf"], sem="const")
        identb = None
        if need_bf:
            identb = P.sbuf("identb", [128, 128], BF16)
            P.op("dve", lambda e: e.tensor_copy(out=identb[:], in_=identf[:]), reads=["identf"], writes=["identb"])
        return identf, identb

    def emit_convert(k0, k1):
        Uf = peer_u.rearrange("l e d -> (l e) d")
        Vf = peer_v.rearrange("l e d -> (l e) d")
        RB = 1024
        k = 0
        for r0 in range(0, 2 * NEXP, RB):
            for (c0, src) in ((0, Uf), (D, Vf)):
                if k0 <= k < k1:
                    P.dma("pool", UV[r0:r0 + RB, c0:c0 + D], src[r0:r0 + RB, :], sem=f"x_cv{k % 4}")
                k += 1

    def phase_mod(l):
        with P.phase("mod"):
            if l == 0:
                emit_convert(0, 12)
            craw = P.sbuf("craw", [128, 2, 16], F32)
            sc = P.sbuf("sc", [128, 16, 2], F32)
            scb = P.sbuf("scb", [128, 16, 2], BF16)
            bb = P.sbuf("bb", [2, 6 * D], F32)
            wts = [P.sbuf(f"wt{i}", [128, 4, 2048], F32) for i in range(2)]
            wtb = [P.sbuf(f"wtb{i}", [128, 4, 2048], BF16) for i in range(2)]
            mrow = [P.sbuf(f"mrow{i}", [2, 2048], F32) for i in range(2)]
            pm = [[P.psum(f"pm{a_}_{b_}", [128, 512], F32) for b_ in range(4)] for a_ in range(2)]
            P.dma("sp", craw[:], cvec.rearrange("s (p j) -> p s j", j=16), writes=["craw"], sem="const")
            P.dma("sp", bb[:], b_ada[l].partition_broadcast(2), writes=["bb"], sem="const2")
            P.op("act", lambda e: e.activation(out=sc[:].rearrange("p j s -> p s j"), in_=craw[:], func=AF.Silu),
                 reads=["craw"], writes=["sc"])
            P.op("dve", lambda e: e.tensor_copy(out=scb[:], in_=sc[:]), reads=["sc"], writes=["scb"])
            wv = w_ada[l].rearrange("(p j) n -> p j n", j=16)
            k = 0
            for ng in range(6):
                g2 = ng % 2
                for jg in range(4):
                    sl = k % 2
                    k += 1
                    wt, wb = wts[sl], wtb[sl]
                    P.dma_split("sp", wt[:], wv[:, jg * 4:(jg + 1) * 4, ng * 2048:(ng + 1) * 2048], 2, writes=[f"wt{sl}"], sem=f"wt{sl}")
                    P.op("act", lambda e, wt=wt, wb=wb: e.copy(out=wb[:, 0:2, :], in_=wt[:, 0:2, :]), reads=[f"wt{sl}"], writes=[f"wtb{sl}a"])
                    P.op("dve", lambda e, wt=wt, wb=wb: e.tensor_copy(out=wb[:, 2:4, :], in_=wt[:, 2:4, :]), reads=[f"wt{sl}"], writes=[f"wtb{sl}b"])
                    for nb4 in range(4):
                        for j in range(4):
                            P.op("pe", lambda e, wb=wb, j=j, jg=jg, nb4=nb4, g2=g2: e.matmul(pm[g2][nb4][0:2, :], lhsT=scb[:, jg * 4 + j, :], rhs=wb[:, j, nb4 * 512:(nb4 + 1) * 512],
                                                                                         start=(jg == 0 and j == 0), stop=(jg == 3 and j == 3)),
                                 reads=["scb", f"wtb{sl}a", f"wtb{sl}b"], writes=[f"pm{g2}_{nb4}"])
                mr = mrow[g2]
                for nb4 in range(4):
                    nb = ng * 4 + nb4
                    addc = 1.0 if (4 <= nb < 8 or 16 <= nb < 20) else 0.0
                    P.op("dve", lambda e, mr=mr, nb=nb, nb4=nb4, addc=addc, g2=g2: e.scalar_tensor_tensor(
                        out=mr[:, nb4 * 512:(nb4 + 1) * 512], in0=pm[g2][nb4][0:2, :], scalar=addc, in1=bb[:, nb * 512:(nb + 1) * 512], op0=ALU.add, op1=ALU.add),
                        reads=[f"pm{g2}_{nb4}", "bb"], writes=[f"mrow{g2}"])
                P.dma("act", MODROW[l, :, ng * 2048:(ng + 1) * 2048], mr[:], reads=[f"mrow{g2}"], writes=[], sem=f"mrow{g2}")

    def load_bc(tile, key, l, s, which, sem):
        P.dma("sp", tile[:], MODROW[l, s, which * D:(which + 1) * D].partition_broadcast(128), writes=[key], sem=sem)

    def phase_modulate(l, which_sh, which_sc, after_attn, blocks, write_h2):
        with P.phase("modulate"):
            identf, identb = load_consts(need_bf=True)
            A = [P.sbuf(f"A{s}", [128, D], F32) for s in range(2)]
            B = [P.sbuf(f"B{s}", [128, D], F32) for s in range(2)]
            classes = sorted({0 if t0 < TL else 1 for t0, _ in blocks})
            for s in classes:
                load_bc(A[s], f"A{s}", l, s, which_sc, f"bcA{s}")
                load_bc(B[s], f"B{s}", l, s, which_sh, f"bcB{s}")
            xt = [P.sbuf(f"xt{i}", [128, D], F32) for i in range(2)]
            hb = [P.sbuf(f"hb{i}", [128, D], BF16) for i in range(2)]
            junk = P.sbuf("junk", [128, D], F32)
            st = [P.sbuf(f"st{i}", [128, 4], F32) for i in range(2)]
            hT = [P.sbuf(f"hT{i}", [128, 16, 512], BF16) for i in range(2)]
            pT = [P.psum(f"pT{i}", [128, 16, 128], BF16) for i in range(2)]
            k = 0
            for bi, (t0, nt) in enumerate(blocks):
                s = 0 if t0 < TL else 1
                hTb = hT[bi % 2]
                for ti in range(nt // 128):
                    r0 = t0 + ti * 128
                    i = k % 2
                    k += 1
                    x_t, h_b, s_t, p_t = xt[i], hb[i], st[i], pT[i]
                    P.dma("sp", x_t[:], xsrc(l, r0, 128, after_attn=after_attn), writes=[f"xt{i}"], sem=f"xt{i}")
                    P.op("act", lambda e, x_t=x_t, s_t=s_t: e.activation(out=junk[:], in_=x_t[:], func=AF.Square, accum_out=s_t[:, 0:1]),
                         reads=[f"xt{i}"], writes=["junk", f"st{i}"])
                    P.op("dve", lambda e, s_t=s_t: e.tensor_scalar(out=s_t[:, 1:2], in0=s_t[:, 0:1], scalar1=1.0 / D, scalar2=EPS, op0=ALU.mult, op1=ALU.add),
                         reads=[f"st{i}"], writes=[f"st{i}"])
                    P.op("act", lambda e, s_t=s_t: e.sqrt(out=s_t[:, 2:3], in_=s_t[:, 1:2]), reads=[f"st{i}"], writes=[f"st{i}"])
                    P.op("dve", lambda e, s_t=s_t: e.reciprocal(out=s_t[:, 3:4], in_=s_t[:, 2:3]), reads=[f"st{i}"], writes=[f"st{i}"])
                    P.op("dve", lambda e, x_t=x_t, s_t=s_t, s=s: e.scalar_tensor_tensor(out=x_t[:], in0=x_t[:], scalar=s_t[:, 3:4], in1=A[s][:], op0=ALU.mult, op1=ALU.mult),
                         reads=[f"xt{i}", f"st{i}", f"A{s}"], writes=[f"xt{i}"])
                    P.op("pool", lambda e, x_t=x_t, s=s: e.tensor_tensor(out=x_t[:], in0=x_t[:], in1=B[s][:], op=ALU.add),
                         reads=[f"xt{i}", f"B{s}"], writes=[f"xt{i}"])
                    if write_h2:
                        P.dma("act", H2[r0:r0 + 128, :], x_t[:], reads=[f"xt{i}"], writes=[], sem=f"h2st{i}")
                    P.op("act", lambda e, x_t=x_t, h_b=h_b: e.copy(out=h_b[:], in_=x_t[:]), reads=[f"xt{i}"], writes=[f"hb{i}"])
                    for j in range(16):
                        P.op("pe", lambda e, h_b=h_b, p_t=p_t, j=j: e.transpose(out=p_t[:, j, :], in_=h_b[:, j * 128:(j + 1) * 128], identity=identb[:]),
                             reads=[f"hb{i}", "identb"], writes=[f"pT{i}"])
                    P.op("dve", lambda e, p_t=p_t, hTb=hTb, ti=ti: e.tensor_copy(out=hTb[:, :, ti * 128:(ti + 1) * 128], in_=p_t[:]),
                         reads=[f"pT{i}"], writes=[f"hT{bi%2}"])
                P.dma_split("act", HT[:, :, t0:t0 + nt], hTb[:, :, 0:nt], 2, reads=[f"hT{bi%2}"], writes=[], sem=f"hTst{bi%2}")

    def proj(W, col_blocks, act_src, blocks_for, mode_for, evac, per_block=None, end_block=None, npp=3):
        wst = [P.sbuf(f"wst{i}", [128, 16, 256], F32) for i in range(2)]
        wbf = [P.sbuf(f"wbf{i}", [128, 16, 512], BF16) for i in range(2)]
        ablk = [P.sbuf(f"ablk{i}", [128, 16, 512], BF16) for i in range(2)]
        pp = [P.psum(f"pp{i}", [128, 512], F32) for i in range(npp)]
        Wv = W.rearrange("(j p) n -> p j n", p=128)
        items = []
        for ci, cb in enumerate(col_blocks):
            for (t0, nt) in blocks_for(cb):
                items.append((ci, cb, t0, nt))

        def load_w(ci, cb):
            for hf in range(2):
                P.dma_split("sp", wst[hf][:], Wv[:, :, cb * 512 + hf * 256:cb * 512 + (hf + 1) * 256], 2, writes=[f"wst{hf}"], sem=f"wst{hf}")

        def load_a(n):
            ci, cb, t0, nt = items[n]
            i = n % 2
            P.dma_split("sp", ablk[i][:, :, 0:nt], act_src[:, :, t0:t0 + nt], 2, writes=[f"ablk{i}"], sem=f"ablk{i}")

        load_w(0, col_blocks[0])
        load_a(0)
        q = 0
        last_ci = -1
        for n, (ci, cb, t0, nt) in enumerate(items):
            if ci != last_ci:
                i = ci % 2
                P.op("act", lambda e, i=i: e.copy(out=wbf[i][:, :, 0:256], in_=wst[0][:]), reads=["wst0"], writes=[f"wbf{i}"])
                P.op("pool", lambda e, i=i: e.tensor_copy(out=wbf[i][:, :, 256:512], in_=wst[1][:]), reads=["wst1"], writes=[f"wbf{i}"])
                if ci + 1 < len(col_blocks):
                    load_w(ci + 1, col_blocks[ci + 1])
                last_ci = ci
            if n + 1 < len(items):
                load_a(n + 1)
            wb = wbf[ci % 2]
            ab = ablk[n % 2]
            if per_block:
                per_block(cb, t0, nt)
            if mode_for(cb) == "tok":
                for ti in range(nt // 128):
                    ps = pp[q % npp]
                    pk = f"pp{q % npp}"
                    q += 1
                    for j in range(16):
                        P.op("pe", lambda e, ps=ps, ab=ab, wb=wb, j=j, ti=ti: e.matmul(ps[:], lhsT=ab[:, j, ti * 128:(ti + 1) * 128], rhs=wb[:, j, :],
                                                                                      start=(j == 0), stop=(j == 15)),
                             reads=[f"ablk{n%2}", f"wbf{ci%2}"], writes=[pk])
                    evac(cb, t0, ti, nt, ps, pk)
            else:
                for cc in range(4):
                    ps = pp[q % npp]
                    pk = f"pp{q % npp}"
                    q += 1
                    for j in range(16):
                        P.op("pe", lambda e, ps=ps, ab=ab, wb=wb, j=j, cc=cc, nt=nt: e.matmul(ps[:, 0:nt], lhsT=wb[:, j, cc * 128:(cc + 1) * 128], rhs=ab[:, j, 0:nt],
                                                                                             start=(j == 0), stop=(j == 15)),
                             reads=[f"ablk{n%2}", f"wbf{ci%2}"], writes=[pk])
                    evac(cb, t0, cc, nt, ps, pk)
            if end_block:
                end_block(cb, t0, nt)

    def phase_inproj(l):
        with P.phase("inproj"):
            identf, identb = load_consts(need_bf=True)
            rC = P.sbuf("rC", [128, 16, 128], F32)
            rS = P.sbuf("rS", [128, 16, 128], F32)
            P.dma_split("sp", rC[:], ropeC.rearrange("(t p) d -> p t d", p=128), 2, writes=["rC"], sem="const")
            P.dma_split("sp", rS[:], ropeS.rearrange("(t p) d -> p t d", p=128), 2, writes=["rS"], sem="const2")
            gq = P.sbuf("gq", [128, 128], F32)
            gk = P.sbuf("gk", [128, 128], F32)
            P.dma("sp", gq[:], q_gain[l].partition_broadcast(128), writes=["gq"], sem="const3")
            P.dma("sp", gk[:], k_gain[l].partition_broadcast(128), writes=["gk"], sem="const4")
            NB = 2
            qf = [P.sbuf(f"qf{i}", [128, 512], F32) for i in range(NB)]
            sq = [P.sbuf(f"sq{i}", [128, 512], F32) for i in range(NB)]
            t1 = [P.sbuf(f"t1{i}", [128, 512], F32) for i in range(NB)]
            t2 = [P.sbuf(f"t2{i}", [128, 512], F32) for i in range(NB)]
            qb = [P.sbuf(f"qb{i}", [128, 512], BF16) for i in range(NB)]
            sst = [P.sbuf(f"sst{i}", [128, 16], F32) for i in range(NB)]
            stage = [P.sbuf(f"stage{i}", [128, 4, 512], BF16) for i in range(2)]
            ev = [P.sbuf(f"ev{i}", [128, 512], F32) for i in range(3)]
            evb = [P.sbuf(f"evb{i}", [128, 512], BF16) for i in range(3)]
            pq = [P.psum(f"pq{i}", [128, 4, 128], BF16) for i in range(2)]
            cnt = {"qk": 0, "ev": 0, "blk": 0}

            def blocks_for(cb):
                if l == 1 and cb not in (4, 5):
                    return BLOCKS_LAT
                return BLOCKS_ALL

            def mode_for(cb):
                return "tok" if cb < 6 else "feat"

            pending = []

            def flush():
                while pending:
                    pending.pop(0)()

            def evac(cb, t0, idx, nt, ps, pk):
                flush()
                if cb < 5:
                    ti = idx
                    r0 = t0 + ti * 128
                    latent = r0 < TL
                    i = cnt["qk"] % NB
                    cnt["qk"] += 1
                    gain = gq if cb < 4 else gk
                    gkey = "gq" if cb < 4 else "gk"
                    q_f, s_q, t_1, t_2, q_b, s_t = qf[i], sq[i], t1[i], t2[i], qb[i], sst[i]
                    P.op("act", lambda e: e.copy(out=q_f[:], in_=ps[:]), reads=[pk], writes=[f"qf{i}"])
                    P.op("dve", lambda e: e.tensor_tensor(out=s_q[:], in0=q_f[:], in1=q_f[:], op=ALU.mult), reads=[f"qf{i}"], writes=[f"sq{i}"])
                    P.op("dve", lambda e: e.tensor_reduce(out=s_t[:, 0:4], in_=s_q[:].rearrange("p (h d) -> p h d", h=4), axis=AX.X, op=ALU.add),
                         reads=[f"sq{i}"], writes=[f"sst{i}"])
                    P.op("dve", lambda e: e.tensor_scalar(out=s_t[:, 4:8], in0=s_t[:, 0:4], scalar1=1.0 / 128, scalar2=EPS, op0=ALU.mult, op1=ALU.add),
                         reads=[f"sst{i}"], writes=[f"sst{i}"])
                    P.op("act", lambda e: e.sqrt(out=s_t[:, 8:12], in_=s_t[:, 4:8]), reads=[f"sst{i}"], writes=[f"sst{i}"])
                    P.op("dve", lambda e: e.reciprocal(out=s_t[:, 12:16], in_=s_t[:, 8:12]), reads=[f"sst{i}"], writes=[f"sst{i}"])
                    P.op("dve", lambda e: e.tensor_tensor(out=s_q[:].rearrange("p (h d) -> p h d", h=4), in0=q_f[:].rearrange("p (h d) -> p h d", h=4),
                                                          in1=s_t[:, 12:16].unsqueeze(2).to_broadcast([128, 4, 128]), op=ALU.mult),
                         reads=[f"qf{i}", f"sst{i}"], writes=[f"sq{i}"])
                    P.op("pool", lambda e: e.tensor_tensor(out=q_f[:].rearrange("p (h d) -> p h d", h=4), in0=s_q[:].rearrange("p (h d) -> p h d", h=4),
                                                           in1=gain[:].unsqueeze(1).to_broadcast([128, 4, 128]), op=ALU.mult),
                         reads=[f"sq{i}", gkey], writes=[f"qf{i}"])
                    if latent:
                        tt = r0 // 128
                        P.op("pool", lambda e: e.tensor_tensor(out=t_1[:].rearrange("p (h d) -> p h d", h=4), in0=q_f[:].rearrange("p (h d) -> p h d", h=4),
                                                               in1=rC[:, tt, :].unsqueeze(1).to_broadcast([128, 4, 128]), op=ALU.mult),
                             reads=[f"qf{i}", "rC"], writes=[f"t1{i}"])
                        qv = q_f[:].rearrange("p (h a two d) -> p h a two d", h=4, a=2, two=2)
                        tv = t_2[:].rearrange("p (h a two d) -> p h a two d", h=4, a=2, two=2)
                        sv = rS[:, tt, :].rearrange("p (a two d) -> p a two d", a=2, two=2)
                        for pr in range(2):
                            P.op("dve", lambda e, pr=pr: e.tensor_tensor(out=tv[:, :, :, pr, :], in0=qv[:, :, :, 1 - pr, :],
                                                                         in1=sv[:, :, pr, :].unsqueeze(1).to_broadcast([128, 4, 2, 32]), op=ALU.mult),
                                 reads=[f"qf{i}", "rS"], writes=[f"t2{i}"])
                        P.op("dve", lambda e: e.tensor_tensor(out=q_b[:], in0=t_1[:], in1=t_2[:], op=ALU.add), reads=[f"t1{i}", f"t2{i}"], writes=[f"qb{i}"])
                    else:
                        P.op("act", lambda e: e.copy(out=q_b[:], in_=q_f[:]), reads=[f"qf{i}"], writes=[f"qb{i}"])
                    p_q = pq[i % 2]
                    sg = cnt["blk"] % 2

                    def later():
                        for hh in range(4):
                            P.op("pe", lambda e, hh=hh: e.transpose(out=p_q[:, hh, :], in_=q_b[:, hh * 128:(hh + 1) * 128], identity=identb[:]),
                                 reads=[f"qb{i}", "identb"], writes=[f"pq{i%2}"])
                        P.op("act", lambda e: e.copy(out=stage[sg][:, :, ti * 128:(ti + 1) * 128], in_=p_q[:]), reads=[f"pq{i%2}"], writes=[f"stage{sg}"])
                    pending.append(later)
                elif cb == 5:
                    ti = idx
                    r0 = t0 + ti * 128
                    i = cnt["ev"] % 3
                    cnt["ev"] += 1
                    P.op("act", lambda e: e.copy(out=evb[i][:], in_=ps[:]), reads=[pk], writes=[f"evb{i}"])
                    P.dma("act", V[r0:r0 + 128, :], evb[i][:], reads=[f"evb{i}"], writes=[], sem=f"evb{i}")
                elif cb < 8:
                    cc = idx
                    i = cnt["ev"] % 3
                    cnt["ev"] += 1
                    P.op("act", lambda e: e.copy(out=ev[i][:, 0:nt], in_=ps[:, 0:nt]), reads=[pk], writes=[f"ev{i}"])
                    P.dma("act", PL[(cb - 6) * 4 + cc][:, t0:t0 + nt], ev[i][:, 0:nt], reads=[f"ev{i}"], writes=[], sem=f"ev{i}")
                else:
                    cc = idx
                    i = cnt["ev"] % 3
                    cnt["ev"] += 1
                    P.op("act", lambda e: e.activation(out=evb[i][:, 0:nt], in_=ps[:, 0:nt], func=AF.Sigmoid), reads=[pk], writes=[f"evb{i}"])
                    P.dma("act", GAB[(cb - 8) * 4 + cc][:, t0:t0 + nt], evb[i][:, 0:nt], reads=[f"evb{i}"], writes=[], sem=f"evb{i}")

            def end_block(cb, t0, nt):
                if cb < 5:
                    flush()
                    sg = cnt["blk"] % 2
                    cnt["blk"] += 1
                    dst = QT[cb * 4:(cb + 1) * 4] if cb < 4 else KT[0:4]
                    P.dma("act", dst.rearrange("h p t -> p h t")[:, :, t0:t0 + nt], stage[sg][:, :, 0:nt], reads=[f"stage{sg}"], writes=[], sem=f"stage{sg}")

            proj(w_in[l], list(range(16)), HT, blocks_for, mode_for, evac, end_block=end_block)

    def phase_pool(l):
        with P.phase("pool"):
            classes = [(0, TL)] + ([(TL, TC)] if l == 0 else [])

            def do_class(off, L):
                W = L + 32
                tag = "L" if off == 0 else "C"
                rc = P.sbuf(f"rc{tag}", [128, 4, L], F32)
                for g in range(4):
                    P.dma("sp", rc[:, g, :], rcnt[g, off:off + L].partition_broadcast(128), writes=[f"rc{tag}"], sem=f"rc{tag}")
                u = [P.sbuf(f"u{tag}{i}", [128, W], F32) for i in range(2)]
                sa = P.sbuf(f"sa{tag}", [128, W], F32)
                sb = P.sbuf(f"sb{tag}", [128, W], F32)
                tmp = P.sbuf(f"tmp{tag}", [128, L], F32)
                pd = [P.sbuf(f"pd{tag}{i}", [128, L], BF16) for i in range(2)]
                for i in range(2):
                    P.op("pool", lambda e, i=i: e.memset(u[i][:], 0.0), writes=[f"u{tag}{i}"])
                P.op("pool", lambda e: e.memset(sa[:], 0.0), writes=[f"sa{tag}"])
                P.op("pool", lambda e: e.memset(sb[:], 0.0), writes=[f"sb{tag}"])
                for c in range(8):
                    g = c // 2
                    i = c % 2
                    uu = u[i]
                    uk = f"u{tag}{i}"
                    P.dma("sp", uu[:, 16:16 + L], PL[c][:, off:off + L], writes=[uk], sem=uk)
                    P.op("dve", lambda e, uu=uu: e.tensor_tensor(out=sa[:, 1:W], in0=uu[:, 1:W], in1=uu[:, 0:W - 1], op=ALU.add), reads=[uk], writes=[f"sa{tag}"])
                    cur, curk = sa, f"sa{tag}"
                    if g >= 1:
                        P.op("pool", lambda e: e.tensor_tensor(out=sb[:, 2:W - 1], in0=sa[:, 3:W], in1=sa[:, 1:W - 2], op=ALU.add), reads=[f"sa{tag}"], writes=[f"sb{tag}"])
                        cur, curk = sb, f"sb{tag}"
                    if g >= 2:
                        P.op("dve", lambda e: e.tensor_tensor(out=sa[:, 4:W - 3], in0=sb[:, 6:W - 1], in1=sb[:, 2:W - 5], op=ALU.add), reads=[f"sb{tag}"], writes=[f"sa{tag}"])
                        cur, curk = sa, f"sa{tag}"
                    if g >= 3:
                        P.op("pool", lambda e: e.tensor_tensor(out=sb[:, 8:W - 7], in0=sa[:, 12:W - 3], in1=sa[:, 4:W - 11], op=ALU.add), reads=[f"sa{tag}"], writes=[f"sb{tag}"])
                        cur, curk = sb, f"sb{tag}"
                    P.op("dve", lambda e, cur=cur, g=g: e.tensor_tensor(out=tmp[:], in0=cur[:, 16:16 + L], in1=rc[:, g, :], op=ALU.mult),
                         reads=[curk, f"rc{tag}"], writes=[f"tmp{tag}"])
                    P.op("pool", lambda e, uu=uu, i=i: e.tensor_tensor(out=pd[i][:], in0=tmp[:], in1=uu[:, 16:16 + L], op=ALU.subtract),
                         reads=[f"tmp{tag}", uk], writes=[f"pd{tag}{i}"])
                    P.dma("act", PD[c][:, off:off + L], pd[i][:], reads=[f"pd{tag}{i}"], writes=[], sem=f"pd{tag}{i}")

            for (off, L) in classes:
                do_class(off, L)

    def phase_attn(l):
        with P.phase("attn"):
            if l == 0:
                emit_convert(12, 32)
            ones = P.sbuf("ones", [128, 128], BF16)
            P.op("dve", lambda e: e.memset(ones[:], 1.0), writes=["ones"])
            kT = [P.sbuf(f"kT{i}", [128, T], BF16) for i in range(2)]
            Vg = [P.sbuf(f"Vg{i}", [128, 18, 128], BF16) for i in range(2)]
            qT = [P.sbuf(f"qT{i}", [128, T], BF16) for i in range(2)]
            pt = [P.sbuf(f"pt{i}", [128, 512], BF16) for i in range(4)]
            rden = [P.sbuf(f"rden{i}", [128, 512], F32) for i in range(2)]
            ob = [P.sbuf(f"ob{i}", [128, 512], BF16) for i in range(2)]
            sps = [P.psum(f"sps{i}", [128, 512], F32) for i in range(2)]
            ops_ = [P.psum(f"ops{i}", [128, 512], F32) for i in range(2)]
            dps = [P.psum(f"dps{i}", [128, 512], F32) for i in range(2)]
            scale = 128.0 ** -0.5
            st = {"nq": 0, "npt": 0, "nsp": 0}

            def do_qblock(g, gi, h, qi, c0, nqc, kts, st):
                oi = st["nq"] % 2
                st["nq"] += 1
                o_ps, d_ps = ops_[oi], dps[oi]
                nk = len(kts)

                def S(kt, si):
                    P.op("pe", lambda e: e.matmul(sps[si][:, 0:nqc], lhsT=kT[gi][:, kt * 128:(kt + 1) * 128], rhs=qT[qi][:, c0:c0 + nqc],
                                                  start=True, stop=True),
                         reads=[f"kT{gi}", f"qT{qi}"], writes=[f"sps{si}"])

                def step(ii, kt, si, pi):
                    P.op("act", lambda e: e.activation(out=pt[pi][:, 0:nqc], in_=sps[si][:, 0:nqc], func=AF.Exp, scale=scale),
                         reads=[f"sps{si}"], writes=[f"pt{pi}"])
                    P.op("pe", lambda e: e.matmul(o_ps[:, 0:nqc], lhsT=Vg[gi][:, kt, :], rhs=pt[pi][:, 0:nqc], start=(ii == 0), stop=(ii == nk - 1)),
                         reads=[f"Vg{gi}", f"pt{pi}"], writes=[f"ops{oi}"])
                    P.op("pe", lambda e: e.matmul(d_ps[:, 0:nqc], lhsT=ones[:], rhs=pt[pi][:, 0:nqc], start=(ii == 0), stop=(ii == nk - 1)),
                         reads=["ones", f"pt{pi}"], writes=[f"dps{oi}"])

                S(kts[0], st["nsp"] % 2)
                for ii, kt in enumerate(kts):
                    si = st["nsp"] % 2
                    st["nsp"] += 1
                    if ii + 1 < nk:
                        S(kts[ii + 1], st["nsp"] % 2)
                    pi = st["npt"] % 4
                    st["npt"] += 1
                    step(ii, kt, si, pi)
                P.op("dve", lambda e: e.reciprocal(out=rden[oi][:, 0:nqc], in_=d_ps[:, 0:nqc]), reads=[f"dps{oi}"], writes=[f"rden{oi}"])
                P.op("dve", lambda e: e.tensor_tensor(out=ob[oi][:, 0:nqc], in0=o_ps[:, 0:nqc], in1=rden[oi][:, 0:nqc], op=ALU.mult),
                     reads=[f"ops{oi}", f"rden{oi}"], writes=[f"ob{oi}"])
                P.dma("sp", AT[h][:, c0:c0 + nqc], ob[oi][:, 0:nqc], reads=[f"ob{oi}"], writes=[], sem=f"ob{oi}")

            for g in range(4):
                gi = g % 2
                P.dma("sp", kT[gi][:], KT[g], writes=[f"kT{gi}"], sem=f"kT{gi}")
                P.dma_split("sp", Vg[gi][:], V.rearrange("(kt p) c -> p kt c", p=128)[:, :, g * 128:(g + 1) * 128], 3, writes=[f"Vg{gi}"], sem=f"Vg{gi}")
                for hh in range(4):
                    h = g * 4 + hh
                    qi = h % 2
                    ncol = T if l == 0 else TL
                    P.dma("sp", qT[qi][:, 0:ncol], QT[h][:, 0:ncol], writes=[f"qT{qi}"], sem=f"qT{qi}")
                    qblocks = [(c0, 512, list(range(18))) for c0 in range(0, TL, 512)]
                    if l == 0:
                        qblocks.append((TL, TC, [16, 17]))
                    for (c0, nqc, kts) in qblocks:
                        do_qblock(g, gi, h, qi, c0, nqc, kts, st)

    def phase_mix(l):
        with P.phase("mix"):
            identf, _ = load_consts()
            blocks = BLOCKS_ALL if l == 0 else BLOCKS_LAT
            wpf = P.sbuf("wpf", [128, 8, 512], F32)
            wpb = P.sbuf("wpb", [128, 8, 512], BF16)
            P.dma("sp", wpf[:], w_pool[l].rearrange("g (kc p) d -> p (g kc) d", p=128), writes=["wpf"], sem="const2")
            P.op("dve", lambda e: e.tensor_copy(out=wpb[:], in_=wpf[:]), reads=["wpf"], writes=["wpb"])
            psr = P.sbuf("psr", [16, 128], F32)
            pscT = P.sbuf("pscT", [128, 16], F32)
            P.dma("sp", psr[:], pool_scale[l].rearrange("(j p) -> j p", p=128), writes=["psr"], sem="const3")
            ptp = P.psum("ptp", [128, 16], F32)
            P.op("pe", lambda e: e.transpose(out=ptp[:], in_=psr[:], identity=identf[0:16, 0:16]), reads=["psr", "identf"], writes=["ptp"])
            P.op("dve", lambda e: e.tensor_copy(out=pscT[:], in_=ptp[:]), reads=["ptp"], writes=["pscT"])
            pdb = [P.sbuf(f"pdb{i}", [128, 2, 512], BF16) for i in range(2)]
            gab = [P.sbuf(f"gab{i}", [128, 4, 512], BF16) for i in range(2)]
            gbb = [P.sbuf(f"gbb{i}", [128, 4, 512], BF16) for i in range(2)]
            m1 = [P.sbuf(f"m1{i}", [128, 512], F32) for i in range(2)]
            m2 = [P.sbuf(f"m2{i}", [128, 512], F32) for i in range(2)]
            mg = [P.sbuf(f"mg{i}", [128, 512], BF16) for i in range(3)]
            pb = [P.psum(f"pb{i}", [128, 512], F32) for i in range(2)]
            cnt = {"blk": 0, "ev": 0}
            cur = {}

            def per_block(cb, t0, nt):
                i = cnt["blk"] % 2
                cnt["blk"] += 1
                cur["i"] = i
                P.dma("sp", pdb[i][:, :, 0:nt], PD[2 * cb:2 * cb + 2].rearrange("c p t -> p c t")[:, :, t0:t0 + nt], writes=[f"pdb{i}"], sem=f"pdb{i}")
                P.dma("sp", gab[i][:, :, 0:nt], GAB[4 * cb:4 * cb + 4].rearrange("c p t -> p c t")[:, :, t0:t0 + nt], writes=[f"gab{i}"], sem=f"gab{i}")
                P.dma("sp", gbb[i][:, :, 0:nt], GAB[16 + 4 * cb:16 + 4 * cb + 4].rearrange("c p t -> p c t")[:, :, t0:t0 + nt], writes=[f"gbb{i}"], sem=f"gbb{i}")

            def evac(cb, t0, cc, nt, ps, pk):
                i = cur["i"]
                dc = cb * 4 + cc
                e2 = cnt["ev"] % 2
                e3 = cnt["ev"] % 3
                cnt["ev"] += 1
                p_b = pb[e2]
                for kc in range(2):
                    P.op("pe", lambda e, kc=kc: e.matmul(p_b[:, 0:nt], lhsT=wpb[:, cb * 2 + kc, cc * 128:(cc + 1) * 128], rhs=pdb[i][:, kc, 0:nt],
                                                         start=(kc == 0), stop=(kc == 1)),
                         reads=["wpb", f"pdb{i}"], writes=[f"pb{e2}"])
                P.op("dve", lambda e: e.tensor_tensor(out=m1[e2][:, 0:nt], in0=ps[:, 0:nt], in1=gab[i][:, cc, 0:nt], op=ALU.mult),
                     reads=[pk, f"gab{i}"], writes=[f"m1{e2}"])
                P.op("dve", lambda e: e.scalar_tensor_tensor(out=m2[e2][:, 0:nt], in0=p_b[:, 0:nt], scalar=pscT[:, dc:dc + 1], in1=gbb[i][:, cc, 0:nt],
                                                             op0=ALU.mult, op1=ALU.mult),
                     reads=[f"pb{e2}", "pscT", f"gbb{i}"], writes=[f"m2{e2}"])
                P.op("pool", lambda e: e.tensor_tensor(out=mg[e3][:, 0:nt], in0=m1[e2][:, 0:nt], in1=m2[e2][:, 0:nt], op=ALU.add),
                     reads=[f"m1{e2}", f"m2{e2}"], writes=[f"mg{e3}"])
                P.dma("act", MG[dc][:, t0:t0 + nt], mg[e3][:, 0:nt], reads=[f"mg{e3}"], writes=[], sem=f"mg{e3}")

            proj(w_br[l], list(range(4)), AT.rearrange("h p t -> p h t"), lambda cb: blocks, lambda cb: "feat", evac, per_block=per_block, npp=2)

    def phase_wout(l):
        with P.phase("wout"):
            blocks = BLOCKS_ALL if l == 0 else BLOCKS_LAT
            G = [P.sbuf(f"G{s}", [128, D], F32) for s in range(2)]
            for s in ([0, 1] if l == 0 else [0]):
                load_bc(G[s], f"G{s}", l, s, G_A, f"bcG{s}")
            xb = [P.sbuf(f"xo{i}", [128, 4, 512], F32) for i in range(2)]
            tt = [P.sbuf(f"to{i}", [128, 512], F32) for i in range(3)]
            cnt = {"ev": 0, "blk": 0}
            cur = {}

            def per_block(cb, t0, nt):
                b = cnt["blk"] % 2
                cnt["blk"] += 1
                cur["b"] = b
                P.dma("sp", xb[b][:, 0:nt // 128, :], xsrc(l, t0, nt, cb * 512, 512).rearrange("(t p) c -> p t c", p=128), writes=[f"xo{b}"], sem=f"xo{b}")

            def evac(cb, t0, ti, nt, ps, pk):
                r0 = t0 + ti * 128
                s = 0 if r0 < TL else 1
                i = cnt["ev"] % 3
                cnt["ev"] += 1
                b = cur["b"]
                P.op("dve", lambda e: e.tensor_tensor(out=tt[i][:], in0=ps[:], in1=G[s][:, cb * 512:(cb + 1) * 512], op=ALU.mult),
                     reads=[pk, f"G{s}"], writes=[f"to{i}"])
                P.op("pool", lambda e: e.tensor_tensor(out=tt[i][:], in0=tt[i][:], in1=xb[b][:, ti, :], op=ALU.add), reads=[f"to{i}", f"xo{b}"], writes=[f"to{i}"])
                P.dma("act", X[r0:r0 + 128, cb * 512:(cb + 1) * 512], tt[i][:], reads=[f"to{i}"], writes=[], sem=f"to{i}")

            proj(w_out[l], list(range(4)), MG.rearrange("h p t -> p h t"), lambda cb: blocks, lambda cb: "tok", evac, per_block=per_block)

    def phase_qpeer(l):
        with P.phase("qpeer"):
            blocks = BLOCKS_ALL if l == 0 else BLOCKS_LAT
            ev = [P.sbuf(f"ev{i}", [128, 512], F32) for i in range(3)]
            cnt = {"ev": 0}

            def evac(cb, t0, cc, nt, ps, pk):
                i = cnt["ev"] % 3
                cnt["ev"] += 1
                P.op("act", lambda e: e.copy(out=ev[i][:, 0:nt], in_=ps[:, 0:nt]), reads=[pk], writes=[f"ev{i}"])
                P.dma("act", QPT[cb * 4 + cc][:, t0:t0 + nt], ev[i][:, 0:nt], reads=[f"ev{i}"], writes=[], sem=f"ev{i}")

            proj(w_qp[l], list(range(4)), HT, lambda cb: blocks, lambda cb: "feat", evac)

    def phase_topk(l):
        with P.phase("topk"):
            if l == 0:
                emit_convert(32, 64)
            identf, _ = load_consts()
            ntiles = (T if l == 0 else TL) // 128
            io16 = P.sbuf("io16", [128, 16], F32)
            P.dma("sp", io16[:], iota16_d, writes=["io16"], sem="const2")
            kraw = P.sbuf("kraw", [128, 16, 128], F32)
            keysT = P.sbuf("keysT", [128, 16, 128], F32)
            P.dma_split("sp", kraw[:], peer_keys[l].rearrange("h p k d -> k (h p) d"), 2, writes=["kraw"], sem="const3")
            pk4 = [P.psum(f"pk4{i}", [128, 4, 128], F32) for i in range(4)]
            for grp in range(4):
                for q in range(4):
                    hp = grp * 4 + q
                    P.op("pe", lambda e, hp=hp, q=q, grp=grp: e.transpose(out=pk4[grp][:, q, :], in_=kraw[:, hp, :], identity=identf[:]),
                         reads=["kraw", "identf"], writes=[f"pk4{grp}"])
                P.op("act", lambda e, grp=grp: e.copy(out=keysT[:, grp * 4:(grp + 1) * 4, :], in_=pk4[grp][:]), reads=[f"pk4{grp}"], writes=["keysT"])
            qt = [P.sbuf(f"qt{i}", [128, 16, 128], F32) for i in range(2)]
            Sbuf = [P.sbuf(f"S{k}", [128, 16, 128], F32) for k in range(2)]
            S2 = P.sbuf("S2", [128, 16, 128], F32)
            m = P.sbuf("m", [128, 16, 16], F32)
            ix = P.sbuf("ix", [128, 16, 16], U32)
            ixf = P.sbuf("ixf", [128, 16, 16], F32)
            i1s = P.sbuf("i1s", [128, 8, 16], F32)
            cand = P.sbuf("cand", [128, 8, 256], F32)
            cand2 = P.sbuf("cand2", [128, 8, 256], F32)
            ts = P.sbuf("ts", [128, 8, 16], F32)
            pos = P.sbuf("pos", [128, 8, 16], U32)
            au = P.sbuf("au", [128, 8, 16], U32)
            bu = P.sbuf("bu", [128, 8, 16], U32)
            af_ = P.sbuf("af", [128, 8, 16], F32)
            bf_ = P.sbuf("bf", [128, 8, 16], F32)
            oh = P.sbuf("oh", [128, 8, 16, 16], F32)
            isel = P.sbuf("isel", [128, 8, 16], F32)
            jsel = P.sbuf("jsel", [128, 8, 16], F32)
            ef = P.sbuf("ef", [128, 128], F32)
            eu = [P.sbuf(f"eu{i}", [128, 128], U32) for i in range(2)]
            dd = P.sbuf("dd", [128, 8, 16], F32)
            ee = P.sbuf("ee", [128, 8, 16], F32)
            zz = P.sbuf("zz", [128, 16], F32)
            gg = [P.sbuf(f"gg{i}", [128, 128], F32) for i in range(2)]
            NEG = -1e30
            dense = peer_mode == "dense"
            if dense:
                io128 = P.sbuf("io128", [128, 128], F32)
                P.dma("sp", io128[:], iota128_d, writes=["io128"], sem="const4")
                tp = P.psum("tp", [128, 3, 128], F32)
                ijg = P.sbuf("ijg", [128, 3, 128], F32)
                Aoh = [P.sbuf(f"Aoh{k}", [128, 16, 128], BF16) for k in range(2)]
                Boh = [P.sbuf(f"Boh{k}", [128, 128], BF16) for k in range(8)]
                gp = [P.psum(f"gp{k}", [128, 4, 128], F32) for k in range(2)]
                stg = [P.sbuf(f"stg{k}", [128, 128, 128], BF16) for k in range(2)]
                gst = {"b": 0, "g": 0}

            def gbuild(tt, i):
                r0 = tt * 128
                sg = tt % 2
                srcs = [isel[:].rearrange("p h k -> p (h k)"), jsel[:].rearrange("p h k -> p (h k)"), gg[i][:]]
                keys = ["sel0", "sel1", f"gg{i}"]
                for q in range(3):
                    P.op("pe", lambda e, q=q: e.transpose(out=tp[:, q, :], in_=srcs[q], identity=identf[:]), reads=[keys[q], "identf"], writes=["tp"])
                P.op("act", lambda e: e.copy(out=ijg[:], in_=tp[:]), reads=["tp"], writes=["ijg"])
                for grp in range(8):
                    a = grp % 2
                    for tq in range(16):
                        tl = grp * 16 + tq
                        P.op("pool", lambda e, a=a, tq=tq, tl=tl: e.tensor_scalar(out=Aoh[a][:, tq, :], in0=io128[:], scalar1=ijg[:, 0, tl:tl + 1], scalar2=None, op0=ALU.is_equal),
                             reads=["io128", "ijg"], writes=[f"Aoh{a}"])
                    for q4 in range(4):
                        gk = gst["g"] % 2
                        gst["g"] += 1
                        for q in range(4):
                            tl = grp * 16 + q4 * 4 + q
                            b = gst["b"] % 8
                            gst["b"] += 1
                            P.op("dve", lambda e, b=b, tl=tl: e.tensor_scalar(out=Boh[b][:], in0=io128[:], scalar1=ijg[:, 1, tl:tl + 1], scalar2=ijg[:, 2, tl:tl + 1],
                                                                              op0=ALU.is_equal, op1=ALU.mult),
                                 reads=["io128", "ijg"], writes=[f"Boh{b}"])
                            P.op("pe", lambda e, b=b, a=a, gk=gk, q=q, q4=q4: e.matmul(gp[gk][:, q, :], lhsT=Boh[b][:], rhs=Aoh[a][:, q4 * 4 + q, :], start=True, stop=True),
                                 reads=[f"Boh{b}", f"Aoh{a}"], writes=[f"gp{gk}"])
                        tl0 = grp * 16 + q4 * 4
                        P.op("act", lambda e, gk=gk, tl0=tl0, sg=sg: e.copy(out=stg[sg][:, :, tl0:tl0 + 4].rearrange("p i t -> p t i"), in_=gp[gk][:]),
                             reads=[f"gp{gk}"], writes=[f"stg{sg}"])
                dst = GTd.rearrange("i j t -> j i t")
                for k in range(16):
                    P.dma("sp", dst[:, k * 8:(k + 1) * 8, r0:r0 + 128], stg[sg][:, k * 8:(k + 1) * 8, :], reads=[f"stg{sg}"], writes=[], sem=f"stg{sg}")

            for tt in range(ntiles):
                r0 = tt * 128
                i = tt % 2
                P.dma_split("sp", qt[i][:], QPT.rearrange("c p t -> p c t")[:, :, r0:r0 + 128], 2, writes=[f"qt{i}"], sem=f"qt{i}")
                for grp in range(4):
                    for q in range(4):
                        hp = grp * 4 + q
                        P.op("pe", lambda e, hp=hp, q=q, grp=grp, i=i: e.matmul(pk4[grp][:, q, :], lhsT=qt[i][:, hp, :], rhs=keysT[:, hp, :], start=True, stop=True),
                             reads=[f"qt{i}", "keysT"], writes=[f"pk4{grp}"])
                    P.op("act", lambda e, grp=grp, Sx=Sbuf[i]: e.copy(out=Sx[:, grp * 4:(grp + 1) * 4, :], in_=pk4[grp][:]), reads=[f"pk4{grp}"], writes=[f"S{i}_{grp}"])
                for hp in range(16):
                    sk = f"S{i}_{hp // 4}"
                    P.op("dve", lambda e, hp=hp, Sx=Sbuf[i]: e.max(out=m[:, hp, 0:8], in_=Sx[:, hp, :]), reads=[sk], writes=[f"ma{hp}"])
                for hp in range(16):
                    sk = f"S{i}_{hp // 4}"
                    P.op("dve", lambda e, hp=hp, Sx=Sbuf[i]: e.max_index(out=ix[:, hp, 0:8], in_max=m[:, hp, 0:8], in_values=Sx[:, hp, :]), reads=[sk, f"ma{hp}"], writes=[f"ixa{hp}"])
                for hp in range(16):
                    sk = f"S{i}_{hp // 4}"
                    P.op("dve", lambda e, hp=hp, Sx=Sbuf[i]: e.match_replace(out=S2[:, hp, :], in_to_replace=m[:, hp, 0:8], in_values=Sx[:, hp, :], imm_value=NEG),
                         reads=[sk, f"ma{hp}"], writes=[f"S2_{hp}"])
                for hp in range(16):
                    P.op("dve", lambda e, hp=hp: e.max(out=m[:, hp, 8:16], in_=S2[:, hp, :]), reads=[f"S2_{hp}"], writes=[f"mb{hp}"])
                for hp in range(16):
                    P.op("dve", lambda e, hp=hp: e.max_index(out=ix[:, hp, 8:16], in_max=m[:, hp, 8:16], in_values=S2[:, hp, :]), reads=[f"S2_{hp}", f"mb{hp}"], writes=[f"ixb{hp}"])
                mkeys = [f"ma{hp}" for hp in range(16)] + [f"mb{hp}" for hp in range(16)]
                ixkeys = [f"ixa{hp}" for hp in range(16)] + [f"ixb{hp}" for hp in range(16)]
                P.op("dve", lambda e: e.tensor_copy(out=ixf[:], in_=ix[:]), reads=ixkeys, writes=["ixf"])
                mv = m[:].rearrange("p (h two) k -> p h two k", two=2)
                iv = ixf[:].rearrange("p (h two) k -> p h two k", two=2)
                cv = cand[:].rearrange("p h (a b) -> p h a b", a=16)
                P.op("dve", lambda e: e.tensor_tensor(out=cv, in0=mv[:, :, 0, :].unsqueeze(3).to_broadcast([128, 8, 16, 16]),
                                                      in1=mv[:, :, 1, :].unsqueeze(2).to_broadcast([128, 8, 16, 16]), op=ALU.add),
                     reads=mkeys, writes=["cand"])
                for h in range(8):
                    P.op("dve", lambda e, h=h: e.max(out=ts[:, h, 0:8], in_=cand[:, h, :]), reads=["cand"], writes=[f"tsa{h}"])
                for h in range(8):
                    P.op("dve", lambda e, h=h: e.max_index(out=pos[:, h, 0:8], in_max=ts[:, h, 0:8], in_values=cand[:, h, :]), reads=["cand", f"tsa{h}"], writes=[f"posa{h}"])
                for h in range(8):
                    P.op("dve", lambda e, h=h: e.match_replace(out=cand2[:, h, :], in_to_replace=ts[:, h, 0:8], in_values=cand[:, h, :], imm_value=NEG),
                         reads=["cand", f"tsa{h}"], writes=[f"c2_{h}"])
                for h in range(8):
                    P.op("dve", lambda e, h=h: e.max(out=ts[:, h, 8:16], in_=cand2[:, h, :]), reads=[f"c2_{h}"], writes=[f"tsb{h}"])
                for h in range(8):
                    P.op("dve", lambda e, h=h: e.max_index(out=pos[:, h, 8:16], in_max=ts[:, h, 8:16], in_values=cand2[:, h, :]), reads=[f"c2_{h}", f"tsb{h}"], writes=[f"posb{h}"])
                tskeys = [f"tsa{h}" for h in range(8)] + [f"tsb{h}" for h in range(8)]
                poskeys = [f"posa{h}" for h in range(8)] + [f"posb{h}" for h in range(8)]
                P.op("dve", lambda e: e.tensor_single_scalar(out=au[:], in_=pos[:], scalar=4, op=ALU.logical_shift_right), reads=poskeys, writes=["au"])
                P.op("dve", lambda e: e.tensor_single_scalar(out=bu[:], in_=pos[:], scalar=15, op=ALU.bitwise_and), reads=poskeys, writes=["bu"])
                P.op("dve", lambda e: e.tensor_copy(out=af_[:], in_=au[:]), reads=["au"], writes=["af"])
                P.op("dve", lambda e: e.tensor_copy(out=bf_[:], in_=bu[:]), reads=["bu"], writes=["bf"])
                for (sel, xf, which, key) in ((isel, af_, 0, "af"), (jsel, bf_, 1, "bf")):
                    P.op("dve", lambda e, xf=xf: e.tensor_tensor(out=oh[:], in0=io16[:].unsqueeze(1).unsqueeze(1).to_broadcast([128, 8, 16, 16]),
                                                                  in1=xf[:].unsqueeze(3).to_broadcast([128, 8, 16, 16]), op=ALU.is_equal),
                         reads=["io16", key], writes=["oh"])
                    P.op("dve", lambda e, which=which: e.tensor_tensor(out=oh[:], in0=oh[:], in1=iv[:, :, which, :].unsqueeze(2).to_broadcast([128, 8, 16, 16]), op=ALU.mult),
                         reads=["oh", "ixf"], writes=["oh"])
                    P.op("dve", lambda e, sel=sel: e.tensor_reduce(out=sel[:], in_=oh[:], axis=AX.X, op=ALU.add), reads=["oh"], writes=["sel%d" % which])
                P.op("dve", lambda e: e.scalar_tensor_tensor(out=ef[:], in0=isel[:].rearrange("p h k -> p (h k)"), scalar=128.0, in1=jsel[:].rearrange("p h k -> p (h k)"),
                                                             op0=ALU.mult, op1=ALU.add),
                     reads=["sel0", "sel1"], writes=["ef"])
                if l > 0:
                    P.op("dve", lambda e: e.tensor_scalar(out=ef[:], in0=ef[:], scalar1=float(l * NEXP), scalar2=None, op0=ALU.add), reads=["ef"], writes=["ef"])
                P.op("dve", lambda e, i=i: e.tensor_copy(out=eu[i][:], in_=ef[:]), reads=["ef"], writes=[f"eu{i}"])
                P.dma("sp", EIDX[r0:r0 + 128, :], eu[i][:], reads=[f"eu{i}"], writes=[], sem=f"eu{i}")
                P.op("dve", lambda e: e.tensor_tensor(out=dd[:], in0=ts[:], in1=ts[:, :, 0:1].to_broadcast([128, 8, 16]), op=ALU.subtract), reads=tskeys, writes=["dd"])
                P.op("act", lambda e: e.activation(out=ee[:], in_=dd[:], func=AF.Exp), reads=["dd"], writes=["ee"])
                P.op("dve", lambda e: e.tensor_reduce(out=zz[:, 0:8], in_=ee[:], axis=AX.X, op=ALU.add), reads=["ee"], writes=["zz"])
                P.op("dve", lambda e: e.reciprocal(out=zz[:, 8:16], in_=zz[:, 0:8]), reads=["zz"], writes=["zz"])
                P.op("dve", lambda e, i=i: e.tensor_tensor(out=gg[i][:].rearrange("p (h k) -> p h k", h=8), in0=ee[:], in1=zz[:, 8:16].unsqueeze(2).to_broadcast([128, 8, 16]), op=ALU.mult),
                     reads=["ee", "zz"], writes=[f"gg{i}"])
                P.dma("sp", GATE[r0:r0 + 128, :], gg[i][:], reads=[f"gg{i}"], writes=[], sem=f"gg{i}")
                if dense:
                    gbuild(tt, i)

    def phase_gather(l):
        with P.phase("gather"):
            P.wait_persistent()
            identf, identb = load_consts(need_bf=True)
            ntiles = (T if l == 0 else TL) // 128
            G = [P.sbuf(f"G{s}", [128, D], F32) for s in range(2)]
            for s in ([0, 1] if l == 0 else [0]):
                load_bc(G[s], f"G{s}", l, s, G_F, f"bcG{s}")
            NS = 8
            LOOK = 5
            uv = [P.sbuf(f"uv{i}", [128, 2 * D], BF16) for i in range(NS)]
            h2 = [P.sbuf(f"h2{i}", [128, D], F32) for i in range(2)]
            xt = [P.sbuf(f"xt{i}", [128, D], F32) for i in range(2)]
            junk = P.sbuf("junk", [128, D], F32)
            eix = [P.sbuf(f"eix{i}", [128, 128], U32) for i in range(2)]
            gat = [P.sbuf(f"gat{i}", [128, 128], F32) for i in range(2)]
            act = [P.sbuf(f"act{i}", [128, 128], F32) for i in range(2)]
            ge = [P.sbuf(f"ge{i}", [128, 128], F32) for i in range(2)]
            dg = [P.sbuf(f"dg{i}", [128, 128], BF16) for i in range(4)]
            tmp = [P.sbuf(f"tmp{i}", [128, 512], F32) for i in range(2)]
            acc = [[P.psum(f"acc{a}_{b}", [128, 512], F32) for b in range(4)] for a in range(2)]
            st = {"ntmp": 0}
            items = [(tt, sidx) for tt in range(ntiles) for sidx in range(128)]

            def loads(tt):
                r0 = tt * 128
                i = tt % 2
                P.dma("sp", h2[i][:], H2[r0:r0 + 128, :], writes=[f"h2{i}"], sem=f"h2{i}")
                P.dma("sp", xt[i][:], X[r0:r0 + 128, :], writes=[f"xt{i}"], sem=f"xt{i}")
                P.dma("sp", eix[i][:], EIDX[r0:r0 + 128, :], writes=[f"eix{i}"], sem=f"eix{i}")
                P.dma("sp", gat[i][:], GATE[r0:r0 + 128, :], writes=[f"gat{i}"], sem=f"gat{i}")

            def gather(n):
                tt, sidx = items[n]
                i = tt % 2
                u = n % NS
                if sidx == 0:
                    loads(tt)
                P.dma_fn("pool", lambda e: e.indirect_dma_start(
                    out=uv[u][:], out_offset=None, in_=UV, in_offset=bass.IndirectOffsetOnAxis(ap=eix[i][:, sidx:sidx + 1], axis=0)),
                    reads=[f"eix{i}"], writes=[f"uv{u}"], sem=f"uv{u}")

            def dot(n):
                tt, sidx = items[n]
                i = tt % 2
                u = n % NS
                P.op("dve", lambda e: e.scalar_tensor_tensor(out=junk[:], in0=h2[i][:], scalar=1.0, in1=uv[u][:, 0:D], op0=ALU.mult, op1=ALU.mult,
                                                             accum_out=act[i][:, sidx:sidx + 1]),
                     reads=[f"h2{i}", f"uv{u}"], writes=[f"a{i}_{sidx}"])
                P.op("act", lambda e: e.activation(out=ge[i][:, sidx:sidx + 1], in_=act[i][:, sidx:sidx + 1], func=AF.Gelu),
                     reads=[f"a{i}_{sidx}"], writes=[f"g{i}_{sidx}"])

            def combine(n):
                tt, sidx = items[n]
                i = tt % 2
                u = n % NS
                d = n % 4
                P.op("dve", lambda e: e.tensor_scalar(out=dg[d][:], in0=identb[:], scalar1=ge[i][:, sidx:sidx + 1], scalar2=gat[i][:, sidx:sidx + 1],
                                                      op0=ALU.mult, op1=ALU.mult),
                     reads=["identb", f"g{i}_{sidx}", f"gat{i}"], writes=[f"dg{d}"])
                for db in range(4):
                    P.op("pe", lambda e, db=db: e.matmul(acc[i][db][:], lhsT=dg[d][:], rhs=uv[u][:, D + db * 512:D + (db + 1) * 512],
                                                         start=(sidx == 0), stop=(sidx == 127)),
                         reads=[f"dg{d}", f"uv{u}"], writes=[f"acc{i}_{db}"])
                if sidx == 127:
                    finalize(tt)

            def finalize(tt):
                r0 = tt * 128
                i = tt % 2
                s = 0 if r0 < TL else 1
                for db in range(4):
                    tq = st["ntmp"] % 2
                    st["ntmp"] += 1
                    P.op("dve", lambda e, tq=tq, db=db: e.tensor_tensor(out=tmp[tq][:], in0=acc[i][db][:], in1=G[s][:, db * 512:(db + 1) * 512], op=ALU.mult),
                         reads=[f"acc{i}_{db}", f"G{s}"], writes=[f"tmp{tq}"])
                    P.op("dve", lambda e, tq=tq, db=db: e.tensor_tensor(out=xt[i][:, db * 512:(db + 1) * 512], in0=tmp[tq][:], in1=xt[i][:, db * 512:(db + 1) * 512], op=ALU.add),
                         reads=[f"tmp{tq}", f"xt{i}"], writes=[f"xt{i}"])
                P.dma("sp", X[r0:r0 + 128, :], xt[i][:], reads=[f"xt{i}"], writes=[], sem=f"xst{i}")

            N = len(items)
            for n in range(min(LOOK, N)):
                gather(n)
            for n in range(N):
                if n + LOOK < N:
                    gather(n + LOOK)
                dot(n)
                if n >= 1:
                    combine(n - 1)
            combine(N - 1)

    def phase_dense(l):
        with P.phase("dense"):
            _, identb = load_consts(need_bf=True)
            ntok = T if l == 0 else TL
            groups = [(0, 768), (768, 768), (1536, ntok - 1536)]
            G = [P.sbuf(f"G{s}", [128, D], F32) for s in range(2)]
            for s in ([0, 1] if l == 0 else [0]):
                load_bc(G[s], f"G{s}", l, s, G_F, f"bcG{s}")
            hT = P.sbuf("hT", [128, 16, 768], BF16)
            acc = P.sbuf("acc", [128, 6, D], F32)
            GA = P.sbuf("GA", [128, 8, 768], BF16)
            Vb = P.sbuf("Vb", [128, 8, D], BF16)
            Ub = [P.sbuf(f"Ub{k}", [128, D], BF16) for k in range(3)]
            UT = [P.sbuf(f"UT{k}", [128, 16, 128], BF16) for k in range(2)]
            gt = [P.sbuf(f"gt{k}", [128, 768], BF16) for k in range(3)]
            gl = [P.sbuf(f"gl{k}", [128, 512], BF16) for k in range(2)]
            xt = [P.sbuf(f"xt{k}", [128, D], F32) for k in range(2)]
            ptu = [P.psum(f"ptu{k}", [128, 16, 128], BF16) for k in range(2)]
            ps1 = [P.psum(f"ps1{k}", [128, 512], F32) for k in range(2)]
            ps2 = [P.psum(f"ps2{k}", [128, 512], F32) for k in range(2)]
            st = {"c": 0, "p1": 0, "p2": 0, "x": 0}

            def chunk(c, ci, g0, gn, nblocks):
                k3 = st["c"] % 3
                k2 = st["c"] % 2
                st["c"] += 1
                row0 = l * NEXP + c * 128
                P.dma("sp", Ub[k3][:], UV[row0:row0 + 128, 0:D], writes=[f"Ub{k3}"], sem=f"Ub{k3}")
                P.dma("sp", Vb[:, ci, :], UV[row0:row0 + 128, D:2 * D], writes=[f"Vb{ci}"], sem=f"Vb{ci}")
                P.dma("sp", gt[k3][:, 0:gn], GTd[c][:, g0:g0 + gn], writes=[f"gt{k3}"], sem=f"gt{k3}")
                for j in range(16):
                    P.op("pe", lambda e, j=j: e.transpose(out=ptu[k2][:, j, :], in_=Ub[k3][:, j * 128:(j + 1) * 128], identity=identb[:]),
                         reads=[f"Ub{k3}", "identb"], writes=[f"ptu{k2}"])
                P.op("pool" if False else "dve", lambda e: e.tensor_copy(out=UT[k2][:], in_=ptu[k2][:]), reads=[f"ptu{k2}"], writes=[f"UT{k2}"])
                for (b0, bn) in nblocks:
                    p1 = st["p1"] % 2
                    st["p1"] += 1
                    for j in range(16):
                        P.op("pe", lambda e, j=j, p1=p1: e.matmul(ps1[p1][:, 0:bn], lhsT=UT[k2][:, j, :], rhs=hT[:, j, b0:b0 + bn], start=(j == 0), stop=(j == 15)),
                             reads=[f"UT{k2}", "hT"], writes=[f"ps1{p1}"])
                    P.op("act", lambda e, p1=p1: e.activation(out=gl[p1][:, 0:bn], in_=ps1[p1][:, 0:bn], func=AF.Gelu), reads=[f"ps1{p1}"], writes=[f"gl{p1}"])
                    P.op("pool", lambda e, p1=p1: e.tensor_tensor(out=GA[:, ci, b0:b0 + bn], in0=gl[p1][:, 0:bn], in1=gt[k3][:, b0:b0 + bn], op=ALU.mult),
                         reads=[f"gl{p1}", f"gt{k3}"], writes=[f"GA{ci}"])

            def combine(cg, ntile):
                for ti in range(ntile):
                    for db in range(4):
                        p2 = st["p2"] % 2
                        st["p2"] += 1
                        for ci in range(8):
                            P.op("pe", lambda e, ci=ci, p2=p2: e.matmul(ps2[p2][:], lhsT=GA[:, ci, ti * 128:(ti + 1) * 128], rhs=Vb[:, ci, db * 512:(db + 1) * 512],
                                                                        start=(ci == 0), stop=(ci == 7)),
                                 reads=[f"GA{ci}", f"Vb{ci}"], writes=[f"ps2{p2}"])
                        if cg == 0:
                            P.op("dve", lambda e, p2=p2: e.tensor_copy(out=acc[:, ti, db * 512:(db + 1) * 512], in_=ps2[p2][:]), reads=[f"ps2{p2}"], writes=[f"acc{ti}_{db}"])
                        else:
                            P.op("dve", lambda e, p2=p2: e.tensor_tensor(out=acc[:, ti, db * 512:(db + 1) * 512], in0=ps2[p2][:], in1=acc[:, ti, db * 512:(db + 1) * 512], op=ALU.add),
                                 reads=[f"ps2{p2}", f"acc{ti}_{db}"], writes=[f"acc{ti}_{db}"])

            def finalize(g0, ti):
                r0 = g0 + ti * 128
                s = 0 if r0 < TL else 1
                k = st["x"] % 2
                st["x"] += 1
                P.dma("sp", xt[k][:], X[r0:r0 + 128, :], writes=[f"xt{k}"], sem=f"xt{k}")
                P.op("pool", lambda e: e.tensor_tensor(out=acc[:, ti, :], in0=acc[:, ti, :], in1=G[s][:], op=ALU.mult),
                     reads=[f"acc{ti}_{db}" for db in range(4)] + [f"G{s}"], writes=[f"acc{ti}_{db}" for db in range(4)])
                P.op("pool", lambda e: e.tensor_tensor(out=xt[k][:], in0=xt[k][:], in1=acc[:, ti, :], op=ALU.add),
                     reads=[f"acc{ti}_{db}" for db in range(4)] + [f"xt{k}"], writes=[f"xt{k}"])
                P.dma("sp", X[r0:r0 + 128, :], xt[k][:], reads=[f"xt{k}"], writes=[], sem=f"xst{k}")

            for (g0, gn) in groups:
                ntile = gn // 128
                nblocks = [(0, 384), (384, 384)] if gn == 768 else [(0, gn)]
                P.dma_split("sp", hT[:, :, 0:gn], HT[:, :, g0:g0 + gn], 2, writes=["hT"], sem="hT")
                for cg in range(16):
                    for ci in range(8):
                        chunk(cg * 8 + ci, ci, g0, gn, nblocks)
                    combine(cg, ntile)
                for ti in range(ntile):
                    finalize(g0, ti)

    def phase_final():
        with P.phase("final"):
            fg = P.sbuf("fg", [128, D], F32)
            P.dma("sp", fg[:], final_gain.partition_broadcast(128), writes=["fg"], sem="const")
            xt = [P.sbuf(f"xt{i}", [128, D], F32) for i in range(3)]
            junk = P.sbuf("junk", [128, D], F32)
            st = [P.sbuf(f"st{i}", [128, 4], F32) for i in range(3)]
            for tt in range(TL // 128):
                r0 = tt * 128
                i = tt % 3
                x_t, s_t = xt[i], st[i]
                P.dma("sp", x_t[:], X[r0:r0 + 128, :], writes=[f"xt{i}"], sem=f"xt{i}")
                P.op("act", lambda e, x_t=x_t, s_t=s_t: e.activation(out=junk[:], in_=x_t[:], func=AF.Square, accum_out=s_t[:, 0:1]),
                     reads=[f"xt{i}"], writes=["junk", f"st{i}"])
                P.op("dve", lambda e, s_t=s_t: e.tensor_scalar(out=s_t[:, 1:2], in0=s_t[:, 0:1], scalar1=1.0 / D, scalar2=EPS, op0=ALU.mult, op1=ALU.add),
                     reads=[f"st{i}"], writes=[f"st{i}"])
                P.op("act", lambda e, s_t=s_t: e.sqrt(out=s_t[:, 2:3], in_=s_t[:, 1:2]), reads=[f"st{i}"], writes=[f"st{i}"])
                P.op("dve", lambda e, s_t=s_t: e.reciprocal(out=s_t[:, 3:4], in_=s_t[:, 2:3]), reads=[f"st{i}"], writes=[f"st{i}"])
                P.op("dve", lambda e, x_t=x_t, s_t=s_t: e.scalar_tensor_tensor(out=x_t[:], in0=x_t[:], scalar=s_t[:, 3:4], in1=fg[:], op0=ALU.mult, op1=ALU.mult),
                     reads=[f"xt{i}", f"st{i}", "fg"], writes=[f"xt{i}"])
                P.dma("pool", out_d[r0:r0 + 128, :], x_t[:], reads=[f"xt{i}"], writes=[], sem=f"ost{i}")

    def stop(l, name):
        return stop_after is not None and stop_after == (l, name)

    done = False
    for l in range(nlayers):
        blocks = BLOCKS_ALL
        steps = [
            ("mod", lambda: phase_mod(l)),
            ("modA", lambda: phase_modulate(l, SH_A, SC_A, False, BLOCKS_ALL, False)),
            ("inproj", lambda: phase_inproj(l)),
            ("pool", lambda: phase_pool(l)),
            ("attn", lambda: phase_attn(l)),
            ("mix", lambda: phase_mix(l)),
            ("wout", lambda: phase_wout(l)),
            ("modF", lambda: phase_modulate(l, SH_F, SC_F, True, BLOCKS_ALL if l == 0 else BLOCKS_LAT, True)),
            ("qpeer", lambda: phase_qpeer(l)),
            ("topk", lambda: phase_topk(l)),
            ("gather", (lambda: phase_dense(l)) if peer_mode == "dense" else (lambda: phase_gather(l))),
        ]
        for name, fn in steps:
            fn()
            if stop(l, name):
                done = True
                break
        if done:
            break
    if not done:
        phase_final()
    P.close()
    return nc, P


def _consts():
    t = np.arange(TL)
    row = (t // 64).astype(np.float32)
    col = (t % 64).astype(np.float32)
    inv = (np.float32(10000.0) ** (-np.arange(32, dtype=np.float32) / np.float32(32))).astype(np.float32)
    ar = (row[:, None] * inv[None, :]).astype(np.float32)
    ac = (col[:, None] * inv[None, :]).astype(np.float32)
    cr, sr, cc, sc = np.cos(ar), np.sin(ar), np.cos(ac), np.sin(ac)
    ropeC = np.concatenate([cr, cr, cc, cc], axis=1).astype(np.float32)
    ropeS = np.concatenate([-sr, sr, -sc, sc], axis=1).astype(np.float32)
    rc = np.zeros((4, T), np.float32)
    for g, w in enumerate((2, 4, 8, 16)):
        for (off, L) in ((0, TL), (TL, TC)):
            tt = np.arange(L)
            lo = np.clip(tt - w // 2, 0, L)
            hi = np.clip(tt + (w - w // 2), 0, L)
            rc[g, off:off + L] = 1.0 / (hi - lo).astype(np.float32)
    identf = np.eye(128, dtype=np.float32)
    iota16 = np.tile(np.arange(16, dtype=np.float32)[None, :], (128, 1))
    iota128 = np.tile(np.arange(128, dtype=np.float32)[None, :], (128, 1))
    return dict(ropeC=ropeC, ropeS=ropeS, rcnt=rc, identf=identf, iota16=iota16, iota128=iota128)


def make_in_map(inputs, b):
    f = lambda a: np.ascontiguousarray(np.asarray(a, dtype=np.float32))
    m = dict(
        x=f(inputs["x"][b]), ctx=f(inputs["ctx"][b]),
        cvec=f(np.stack([np.asarray(inputs["c"][b]), np.asarray(inputs["c_ctx"])], axis=0)),
    )
    for k in ["w_ada", "b_ada", "w_in", "q_gain", "k_gain", "w_br_attn", "w_pool", "pool_scale", "w_out", "w_q_peer",
              "peer_keys", "peer_u", "peer_v", "final_gain"]:
        m[k] = f(inputs[k])
    m.update(_consts())
    return m


def kernel(**inputs):
    nc, _ = build_program()
    shared = None
    in_maps = []
    for b in range(8):
        m = make_in_map(inputs, b) if shared is None else dict(shared)
        if shared is None:
            shared = m
        else:
            m["x"] = np.ascontiguousarray(np.asarray(inputs["x"][b], dtype=np.float32))
            m["ctx"] = np.ascontiguousarray(np.asarray(inputs["ctx"][b], dtype=np.float32))
            m["cvec"] = np.ascontiguousarray(np.stack([np.asarray(inputs["c"][b]), np.asarray(inputs["c_ctx"])], axis=0).astype(np.float32))
        in_maps.append(m)
    res = run_bass_kernel_spmd(nc, in_maps, core_ids=list(range(8)))
    return np.stack([np.asarray(r["out"], dtype=np.float32) for r in res.results], axis=0)
```

```python
from contextlib import ExitStack, contextmanager
import numpy as np
import concourse.bass as bass
import concourse.mybir as mybir
from concourse.bass_utils import run_bass_kernel_spmd

F32 = mybir.dt.float32
BF16 = mybir.dt.bfloat16
U32 = mybir.dt.uint32
AF = mybir.ActivationFunctionType
ALU = mybir.AluOpType
AX = mybir.AxisListType

ENGS = ["pe", "act", "dve", "pool", "sp"]

D = 2048
TL = 2048
TC = 256
T = TL + TC
NEXP = 16384
EPS = 1e-6
SH_A, SC_A, G_A, SH_F, SC_F, G_F = range(6)


class Prog:
    def __init__(self, nc, same_engine_sync=True):
        self.nc = nc
        self.stack = ExitStack()
        self.pstack = None
        self.streams = {e: [] for e in ENGS}
        self.sems = {}
        self.count = {}
        self.waited = {e: {} for e in ENGS}
        self.last_write = {}
        self.reads = {}
        self.same_engine_sync = same_engine_sync
        self.n_ops = 0
        self.phase_sems = {}
        self.persist = set()
        self.uid = 0
        for e in ENGS:
            self._sem("e_" + e)

    def _sem(self, name):
        if name not in self.sems:
            self.sems[name] = self.stack.enter_context(self.nc.semaphore(name))
            self.count[name] = 0
        return self.sems[name]

    def sbuf(self, name, shape, dtype):
        self.uid += 1
        return self.pstack.enter_context(self.nc.sbuf_tensor(f"{name}_s{self.uid}", list(shape), dtype))

    def psum(self, name, shape, dtype):
        self.uid += 1
        return self.pstack.enter_context(self.nc.psum_tensor(f"{name}_p{self.uid}", list(shape), dtype))

    def _wait(self, eng, sem, val):
        if sem == "e_pe" and eng == "pe":
            return
        if sem == "e_" + eng and not self.same_engine_sync:
            return
        if self.waited[eng].get(sem, 0) >= val:
            return
        self.waited[eng][sem] = val
        self.streams[eng].append(("wait", sem, val))

    def _deps(self, eng, reads, writes):
        deps = {}
        for k in reads:
            lw = self.last_write.get(k)
            if lw:
                deps[lw[0]] = max(deps.get(lw[0], 0), lw[1])
        for k in writes:
            lw = self.last_write.get(k)
            if lw:
                deps[lw[0]] = max(deps.get(lw[0], 0), lw[1])
            for s, v in self.reads.get(k, {}).items():
                deps[s] = max(deps.get(s, 0), v)
        for s, v in deps.items():
            self._wait(eng, s, v)

    def _record(self, ev, reads, writes):
        for k in reads:
            d = self.reads.setdefault(k, {})
            d[ev[0]] = max(d.get(ev[0], 0), ev[1])
        for k in writes:
            self.last_write[k] = ev
            self.reads[k] = {}

    def op(self, eng, fn, reads=(), writes=()):
        self._deps(eng, reads, writes)
        sem = "e_" + eng
        self.count[sem] += 1
        ev = (sem, self.count[sem])
        self.streams[eng].append(("op", fn, sem, 1))
        self._record(ev, reads, writes)
        self.n_ops += 1

    def dma(self, queue, out, in_, reads=(), writes=(), sem=None, **kw):
        self.dma_fn(queue, lambda e, o=out, i=in_, kw=kw: e.dma_start(out=o, in_=i, **kw), reads, writes, sem)

    def dma_fn(self, queue, fn, reads=(), writes=(), sem=None):
        self._deps(queue, reads, writes)
        sem = sem or "default"
        if sem.startswith("x_"):
            self.persist.add(sem)
        else:
            if sem not in self.phase_sems:
                self.phase_sems[sem] = "d_%d" % len(self.phase_sems)
            sem = self.phase_sems[sem]
        self._sem(sem)
        self.count[sem] += 16
        ev = (sem, self.count[sem])
        self.streams[queue].append(("op", fn, sem, 16))
        self._record(ev, reads, writes)
        self.n_ops += 1

    def dma_split(self, queue, out, in_, n, reads=(), writes=(), sem=None):
        a = out.shape[1]
        step = (a + n - 1) // n
        for k in range(0, a, step):
            self.dma(queue, out[:, k:min(a, k + step), :], in_[:, k:min(a, k + step), :], reads=reads, writes=writes, sem=sem)

    def wait_persistent(self):
        for e in ENGS:
            for s in sorted(self.persist):
                self._wait(e, s, self.count[s])
        self.persist = set()

    def barrier(self):
        for e in ENGS:
            for s, c in self.count.items():
                if c > 0 and s not in self.persist:
                    self._wait(e, s, c)
        self.last_write = {}
        self.reads = {}

    def emit_block(self):
        nc = self.nc
        streams = self.streams
        self.streams = {e: [] for e in ENGS}
        with nc.Block() as block:
            def replay(name):
                def f(engine):
                    for rec in streams[name]:
                        if rec[0] == "wait":
                            engine.wait_ge(self.sems[rec[1]], rec[2])
                        else:
                            rec[1](engine).then_inc(self.sems[rec[2]], rec[3])
                return f
            block.tensor(replay("pe"))
            block.scalar(replay("act"))
            block.vector(replay("dve"))
            block.gpsimd(replay("pool"))
            block.sync(replay("sp"))

    @contextmanager
    def phase(self, name=""):
        self.pstack = ExitStack()
        self.phase_sems = {}
        try:
            yield
            self.barrier()
            self.emit_block()
        finally:
            self.pstack.close()
            self.pstack = None

    def close(self):
        self.stack.close()


ALL_PHASES = ["mod", "modA", "inproj", "pool", "attn", "mix", "wout", "modF", "qpeer", "topk", "gather"]


def build_program(dbg=(), nlayers=2, stop_after=None, same_engine_sync=True, peer_mode="gather"):
    nc = bass.Bass("TRN2", target_bir_lowering=False)

    def inp(name, shape, dt=F32):
        return nc.dram_tensor(name, list(shape), dt, kind="ExternalInput").ap()

    def scratch(name, shape, dt):
        kind = "ExternalOutput" if name in dbg else "Internal"
        return nc.dram_tensor(name, list(shape), dt, kind=kind).ap()

    x_in = inp("x", [TL, D])
    ctx_in = inp("ctx", [TC, D])
    cvec = inp("cvec", [2, D])
    w_ada = inp("w_ada", [2, D, 6 * D])
    b_ada = inp("b_ada", [2, 6 * D])
    w_in = inp("w_in", [2, D, 8192])
    q_gain = inp("q_gain", [2, 128])
    k_gain = inp("k_gain", [2, 128])
    w_br = inp("w_br_attn", [2, D, D])
    w_pool = inp("w_pool", [2, 4, 256, 512])
    pool_scale = inp("pool_scale", [2, D])
    w_out = inp("w_out", [2, D, D])
    w_qp = inp("w_q_peer", [2, D, D])
    peer_keys = inp("peer_keys", [2, 8, 2, 128, 128])
    peer_u = inp("peer_u", [2, NEXP, D])
    peer_v = inp("peer_v", [2, NEXP, D])
    final_gain = inp("final_gain", [D])
    ropeC = inp("ropeC", [TL, 128])
    ropeS = inp("ropeS", [TL, 128])
    rcnt = inp("rcnt", [4, T])
    identf_d = inp("identf", [128, 128])
    iota16_d = inp("iota16", [128, 16])
    iota128_d = inp("iota128", [128, 128])
    out_d = nc.dram_tensor("out", [TL, D], F32, kind="ExternalOutput").ap()

    MODROW = scratch("MODROW", [2, 2, 6 * D], F32)
    X = scratch("X", [T, D], F32)
    HT = scratch("HT", [128, 16, T], BF16)
    QT = scratch("QT", [16, 128, T], BF16)
    KT = scratch("KT", [4, 128, T], BF16)
    V = scratch("V", [T, 512], BF16)
    PL = scratch("PL", [8, 128, T], F32)
    PD = scratch("PD", [8, 128, T], BF16)
    GAB = scratch("GAB", [32, 128, T], BF16)
    AT = scratch("AT", [16, 128, T], BF16)
    MG = scratch("MG", [16, 128, T], BF16)
    H2 = scratch("H2", [T, D], F32)
    QPT = scratch("QPT", [16, 128, T], F32)
    EIDX = scratch("EIDX", [T, 128], U32)
    GATE = scratch("GATE", [T, 128], F32)
    GTd = scratch("GTd", [128, 128, T], BF16)
    UV = scratch("UV", [2 * NEXP, 2 * D], BF16)

    P = Prog(nc, same_engine_sync=same_engine_sync)

    BLOCKS_ALL = [(0, 512), (512, 512), (1024, 512), (1536, 512), (2048, 256)]
    BLOCKS_LAT = BLOCKS_ALL[:4]

    def xsrc(l, r0, nr, c0=0, ncol=D, after_attn=False):
        if l == 0 and not after_attn:
            if r0 < TL:
                return x_in[r0:r0 + nr, c0:c0 + ncol]
            return ctx_in[r0 - TL:r0 - TL + nr, c0:c0 + ncol]
        return X[r0:r0 + nr, c0:c0 + ncol]

    def load_consts(need_bf=False):
        identf = P.sbuf("identf", [128, 128], F32)
        P.dma("sp", identf[:], identf_d, writes=["identf"], sem="const")
        identb = None
        if need_bf:
            identb = P.sbuf("identb", [128, 128], BF16)
            P.op("dve", lambda e: e.tensor_copy(out=identb[:], in_=identf[:]), reads=["identf"], writes=["identb"])
        return identf, identb

    def emit_convert(k0, k1):
        Uf = peer_u.rearrange("l e d -> (l e) d")
        Vf = peer_v.rearrange("l e d -> (l e) d")
        RB = 1024
        k = 0
        for r0 in range(0, 2 * NEXP, RB):
            for (c0, src) in ((0, Uf), (D, Vf)):
                if k0 <= k < k1:
                    P.dma("pool", UV[r0:r0 + RB, c0:c0 + D], src[r0:r0 + RB, :], sem=f"x_cv{k % 4}")
                k += 1

    def phase_mod(l):
        with P.phase("mod"):
            if l == 0:
                emit_convert(0, 12)
            craw = P.sbuf("craw", [128, 2, 16], F32)
            sc = P.sbuf("sc", [128, 16, 2], F32)
            scb = P.sbuf("scb", [128, 16, 2], BF16)
            bb = P.sbuf("bb", [2, 6 * D], F32)
            wts = [P.sbuf(f"wt{i}", [128, 4, 2048], F32) for i in range(2)]
            wtb = [P.sbuf(f"wtb{i}", [128, 4, 2048], BF16) for i in range(2)]
            mrow = [P.sbuf(f"mrow{i}", [2, 2048], F32) for i in range(2)]
            pm = [[P.psum(f"pm{a_}_{b_}", [128, 512], F32) for b_ in range(4)] for a_ in range(2)]
            P.dma("sp", craw[:], cvec.rearrange("s (p j) -> p s j", j=16), writes=["craw"], sem="const")
            P.dma("sp", bb[:], b_ada[l].partition_broadcast(2), writes=["bb"], sem="const2")
            P.op("act", lambda e: e.activation(out=sc[:].rearrange("p j s -> p s j"), in_=craw[:], func=AF.Silu),
                 reads=["craw"], writes=["sc"])
            P.op("dve", lambda e: e.tensor_copy(out=scb[:], in_=sc[:]), reads=["sc"], writes=["scb"])
            wv = w_ada[l].rearrange("(p j) n -> p j n", j=16)
            k = 0
            for ng in range(6):
                g2 = ng % 2
                for jg in range(4):
                    sl = k % 2
                    k += 1
                    wt, wb = wts[sl], wtb[sl]
                    P.dma_split("sp", wt[:], wv[:, jg * 4:(jg + 1) * 4, ng * 2048:(ng + 1) * 2048], 2, writes=[f"wt{sl}"], sem=f"wt{sl}")
                    P.op("act", lambda e, wt=wt, wb=wb: e.copy(out=wb[:, 0:2, :], in_=wt[:, 0:2, :]), reads=[f"wt{sl}"], writes=[f"wtb{sl}a"])
                    P.op("dve", lambda e, wt=wt, wb=wb: e.tensor_copy(out=wb[:, 2:4, :], in_=wt[:, 2:4, :]), reads=[f"wt{sl}"], writes=[f"wtb{sl}b"])
                    for nb4 in range(4):
                        for j in range(4):
                            P.op("pe", lambda e, wb=wb, j=j, jg=jg, nb4=nb4, g2=g2: e.matmul(pm[g2][nb4][0:2, :], lhsT=scb[:, jg * 4 + j, :], rhs=wb[:, j, nb4 * 512:(nb4 + 1) * 512],
                                                                                         start=(jg == 0 and j == 0), stop=(jg == 3 and j == 3)),
                                 reads=["scb", f"wtb{sl}a", f"wtb{sl}b"], writes=[f"pm{g2}_{nb4}"])
                mr = mrow[g2]
                for nb4 in range(4):
                    nb = ng * 4 + nb4
                    addc = 1.0 if (4 <= nb < 8 or 16 <= nb < 20) else 0.0
                    P.op("dve", lambda e, mr=mr, nb=nb, nb4=nb4, addc=addc, g2=g2: e.scalar_tensor_tensor(
                        out=mr[:, nb4 * 512:(nb4 + 1) * 512], in0=pm[g2][nb4][0:2, :], scalar=addc, in1=bb[:, nb * 512:(nb + 1) * 512], op0=ALU.add, op1=ALU.add),
                        reads=[f"pm{g2}_{nb4}", "bb"], writes=[f"mrow{g2}"])
                P.dma("act", MODROW[l, :, ng * 2048:(ng + 1) * 2048], mr[:], reads=[f"mrow{g2}"], writes=[], sem=f"mrow{g2}")

    def load_bc(tile, key, l, s, which, sem):
        P.dma("sp", tile[:], MODROW[l, s, which * D:(which + 1) * D].partition_broadcast(128), writes=[key], sem=sem)

    def phase_modulate(l, which_sh, which_sc, after_attn, blocks, write_h2):
        with P.phase("modulate"):
            identf, identb = load_consts(need_bf=True)
            A = [P.sbuf(f"A{s}", [128, D], F32) for s in range(2)]
            B = [P.sbuf(f"B{s}", [128, D], F32) for s in range(2)]
            classes = sorted({0 if t0 < TL else 1 for t0, _ in blocks})
            for s in classes:
                load_bc(A[s], f"A{s}", l, s, which_sc, f"bcA{s}")
                load_bc(B[s], f"B{s}", l, s, which_sh, f"bcB{s}")
            xt = [P.sbuf(f"xt{i}", [128, D], F32) for i in range(2)]
            hb = [P.sbuf(f"hb{i}", [128, D], BF16) for i in range(2)]
            junk = P.sbuf("junk", [128, D], F32)
            st = [P.sbuf(f"st{i}", [128, 4], F32) for i in range(2)]
            hT = [P.sbuf(f"hT{i}", [128, 16, 512], BF16) for i in range(2)]
            pT = [P.psum(f"pT{i}", [128, 16, 128], BF16) for i in range(2)]
            k = 0
            for bi, (t0, nt) in enumerate(blocks):
                s = 0 if t0 < TL else 1
                hTb = hT[bi % 2]
                for ti in range(nt // 128):
                    r0 = t0 + ti * 128
                    i = k % 2
                    k += 1
                    x_t, h_b, s_t, p_t = xt[i], hb[i], st[i], pT[i]
                    P.dma("sp", x_t[:], xsrc(l, r0, 128, after_attn=after_attn), writes=[f"xt{i}"], sem=f"xt{i}")
                    P.op("act", lambda e, x_t=x_t, s_t=s_t: e.activation(out=junk[:], in_=x_t[:], func=AF.Square, accum_out=s_t[:, 0:1]),
                         reads=[f"xt{i}"], writes=["junk", f"st{i}"])
                    P.op("dve", lambda e, s_t=s_t: e.tensor_scalar(out=s_t[:, 1:2], in0=s_t[:, 0:1], scalar1=1.0 / D, scalar2=EPS, op0=ALU.mult, op1=ALU.add),
                         reads=[f"st{i}"], writes=[f"st{i}"])
                    P.op("act", lambda e, s_t=s_t: e.sqrt(out=s_t[:, 2:3], in_=s_t[:, 1:2]), reads=[f"st{i}"], writes=[f"st{i}"])
                    P.op("dve", lambda e, s_t=s_t: e.reciprocal(out=s_t[:, 3:4], in_=s_t[:, 2:3]), reads=[f"st{i}"], writes=[f"st{i}"])
                    P.op("dve", lambda e, x_t=x_t, s_t=s_t, s=s: e.scalar_tensor_tensor(out=x_t[:], in0=x_t[:], scalar=s_t[:, 3:4], in1=A[s][:], op0=ALU.mult, op1=ALU.mult),
                         reads=[f"xt{i}", f"st{i}", f"A{s}"], writes=[f"xt{i}"])
                    P.op("pool", lambda e, x_t=x_t, s=s: e.tensor_tensor(out=x_t[:], in0=x_t[:], in1=B[s][:], op=ALU.add),
                         reads=[f"xt{i}", f"B{s}"], writes=[f"xt{i}"])
                    if write_h2:
                        P.dma("act", H2[r0:r0 + 128, :], x_t[:], reads=[f"xt{i}"], writes=[], sem=f"h2st{i}")
                    P.op("act", lambda e, x_t=x_t, h_b=h_b: e.copy(out=h_b[:], in_=x_t[:]), reads=[f"xt{i}"], writes=[f"hb{i}"])
                    for j in range(16):
                        P.op("pe", lambda e, h_b=h_b, p_t=p_t, j=j: e.transpose(out=p_t[:, j, :], in_=h_b[:, j * 128:(j + 1) * 128], identity=identb[:]),
                             reads=[f"hb{i}", "identb"], writes=[f"pT{i}"])
                    P.op("dve", lambda e, p_t=p_t, hTb=hTb, ti=ti: e.tensor_copy(out=hTb[:, :, ti * 128:(ti + 1) * 128], in_=p_t[:]),
                         reads=[f"pT{i}"], writes=[f"hT{bi%2}"])
                P.dma_split("act", HT[:, :, t0:t0 + nt], hTb[:, :, 0:nt], 2, reads=[f"hT{bi%2}"], writes=[], sem=f"hTst{bi%2}")

    def proj(W, col_blocks, act_src, blocks_for, mode_for, evac, per_block=None, end_block=None, npp=3):
        wst = [P.sbuf(f"wst{i}", [128, 16, 256], F32) for i in range(2)]
        wbf = [P.sbuf(f"wbf{i}", [128, 16, 512], BF16) for i in range(2)]
        ablk = [P.sbuf(f"ablk{i}", [128, 16, 512], BF16) for i in range(2)]
        pp = [P.psum(f"pp{i}", [128, 512], F32) for i in range(npp)]
        Wv = W.rearrange("(j p) n -> p j n", p=128)
        items = []
        for ci, cb in enumerate(col_blocks):
            for (t0, nt) in blocks_for(cb):
                items.append((ci, cb, t0, nt))

        def load_w(ci, cb):
            for hf in range(2):
                P.dma_split("sp", wst[hf][:], Wv[:, :, cb * 512 + hf * 256:cb * 512 + (hf + 1) * 256], 2, writes=[f"wst{hf}"], sem=f"wst{hf}")

        def load_a(n):
            ci, cb, t0, nt = items[n]
            i = n % 2
            P.dma_split("sp", ablk[i][:, :, 0:nt], act_src[:, :, t0:t0 + nt], 2, writes=[f"ablk{i}"], sem=f"ablk{i}")

        load_w(0, col_blocks[0])
        load_a(0)
        q = 0
        last_ci = -1
        for n, (ci, cb, t0, nt) in enumerate(items):
            if ci != last_ci:
                i = ci % 2
                P.op("act", lambda e, i=i: e.copy(out=wbf[i][:, :, 0:256], in_=wst[0][:]), reads=["wst0"], writes=[f"wbf{i}"])
                P.op("pool", lambda e, i=i: e.tensor_copy(out=wbf[i][:, :, 256:512], in_=wst[1][:]), reads=["wst1"], writes=[f"wbf{i}"])
                if ci + 1 < len(col_blocks):
                    load_w(ci + 1, col_blocks[ci + 1])
                last_ci = ci
            if n + 1 < len(items):
                load_a(n + 1)
            wb = wbf[ci % 2]
            ab = ablk[n % 2]
            if per_block:
                per_block(cb, t0, nt)
            if mode_for(cb) == "tok":
                for ti in range(nt // 128):
                    ps = pp[q % npp]
                    pk = f"pp{q % npp}"
                    q += 1
                    for j in range(16):
                        P.op("pe", lambda e, ps=ps, ab=ab, wb=wb, j=j, ti=ti: e.matmul(ps[:], lhsT=ab[:, j, ti * 128:(ti + 1) * 128], rhs=wb[:, j, :],
                                                                                      start=(j == 0), stop=(j == 15)),
                             reads=[f"ablk{n%2}", f"wbf{ci%2}"], writes=[pk])
                    evac(cb, t0, ti, nt, ps, pk)
            else:
                for cc in range(4):
                    ps = pp[q % npp]
                    pk = f"pp{q % npp}"
                    q += 1
                    for j in range(16):
                        P.op("pe", lambda e, ps=ps, ab=ab, wb=wb, j=j, cc=cc, nt=nt: e.matmul(ps[:, 0:nt], lhsT=wb[:, j, cc * 128:(cc + 1) * 128], rhs=ab[:, j, 0:nt],
                                                                                             start=(j == 0), stop=(j == 15)),
                             reads=[f"ablk{n%2}", f"wbf{ci%2}"], writes=[pk])
                    evac(cb, t0, cc, nt, ps, pk)
            if end_block:
                end_block(cb, t0, nt)

    def phase_inproj(l):
        with P.phase("inproj"):
            identf, identb = load_consts(need_bf=True)
            rC = P.sbuf("rC", [128, 16, 128], F32)
            rS = P.sbuf("rS", [128, 16, 128], F32)
            P.dma_split("sp", rC[:], ropeC.rearrange("(t p) d -> p t d", p=128), 2, writes=["rC"], sem="const")
            P.dma_split("sp", rS[:], ropeS.rearrange("(t p) d -> p t d", p=128), 2, writes=["rS"], sem="const2")
            gq = P.sbuf("gq", [128, 128], F32)
            gk = P.sbuf("gk", [128, 128], F32)
            P.dma("sp", gq[:], q_gain[l].partition_broadcast(128), writes=["gq"], sem="const3")
            P.dma("sp", gk[:], k_gain[l].partition_broadcast(128), writes=["gk"], sem="const4")
            NB = 2
            qf = [P.sbuf(f"qf{i}", [128, 512], F32) for i in range(NB)]
            sq = [P.sbuf(f"sq{i}", [128, 512], F32) for i in range(NB)]
            t1 = [P.sbuf(f"t1{i}", [128, 512], F32) for i in range(NB)]
            t2 = [P.sbuf(f"t2{i}", [128, 512], F32) for i in range(NB)]
            qb = [P.sbuf(f"qb{i}", [128, 512], BF16) for i in range(NB)]
            sst = [P.sbuf(f"sst{i}", [128, 16], F32) for i in range(NB)]
            stage = [P.sbuf(f"stage{i}", [128, 4, 512], BF16) for i in range(2)]
            ev = [P.sbuf(f"ev{i}", [128, 512], F32) for i in range(3)]
            evb = [P.sbuf(f"evb{i}", [128, 512], BF16) for i in range(3)]
            pq = [P.psum(f"pq{i}", [128, 4, 128], BF16) for i in range(2)]
            cnt = {"qk": 0, "ev": 0, "blk": 0}

            def blocks_for(cb):
                if l == 1 and cb not in (4, 5):
                    return BLOCKS_LAT
                return BLOCKS_ALL

            def mode_for(cb):
                return "tok" if cb < 6 else "feat"

            pending = []

            def flush():
                while pending:
                    pending.pop(0)()

            def evac(cb, t0, idx, nt, ps, pk):
                flush()
                if cb < 5:
                    ti = idx
                    r0 = t0 + ti * 128
                    latent = r0 < TL
                    i = cnt["qk"] % NB
                    cnt["qk"] += 1
                    gain = gq if cb < 4 else gk
                    gkey = "gq" if cb < 4 else "gk"
                    q_f, s_q, t_1, t_2, q_b, s_t = qf[i], sq[i], t1[i], t2[i], qb[i], sst[i]
                    P.op("act", lambda e: e.copy(out=q_f[:], in_=ps[:]), reads=[pk], writes=[f"qf{i}"])
                    P.op("dve", lambda e: e.tensor_tensor(out=s_q[:], in0=q_f[:], in1=q_f[:], op=ALU.mult), reads=[f"qf{i}"], writes=[f"sq{i}"])
                    P.op("dve", lambda e: e.tensor_reduce(out=s_t[:, 0:4], in_=s_q[:].rearrange("p (h d) -> p h d", h=4), axis=AX.X, op=ALU.add),
                         reads=[f"sq{i}"], writes=[f"sst{i}"])
                    P.op("dve", lambda e: e.tensor_scalar(out=s_t[:, 4:8], in0=s_t[:, 0:4], scalar1=1.0 / 128, scalar2=EPS, op0=ALU.mult, op1=ALU.add),
                         reads=[f"sst{i}"], writes=[f"sst{i}"])
                    P.op("act", lambda e: e.sqrt(out=s_t[:, 8:12], in_=s_t[:, 4:8]), reads=[f"sst{i}"], writes=[f"sst{i}"])
                    P.op("dve", lambda e: e.reciprocal(out=s_t[:, 12:16], in_=s_t[:, 8:12]), reads=[f"sst{i}"], writes=[f"sst{i}"])
                    P.op("dve", lambda e: e.tensor_tensor(out=s_q[:].rearrange("p (h d) -> p h d", h=4), in0=q_f[:].rearrange("p (h d) -> p h d", h=4),
                                                          in1=s_t[:, 12:16].unsqueeze(2).to_broadcast([128, 4, 128]), op=ALU.mult),
                         reads=[f"qf{i}", f"sst{i}"], writes=[f"sq{i}"])
                    P.op("pool", lambda e: e.tensor_tensor(out=q_f[:].rearrange("p (h d) -> p h d", h=4), in0=s_q[:].rearrange("p (h d) -> p h d", h=4),
                                                           in1=gain[:].unsqueeze(1).to_broadcast([128, 4, 128]), op=ALU.mult),
                         reads=[f"sq{i}", gkey], writes=[f"qf{i}"])
                    if latent:
                        tt = r0 // 128
                        P.op("pool", lambda e: e.tensor_tensor(out=t_1[:].rearrange("p (h d) -> p h d", h=4), in0=q_f[:].rearrange("p (h d) -> p h d", h=4),
                                                               in1=rC[:, tt, :].unsqueeze(1).to_broadcast([128, 4, 128]), op=ALU.mult),
                             reads=[f"qf{i}", "rC"], writes=[f"t1{i}"])
                        qv = q_f[:].rearrange("p (h a two d) -> p h a two d", h=4, a=2, two=2)
                        tv = t_2[:].rearrange("p (h a two d) -> p h a two d", h=4, a=2, two=2)
                        sv = rS[:, tt, :].rearrange("p (a two d) -> p a two d", a=2, two=2)
                        for pr in range(2):
                            P.op("dve", lambda e, pr=pr: e.tensor_tensor(out=tv[:, :, :, pr, :], in0=qv[:, :, :, 1 - pr, :],
                                                                         in1=sv[:, :, pr, :].unsqueeze(1).to_broadcast([128, 4, 2, 32]), op=ALU.mult),
                                 reads=[f"qf{i}", "rS"], writes=[f"t2{i}"])
                        P.op("dve", lambda e: e.tensor_tensor(out=q_b[:], in0=t_1[:], in1=t_2[:], op=ALU.add), reads=[f"t1{i}", f"t2{i}"], writes=[f"qb{i}"])
                    else:
                        P.op("act", lambda e: e.copy(out=q_b[:], in_=q_f[:]), reads=[f"qf{i}"], writes=[f"qb{i}"])
                    p_q = pq[i % 2]
                    sg = cnt["blk"] % 2

                    def later():
                        for hh in range(4):
                            P.op("pe", lambda e, hh=hh: e.transpose(out=p_q[:, hh, :], in_=q_b[:, hh * 128:(hh + 1) * 128], identity=identb[:]),
                                 reads=[f"qb{i}", "identb"], writes=[f"pq{i%2}"])
                        P.op("act", lambda e: e.copy(out=stage[sg][:, :, ti * 128:(ti + 1) * 128], in_=p_q[:]), reads=[f"pq{i%2}"], writes=[f"stage{sg}"])
                    pending.append(later)
                elif cb == 5:
                    ti = idx
                    r0 = t0 + ti * 128
                    i = cnt["ev"] % 3
                    cnt["ev"] += 1
                    P.op("act", lambda e: e.copy(out=evb[i][:], in_=ps[:]), reads=[pk], writes=[f"evb{i}"])
                    P.dma("act", V[r0:r0 + 128, :], evb[i][:], reads=[f"evb{i}"], writes=[], sem=f"evb{i}")
                elif cb < 8:
                    cc = idx
                    i = cnt["ev"] % 3
                    cnt["ev"] += 1
                    P.op("act", lambda e: e.copy(out=ev[i][:, 0:nt], in_=ps[:, 0:nt]), reads=[pk], writes=[f"ev{i}"])
                    P.dma("act", PL[(cb - 6) * 4 + cc][:, t0:t0 + nt], ev[i][:, 0:nt], reads=[f"ev{i}"], writes=[], sem=f"ev{i}")
                else:
                    cc = idx
                    i = cnt["ev"] % 3
                    cnt["ev"] += 1
                    P.op("act", lambda e: e.activation(out=evb[i][:, 0:nt], in_=ps[:, 0:nt], func=AF.Sigmoid), reads=[pk], writes=[f"evb{i}"])
                    P.dma("act", GAB[(cb - 8) * 4 + cc][:, t0:t0 + nt], evb[i][:, 0:nt], reads=[f"evb{i}"], writes=[], sem=f"evb{i}")

            def end_block(cb, t0, nt):
                if cb < 5:
                    flush()
                    sg = cnt["blk"] % 2
                    cnt["blk"] += 1
                    dst = QT[cb * 4:(cb + 1) * 4] if cb < 4 else KT[0:4]
                    P.dma("act", dst.rearrange("h p t -> p h t")[:, :, t0:t0 + nt], stage[sg][:, :, 0:nt], reads=[f"stage{sg}"], writes=[], sem=f"stage{sg}")

            proj(w_in[l], list(range(16)), HT, blocks_for, mode_for, evac, end_block=end_block)

    def phase_pool(l):
        with P.phase("pool"):
            classes = [(0, TL)] + ([(TL, TC)] if l == 0 else [])

            def do_class(off, L):
                W = L + 32
                tag = "L" if off == 0 else "C"
                rc = P.sbuf(f"rc{tag}", [128, 4, L], F32)
                for g in range(4):
                    P.dma("sp", rc[:, g, :], rcnt[g, off:off + L].partition_broadcast(128), writes=[f"rc{tag}"], sem=f"rc{tag}")
                u = [P.sbuf(f"u{tag}{i}", [128, W], F32) for i in range(2)]
                sa = P.sbuf(f"sa{tag}", [128, W], F32)
                sb = P.sbuf(f"sb{tag}", [128, W], F32)
                tmp = P.sbuf(f"tmp{tag}", [128, L], F32)
                pd = [P.sbuf(f"pd{tag}{i}", [128, L], BF16) for i in range(2)]
                for i in range(2):
                    P.op("pool", lambda e, i=i: e.memset(u[i][:], 0.0), writes=[f"u{tag}{i}"])
                P.op("pool", lambda e: e.memset(sa[:], 0.0), writes=[f"sa{tag}"])
                P.op("pool", lambda e: e.memset(sb[:], 0.0), writes=[f"sb{tag}"])
                for c in range(8):
                    g = c // 2
                    i = c % 2
                    uu = u[i]
                    uk = f"u{tag}{i}"
                    P.dma("sp", uu[:, 16:16 + L], PL[c][:, off:off + L], writes=[uk], sem=uk)
                    P.op("dve", lambda e, uu=uu: e.tensor_tensor(out=sa[:, 1:W], in0=uu[:, 1:W], in1=uu[:, 0:W - 1], op=ALU.add), reads=[uk], writes=[f"sa{tag}"])
                    cur, curk = sa, f"sa{tag}"
                    if g >= 1:
                        P.op("pool", lambda e: e.tensor_tensor(out=sb[:, 2:W - 1], in0=sa[:, 3:W], in1=sa[:, 1:W - 2], op=ALU.add), reads=[f"sa{tag}"], writes=[f"sb{tag}"])
                        cur, curk = sb, f"sb{tag}"
                    if g >= 2:
                        P.op("dve", lambda e: e.tensor_tensor(out=sa[:, 4:W - 3], in0=sb[:, 6:W - 1], in1=sb[:, 2:W - 5], op=ALU.add), reads=[f"sb{tag}"], writes=[f"sa{tag}"])
                        cur, curk = sa, f"sa{tag}"
                    if g >= 3:
                        P.op("pool", lambda e: e.tensor_tensor(out=sb[:, 8:W - 7], in0=sa[:, 12:W - 3], in1=sa[:, 4:W - 11], op=ALU.add), reads=[f"sa{tag}"], writes=[f"sb{tag}"])
                        cur, curk = sb, f"sb{tag}"
                    P.op("dve", lambda e, cur=cur, g=g: e.tensor_tensor(out=tmp[:], in0=cur[:, 16:16 + L], in1=rc[:, g, :], op=ALU.mult),
                         reads=[curk, f"rc{tag}"], writes=[f"tmp{tag}"])
                    P.op("pool", lambda e, uu=uu, i=i: e.tensor_tensor(out=pd[i][:], in0=tmp[:], in1=uu[:, 16:16 + L], op=ALU.subtract),
                         reads=[f"tmp{tag}", uk], writes=[f"pd{tag}{i}"])
                    P.dma("act", PD[c][:, off:off + L], pd[i][:], reads=[f"pd{tag}{i}"], writes=[], sem=f"pd{tag}{i}")

            for (off, L) in classes:
                do_class(off, L)

    def phase_attn(l):
        with P.phase("attn"):
            if l == 0:
                emit_convert(12, 32)
            ones = P.sbuf("ones", [128, 128], BF16)
            P.op("dve", lambda e: e.memset(ones[:], 1.0), writes=["ones"])
            kT = [P.sbuf(f"kT{i}", [128, T], BF16) for i in range(2)]
            Vg = [P.sbuf(f"Vg{i}", [128, 18, 128], BF16) for i in range(2)]
            qT = [P.sbuf(f"qT{i}", [128, T], BF16) for i in range(2)]
            pt = [P.sbuf(f"pt{i}", [128, 512], BF16) for i in range(6)]
            rden = [P.sbuf(f"rden{i}", [128, 512], F32) for i in range(2)]
            ob = [P.sbuf(f"ob{i}", [128, 512], BF16) for i in range(2)]
            sps = [P.psum(f"sps{i}", [128, 512], F32) for i in range(3)]
            ops_ = [P.psum(f"ops{i}", [128, 512], F32) for i in range(2)]
            dps = [P.psum(f"dps{i}", [128, 512], F32) for i in range(2)]
            scale = 128.0 ** -0.5
            st = {"nq": 0, "npt": 0, "nsp": 0}

            def do_qblock(g, gi, h, qi, c0, nqc, kts, st):
                oi = st["nq"] % 2
                st["nq"] += 1
                o_ps, d_ps = ops_[oi], dps[oi]
                nk = len(kts)

                def S(kt, si):
                    P.op("pe", lambda e: e.matmul(sps[si][:, 0:nqc], lhsT=kT[gi][:, kt * 128:(kt + 1) * 128], rhs=qT[qi][:, c0:c0 + nqc],
                                                  start=True, stop=True),
                         reads=[f"kT{gi}", f"qT{qi}"], writes=[f"sps{si}"])

                def step(ii, kt, si, pi):
                    P.op("act", lambda e: e.activation(out=pt[pi][:, 0:nqc], in_=sps[si][:, 0:nqc], func=AF.Exp, scale=scale),
                         reads=[f"sps{si}"], writes=[f"pt{pi}"])
                    P.op("pe", lambda e: e.matmul(o_ps[:, 0:nqc], lhsT=Vg[gi][:, kt, :], rhs=pt[pi][:, 0:nqc], start=(ii == 0), stop=(ii == nk - 1)),
                         reads=[f"Vg{gi}", f"pt{pi}"], writes=[f"ops{oi}"])
                    P.op("pe", lambda e: e.matmul(d_ps[:, 0:nqc], lhsT=ones[:], rhs=pt[pi][:, 0:nqc], start=(ii == 0), stop=(ii == nk - 1)),
                         reads=["ones", f"pt{pi}"], writes=[f"dps{oi}"])

                base = st["nsp"]
                st["nsp"] += nk
                for pre in range(min(2, nk)):
                    S(kts[pre], (base + pre) % 3)
                for ii, kt in enumerate(kts):
                    si = (base + ii) % 3
                    if ii + 2 < nk:
                        S(kts[ii + 2], (base + ii + 2) % 3)
                    pi = st["npt"] % 6
                    st["npt"] += 1
                    step(ii, kt, si, pi)
                P.op("dve", lambda e: e.reciprocal(out=rden[oi][:, 0:nqc], in_=d_ps[:, 0:nqc]), reads=[f"dps{oi}"], writes=[f"rden{oi}"])
                P.op("dve", lambda e: e.tensor_tensor(out=ob[oi][:, 0:nqc], in0=o_ps[:, 0:nqc], in1=rden[oi][:, 0:nqc], op=ALU.mult),
                     reads=[f"ops{oi}", f"rden{oi}"], writes=[f"ob{oi}"])
                P.dma("sp", AT[h][:, c0:c0 + nqc], ob[oi][:, 0:nqc], reads=[f"ob{oi}"], writes=[], sem=f"ob{oi}")

            for g in range(4):
                gi = g % 2
                P.dma("sp", kT[gi][:], KT[g], writes=[f"kT{gi}"], sem=f"kT{gi}")
                P.dma_split("sp", Vg[gi][:], V.rearrange("(kt p) c -> p kt c", p=128)[:, :, g * 128:(g + 1) * 128], 3, writes=[f"Vg{gi}"], sem=f"Vg{gi}")
                for hh in range(4):
                    h = g * 4 + hh
                    qi = h % 2
                    ncol = T if l == 0 else TL
                    P.dma("sp", qT[qi][:, 0:ncol], QT[h][:, 0:ncol], writes=[f"qT{qi}"], sem=f"qT{qi}")
                    qblocks = [(c0, 512, list(range(18))) for c0 in range(0, TL, 512)]
                    if l == 0:
                        qblocks.append((TL, TC, [16, 17]))
                    for (c0, nqc, kts) in qblocks:
                        do_qblock(g, gi, h, qi, c0, nqc, kts, st)

    def phase_mix(l):
        with P.phase("mix"):
            identf, _ = load_consts()
            blocks = BLOCKS_ALL if l == 0 else BLOCKS_LAT
            wpf = P.sbuf("wpf", [128, 8, 512], F32)
            wpb = P.sbuf("wpb", [128, 8, 512], BF16)
            P.dma("sp", wpf[:], w_pool[l].rearrange("g (kc p) d -> p (g kc) d", p=128), writes=["wpf"], sem="const2")
            P.op("dve", lambda e: e.tensor_copy(out=wpb[:], in_=wpf[:]), reads=["wpf"], writes=["wpb"])
            psr = P.sbuf("psr", [16, 128], F32)
            pscT = P.sbuf("pscT", [128, 16], F32)
            P.dma("sp", psr[:], pool_scale[l].rearrange("(j p) -> j p", p=128), writes=["psr"], sem="const3")
            ptp = P.psum("ptp", [128, 16], F32)
            P.op("pe", lambda e: e.transpose(out=ptp[:], in_=psr[:], identity=identf[0:16, 0:16]), reads=["psr", "identf"], writes=["ptp"])
            P.op("dve", lambda e: e.tensor_copy(out=pscT[:], in_=ptp[:]), reads=["ptp"], writes=["pscT"])
            pdb = [P.sbuf(f"pdb{i}", [128, 2, 512], BF16) for i in range(2)]
            gab = [P.sbuf(f"gab{i}", [128, 4, 512], BF16) for i in range(2)]
            gbb = [P.sbuf(f"gbb{i}", [128, 4, 512], BF16) for i in range(2)]
            m1 = [P.sbuf(f"m1{i}", [128, 512], F32) for i in range(2)]
            m2 = [P.sbuf(f"m2{i}", [128, 512], F32) for i in range(2)]
            mg = [P.sbuf(f"mg{i}", [128, 512], BF16) for i in range(3)]
            pb = [P.psum(f"pb{i}", [128, 512], F32) for i in range(2)]
            cnt = {"blk": 0, "ev": 0}
            cur = {}

            def per_block(cb, t0, nt):
                i = cnt["blk"] % 2
                cnt["blk"] += 1
                cur["i"] = i
                P.dma("sp", pdb[i][:, :, 0:nt], PD[2 * cb:2 * cb + 2].rearrange("c p t -> p c t")[:, :, t0:t0 + nt], writes=[f"pdb{i}"], sem=f"pdb{i}")
                P.dma("sp", gab[i][:, :, 0:nt], GAB[4 * cb:4 * cb + 4].rearrange("c p t -> p c t")[:, :, t0:t0 + nt], writes=[f"gab{i}"], sem=f"gab{i}")
                P.dma("sp", gbb[i][:, :, 0:nt], GAB[16 + 4 * cb:16 + 4 * cb + 4].rearrange("c p t -> p c t")[:, :, t0:t0 + nt], writes=[f"gbb{i}"], sem=f"gbb{i}")

            def evac(cb, t0, cc, nt, ps, pk):
                i = cur["i"]
                dc = cb * 4 + cc
                e2 = cnt["ev"] % 2
                e3 = cnt["ev"] % 3
                cnt["ev"] += 1
                p_b = pb[e2]
                for kc in range(2):
                    P.op("pe", lambda e, kc=kc: e.matmul(p_b[:, 0:nt], lhsT=wpb[:, cb * 2 + kc, cc * 128:(cc + 1) * 128], rhs=pdb[i][:, kc, 0:nt],
                                                         start=(kc == 0), stop=(kc == 1)),
                         reads=["wpb", f"pdb{i}"], writes=[f"pb{e2}"])
                P.op("dve", lambda e: e.tensor_tensor(out=m1[e2][:, 0:nt], in0=ps[:, 0:nt], in1=gab[i][:, cc, 0:nt], op=ALU.mult),
                     reads=[pk, f"gab{i}"], writes=[f"m1{e2}"])
                P.op("dve", lambda e: e.scalar_tensor_tensor(out=m2[e2][:, 0:nt], in0=p_b[:, 0:nt], scalar=pscT[:, dc:dc + 1], in1=gbb[i][:, cc, 0:nt],
                                                             op0=ALU.mult, op1=ALU.mult),
                     reads=[f"pb{e2}", "pscT", f"gbb{i}"], writes=[f"m2{e2}"])
                P.op("pool", lambda e: e.tensor_tensor(out=mg[e3][:, 0:nt], in0=m1[e2][:, 0:nt], in1=m2[e2][:, 0:nt], op=ALU.add),
                     reads=[f"m1{e2}", f"m2{e2}"], writes=[f"mg{e3}"])
                P.dma("act", MG[dc][:, t0:t0 + nt], mg[e3][:, 0:nt], reads=[f"mg{e3}"], writes=[], sem=f"mg{e3}")

            proj(w_br[l], list(range(4)), AT.rearrange("h p t -> p h t"), lambda cb: blocks, lambda cb: "feat", evac, per_block=per_block, npp=2)

    def phase_wout(l):
        with P.phase("wout"):
            blocks = BLOCKS_ALL if l == 0 else BLOCKS_LAT
            G = [P.sbuf(f"G{s}", [128, D], F32) for s in range(2)]
            for s in ([0, 1] if l == 0 else [0]):
                load_bc(G[s], f"G{s}", l, s, G_A, f"bcG{s}")
            xb = [P.sbuf(f"xo{i}", [128, 4, 512], F32) for i in range(2)]
            tt = [P.sbuf(f"to{i}", [128, 512], F32) for i in range(3)]
            cnt = {"ev": 0, "blk": 0}
            cur = {}

            def per_block(cb, t0, nt):
                b = cnt["blk"] % 2
                cnt["blk"] += 1
                cur["b"] = b
                P.dma("sp", xb[b][:, 0:nt // 128, :], xsrc(l, t0, nt, cb * 512, 512).rearrange("(t p) c -> p t c", p=128), writes=[f"xo{b}"], sem=f"xo{b}")

            def evac(cb, t0, ti, nt, ps, pk):
                r0 = t0 + ti * 128
                s = 0 if r0 < TL else 1
                i = cnt["ev"] % 3
                cnt["ev"] += 1
                b = cur["b"]
                P.op("dve", lambda e: e.tensor_tensor(out=tt[i][:], in0=ps[:], in1=G[s][:, cb * 512:(cb + 1) * 512], op=ALU.mult),
                     reads=[pk, f"G{s}"], writes=[f"to{i}"])
                P.op("pool", lambda e: e.tensor_tensor(out=tt[i][:], in0=tt[i][:], in1=xb[b][:, ti, :], op=ALU.add), reads=[f"to{i}", f"xo{b}"], writes=[f"to{i}"])
                P.dma("act", X[r0:r0 + 128, cb * 512:(cb + 1) * 512], tt[i][:], reads=[f"to{i}"], writes=[], sem=f"to{i}")

            proj(w_out[l], list(range(4)), MG.rearrange("h p t -> p h t"), lambda cb: blocks, lambda cb: "tok", evac, per_block=per_block)

    def phase_qpeer(l):
        with P.phase("qpeer"):
            blocks = BLOCKS_ALL if l == 0 else BLOCKS_LAT
            ev = [P.sbuf(f"ev{i}", [128, 512], F32) for i in range(3)]
            cnt = {"ev": 0}

            def evac(cb, t0, cc, nt, ps, pk):
                i = cnt["ev"] % 3
                cnt["ev"] += 1
                P.op("act", lambda e: e.copy(out=ev[i][:, 0:nt], in_=ps[:, 0:nt]), reads=[pk], writes=[f"ev{i}"])
                P.dma("act", QPT[cb * 4 + cc][:, t0:t0 + nt], ev[i][:, 0:nt], reads=[f"ev{i}"], writes=[], sem=f"ev{i}")

            proj(w_qp[l], list(range(4)), HT, lambda cb: blocks, lambda cb: "feat", evac)

    def phase_topk(l):
        with P.phase("topk"):
            if l == 0:
                emit_convert(32, 64)
            identf, _ = load_consts()
            ntiles = (T if l == 0 else TL) // 128
            io16 = P.sbuf("io16", [128, 16], F32)
            P.dma("sp", io16[:], iota16_d, writes=["io16"], sem="const2")
            kraw = P.sbuf("kraw", [128, 16, 128], F32)
            keysT = P.sbuf("keysT", [128, 16, 128], F32)
            P.dma_split("sp", kraw[:], peer_keys[l].rearrange("h p k d -> k (h p) d"), 2, writes=["kraw"], sem="const3")
            pk4 = [P.psum(f"pk4{i}", [128, 4, 128], F32) for i in range(4)]
            for grp in range(4):
                for q in range(4):
                    hp = grp * 4 + q
                    P.op("pe", lambda e, hp=hp, q=q, grp=grp: e.transpose(out=pk4[grp][:, q, :], in_=kraw[:, hp, :], identity=identf[:]),
                         reads=["kraw", "identf"], writes=[f"pk4{grp}"])
                P.op("act", lambda e, grp=grp: e.copy(out=keysT[:, grp * 4:(grp + 1) * 4, :], in_=pk4[grp][:]), reads=[f"pk4{grp}"], writes=["keysT"])
            qt = [P.sbuf(f"qt{i}", [128, 16, 128], F32) for i in range(2)]
            Sbuf = [P.sbuf(f"S{k}", [128, 16, 128], F32) for k in range(2)]
            S2 = P.sbuf("S2", [128, 16, 128], F32)
            m = P.sbuf("m", [128, 16, 16], F32)
            ix = P.sbuf("ix", [128, 16, 16], U32)
            ixf = P.sbuf("ixf", [128, 16, 16], F32)
            i1s = P.sbuf("i1s", [128, 8, 16], F32)
            cand = P.sbuf("cand", [128, 8, 256], F32)
            cand2 = P.sbuf("cand2", [128, 8, 256], F32)
            ts = P.sbuf("ts", [128, 8, 16], F32)
            pos = P.sbuf("pos", [128, 8, 16], U32)
            au = P.sbuf("au", [128, 8, 16], U32)
            bu = P.sbuf("bu", [128, 8, 16], U32)
            af_ = P.sbuf("af", [128, 8, 16], F32)
            bf_ = P.sbuf("bf", [128, 8, 16], F32)
            oh = P.sbuf("oh", [128, 8, 16, 16], F32)
            isel = P.sbuf("isel", [128, 8, 16], F32)
            jsel = P.sbuf("jsel", [128, 8, 16], F32)
            ef = P.sbuf("ef", [128, 128], F32)
            eu = [P.sbuf(f"eu{i}", [128, 128], U32) for i in range(2)]
            dd = P.sbuf("dd", [128, 8, 16], F32)
            ee = P.sbuf("ee", [128, 8, 16], F32)
            zz = P.sbuf("zz", [128, 16], F32)
            gg = [P.sbuf(f"gg{i}", [128, 128], F32) for i in range(2)]
            NEG = -1e30
            dense = peer_mode == "dense"
            if dense:
                io128 = P.sbuf("io128", [128, 128], F32)
                P.dma("sp", io128[:], iota128_d, writes=["io128"], sem="const4")
                tp = P.psum("tp", [128, 3, 128], F32)
                ijg = P.sbuf("ijg", [128, 3, 128], F32)
                Aoh = [P.sbuf(f"Aoh{k}", [128, 16, 128], BF16) for k in range(2)]
                Boh = [P.sbuf(f"Boh{k}", [128, 128], BF16) for k in range(8)]
                gp = [P.psum(f"gp{k}", [128, 4, 128], F32) for k in range(2)]
                stg = [P.sbuf(f"stg{k}", [128, 128, 128], BF16) for k in range(2)]
                gst = {"b": 0, "g": 0}

            def gbuild(tt, i):
                r0 = tt * 128
                sg = tt % 2
                srcs = [isel[:].rearrange("p h k -> p (h k)"), jsel[:].rearrange("p h k -> p (h k)"), gg[i][:]]
                keys = ["sel0", "sel1", f"gg{i}"]
                for q in range(3):
                    P.op("pe", lambda e, q=q: e.transpose(out=tp[:, q, :], in_=srcs[q], identity=identf[:]), reads=[keys[q], "identf"], writes=["tp"])
                P.op("act", lambda e: e.copy(out=ijg[:], in_=tp[:]), reads=["tp"], writes=["ijg"])
                for grp in range(8):
                    a = grp % 2
                    for tq in range(16):
                        tl = grp * 16 + tq
                        P.op("pool", lambda e, a=a, tq=tq, tl=tl: e.tensor_scalar(out=Aoh[a][:, tq, :], in0=io128[:], scalar1=ijg[:, 0, tl:tl + 1], scalar2=None, op0=ALU.is_equal),
                             reads=["io128", "ijg"], writes=[f"Aoh{a}"])
                    for q4 in range(4):
                        gk = gst["g"] % 2
                        gst["g"] += 1
                        for q in range(4):
                            tl = grp * 16 + q4 * 4 + q
                            b = gst["b"] % 8
                            gst["b"] += 1
                            P.op("dve", lambda e, b=b, tl=tl: e.tensor_scalar(out=Boh[b][:], in0=io128[:], scalar1=ijg[:, 1, tl:tl + 1], scalar2=ijg[:, 2, tl:tl + 1],
                                                                              op0=ALU.is_equal, op1=ALU.mult),
                                 reads=["io128", "ijg"], writes=[f"Boh{b}"])
                            P.op("pe", lambda e, b=b, a=a, gk=gk, q=q, q4=q4: e.matmul(gp[gk][:, q, :], lhsT=Boh[b][:], rhs=Aoh[a][:, q4 * 4 + q, :], start=True, stop=True),
                                 reads=[f"Boh{b}", f"Aoh{a}"], writes=[f"gp{gk}"])
                        tl0 = grp * 16 + q4 * 4
                        P.op("act", lambda e, gk=gk, tl0=tl0, sg=sg: e.copy(out=stg[sg][:, :, tl0:tl0 + 4].rearrange("p i t -> p t i"), in_=gp[gk][:]),
                             reads=[f"gp{gk}"], writes=[f"stg{sg}"])
                dst = GTd.rearrange("i j t -> j i t")
                for k in range(16):
                    P.dma("sp", dst[:, k * 8:(k + 1) * 8, r0:r0 + 128], stg[sg][:, k * 8:(k + 1) * 8, :], reads=[f"stg{sg}"], writes=[], sem=f"stg{sg}")

            for tt in range(ntiles):
                r0 = tt * 128
                i = tt % 2
                P.dma_split("sp", qt[i][:], QPT.rearrange("c p t -> p c t")[:, :, r0:r0 + 128], 2, writes=[f"qt{i}"], sem=f"qt{i}")
                for grp in range(4):
                    for q in range(4):
                        hp = grp * 4 + q
                        P.op("pe", lambda e, hp=hp, q=q, grp=grp, i=i: e.matmul(pk4[grp][:, q, :], lhsT=qt[i][:, hp, :], rhs=keysT[:, hp, :], start=True, stop=True),
                             reads=[f"qt{i}", "keysT"], writes=[f"pk4{grp}"])
                    P.op("act", lambda e, grp=grp, Sx=Sbuf[i]: e.copy(out=Sx[:, grp * 4:(grp + 1) * 4, :], in_=pk4[grp][:]), reads=[f"pk4{grp}"], writes=[f"S{i}_{grp}"])
                for hp in range(16):
                    sk = f"S{i}_{hp // 4}"
                    P.op("dve", lambda e, hp=hp, Sx=Sbuf[i]: e.max(out=m[:, hp, 0:8], in_=Sx[:, hp, :]), reads=[sk], writes=[f"ma{hp}"])
                for hp in range(16):
                    sk = f"S{i}_{hp // 4}"
                    P.op("dve", lambda e, hp=hp, Sx=Sbuf[i]: e.max_index(out=ix[:, hp, 0:8], in_max=m[:, hp, 0:8], in_values=Sx[:, hp, :]), reads=[sk, f"ma{hp}"], writes=[f"ixa{hp}"])
                for hp in range(16):
                    sk = f"S{i}_{hp // 4}"
                    P.op("dve", lambda e, hp=hp, Sx=Sbuf[i]: e.match_replace(out=S2[:, hp, :], in_to_replace=m[:, hp, 0:8], in_values=Sx[:, hp, :], imm_value=NEG),
                         reads=[sk, f"ma{hp}"], writes=[f"S2_{hp}"])
                for hp in range(16):
                    P.op("dve", lambda e, hp=hp: e.max(out=m[:, hp, 8:16], in_=S2[:, hp, :]), reads=[f"S2_{hp}"], writes=[f"mb{hp}"])
                for hp in range(16):
                    P.op("dve", lambda e, hp=hp: e.max_index(out=ix[:, hp, 8:16], in_max=m[:, hp, 8:16], in_values=S2[:, hp, :]), reads=[f"S2_{hp}", f"mb{hp}"], writes=[f"ixb{hp}"])
                mkeys = [f"ma{hp}" for hp in range(16)] + [f"mb{hp}" for hp in range(16)]
                ixkeys = [f"ixa{hp}" for hp in range(16)] + [f"ixb{hp}" for hp in range(16)]
                P.op("dve", lambda e: e.tensor_copy(out=ixf[:], in_=ix[:]), reads=ixkeys, writes=["ixf"])
                mv = m[:].rearrange("p (h two) k -> p h two k", two=2)
                iv = ixf[:].rearrange("p (h two) k -> p h two k", two=2)
                cv = cand[:].rearrange("p h (a b) -> p h a b", a=16)
                P.op("dve", lambda e: e.tensor_tensor(out=cv, in0=mv[:, :, 0, :].unsqueeze(3).to_broadcast([128, 8, 16, 16]),
                                                      in1=mv[:, :, 1, :].unsqueeze(2).to_broadcast([128, 8, 16, 16]), op=ALU.add),
                     reads=mkeys, writes=["cand"])
                for h in range(8):
                    P.op("dve", lambda e, h=h: e.max(out=ts[:, h, 0:8], in_=cand[:, h, :]), reads=["cand"], writes=[f"tsa{h}"])
                for h in range(8):
                    P.op("dve", lambda e, h=h: e.max_index(out=pos[:, h, 0:8], in_max=ts[:, h, 0:8], in_values=cand[:, h, :]), reads=["cand", f"tsa{h}"], writes=[f"posa{h}"])
                for h in range(8):
                    P.op("dve", lambda e, h=h: e.match_replace(out=cand2[:, h, :], in_to_replace=ts[:, h, 0:8], in_values=cand[:, h, :], imm_value=NEG),
                         reads=["cand", f"tsa{h}"], writes=[f"c2_{h}"])
                for h in range(8):
                    P.op("dve", lambda e, h=h: e.max(out=ts[:, h, 8:16], in_=cand2[:, h, :]), reads=[f"c2_{h}"], writes=[f"tsb{h}"])
                for h in range(8):
                    P.op("dve", lambda e, h=h: e.max_index(out=pos[:, h, 8:16], in_max=ts[:, h, 8:16], in_values=cand2[:, h, :]), reads=[f"c2_{h}", f"tsb{h}"], writes=[f"posb{h}"])
                tskeys = [f"tsa{h}" for h in range(8)] + [f"tsb{h}" for h in range(8)]
                poskeys = [f"posa{h}" for h in range(8)] + [f"posb{h}" for h in range(8)]
                P.op("dve", lambda e: e.tensor_single_scalar(out=au[:], in_=pos[:], scalar=4, op=ALU.logical_shift_right), reads=poskeys, writes=["au"])
                P.op("dve", lambda e: e.tensor_single_scalar(out=bu[:], in_=pos[:], scalar=15, op=ALU.bitwise_and), reads=poskeys, writes=["bu"])
                P.op("dve", lambda e: e.tensor_copy(out=af_[:], in_=au[:]), reads=["au"], writes=["af"])
                P.op("dve", lambda e: e.tensor_copy(out=bf_[:], in_=bu[:]), reads=["bu"], writes=["bf"])
                for (sel, xf, which, key) in ((isel, af_, 0, "af"), (jsel, bf_, 1, "bf")):
                    P.op("dve", lambda e, xf=xf: e.tensor_tensor(out=oh[:], in0=io16[:].unsqueeze(1).unsqueeze(1).to_broadcast([128, 8, 16, 16]),
                                                                  in1=xf[:].unsqueeze(3).to_broadcast([128, 8, 16, 16]), op=ALU.is_equal),
                         reads=["io16", key], writes=["oh"])
                    P.op("dve", lambda e, which=which: e.tensor_tensor(out=oh[:], in0=oh[:], in1=iv[:, :, which, :].unsqueeze(2).to_broadcast([128, 8, 16, 16]), op=ALU.mult),
                         reads=["oh", "ixf"], writes=["oh"])
                    P.op("dve", lambda e, sel=sel: e.tensor_reduce(out=sel[:], in_=oh[:], axis=AX.X, op=ALU.add), reads=["oh"], writes=["sel%d" % which])
                P.op("dve", lambda e: e.scalar_tensor_tensor(out=ef[:], in0=isel[:].rearrange("p h k -> p (h k)"), scalar=128.0, in1=jsel[:].rearrange("p h k -> p (h k)"),
                                                             op0=ALU.mult, op1=ALU.add),
                     reads=["sel0", "sel1"], writes=["ef"])
                if l > 0:
                    P.op("dve", lambda e: e.tensor_scalar(out=ef[:], in0=ef[:], scalar1=float(l * NEXP), scalar2=None, op0=ALU.add), reads=["ef"], writes=["ef"])
                P.op("dve", lambda e, i=i: e.tensor_copy(out=eu[i][:], in_=ef[:]), reads=["ef"], writes=[f"eu{i}"])
                P.dma("sp", EIDX[r0:r0 + 128, :], eu[i][:], reads=[f"eu{i}"], writes=[], sem=f"eu{i}")
                P.op("dve", lambda e: e.tensor_tensor(out=dd[:], in0=ts[:], in1=ts[:, :, 0:1].to_broadcast([128, 8, 16]), op=ALU.subtract), reads=tskeys, writes=["dd"])
                P.op("act", lambda e: e.activation(out=ee[:], in_=dd[:], func=AF.Exp), reads=["dd"], writes=["ee"])
                P.op("dve", lambda e: e.tensor_reduce(out=zz[:, 0:8], in_=ee[:], axis=AX.X, op=ALU.add), reads=["ee"], writes=["zz"])
                P.op("dve", lambda e: e.reciprocal(out=zz[:, 8:16], in_=zz[:, 0:8]), reads=["zz"], writes=["zz"])
                P.op("dve", lambda e, i=i: e.tensor_tensor(out=gg[i][:].rearrange("p (h k) -> p h k", h=8), in0=ee[:], in1=zz[:, 8:16].unsqueeze(2).to_broadcast([128, 8, 16]), op=ALU.mult),
                     reads=["ee", "zz"], writes=[f"gg{i}"])
                P.dma("sp", GATE[r0:r0 + 128, :], gg[i][:], reads=[f"gg{i}"], writes=[], sem=f"gg{i}")
                if dense:
                    gbuild(tt, i)

    def phase_gather(l):
        with P.phase("gather"):
            P.wait_persistent()
            identf, identb = load_consts(need_bf=True)
            ntiles = (T if l == 0 else TL) // 128
            G = [P.sbuf(f"G{s}", [128, D], F32) for s in range(2)]
            for s in ([0, 1] if l == 0 else [0]):
                load_bc(G[s], f"G{s}", l, s, G_F, f"bcG{s}")
            NS = 8
            LOOK = 5
            uv = [P.sbuf(f"uv{i}", [128, 2 * D], BF16) for i in range(NS)]
            h2 = [P.sbuf(f"h2{i}", [128, D], F32) for i in range(2)]
            xt = [P.sbuf(f"xt{i}", [128, D], F32) for i in range(2)]
            junk = P.sbuf("junk", [128, D], F32)
            eix = [P.sbuf(f"eix{i}", [128, 128], U32) for i in range(2)]
            gat = [P.sbuf(f"gat{i}", [128, 128], F32) for i in range(2)]
            act = [P.sbuf(f"act{i}", [128, 128], F32) for i in range(2)]
            ge = [P.sbuf(f"ge{i}", [128, 128], F32) for i in range(2)]
            dg = [P.sbuf(f"dg{i}", [128, 128], BF16) for i in range(4)]
            tmp = [P.sbuf(f"tmp{i}", [128, 512], F32) for i in range(2)]
            acc = [[P.psum(f"acc{a}_{b}", [128, 512], F32) for b in range(4)] for a in range(2)]
            st = {"ntmp": 0}
            items = [(tt, sidx) for tt in range(ntiles) for sidx in range(128)]

            def loads(tt):
                r0 = tt * 128
                i = tt % 2
                P.dma("sp", h2[i][:], H2[r0:r0 + 128, :], writes=[f"h2{i}"], sem=f"h2{i}")
                P.dma("sp", xt[i][:], X[r0:r0 + 128, :], writes=[f"xt{i}"], sem=f"xt{i}")
                P.dma("sp", eix[i][:], EIDX[r0:r0 + 128, :], writes=[f"eix{i}"], sem=f"eix{i}")
                P.dma("sp", gat[i][:], GATE[r0:r0 + 128, :], writes=[f"gat{i}"], sem=f"gat{i}")

            def gather(n):
                tt, sidx = items[n]
                i = tt % 2
                u = n % NS
                if sidx == 0:
                    loads(tt)
                P.dma_fn("pool", lambda e: e.indirect_dma_start(
                    out=uv[u][:], out_offset=None, in_=UV, in_offset=bass.IndirectOffsetOnAxis(ap=eix[i][:, sidx:sidx + 1], axis=0)),
                    reads=[f"eix{i}"], writes=[f"uv{u}"], sem=f"uv{u}")

            def dot(n):
                tt, sidx = items[n]
                i = tt % 2
                u = n % NS
                P.op("dve", lambda e: e.scalar_tensor_tensor(out=junk[:], in0=h2[i][:], scalar=1.0, in1=uv[u][:, 0:D], op0=ALU.mult, op1=ALU.mult,
                                                             accum_out=act[i][:, sidx:sidx + 1]),
                     reads=[f"h2{i}", f"uv{u}"], writes=[f"a{i}_{sidx}"])
                P.op("act", lambda e: e.activation(out=ge[i][:, sidx:sidx + 1], in_=act[i][:, sidx:sidx + 1], func=AF.Gelu),
                     reads=[f"a{i}_{sidx}"], writes=[f"g{i}_{sidx}"])

            def combine(n):
                tt, sidx = items[n]
                i = tt % 2
                u = n % NS
                d = n % 4
                P.op("dve", lambda e: e.tensor_scalar(out=dg[d][:], in0=identb[:], scalar1=ge[i][:, sidx:sidx + 1], scalar2=gat[i][:, sidx:sidx + 1],
                                                      op0=ALU.mult, op1=ALU.mult),
                     reads=["identb", f"g{i}_{sidx}", f"gat{i}"], writes=[f"dg{d}"])
                for db in range(4):
                    P.op("pe", lambda e, db=db: e.matmul(acc[i][db][:], lhsT=dg[d][:], rhs=uv[u][:, D + db * 512:D + (db + 1) * 512],
                                                         start=(sidx == 0), stop=(sidx == 127)),
                         reads=[f"dg{d}", f"uv{u}"], writes=[f"acc{i}_{db}"])
                if sidx == 127:
                    finalize(tt)

            def finalize(tt):
                r0 = tt * 128
                i = tt % 2
                s = 0 if r0 < TL else 1
                for db in range(4):
                    tq = st["ntmp"] % 2
                    st["ntmp"] += 1
                    P.op("dve", lambda e, tq=tq, db=db: e.tensor_tensor(out=tmp[tq][:], in0=acc[i][db][:], in1=G[s][:, db * 512:(db + 1) * 512], op=ALU.mult),
                         reads=[f"acc{i}_{db}", f"G{s}"], writes=[f"tmp{tq}"])
                    P.op("dve", lambda e, tq=tq, db=db: e.tensor_tensor(out=xt[i][:, db * 512:(db + 1) * 512], in0=tmp[tq][:], in1=xt[i][:, db * 512:(db + 1) * 512], op=ALU.add),
                         reads=[f"tmp{tq}", f"xt{i}"], writes=[f"xt{i}"])
                P.dma("sp", X[r0:r0 + 128, :], xt[i][:], reads=[f"xt{i}"], writes=[], sem=f"xst{i}")

            N = len(items)
            for n in range(min(LOOK, N)):
                gather(n)
            for n in range(N):
                if n + LOOK < N:
                    gather(n + LOOK)
                dot(n)
                if n >= 1:
                    combine(n - 1)
            combine(N - 1)

    def phase_dense(l):
        with P.phase("dense"):
            _, identb = load_consts(need_bf=True)
            ntok = T if l == 0 else TL
            groups = [(0, 768), (768, 768), (1536, ntok - 1536)]
            G = [P.sbuf(f"G{s}", [128, D], F32) for s in range(2)]
            for s in ([0, 1] if l == 0 else [0]):
                load_bc(G[s], f"G{s}", l, s, G_F, f"bcG{s}")
            hT = P.sbuf("hT", [128, 16, 768], BF16)
            acc = P.sbuf("acc", [128, 6, D], F32)
            GA = P.sbuf("GA", [128, 8, 768], BF16)
            Vb = P.sbuf("Vb", [128, 8, D], BF16)
            Ub = [P.sbuf(f"Ub{k}", [128, D], BF16) for k in range(3)]
            UT = [P.sbuf(f"UT{k}", [128, 16, 128], BF16) for k in range(2)]
            gt = [P.sbuf(f"gt{k}", [128, 768], BF16) for k in range(3)]
            gl = [P.sbuf(f"gl{k}", [128, 512], BF16) for k in range(2)]
            xt = [P.sbuf(f"xt{k}", [128, D], F32) for k in range(2)]
            ptu = [P.psum(f"ptu{k}", [128, 16, 128], BF16) for k in range(2)]
            ps1 = [P.psum(f"ps1{k}", [128, 512], F32) for k in range(2)]
            ps2 = [P.psum(f"ps2{k}", [128, 512], F32) for k in range(2)]
            st = {"c": 0, "p1": 0, "p2": 0, "x": 0}

            def chunk(c, ci, g0, gn, nblocks):
                k3 = st["c"] % 3
                k2 = st["c"] % 2
                st["c"] += 1
                row0 = l * NEXP + c * 128
                P.dma("sp", Ub[k3][:], UV[row0:row0 + 128, 0:D], writes=[f"Ub{k3}"], sem=f"Ub{k3}")
                P.dma("sp", Vb[:, ci, :], UV[row0:row0 + 128, D:2 * D], writes=[f"Vb{ci}"], sem=f"Vb{ci}")
                P.dma("sp", gt[k3][:, 0:gn], GTd[c][:, g0:g0 + gn], writes=[f"gt{k3}"], sem=f"gt{k3}")
                for j in range(16):
                    P.op("pe", lambda e, j=j: e.transpose(out=ptu[k2][:, j, :], in_=Ub[k3][:, j * 128:(j + 1) * 128], identity=identb[:]),
                         reads=[f"Ub{k3}", "identb"], writes=[f"ptu{k2}"])
                P.op("pool" if False else "dve", lambda e: e.tensor_copy(out=UT[k2][:], in_=ptu[k2][:]), reads=[f"ptu{k2}"], writes=[f"UT{k2}"])
                for (b0, bn) in nblocks:
                    p1 = st["p1"] % 2
                    st["p1"] += 1
                    for j in range(16):
                        P.op("pe", lambda e, j=j, p1=p1: e.matmul(ps1[p1][:, 0:bn], lhsT=UT[k2][:, j, :], rhs=hT[:, j, b0:b0 + bn], start=(j == 0), stop=(j == 15)),
                             reads=[f"UT{k2}", "hT"], writes=[f"ps1{p1}"])
                    P.op("act", lambda e, p1=p1: e.activation(out=gl[p1][:, 0:bn], in_=ps1[p1][:, 0:bn], func=AF.Gelu), reads=[f"ps1{p1}"], writes=[f"gl{p1}"])
                    P.op("pool", lambda e, p1=p1: e.tensor_tensor(out=GA[:, ci, b0:b0 + bn], in0=gl[p1][:, 0:bn], in1=gt[k3][:, b0:b0 + bn], op=ALU.mult),
                         reads=[f"gl{p1}", f"gt{k3}"], writes=[f"GA{ci}"])

            def combine(cg, ntile):
                for ti in range(ntile):
                    for db in range(4):
                        p2 = st["p2"] % 2
                        st["p2"] += 1
                        for ci in range(8):
                            P.op("pe", lambda e, ci=ci, p2=p2: e.matmul(ps2[p2][:], lhsT=GA[:, ci, ti * 128:(ti + 1) * 128], rhs=Vb[:, ci, db * 512:(db + 1) * 512],
                                                                        start=(ci == 0), stop=(ci == 7)),
                                 reads=[f"GA{ci}", f"Vb{ci}"], writes=[f"ps2{p2}"])
                        if cg == 0:
                            P.op("dve", lambda e, p2=p2: e.tensor_copy(out=acc[:, ti, db * 512:(db + 1) * 512], in_=ps2[p2][:]), reads=[f"ps2{p2}"], writes=[f"acc{ti}_{db}"])
                        else:
                            P.op("dve", lambda e, p2=p2: e.tensor_tensor(out=acc[:, ti, db * 512:(db + 1) * 512], in0=ps2[p2][:], in1=acc[:, ti, db * 512:(db + 1) * 512], op=ALU.add),
                                 reads=[f"ps2{p2}", f"acc{ti}_{db}"], writes=[f"acc{ti}_{db}"])

            def finalize(g0, ti):
                r0 = g0 + ti * 128
                s = 0 if r0 < TL else 1
                k = st["x"] % 2
                st["x"] += 1
                P.dma("sp", xt[k][:], X[r0:r0 + 128, :], writes=[f"xt{k}"], sem=f"xt{k}")
                P.op("pool", lambda e: e.tensor_tensor(out=acc[:, ti, :], in0=acc[:, ti, :], in1=G[s][:], op=ALU.mult),
                     reads=[f"acc{ti}_{db}" for db in range(4)] + [f"G{s}"], writes=[f"acc{ti}_{db}" for db in range(4)])
                P.op("pool", lambda e: e.tensor_tensor(out=xt[k][:], in0=xt[k][:], in1=acc[:, ti, :], op=ALU.add),
                     reads=[f"acc{ti}_{db}" for db in range(4)] + [f"xt{k}"], writes=[f"xt{k}"])
                P.dma("sp", X[r0:r0 + 128, :], xt[k][:], reads=[f"xt{k}"], writes=[], sem=f"xst{k}")

            for (g0, gn) in groups:
                ntile = gn // 128
                nblocks = [(0, 384), (384, 384)] if gn == 768 else [(0, gn)]
                P.dma_split("sp", hT[:, :, 0:gn], HT[:, :, g0:g0 + gn], 2, writes=["hT"], sem="hT")
                for cg in range(16):
                    for ci in range(8):
                        chunk(cg * 8 + ci, ci, g0, gn, nblocks)
                    combine(cg, ntile)
                for ti in range(ntile):
                    finalize(g0, ti)

    def phase_final():
        with P.phase("final"):
            fg = P.sbuf("fg", [128, D], F32)
            P.dma("sp", fg[:], final_gain.partition_broadcast(128), writes=["fg"], sem="const")
            xt = [P.sbuf(f"xt{i}", [128, D], F32) for i in range(3)]
            junk = P.sbuf("junk", [128, D], F32)
            st = [P.sbuf(f"st{i}", [128, 4], F32) for i in range(3)]
            for tt in range(TL // 128):
                r0 = tt * 128
                i = tt % 3
                x_t, s_t = xt[i], st[i]
                P.dma("sp", x_t[:], X[r0:r0 + 128, :], writes=[f"xt{i}"], sem=f"xt{i}")
                P.op("act", lambda e, x_t=x_t, s_t=s_t: e.activation(out=junk[:], in_=x_t[:], func=AF.Square, accum_out=s_t[:, 0:1]),
                     reads=[f"xt{i}"], writes=["junk", f"st{i}"])
                P.op("dve", lambda e, s_t=s_t: e.tensor_scalar(out=s_t[:, 1:2], in0=s_t[:, 0:1], scalar1=1.0 / D, scalar2=EPS, op0=ALU.mult, op1=ALU.add),
                     reads=[f"st{i}"], writes=[f"st{i}"])
                P.op("act", lambda e, s_t=s_t: e.sqrt(out=s_t[:, 2:3], in_=s_t[:, 1:2]), reads=[f"st{i}"], writes=[f"st{i}"])
                P.op("dve", lambda e, s_t=s_t: e.reciprocal(out=s_t[:, 3:4], in_=s_t[:, 2:3]), reads=[f"st{i}"], writes=[f"st{i}"])
                P.op("dve", lambda e, x_t=x_t, s_t=s_t: e.scalar_tensor_tensor(out=x_t[:], in0=x_t[:], scalar=s_t[:, 3:4], in1=fg[:], op0=ALU.mult, op1=ALU.mult),
                     reads=[f"xt{i}", f"st{i}", "fg"], writes=[f"xt{i}"])
                P.dma("pool", out_d[r0:r0 + 128, :], x_t[:], reads=[f"xt{i}"], writes=[], sem=f"ost{i}")

    def stop(l, name):
        return stop_after is not None and stop_after == (l, name)

    done = False
    for l in range(nlayers):
        blocks = BLOCKS_ALL
        steps = [
            ("mod", lambda: phase_mod(l)),
            ("modA", lambda: phase_modulate(l, SH_A, SC_A, False, BLOCKS_ALL, False)),
            ("inproj", lambda: phase_inproj(l)),
            ("pool", lambda: phase_pool(l)),
            ("attn", lambda: phase_attn(l)),
            ("mix", lambda: phase_mix(l)),
            ("wout", lambda: phase_wout(l)),
            ("modF", lambda: phase_modulate(l, SH_F, SC_F, True, BLOCKS_ALL if l == 0 else BLOCKS_LAT, True)),
            ("qpeer", lambda: phase_qpeer(l)),
            ("topk", lambda: phase_topk(l)),
            ("gather", (lambda: phase_dense(l)) if peer_mode == "dense" else (lambda: phase_gather(l))),
        ]
        for name, fn in steps:
            fn()
            if stop(l, name):
                done = True
                break
        if done:
            break
    if not done:
        phase_final()
    P.close()
    return nc, P


def _consts():
    t = np.arange(TL)
    row = (t // 64).astype(np.float32)
    col = (t % 64).astype(np.float32)
    inv = (np.float32(10000.0) ** (-np.arange(32, dtype=np.float32) / np.float32(32))).astype(np.float32)
    ar = (row[:, None] * inv[None, :]).astype(np.float32)
    ac = (col[:, None] * inv[None, :]).astype(np.float32)
    cr, sr, cc, sc = np.cos(ar), np.sin(ar), np.cos(ac), np.sin(ac)
    ropeC = np.concatenate([cr, cr, cc, cc], axis=1).astype(np.float32)
    ropeS = np.concatenate([-sr, sr, -sc, sc], axis=1).astype(np.float32)
    rc = np.zeros((4, T), np.float32)
    for g, w in enumerate((2, 4, 8, 16)):
        for (off, L) in ((0, TL), (TL, TC)):
            tt = np.arange(L)
            lo = np.clip(tt - w // 2, 0, L)
            hi = np.clip(tt + (w - w // 2), 0, L)
            rc[g, off:off + L] = 1.0 / (hi - lo).astype(np.float32)
    identf = np.eye(128, dtype=np.float32)
    iota16 = np.tile(np.arange(16, dtype=np.float32)[None, :], (128, 1))
    iota128 = np.tile(np.arange(128, dtype=np.float32)[None, :], (128, 1))
    return dict(ropeC=ropeC, ropeS=ropeS, rcnt=rc, identf=identf, iota16=iota16, iota128=iota128)


def make_in_map(inputs, b):
    f = lambda a: np.ascontiguousarray(np.asarray(a, dtype=np.float32))
    m = dict(
        x=f(inputs["x"][b]), ctx=f(inputs["ctx"][b]),
        cvec=f(np.stack([np.asarray(inputs["c"][b]), np.asarray(inputs["c_ctx"])], axis=0)),
    )
    for k in ["w_ada", "b_ada", "w_in", "q_gain", "k_gain", "w_br_attn", "w_pool", "pool_scale", "w_out", "w_q_peer",
              "peer_keys", "peer_u", "peer_v", "final_gain"]:
        m[k] = f(inputs[k])
    m.update(_consts())
    return m


def kernel(**inputs):
    nc, _ = build_program()
    shared = None
    in_maps = []
    for b in range(8):
        m = make_in_map(inputs, b) if shared is None else dict(shared)
        if shared is None:
            shared = m
        else:
            m["x"] = np.ascontiguousarray(np.asarray(inputs["x"][b], dtype=np.float32))
            m["ctx"] = np.ascontiguousarray(np.asarray(inputs["ctx"][b], dtype=np.float32))
            m["cvec"] = np.ascontiguousarray(np.stack([np.asarray(inputs["c"][b]), np.asarray(inputs["c_ctx"])], axis=0).astype(np.float32))
        in_maps.append(m)
    res = run_bass_kernel_spmd(nc, in_maps, core_ids=list(range(8)))
    return np.stack([np.asarray(r["out"], dtype=np.float32) for r in res.results], axis=0)
```

```python
from contextlib import ExitStack, contextmanager
import numpy as np
import concourse.bass as bass
import concourse.mybir as mybir
from concourse.bass_utils import run_bass_kernel_spmd

F32 = mybir.dt.float32
BF16 = mybir.dt.bfloat16
U32 = mybir.dt.uint32
AF = mybir.ActivationFunctionType
ALU = mybir.AluOpType
AX = mybir.AxisListType

ENGS = ["pe", "act", "dve", "pool", "sp"]

D = 2048
TL = 2048
TC = 256
T = TL + TC
NEXP = 16384
EPS = 1e-6
SH_A, SC_A, G_A, SH_F, SC_F, G_F = range(6)


class Prog:
    def __init__(self, nc, same_engine_sync=True):
        self.nc = nc
        self.stack = ExitStack()
        self.pstack = None
        self.streams = {e: [] for e in ENGS}
        self.sems = {}
        self.count = {}
        self.waited = {e: {} for e in ENGS}
        self.last_write = {}
        self.reads = {}
        self.same_engine_sync = same_engine_sync
        self.n_ops = 0
        self.phase_sems = {}
        self.persist = set()
        self.uid = 0
        for e in ENGS:
            self._sem("e_" + e)

    def _sem(self, name):
        if name not in self.sems:
            self.sems[name] = self.stack.enter_context(self.nc.semaphore(name))
            self.count[name] = 0
        return self.sems[name]

    def sbuf(self, name, shape, dtype):
        self.uid += 1
        return self.pstack.enter_context(self.nc.sbuf_tensor(f"{name}_s{self.uid}", list(shape), dtype))

    def psum(self, name, shape, dtype):
        self.uid += 1
        return self.pstack.enter_context(self.nc.psum_tensor(f"{name}_p{self.uid}", list(shape), dtype))

    def _wait(self, eng, sem, val):
        if sem == "e_pe" and eng == "pe":
            return
        if sem == "e_" + eng and not self.same_engine_sync:
            return
        if self.waited[eng].get(sem, 0) >= val:
            return
        self.waited[eng][sem] = val
        self.streams[eng].append(("wait", sem, val))

    def _deps(self, eng, reads, writes):
        deps = {}
        for k in reads:
            lw = self.last_write.get(k)
            if lw:
                deps[lw[0]] = max(deps.get(lw[0], 0), lw[1])
        for k in writes:
            lw = self.last_write.get(k)
            if lw:
                deps[lw[0]] = max(deps.get(lw[0], 0), lw[1])
            for s, v in self.reads.get(k, {}).items():
                deps[s] = max(deps.get(s, 0), v)
        for s, v in deps.items():
            self._wait(eng, s, v)

    def _record(self, ev, reads, writes):
        for k in reads:
            d = self.reads.setdefault(k, {})
            d[ev[0]] = max(d.get(ev[0], 0), ev[1])
        for k in writes:
            self.last_write[k] = ev
            self.reads[k] = {}

    def op(self, eng, fn, reads=(), writes=()):
        self._deps(eng, reads, writes)
        sem = "e_" + eng
        self.count[sem] += 1
        ev = (sem, self.count[sem])
        self.streams[eng].append(("op", fn, sem, 1))
        self._record(ev, reads, writes)
        self.n_ops += 1

    def dma(self, queue, out, in_, reads=(), writes=(), sem=None, **kw):
        self.dma_fn(queue, lambda e, o=out, i=in_, kw=kw: e.dma_start(out=o, in_=i, **kw), reads, writes, sem)

    def dma_fn(self, queue, fn, reads=(), writes=(), sem=None):
        self._deps(queue, reads, writes)
        sem = sem or "default"
        if sem.startswith("x_"):
            self.persist.add(sem)
        else:
            if sem not in self.phase_sems:
                self.phase_sems[sem] = "d_%d" % len(self.phase_sems)
            sem = self.phase_sems[sem]
        self._sem(sem)
        self.count[sem] += 16
        ev = (sem, self.count[sem])
        self.streams[queue].append(("op", fn, sem, 16))
        self._record(ev, reads, writes)
        self.n_ops += 1

    def dma_split(self, queue, out, in_, n, reads=(), writes=(), sem=None):
        a = out.shape[1]
        step = (a + n - 1) // n
        for k in range(0, a, step):
            self.dma(queue, out[:, k:min(a, k + step), :], in_[:, k:min(a, k + step), :], reads=reads, writes=writes, sem=sem)

    def wait_persistent(self):
        for e in ENGS:
            for s in sorted(self.persist):
                self._wait(e, s, self.count[s])
        self.persist = set()

    def barrier(self):
        for e in ENGS:
            for s, c in self.count.items():
                if c > 0 and s not in self.persist:
                    self._wait(e, s, c)
        self.last_write = {}
        self.reads = {}

    def emit_block(self):
        nc = self.nc
        streams = self.streams
        self.streams = {e: [] for e in ENGS}
        with nc.Block() as block:
            def replay(name):
                def f(engine):
                    for rec in streams[name]:
                        if rec[0] == "wait":
                            engine.wait_ge(self.sems[rec[1]], rec[2])
                        else:
                            rec[1](engine).then_inc(self.sems[rec[2]], rec[3])
                return f
            block.tensor(replay("pe"))
            block.scalar(replay("act"))
            block.vector(replay("dve"))
            block.gpsimd(replay("pool"))
            block.sync(replay("sp"))

    @contextmanager
    def phase(self, name=""):
        self.pstack = ExitStack()
        self.phase_sems = {}
        try:
            yield
            self.barrier()
            self.emit_block()
        finally:
            self.pstack.close()
            self.pstack = None

    def close(self):
        self.stack.close()


ALL_PHASES = ["mod", "modA", "inproj", "pool", "attn", "mix", "wout", "modF", "qpeer", "topk", "gather"]


def build_program(dbg=(), nlayers=2, stop_after=None, same_engine_sync=True, peer_mode="gather"):
    nc = bass.Bass("TRN2", target_bir_lowering=False)

    def inp(name, shape, dt=F32):
        return nc.dram_tensor(name, list(shape), dt, kind="ExternalInput").ap()

    def scratch(name, shape, dt):
        kind = "ExternalOutput" if name in dbg else "Internal"
        return nc.dram_tensor(name, list(shape), dt, kind=kind).ap()

    x_in = inp("x", [TL, D])
    ctx_in = inp("ctx", [TC, D])
    cvec = inp("cvec", [2, D])
    w_ada = inp("w_ada", [2, D, 6 * D])
    b_ada = inp("b_ada", [2, 6 * D])
    w_in = inp("w_in", [2, D, 8192])
    q_gain = inp("q_gain", [2, 128])
    k_gain = inp("k_gain", [2, 128])
    w_br = inp("w_br_attn", [2, D, D])
    w_pool = inp("w_pool", [2, 4, 256, 512])
    pool_scale = inp("pool_scale", [2, D])
    w_out = inp("w_out", [2, D, D])
    w_qp = inp("w_q_peer", [2, D, D])
    peer_keys = inp("peer_keys", [2, 8, 2, 128, 128])
    peer_u = inp("peer_u", [2, NEXP, D])
    peer_v = inp("peer_v", [2, NEXP, D])
    final_gain = inp("final_gain", [D])
    ropeC = inp("ropeC", [TL, 128])
    ropeS = inp("ropeS", [TL, 128])
    rcnt = inp("rcnt", [4, T])
    identf_d = inp("identf", [128, 128])
    iota16_d = inp("iota16", [128, 16])
    iota128_d = inp("iota128", [128, 128])
    out_d = nc.dram_tensor("out", [TL, D], F32, kind="ExternalOutput").ap()

    MODROW = scratch("MODROW", [2, 2, 6 * D], F32)
    X = scratch("X", [T, D], F32)
    HT = scratch("HT", [128, 16, T], BF16)
    QT = scratch("QT", [16, 128, T], BF16)
    KT = scratch("KT", [4, 128, T], BF16)
    V = scratch("V", [T, 512], BF16)
    PL = scratch("PL", [8, 128, T], F32)
    PD = scratch("PD", [8, 128, T], BF16)
    GAB = scratch("GAB", [32, 128, T], BF16)
    AT = scratch("AT", [16, 128, T], BF16)
    MG = scratch("MG", [16, 128, T], BF16)
    H2 = scratch("H2", [T, D], F32)
    QPT = scratch("QPT", [16, 128, T], F32)
    EIDX = scratch("EIDX", [T, 128], U32)
    GATE = scratch("GATE", [T, 128], F32)
    GTd = scratch("GTd", [128, 128, T], BF16)
    UV = scratch("UV", [2 * NEXP, 2 * D], BF16)

    P = Prog(nc, same_engine_sync=same_engine_sync)

    BLOCKS_ALL = [(0, 512), (512, 512), (1024, 512), (1536, 512), (2048, 256)]
    BLOCKS_LAT = BLOCKS_ALL[:4]

    def xsrc(l, r0, nr, c0=0, ncol=D, after_attn=False):
        if l == 0 and not after_attn:
            if r0 < TL:
                return x_in[r0:r0 + nr, c0:c0 + ncol]
            return ctx_in[r0 - TL:r0 - TL + nr, c0:c0 + ncol]
        return X[r0:r0 + nr, c0:c0 + ncol]

    def load_consts(need_bf=False):
        identf = P.sbuf("identf", [128, 128], F32)
        P.dma("sp", identf[:], identf_d, writes=["identf"], sem="const")
        identb = None
        if need_bf:
            identb = P.sbuf("identb", [128, 128], BF16)
            P.op("dve", lambda e: e.tensor_copy(out=identb[:], in_=identf[:]), reads=["identf"], writes=["identb"])
        return identf, identb

    def emit_convert(k0, k1):
        Uf = peer_u.rearrange("l e d -> (l e) d")
        Vf = peer_v.rearrange("l e d -> (l e) d")
        RB = 1024
        k = 0
        for r0 in range(0, 2 * NEXP, RB):
            for (c0, src) in ((0, Uf), (D, Vf)):
                if k0 <= k < k1:
                    P.dma("pool", UV[r0:r0 + RB, c0:c0 + D], src[r0:r0 + RB, :], sem=f"x_cv{k % 4}")
                k += 1

    def phase_mod(l):
        with P.phase("mod"):
            if l == 0:
                emit_convert(0, 12)
            craw = P.sbuf("craw", [128, 2, 16], F32)
            sc = P.sbuf("sc", [128, 16, 2], F32)
            scb = P.sbuf("scb", [128, 16, 2], BF16)
            bb = P.sbuf("bb", [2, 6 * D], F32)
            wts = [P.sbuf(f"wt{i}", [128, 4, 2048], F32) for i in range(2)]
            wtb = [P.sbuf(f"wtb{i}", [128, 4, 2048], BF16) for i in range(2)]
            mrow = [P.sbuf(f"mrow{i}", [2, 2048], F32) for i in range(2)]
            pm = [[P.psum(f"pm{a_}_{b_}", [128, 512], F32) for b_ in range(4)] for a_ in range(2)]
            P.dma("sp", craw[:], cvec.rearrange("s (p j) -> p s j", j=16), writes=["craw"], sem="const")
            P.dma("sp", bb[:], b_ada[l].partition_broadcast(2), writes=["bb"], sem="const2")
            P.op("act", lambda e: e.activation(out=sc[:].rearrange("p j s -> p s j"), in_=craw[:], func=AF.Silu),
                 reads=["craw"], writes=["sc"])
            P.op("dve", lambda e: e.tensor_copy(out=scb[:], in_=sc[:]), reads=["sc"], writes=["scb"])
            wv = w_ada[l].rearrange("(p j) n -> p j n", j=16)
            k = 0
            for ng in range(6):
                g2 = ng % 2
                for jg in range(4):
                    sl = k % 2
                    k += 1
                    wt, wb = wts[sl], wtb[sl]
                    P.dma_split("sp", wt[:], wv[:, jg * 4:(jg + 1) * 4, ng * 2048:(ng + 1) * 2048], 2, writes=[f"wt{sl}"], sem=f"wt{sl}")
                    P.op("act", lambda e, wt=wt, wb=wb: e.copy(out=wb[:, 0:2, :], in_=wt[:, 0:2, :]), reads=[f"wt{sl}"], writes=[f"wtb{sl}a"])
                    P.op("dve", lambda e, wt=wt, wb=wb: e.tensor_copy(out=wb[:, 2:4, :], in_=wt[:, 2:4, :]), reads=[f"wt{sl}"], writes=[f"wtb{sl}b"])
                    for nb4 in range(4):
                        for j in range(4):
                            P.op("pe", lambda e, wb=wb, j=j, jg=jg, nb4=nb4, g2=g2: e.matmul(pm[g2][nb4][0:2, :], lhsT=scb[:, jg * 4 + j, :], rhs=wb[:, j, nb4 * 512:(nb4 + 1) * 512],
                                                                                         start=(jg == 0 and j == 0), stop=(jg == 3 and j == 3)),
                                 reads=["scb", f"wtb{sl}a", f"wtb{sl}b"], writes=[f"pm{g2}_{nb4}"])
                mr = mrow[g2]
                for nb4 in range(4):
                    nb = ng * 4 + nb4
                    addc = 1.0 if (4 <= nb < 8 or 16 <= nb < 20) else 0.0
                    P.op("dve", lambda e, mr=mr, nb=nb, nb4=nb4, addc=addc, g2=g2: e.scalar_tensor_tensor(
                        out=mr[:, nb4 * 512:(nb4 + 1) * 512], in0=pm[g2][nb4][0:2, :], scalar=addc, in1=bb[:, nb * 512:(nb + 1) * 512], op0=ALU.add, op1=ALU.add),
                        reads=[f"pm{g2}_{nb4}", "bb"], writes=[f"mrow{g2}"])
                P.dma("act", MODROW[l, :, ng * 2048:(ng + 1) * 2048], mr[:], reads=[f"mrow{g2}"], writes=[], sem=f"mrow{g2}")

    def load_bc(tile, key, l, s, which, sem):
        P.dma("sp", tile[:], MODROW[l, s, which * D:(which + 1) * D].partition_broadcast(128), writes=[key], sem=sem)

    def phase_modulate(l, which_sh, which_sc, after_attn, blocks, write_h2):
        with P.phase("modulate"):
            identf, identb = load_consts(need_bf=True)
            A = [P.sbuf(f"A{s}", [128, D], F32) for s in range(2)]
            B = [P.sbuf(f"B{s}", [128, D], F32) for s in range(2)]
            classes = sorted({0 if t0 < TL else 1 for t0, _ in blocks})
            for s in classes:
                load_bc(A[s], f"A{s}", l, s, which_sc, f"bcA{s}")
                load_bc(B[s], f"B{s}", l, s, which_sh, f"bcB{s}")
            xt = [P.sbuf(f"xt{i}", [128, D], F32) for i in range(2)]
            hb = [P.sbuf(f"hb{i}", [128, D], BF16) for i in range(2)]
            junk = P.sbuf("junk", [128, D], F32)
            st = [P.sbuf(f"st{i}", [128, 4], F32) for i in range(2)]
            hT = [P.sbuf(f"hT{i}", [128, 16, 512], BF16) for i in range(2)]
            pT = [P.psum(f"pT{i}", [128, 16, 128], BF16) for i in range(2)]
            k = 0
            for bi, (t0, nt) in enumerate(blocks):
                s = 0 if t0 < TL else 1
                hTb = hT[bi % 2]
                for ti in range(nt // 128):
                    r0 = t0 + ti * 128
                    i = k % 2
                    k += 1
                    x_t, h_b, s_t, p_t = xt[i], hb[i], st[i], pT[i]
                    P.dma("sp", x_t[:], xsrc(l, r0, 128, after_attn=after_attn), writes=[f"xt{i}"], sem=f"xt{i}")
                    P.op("act", lambda e, x_t=x_t, s_t=s_t: e.activation(out=junk[:], in_=x_t[:], func=AF.Square, accum_out=s_t[:, 0:1]),
                         reads=[f"xt{i}"], writes=["junk", f"st{i}"])
                    P.op("dve", lambda e, s_t=s_t: e.tensor_scalar(out=s_t[:, 1:2], in0=s_t[:, 0:1], scalar1=1.0 / D, scalar2=EPS, op0=ALU.mult, op1=ALU.add),
                         reads=[f"st{i}"], writes=[f"st{i}"])
                    P.op("act", lambda e, s_t=s_t: e.sqrt(out=s_t[:, 2:3], in_=s_t[:, 1:2]), reads=[f"st{i}"], writes=[f"st{i}"])
                    P.op("dve", lambda e, s_t=s_t: e.reciprocal(out=s_t[:, 3:4], in_=s_t[:, 2:3]), reads=[f"st{i}"], writes=[f"st{i}"])
                    P.op("dve", lambda e, x_t=x_t, s_t=s_t, s=s: e.scalar_tensor_tensor(out=x_t[:], in0=x_t[:], scalar=s_t[:, 3:4], in1=A[s][:], op0=ALU.mult, op1=ALU.mult),
                         reads=[f"xt{i}", f"st{i}", f"A{s}"], writes=[f"xt{i}"])
                    P.op("pool", lambda e, x_t=x_t, s=s: e.tensor_tensor(out=x_t[:], in0=x_t[:], in1=B[s][:], op=ALU.add),
                         reads=[f"xt{i}", f"B{s}"], writes=[f"xt{i}"])
                    if write_h2:
                        P.dma("act", H2[r0:r0 + 128, :], x_t[:], reads=[f"xt{i}"], writes=[], sem=f"h2st{i}")
                    P.op("act", lambda e, x_t=x_t, h_b=h_b: e.copy(out=h_b[:], in_=x_t[:]), reads=[f"xt{i}"], writes=[f"hb{i}"])
                    for j in range(16):
                        P.op("pe", lambda e, h_b=h_b, p_t=p_t, j=j: e.transpose(out=p_t[:, j, :], in_=h_b[:, j * 128:(j + 1) * 128], identity=identb[:]),
                             reads=[f"hb{i}", "identb"], writes=[f"pT{i}"])
                    P.op("dve", lambda e, p_t=p_t, hTb=hTb, ti=ti: e.tensor_copy(out=hTb[:, :, ti * 128:(ti + 1) * 128], in_=p_t[:]),
                         reads=[f"pT{i}"], writes=[f"hT{bi%2}"])
                P.dma_split("act", HT[:, :, t0:t0 + nt], hTb[:, :, 0:nt], 2, reads=[f"hT{bi%2}"], writes=[], sem=f"hTst{bi%2}")

    def proj(W, col_blocks, act_src, blocks_for, mode_for, evac, per_block=None, end_block=None, npp=3):
        wst = [P.sbuf(f"wst{i}", [128, 16, 256], F32) for i in range(2)]
        wbf = [P.sbuf(f"wbf{i}", [128, 16, 512], BF16) for i in range(2)]
        ablk = [P.sbuf(f"ablk{i}", [128, 16, 512], BF16) for i in range(2)]
        pp = [P.psum(f"pp{i}", [128, 512], F32) for i in range(npp)]
        Wv = W.rearrange("(j p) n -> p j n", p=128)
        items = []
        for ci, cb in enumerate(col_blocks):
            for (t0, nt) in blocks_for(cb):
                items.append((ci, cb, t0, nt))

        def load_w(ci, cb):
            for hf in range(2):
                P.dma_split("sp", wst[hf][:], Wv[:, :, cb * 512 + hf * 256:cb * 512 + (hf + 1) * 256], 2, writes=[f"wst{hf}"], sem=f"wst{hf}")

        def load_a(n):
            ci, cb, t0, nt = items[n]
            i = n % 2
            P.dma_split("sp", ablk[i][:, :, 0:nt], act_src[:, :, t0:t0 + nt], 2, writes=[f"ablk{i}"], sem=f"ablk{i}")

        load_w(0, col_blocks[0])
        load_a(0)
        q = 0
        last_ci = -1
        for n, (ci, cb, t0, nt) in enumerate(items):
            if ci != last_ci:
                i = ci % 2
                P.op("act", lambda e, i=i: e.copy(out=wbf[i][:, :, 0:256], in_=wst[0][:]), reads=["wst0"], writes=[f"wbf{i}"])
                P.op("pool", lambda e, i=i: e.tensor_copy(out=wbf[i][:, :, 256:512], in_=wst[1][:]), reads=["wst1"], writes=[f"wbf{i}"])
                if ci + 1 < len(col_blocks):
                    load_w(ci + 1, col_blocks[ci + 1])
                last_ci = ci
            if n + 1 < len(items):
                load_a(n + 1)
            wb = wbf[ci % 2]
            ab = ablk[n % 2]
            if per_block:
                per_block(cb, t0, nt)
            if mode_for(cb) == "tok":
                for ti in range(nt // 128):
                    ps = pp[q % npp]
                    pk = f"pp{q % npp}"
                    q += 1
                    for j in range(16):
                        P.op("pe", lambda e, ps=ps, ab=ab, wb=wb, j=j, ti=ti: e.matmul(ps[:], lhsT=ab[:, j, ti * 128:(ti + 1) * 128], rhs=wb[:, j, :],
                                                                                      start=(j == 0), stop=(j == 15)),
                             reads=[f"ablk{n%2}", f"wbf{ci%2}"], writes=[pk])
                    evac(cb, t0, ti, nt, ps, pk)
            else:
                for cc in range(4):
                    ps = pp[q % npp]
                    pk = f"pp{q % npp}"
                    q += 1
                    for j in range(16):
                        P.op("pe", lambda e, ps=ps, ab=ab, wb=wb, j=j, cc=cc, nt=nt: e.matmul(ps[:, 0:nt], lhsT=wb[:, j, cc * 128:(cc + 1) * 128], rhs=ab[:, j, 0:nt],
                                                                                             start=(j == 0), stop=(j == 15)),
                             reads=[f"ablk{n%2}", f"wbf{ci%2}"], writes=[pk])
                    evac(cb, t0, cc, nt, ps, pk)
            if end_block:
                end_block(cb, t0, nt)

    def phase_inproj(l):
        with P.phase("inproj"):
            identf, identb = load_consts(need_bf=True)
            rC = P.sbuf("rC", [128, 16, 128], F32)
            rS = P.sbuf("rS", [128, 16, 128], F32)
            P.dma_split("sp", rC[:], ropeC.rearrange("(t p) d -> p t d", p=128), 2, writes=["rC"], sem="const")
            P.dma_split("sp", rS[:], ropeS.rearrange("(t p) d -> p t d", p=128), 2, writes=["rS"], sem="const2")
            gq = P.sbuf("gq", [128, 128], F32)
            gk = P.sbuf("gk", [128, 128], F32)
            P.dma("sp", gq[:], q_gain[l].partition_broadcast(128), writes=["gq"], sem="const3")
            P.dma("sp", gk[:], k_gain[l].partition_broadcast(128), writes=["gk"], sem="const4")
            NB = 2
            qf = [P.sbuf(f"qf{i}", [128, 512], F32) for i in range(NB)]
            sq = [P.sbuf(f"sq{i}", [128, 512], F32) for i in range(NB)]
            t1 = [P.sbuf(f"t1{i}", [128, 512], F32) for i in range(NB)]
            t2 = [P.sbuf(f"t2{i}", [128, 512], F32) for i in range(NB)]
            qb = [P.sbuf(f"qb{i}", [128, 512], BF16) for i in range(NB)]
            sst = [P.sbuf(f"sst{i}", [128, 16], F32) for i in range(NB)]
            stage = [P.sbuf(f"stage{i}", [128, 4, 512], BF16) for i in range(2)]
            ev = [P.sbuf(f"ev{i}", [128, 512], F32) for i in range(3)]
            evb = [P.sbuf(f"evb{i}", [128, 512], BF16) for i in range(3)]
            pq = [P.psum(f"pq{i}", [128, 4, 128], BF16) for i in range(2)]
            cnt = {"qk": 0, "ev": 0, "blk": 0}

            def blocks_for(cb):
                if l == 1 and cb not in (4, 5):
                    return BLOCKS_LAT
                return BLOCKS_ALL

            def mode_for(cb):
                return "tok" if cb < 6 else "feat"

            pending = []

            def flush():
                while pending:
                    pending.pop(0)()

            def evac(cb, t0, idx, nt, ps, pk):
                flush()
                if cb < 5:
                    ti = idx
                    r0 = t0 + ti * 128
                    latent = r0 < TL
                    i = cnt["qk"] % NB
                    cnt["qk"] += 1
                    gain = gq if cb < 4 else gk
                    gkey = "gq" if cb < 4 else "gk"
                    q_f, s_q, t_1, t_2, q_b, s_t = qf[i], sq[i], t1[i], t2[i], qb[i], sst[i]
                    P.op("act", lambda e: e.copy(out=q_f[:], in_=ps[:]), reads=[pk], writes=[f"qf{i}"])
                    P.op("dve", lambda e: e.tensor_tensor(out=s_q[:], in0=q_f[:], in1=q_f[:], op=ALU.mult), reads=[f"qf{i}"], writes=[f"sq{i}"])
                    P.op("dve", lambda e: e.tensor_reduce(out=s_t[:, 0:4], in_=s_q[:].rearrange("p (h d) -> p h d", h=4), axis=AX.X, op=ALU.add),
                         reads=[f"sq{i}"], writes=[f"sst{i}"])
                    P.op("dve", lambda e: e.tensor_scalar(out=s_t[:, 4:8], in0=s_t[:, 0:4], scalar1=1.0 / 128, scalar2=EPS, op0=ALU.mult, op1=ALU.add),
                         reads=[f"sst{i}"], writes=[f"sst{i}"])
                    P.op("act", lambda e: e.sqrt(out=s_t[:, 8:12], in_=s_t[:, 4:8]), reads=[f"sst{i}"], writes=[f"sst{i}"])
                    P.op("dve", lambda e: e.reciprocal(out=s_t[:, 12:16], in_=s_t[:, 8:12]), reads=[f"sst{i}"], writes=[f"sst{i}"])
                    P.op("dve", lambda e: e.tensor_tensor(out=s_q[:].rearrange("p (h d) -> p h d", h=4), in0=q_f[:].rearrange("p (h d) -> p h d", h=4),
                                                          in1=s_t[:, 12:16].unsqueeze(2).to_broadcast([128, 4, 128]), op=ALU.mult),
                         reads=[f"qf{i}", f"sst{i}"], writes=[f"sq{i}"])
                    P.op("pool", lambda e: e.tensor_tensor(out=q_f[:].rearrange("p (h d) -> p h d", h=4), in0=s_q[:].rearrange("p (h d) -> p h d", h=4),
                                                           in1=gain[:].unsqueeze(1).to_broadcast([128, 4, 128]), op=ALU.mult),
                         reads=[f"sq{i}", gkey], writes=[f"qf{i}"])
                    if latent:
                        tt = r0 // 128
                        P.op("pool", lambda e: e.tensor_tensor(out=t_1[:].rearrange("p (h d) -> p h d", h=4), in0=q_f[:].rearrange("p (h d) -> p h d", h=4),
                                                               in1=rC[:, tt, :].unsqueeze(1).to_broadcast([128, 4, 128]), op=ALU.mult),
                             reads=[f"qf{i}", "rC"], writes=[f"t1{i}"])
                        qv = q_f[:].rearrange("p (h a two d) -> p h a two d", h=4, a=2, two=2)
                        tv = t_2[:].rearrange("p (h a two d) -> p h a two d", h=4, a=2, two=2)
                        sv = rS[:, tt, :].rearrange("p (a two d) -> p a two d", a=2, two=2)
                        for pr in range(2):
                            P.op("dve", lambda e, pr=pr: e.tensor_tensor(out=tv[:, :, :, pr, :], in0=qv[:, :, :, 1 - pr, :],
                                                                         in1=sv[:, :, pr, :].unsqueeze(1).to_broadcast([128, 4, 2, 32]), op=ALU.mult),
                                 reads=[f"qf{i}", "rS"], writes=[f"t2{i}"])
                        P.op("dve", lambda e: e.tensor_tensor(out=q_b[:], in0=t_1[:], in1=t_2[:], op=ALU.add), reads=[f"t1{i}", f"t2{i}"], writes=[f"qb{i}"])
                    else:
                        P.op("act", lambda e: e.copy(out=q_b[:], in_=q_f[:]), reads=[f"qf{i}"], writes=[f"qb{i}"])
                    p_q = pq[i % 2]
                    sg = cnt["blk"] % 2

                    def later():
                        for hh in range(4):
                            P.op("pe", lambda e, hh=hh: e.transpose(out=p_q[:, hh, :], in_=q_b[:, hh * 128:(hh + 1) * 128], identity=identb[:]),
                                 reads=[f"qb{i}", "identb"], writes=[f"pq{i%2}"])
                        P.op("act", lambda e: e.copy(out=stage[sg][:, :, ti * 128:(ti + 1) * 128], in_=p_q[:]), reads=[f"pq{i%2}"], writes=[f"stage{sg}"])
                    pending.append(later)
                elif cb == 5:
                    ti = idx
                    r0 = t0 + ti * 128
                    i = cnt["ev"] % 3
                    cnt["ev"] += 1
                    P.op("act", lambda e: e.copy(out=evb[i][:], in_=ps[:]), reads=[pk], writes=[f"evb{i}"])
                    P.dma("act", V[r0:r0 + 128, :], evb[i][:], reads=[f"evb{i}"], writes=[], sem=f"evb{i}")
                elif cb < 8:
                    cc = idx
                    i = cnt["ev"] % 3
                    cnt["ev"] += 1
                    P.op("act", lambda e: e.copy(out=ev[i][:, 0:nt], in_=ps[:, 0:nt]), reads=[pk], writes=[f"ev{i}"])
                    P.dma("act", PL[(cb - 6) * 4 + cc][:, t0:t0 + nt], ev[i][:, 0:nt], reads=[f"ev{i}"], writes=[], sem=f"ev{i}")
                else:
                    cc = idx
                    i = cnt["ev"] % 3
                    cnt["ev"] += 1
                    P.op("act", lambda e: e.activation(out=evb[i][:, 0:nt], in_=ps[:, 0:nt], func=AF.Sigmoid), reads=[pk], writes=[f"evb{i}"])
                    P.dma("act", GAB[(cb - 8) * 4 + cc][:, t0:t0 + nt], evb[i][:, 0:nt], reads=[f"evb{i}"], writes=[], sem=f"evb{i}")

            def end_block(cb, t0, nt):
                if cb < 5:
                    flush()
                    sg = cnt["blk"] % 2
                    cnt["blk"] += 1
                    dst = QT[cb * 4:(cb + 1) * 4] if cb < 4 else KT[0:4]
                    P.dma("act", dst.rearrange("h p t -> p h t")[:, :, t0:t0 + nt], stage[sg][:, :, 0:nt], reads=[f"stage{sg}"], writes=[], sem=f"stage{sg}")

            proj(w_in[l], list(range(16)), HT, blocks_for, mode_for, evac, end_block=end_block)

    def phase_pool(l):
        with P.phase("pool"):
            classes = [(0, TL)] + ([(TL, TC)] if l == 0 else [])

            def do_class(off, L):
                W = L + 32
                tag = "L" if off == 0 else "C"
                rc = P.sbuf(f"rc{tag}", [128, 4, L], F32)
                for g in range(4):
                    P.dma("sp", rc[:, g, :], rcnt[g, off:off + L].partition_broadcast(128), writes=[f"rc{tag}"], sem=f"rc{tag}")
                u = [P.sbuf(f"u{tag}{i}", [128, W], F32) for i in range(2)]
                sa = P.sbuf(f"sa{tag}", [128, W], F32)
                sb = P.sbuf(f"sb{tag}", [128, W], F32)
                tmp = P.sbuf(f"tmp{tag}", [128, L], F32)
                pd = [P.sbuf(f"pd{tag}{i}", [128, L], BF16) for i in range(2)]
                for i in range(2):
                    P.op("pool", lambda e, i=i: e.memset(u[i][:], 0.0), writes=[f"u{tag}{i}"])
                P.op("pool", lambda e: e.memset(sa[:], 0.0), writes=[f"sa{tag}"])
                P.op("pool", lambda e: e.memset(sb[:], 0.0), writes=[f"sb{tag}"])
                for c in range(8):
                    g = c // 2
                    i = c % 2
                    uu = u[i]
                    uk = f"u{tag}{i}"
                    P.dma("sp", uu[:, 16:16 + L], PL[c][:, off:off + L], writes=[uk], sem=uk)
                    P.op("dve", lambda e, uu=uu: e.tensor_tensor(out=sa[:, 1:W], in0=uu[:, 1:W], in1=uu[:, 0:W - 1], op=ALU.add), reads=[uk], writes=[f"sa{tag}"])
                    cur, curk = sa, f"sa{tag}"
                    if g >= 1:
                        P.op("pool", lambda e: e.tensor_tensor(out=sb[:, 2:W - 1], in0=sa[:, 3:W], in1=sa[:, 1:W - 2], op=ALU.add), reads=[f"sa{tag}"], writes=[f"sb{tag}"])
                        cur, curk = sb, f"sb{tag}"
                    if g >= 2:
                        P.op("dve", lambda e: e.tensor_tensor(out=sa[:, 4:W - 3], in0=sb[:, 6:W - 1], in1=sb[:, 2:W - 5], op=ALU.add), reads=[f"sb{tag}"], writes=[f"sa{tag}"])
                        cur, curk = sa, f"sa{tag}"
                    if g >= 3:
                        P.op("pool", lambda e: e.tensor_tensor(out=sb[:, 8:W - 7], in0=sa[:, 12:W - 3], in1=sa[:, 4:W - 11], op=ALU.add), reads=[f"sa{tag}"], writes=[f"sb{tag}"])
                        cur, curk = sb, f"sb{tag}"
                    P.op("dve", lambda e, cur=cur, g=g: e.tensor_tensor(out=tmp[:], in0=cur[:, 16:16 + L], in1=rc[:, g, :], op=ALU.mult),
                         reads=[curk, f"rc{tag}"], writes=[f"tmp{tag}"])
                    P.op("pool", lambda e, uu=uu, i=i: e.tensor_tensor(out=pd[i][:], in0=tmp[:], in1=uu[:, 16:16 + L], op=ALU.subtract),
                         reads=[f"tmp{tag}", uk], writes=[f"pd{tag}{i}"])
                    P.dma("act", PD[c][:, off:off + L], pd[i][:], reads=[f"pd{tag}{i}"], writes=[], sem=f"pd{tag}{i}")

            for (off, L) in classes:
                do_class(off, L)

    def phase_attn(l):
        with P.phase("attn"):
            if l == 0:
                emit_convert(12, 48)
            ones = P.sbuf("ones", [128, 128], BF16)
            P.op("dve", lambda e: e.memset(ones[:], 1.0), writes=["ones"])
            kT = [P.sbuf(f"kT{i}", [128, T], BF16) for i in range(2)]
            Vg = [P.sbuf(f"Vg{i}", [128, 18, 128], BF16) for i in range(2)]
            qT = [P.sbuf(f"qT{i}", [128, T], BF16) for i in range(2)]
            pt = [P.sbuf(f"pt{i}", [128, 512], BF16) for i in range(6)]
            rden = [P.sbuf(f"rden{i}", [128, 512], F32) for i in range(2)]
            ob = [P.sbuf(f"ob{i}", [128, 512], BF16) for i in range(2)]
            sps = [P.psum(f"sps{i}", [128, 512], F32) for i in range(3)]
            ops_ = [P.psum(f"ops{i}", [128, 512], F32) for i in range(2)]
            dps = [P.psum(f"dps{i}", [128, 512], F32) for i in range(2)]
            scale = 128.0 ** -0.5
            st = {"nq": 0, "npt": 0, "nsp": 0}

            def do_qblock(g, gi, h, qi, c0, nqc, kts, st):
                oi = st["nq"] % 2
                st["nq"] += 1
                o_ps, d_ps = ops_[oi], dps[oi]
                nk = len(kts)

                def S(kt, si):
                    P.op("pe", lambda e: e.matmul(sps[si][:, 0:nqc], lhsT=kT[gi][:, kt * 128:(kt + 1) * 128], rhs=qT[qi][:, c0:c0 + nqc],
                                                  start=True, stop=True),
                         reads=[f"kT{gi}", f"qT{qi}"], writes=[f"sps{si}"])

                def step(ii, kt, si, pi):
                    P.op("act", lambda e: e.activation(out=pt[pi][:, 0:nqc], in_=sps[si][:, 0:nqc], func=AF.Exp, scale=scale),
                         reads=[f"sps{si}"], writes=[f"pt{pi}"])
                    P.op("pe", lambda e: e.matmul(o_ps[:, 0:nqc], lhsT=Vg[gi][:, kt, :], rhs=pt[pi][:, 0:nqc], start=(ii == 0), stop=(ii == nk - 1)),
                         reads=[f"Vg{gi}", f"pt{pi}"], writes=[f"ops{oi}"])
                    P.op("pe", lambda e: e.matmul(d_ps[:, 0:nqc], lhsT=ones[:], rhs=pt[pi][:, 0:nqc], start=(ii == 0), stop=(ii == nk - 1)),
                         reads=["ones", f"pt{pi}"], writes=[f"dps{oi}"])

                base = st["nsp"]
                st["nsp"] += nk
                for pre in range(min(2, nk)):
                    S(kts[pre], (base + pre) % 3)
                for ii, kt in enumerate(kts):
                    si = (base + ii) % 3
                    if ii + 2 < nk:
                        S(kts[ii + 2], (base + ii + 2) % 3)
                    pi = st["npt"] % 6
                    st["npt"] += 1
                    step(ii, kt, si, pi)
                P.op("dve", lambda e: e.reciprocal(out=rden[oi][:, 0:nqc], in_=d_ps[:, 0:nqc]), reads=[f"dps{oi}"], writes=[f"rden{oi}"])
                P.op("dve", lambda e: e.tensor_tensor(out=ob[oi][:, 0:nqc], in0=o_ps[:, 0:nqc], in1=rden[oi][:, 0:nqc], op=ALU.mult),
                     reads=[f"ops{oi}", f"rden{oi}"], writes=[f"ob{oi}"])
                P.dma("sp", AT[h][:, c0:c0 + nqc], ob[oi][:, 0:nqc], reads=[f"ob{oi}"], writes=[], sem=f"ob{oi}")

            for g in range(4):
                gi = g % 2
                P.dma("sp", kT[gi][:], KT[g], writes=[f"kT{gi}"], sem=f"kT{gi}")
                P.dma_split("sp", Vg[gi][:], V.rearrange("(kt p) c -> p kt c", p=128)[:, :, g * 128:(g + 1) * 128], 3, writes=[f"Vg{gi}"], sem=f"Vg{gi}")
                for hh in range(4):
                    h = g * 4 + hh
                    qi = h % 2
                    ncol = T if l == 0 else TL
                    P.dma("sp", qT[qi][:, 0:ncol], QT[h][:, 0:ncol], writes=[f"qT{qi}"], sem=f"qT{qi}")
                    qblocks = [(c0, 512, list(range(18))) for c0 in range(0, TL, 512)]
                    if l == 0:
                        qblocks.append((TL, TC, [16, 17]))
                    for (c0, nqc, kts) in qblocks:
                        do_qblock(g, gi, h, qi, c0, nqc, kts, st)

    def phase_mix(l):
        with P.phase("mix"):
            identf, _ = load_consts()
            blocks = BLOCKS_ALL if l == 0 else BLOCKS_LAT
            wpf = P.sbuf("wpf", [128, 8, 512], F32)
            wpb = P.sbuf("wpb", [128, 8, 512], BF16)
            P.dma("sp", wpf[:], w_pool[l].rearrange("g (kc p) d -> p (g kc) d", p=128), writes=["wpf"], sem="const2")
            P.op("dve", lambda e: e.tensor_copy(out=wpb[:], in_=wpf[:]), reads=["wpf"], writes=["wpb"])
            psr = P.sbuf("psr", [16, 128], F32)
            pscT = P.sbuf("pscT", [128, 16], F32)
            P.dma("sp", psr[:], pool_scale[l].rearrange("(j p) -> j p", p=128), writes=["psr"], sem="const3")
            ptp = P.psum("ptp", [128, 16], F32)
            P.op("pe", lambda e: e.transpose(out=ptp[:], in_=psr[:], identity=identf[0:16, 0:16]), reads=["psr", "identf"], writes=["ptp"])
            P.op("dve", lambda e: e.tensor_copy(out=pscT[:], in_=ptp[:]), reads=["ptp"], writes=["pscT"])
            pdb = [P.sbuf(f"pdb{i}", [128, 2, 512], BF16) for i in range(2)]
            gab = [P.sbuf(f"gab{i}", [128, 4, 512], BF16) for i in range(2)]
            gbb = [P.sbuf(f"gbb{i}", [128, 4, 512], BF16) for i in range(2)]
            m1 = [P.sbuf(f"m1{i}", [128, 512], F32) for i in range(2)]
            m2 = [P.sbuf(f"m2{i}", [128, 512], F32) for i in range(2)]
            mg = [P.sbuf(f"mg{i}", [128, 512], BF16) for i in range(3)]
            pb = [P.psum(f"pb{i}", [128, 512], F32) for i in range(2)]
            cnt = {"blk": 0, "ev": 0}
            cur = {}

            def per_block(cb, t0, nt):
                i = cnt["blk"] % 2
                cnt["blk"] += 1
                cur["i"] = i
                P.dma("sp", pdb[i][:, :, 0:nt], PD[2 * cb:2 * cb + 2].rearrange("c p t -> p c t")[:, :, t0:t0 + nt], writes=[f"pdb{i}"], sem=f"pdb{i}")
                P.dma("sp", gab[i][:, :, 0:nt], GAB[4 * cb:4 * cb + 4].rearrange("c p t -> p c t")[:, :, t0:t0 + nt], writes=[f"gab{i}"], sem=f"gab{i}")
                P.dma("sp", gbb[i][:, :, 0:nt], GAB[16 + 4 * cb:16 + 4 * cb + 4].rearrange("c p t -> p c t")[:, :, t0:t0 + nt], writes=[f"gbb{i}"], sem=f"gbb{i}")

            def evac(cb, t0, cc, nt, ps, pk):
                i = cur["i"]
                dc = cb * 4 + cc
                e2 = cnt["ev"] % 2
                e3 = cnt["ev"] % 3
                cnt["ev"] += 1
                p_b = pb[e2]
                for kc in range(2):
                    P.op("pe", lambda e, kc=kc: e.matmul(p_b[:, 0:nt], lhsT=wpb[:, cb * 2 + kc, cc * 128:(cc + 1) * 128], rhs=pdb[i][:, kc, 0:nt],
                                                         start=(kc == 0), stop=(kc == 1)),
                         reads=["wpb", f"pdb{i}"], writes=[f"pb{e2}"])
                P.op("dve", lambda e: e.tensor_tensor(out=m1[e2][:, 0:nt], in0=ps[:, 0:nt], in1=gab[i][:, cc, 0:nt], op=ALU.mult),
                     reads=[pk, f"gab{i}"], writes=[f"m1{e2}"])
                P.op("dve", lambda e: e.scalar_tensor_tensor(out=m2[e2][:, 0:nt], in0=p_b[:, 0:nt], scalar=pscT[:, dc:dc + 1], in1=gbb[i][:, cc, 0:nt],
                                                             op0=ALU.mult, op1=ALU.mult),
                     reads=[f"pb{e2}", "pscT", f"gbb{i}"], writes=[f"m2{e2}"])
                P.op("pool", lambda e: e.tensor_tensor(out=mg[e3][:, 0:nt], in0=m1[e2][:, 0:nt], in1=m2[e2][:, 0:nt], op=ALU.add),
                     reads=[f"m1{e2}", f"m2{e2}"], writes=[f"mg{e3}"])
                P.dma("act", MG[dc][:, t0:t0 + nt], mg[e3][:, 0:nt], reads=[f"mg{e3}"], writes=[], sem=f"mg{e3}")

            proj(w_br[l], list(range(4)), AT.rearrange("h p t -> p h t"), lambda cb: blocks, lambda cb: "feat", evac, per_block=per_block, npp=2)

    def phase_wout(l):
        with P.phase("wout"):
            blocks = BLOCKS_ALL if l == 0 else BLOCKS_LAT
            G = [P.sbuf(f"G{s}", [128, D], F32) for s in range(2)]
            for s in ([0, 1] if l == 0 else [0]):
                load_bc(G[s], f"G{s}", l, s, G_A, f"bcG{s}")
            xb = [P.sbuf(f"xo{i}", [128, 4, 512], F32) for i in range(2)]
            tt = [P.sbuf(f"to{i}", [128, 512], F32) for i in range(3)]
            cnt = {"ev": 0, "blk": 0}
            cur = {}

            def per_block(cb, t0, nt):
                b = cnt["blk"] % 2
                cnt["blk"] += 1
                cur["b"] = b
                P.dma("sp", xb[b][:, 0:nt // 128, :], xsrc(l, t0, nt, cb * 512, 512).rearrange("(t p) c -> p t c", p=128), writes=[f"xo{b}"], sem=f"xo{b}")

            def evac(cb, t0, ti, nt, ps, pk):
                r0 = t0 + ti * 128
                s = 0 if r0 < TL else 1
                i = cnt["ev"] % 3
                cnt["ev"] += 1
                b = cur["b"]
                P.op("dve", lambda e: e.tensor_tensor(out=tt[i][:], in0=ps[:], in1=G[s][:, cb * 512:(cb + 1) * 512], op=ALU.mult),
                     reads=[pk, f"G{s}"], writes=[f"to{i}"])
                P.op("pool", lambda e: e.tensor_tensor(out=tt[i][:], in0=tt[i][:], in1=xb[b][:, ti, :], op=ALU.add), reads=[f"to{i}", f"xo{b}"], writes=[f"to{i}"])
                P.dma("act", X[r0:r0 + 128, cb * 512:(cb + 1) * 512], tt[i][:], reads=[f"to{i}"], writes=[], sem=f"to{i}")

            proj(w_out[l], list(range(4)), MG.rearrange("h p t -> p h t"), lambda cb: blocks, lambda cb: "tok", evac, per_block=per_block)

    def phase_qpeer(l):
        with P.phase("qpeer"):
            blocks = BLOCKS_ALL if l == 0 else BLOCKS_LAT
            ev = [P.sbuf(f"ev{i}", [128, 512], F32) for i in range(3)]
            cnt = {"ev": 0}

            def evac(cb, t0, cc, nt, ps, pk):
                i = cnt["ev"] % 3
                cnt["ev"] += 1
                P.op("act", lambda e: e.copy(out=ev[i][:, 0:nt], in_=ps[:, 0:nt]), reads=[pk], writes=[f"ev{i}"])
                P.dma("act", QPT[cb * 4 + cc][:, t0:t0 + nt], ev[i][:, 0:nt], reads=[f"ev{i}"], writes=[], sem=f"ev{i}")

            proj(w_qp[l], list(range(4)), HT, lambda cb: blocks, lambda cb: "feat", evac)

    def phase_topk(l):
        with P.phase("topk"):
            if l == 0:
                emit_convert(48, 64)
            identf, _ = load_consts()
            ntiles = (T if l == 0 else TL) // 128
            io16 = P.sbuf("io16", [128, 16], F32)
            P.dma("sp", io16[:], iota16_d, writes=["io16"], sem="const2")
            kraw = P.sbuf("kraw", [128, 16, 128], F32)
            keysT = P.sbuf("keysT", [128, 16, 128], F32)
            P.dma_split("sp", kraw[:], peer_keys[l].rearrange("h p k d -> k (h p) d"), 2, writes=["kraw"], sem="const3")
            pk4 = [P.psum(f"pk4{i}", [128, 4, 128], F32) for i in range(4)]
            for grp in range(4):
                for q in range(4):
                    hp = grp * 4 + q
                    P.op("pe", lambda e, hp=hp, q=q, grp=grp: e.transpose(out=pk4[grp][:, q, :], in_=kraw[:, hp, :], identity=identf[:]),
                         reads=["kraw", "identf"], writes=[f"pk4{grp}"])
                P.op("act", lambda e, grp=grp: e.copy(out=keysT[:, grp * 4:(grp + 1) * 4, :], in_=pk4[grp][:]), reads=[f"pk4{grp}"], writes=["keysT"])
            qt = [P.sbuf(f"qt{i}", [128, 16, 128], F32) for i in range(2)]
            Sbuf = [P.sbuf(f"S{k}", [128, 16, 128], F32) for k in range(2)]
            S2 = P.sbuf("S2", [128, 16, 128], F32)
            m = P.sbuf("m", [128, 16, 16], F32)
            ix = P.sbuf("ix", [128, 16, 16], U32)
            ixf = P.sbuf("ixf", [128, 16, 16], F32)
            i1s = P.sbuf("i1s", [128, 8, 16], F32)
            cand = P.sbuf("cand", [128, 8, 256], F32)
            cand2 = P.sbuf("cand2", [128, 8, 256], F32)
            ts = P.sbuf("ts", [128, 8, 16], F32)
            pos = P.sbuf("pos", [128, 8, 16], U32)
            au = P.sbuf("au", [128, 8, 16], U32)
            bu = P.sbuf("bu", [128, 8, 16], U32)
            af_ = P.sbuf("af", [128, 8, 16], F32)
            bf_ = P.sbuf("bf", [128, 8, 16], F32)
            oh = P.sbuf("oh", [128, 8, 16, 16], F32)
            isel = P.sbuf("isel", [128, 8, 16], F32)
            jsel = P.sbuf("jsel", [128, 8, 16], F32)
            ef = P.sbuf("ef", [128, 128], F32)
            eu = [P.sbuf(f"eu{i}", [128, 128], U32) for i in range(2)]
            dd = P.sbuf("dd", [128, 8, 16], F32)
            ee = P.sbuf("ee", [128, 8, 16], F32)
            zz = P.sbuf("zz", [128, 16], F32)
            gg = [P.sbuf(f"gg{i}", [128, 128], F32) for i in range(2)]
            NEG = -1e30
            dense = peer_mode == "dense"
            if dense:
                io128 = P.sbuf("io128", [128, 128], F32)
                P.dma("sp", io128[:], iota128_d, writes=["io128"], sem="const4")
                tp = P.psum("tp", [128, 3, 128], F32)
                ijg = P.sbuf("ijg", [128, 3, 128], F32)
                Aoh = [P.sbuf(f"Aoh{k}", [128, 16, 128], BF16) for k in range(2)]
                Boh = [P.sbuf(f"Boh{k}", [128, 128], BF16) for k in range(8)]
                gp = [P.psum(f"gp{k}", [128, 4, 128], F32) for k in range(2)]
                stg = [P.sbuf(f"stg{k}", [128, 128, 128], BF16) for k in range(2)]
                gst = {"b": 0, "g": 0}

            def gbuild(tt, i):
                r0 = tt * 128
                sg = tt % 2
                srcs = [isel[:].rearrange("p h k -> p (h k)"), jsel[:].rearrange("p h k -> p (h k)"), gg[i][:]]
                keys = ["sel0", "sel1", f"gg{i}"]
                for q in range(3):
                    P.op("pe", lambda e, q=q: e.transpose(out=tp[:, q, :], in_=srcs[q], identity=identf[:]), reads=[keys[q], "identf"], writes=["tp"])
                P.op("act", lambda e: e.copy(out=ijg[:], in_=tp[:]), reads=["tp"], writes=["ijg"])
                for grp in range(8):
                    a = grp % 2
                    for tq in range(16):
                        tl = grp * 16 + tq
                        P.op("pool", lambda e, a=a, tq=tq, tl=tl: e.tensor_scalar(out=Aoh[a][:, tq, :], in0=io128[:], scalar1=ijg[:, 0, tl:tl + 1], scalar2=None, op0=ALU.is_equal),
                             reads=["io128", "ijg"], writes=[f"Aoh{a}"])
                    for q4 in range(4):
                        gk = gst["g"] % 2
                        gst["g"] += 1
                        for q in range(4):
                            tl = grp * 16 + q4 * 4 + q
                            b = gst["b"] % 8
                            gst["b"] += 1
                            P.op("dve", lambda e, b=b, tl=tl: e.tensor_scalar(out=Boh[b][:], in0=io128[:], scalar1=ijg[:, 1, tl:tl + 1], scalar2=ijg[:, 2, tl:tl + 1],
                                                                              op0=ALU.is_equal, op1=ALU.mult),
                                 reads=["io128", "ijg"], writes=[f"Boh{b}"])
                            P.op("pe", lambda e, b=b, a=a, gk=gk, q=q, q4=q4: e.matmul(gp[gk][:, q, :], lhsT=Boh[b][:], rhs=Aoh[a][:, q4 * 4 + q, :], start=True, stop=True),
                                 reads=[f"Boh{b}", f"Aoh{a}"], writes=[f"gp{gk}"])
                        tl0 = grp * 16 + q4 * 4
                        P.op("act", lambda e, gk=gk, tl0=tl0, sg=sg: e.copy(out=stg[sg][:, :, tl0:tl0 + 4].rearrange("p i t -> p t i"), in_=gp[gk][:]),
                             reads=[f"gp{gk}"], writes=[f"stg{sg}"])
                dst = GTd.rearrange("i j t -> j i t")
                for k in range(16):
                    P.dma("sp", dst[:, k * 8:(k + 1) * 8, r0:r0 + 128], stg[sg][:, k * 8:(k + 1) * 8, :], reads=[f"stg{sg}"], writes=[], sem=f"stg{sg}")

            for tt in range(ntiles):
                r0 = tt * 128
                i = tt % 2
                P.dma_split("sp", qt[i][:], QPT.rearrange("c p t -> p c t")[:, :, r0:r0 + 128], 2, writes=[f"qt{i}"], sem=f"qt{i}")
                for grp in range(4):
                    for q in range(4):
                        hp = grp * 4 + q
                        P.op("pe", lambda e, hp=hp, q=q, grp=grp, i=i: e.matmul(pk4[grp][:, q, :], lhsT=qt[i][:, hp, :], rhs=keysT[:, hp, :], start=True, stop=True),
                             reads=[f"qt{i}", "keysT"], writes=[f"pk4{grp}"])
                    P.op("act", lambda e, grp=grp, Sx=Sbuf[i]: e.copy(out=Sx[:, grp * 4:(grp + 1) * 4, :], in_=pk4[grp][:]), reads=[f"pk4{grp}"], writes=[f"S{i}_{grp}"])
                for hp in range(16):
                    sk = f"S{i}_{hp // 4}"
                    P.op("dve", lambda e, hp=hp, Sx=Sbuf[i]: e.max(out=m[:, hp, 0:8], in_=Sx[:, hp, :]), reads=[sk], writes=[f"ma{hp}"])
                for hp in range(16):
                    sk = f"S{i}_{hp // 4}"
                    P.op("dve", lambda e, hp=hp, Sx=Sbuf[i]: e.max_index(out=ix[:, hp, 0:8], in_max=m[:, hp, 0:8], in_values=Sx[:, hp, :]), reads=[sk, f"ma{hp}"], writes=[f"ixa{hp}"])
                for hp in range(16):
                    sk = f"S{i}_{hp // 4}"
                    P.op("dve", lambda e, hp=hp, Sx=Sbuf[i]: e.match_replace(out=S2[:, hp, :], in_to_replace=m[:, hp, 0:8], in_values=Sx[:, hp, :], imm_value=NEG),
                         reads=[sk, f"ma{hp}"], writes=[f"S2_{hp}"])
                for hp in range(16):
                    P.op("dve", lambda e, hp=hp: e.max(out=m[:, hp, 8:16], in_=S2[:, hp, :]), reads=[f"S2_{hp}"], writes=[f"mb{hp}"])
                for hp in range(16):
                    P.op("dve", lambda e, hp=hp: e.max_index(out=ix[:, hp, 8:16], in_max=m[:, hp, 8:16], in_values=S2[:, hp, :]), reads=[f"S2_{hp}", f"mb{hp}"], writes=[f"ixb{hp}"])
                mkeys = [f"ma{hp}" for hp in range(16)] + [f"mb{hp}" for hp in range(16)]
                ixkeys = [f"ixa{hp}" for hp in range(16)] + [f"ixb{hp}" for hp in range(16)]
                P.op("dve", lambda e: e.tensor_copy(out=ixf[:], in_=ix[:]), reads=ixkeys, writes=["ixf"])
                mv = m[:].rearrange("p (h two) k -> p h two k", two=2)
                iv = ixf[:].rearrange("p (h two) k -> p h two k", two=2)
                cv = cand[:].rearrange("p h (a b) -> p h a b", a=16)
                P.op("dve", lambda e: e.tensor_tensor(out=cv, in0=mv[:, :, 0, :].unsqueeze(3).to_broadcast([128, 8, 16, 16]),
                                                      in1=mv[:, :, 1, :].unsqueeze(2).to_broadcast([128, 8, 16, 16]), op=ALU.add),
                     reads=mkeys, writes=["cand"])
                for h in range(8):
                    P.op("dve", lambda e, h=h: e.max(out=ts[:, h, 0:8], in_=cand[:, h, :]), reads=["cand"], writes=[f"tsa{h}"])
                for h in range(8):
                    P.op("dve", lambda e, h=h: e.max_index(out=pos[:, h, 0:8], in_max=ts[:, h, 0:8], in_values=cand[:, h, :]), reads=["cand", f"tsa{h}"], writes=[f"posa{h}"])
                for h in range(8):
                    P.op("dve", lambda e, h=h: e.match_replace(out=cand2[:, h, :], in_to_replace=ts[:, h, 0:8], in_values=cand[:, h, :], imm_value=NEG),
                         reads=["cand", f"tsa{h}"], writes=[f"c2_{h}"])
                for h in range(8):
                    P.op("dve", lambda e, h=h: e.max(out=ts[:, h, 8:16], in_=cand2[:, h, :]), reads=[f"c2_{h}"], writes=[f"tsb{h}"])
                for h in range(8):
                    P.op("dve", lambda e, h=h: e.max_index(out=pos[:, h, 8:16], in_max=ts[:, h, 8:16], in_values=cand2[:, h, :]), reads=[f"c2_{h}", f"tsb{h}"], writes=[f"posb{h}"])
                tskeys = [f"tsa{h}" for h in range(8)] + [f"tsb{h}" for h in range(8)]
                poskeys = [f"posa{h}" for h in range(8)] + [f"posb{h}" for h in range(8)]
                P.op("dve", lambda e: e.tensor_single_scalar(out=au[:], in_=pos[:], scalar=4, op=ALU.logical_shift_right), reads=poskeys, writes=["au"])
                P.op("dve", lambda e: e.tensor_single_scalar(out=bu[:], in_=pos[:], scalar=15, op=ALU.bitwise_and), reads=poskeys, writes=["bu"])
                P.op("dve", lambda e: e.tensor_copy(out=af_[:], in_=au[:]), reads=["au"], writes=["af"])
                P.op("dve", lambda e: e.tensor_copy(out=bf_[:], in_=bu[:]), reads=["bu"], writes=["bf"])
                for (sel, xf, which, key) in ((isel, af_, 0, "af"), (jsel, bf_, 1, "bf")):
                    P.op("dve", lambda e, xf=xf: e.tensor_tensor(out=oh[:], in0=io16[:].unsqueeze(1).unsqueeze(1).to_broadcast([128, 8, 16, 16]),
                                                                  in1=xf[:].unsqueeze(3).to_broadcast([128, 8, 16, 16]), op=ALU.is_equal),
                         reads=["io16", key], writes=["oh"])
                    P.op("dve", lambda e, which=which: e.tensor_tensor(out=oh[:], in0=oh[:], in1=iv[:, :, which, :].unsqueeze(2).to_broadcast([128, 8, 16, 16]), op=ALU.mult),
                         reads=["oh", "ixf"], writes=["oh"])
                    P.op("dve", lambda e, sel=sel: e.tensor_reduce(out=sel[:], in_=oh[:], axis=AX.X, op=ALU.add), reads=["oh"], writes=["sel%d" % which])
                P.op("dve", lambda e: e.scalar_tensor_tensor(out=ef[:], in0=isel[:].rearrange("p h k -> p (h k)"), scalar=128.0, in1=jsel[:].rearrange("p h k -> p (h k)"),
                                                             op0=ALU.mult, op1=ALU.add),
                     reads=["sel0", "sel1"], writes=["ef"])
                if l > 0:
                    P.op("dve", lambda e: e.tensor_scalar(out=ef[:], in0=ef[:], scalar1=float(l * NEXP), scalar2=None, op0=ALU.add), reads=["ef"], writes=["ef"])
                P.op("dve", lambda e, i=i: e.tensor_copy(out=eu[i][:], in_=ef[:]), reads=["ef"], writes=[f"eu{i}"])
                P.dma("sp", EIDX[r0:r0 + 128, :], eu[i][:], reads=[f"eu{i}"], writes=[], sem=f"eu{i}")
                P.op("dve", lambda e: e.tensor_tensor(out=dd[:], in0=ts[:], in1=ts[:, :, 0:1].to_broadcast([128, 8, 16]), op=ALU.subtract), reads=tskeys, writes=["dd"])
                P.op("act", lambda e: e.activation(out=ee[:], in_=dd[:], func=AF.Exp), reads=["dd"], writes=["ee"])
                P.op("dve", lambda e: e.tensor_reduce(out=zz[:, 0:8], in_=ee[:], axis=AX.X, op=ALU.add), reads=["ee"], writes=["zz"])
                P.op("dve", lambda e: e.reciprocal(out=zz[:, 8:16], in_=zz[:, 0:8]), reads=["zz"], writes=["zz"])
                P.op("dve", lambda e, i=i: e.tensor_tensor(out=gg[i][:].rearrange("p (h k) -> p h k", h=8), in0=ee[:], in1=zz[:, 8:16].unsqueeze(2).to_broadcast([128, 8, 16]), op=ALU.mult),
                     reads=["ee", "zz"], writes=[f"gg{i}"])
                P.dma("sp", GATE[r0:r0 + 128, :], gg[i][:], reads=[f"gg{i}"], writes=[], sem=f"gg{i}")
                if dense:
                    gbuild(tt, i)

    def phase_gather(l):
        with P.phase("gather"):
            P.wait_persistent()
            identf, identb = load_consts(need_bf=True)
            ntiles = (T if l == 0 else TL) // 128
            G = [P.sbuf(f"G{s}", [128, D], F32) for s in range(2)]
            for s in ([0, 1] if l == 0 else [0]):
                load_bc(G[s], f"G{s}", l, s, G_F, f"bcG{s}")
            NS = 8
            LOOK = 5
            uv = [P.sbuf(f"uv{i}", [128, 2 * D], BF16) for i in range(NS)]
            h2 = [P.sbuf(f"h2{i}", [128, D], F32) for i in range(2)]
            xt = [P.sbuf(f"xt{i}", [128, D], F32) for i in range(2)]
            junk = P.sbuf("junk", [128, D], F32)
            eix = [P.sbuf(f"eix{i}", [128, 128], U32) for i in range(2)]
            gat = [P.sbuf(f"gat{i}", [128, 128], F32) for i in range(2)]
            act = [P.sbuf(f"act{i}", [128, 128], F32) for i in range(2)]
            ge = [P.sbuf(f"ge{i}", [128, 128], F32) for i in range(2)]
            dg = [P.sbuf(f"dg{i}", [128, 128], BF16) for i in range(4)]
            tmp = [P.sbuf(f"tmp{i}", [128, 512], F32) for i in range(2)]
            acc = [[P.psum(f"acc{a}_{b}", [128, 512], F32) for b in range(4)] for a in range(2)]
            st = {"ntmp": 0}
            items = [(tt, sidx) for tt in range(ntiles) for sidx in range(128)]

            def loads(tt):
                r0 = tt * 128
                i = tt % 2
                P.dma("sp", h2[i][:], H2[r0:r0 + 128, :], writes=[f"h2{i}"], sem=f"h2{i}")
                P.dma("sp", xt[i][:], X[r0:r0 + 128, :], writes=[f"xt{i}"], sem=f"xt{i}")
                P.dma("sp", eix[i][:], EIDX[r0:r0 + 128, :], writes=[f"eix{i}"], sem=f"eix{i}")
                P.dma("sp", gat[i][:], GATE[r0:r0 + 128, :], writes=[f"gat{i}"], sem=f"gat{i}")

            def gather(n):
                tt, sidx = items[n]
                i = tt % 2
                u = n % NS
                if sidx == 0:
                    loads(tt)
                P.dma_fn("pool", lambda e: e.indirect_dma_start(
                    out=uv[u][:], out_offset=None, in_=UV, in_offset=bass.IndirectOffsetOnAxis(ap=eix[i][:, sidx:sidx + 1], axis=0)),
                    reads=[f"eix{i}"], writes=[f"uv{u}"], sem=f"uv{u}")

            def dot(n):
                tt, sidx = items[n]
                i = tt % 2
                u = n % NS
                P.op("dve", lambda e: e.scalar_tensor_tensor(out=junk[:], in0=h2[i][:], scalar=1.0, in1=uv[u][:, 0:D], op0=ALU.mult, op1=ALU.mult,
                                                             accum_out=act[i][:, sidx:sidx + 1]),
                     reads=[f"h2{i}", f"uv{u}"], writes=[f"a{i}_{sidx}"])
                P.op("act", lambda e: e.activation(out=ge[i][:, sidx:sidx + 1], in_=act[i][:, sidx:sidx + 1], func=AF.Gelu),
                     reads=[f"a{i}_{sidx}"], writes=[f"g{i}_{sidx}"])
                P.op("act", lambda e: e.activation(out=ge[i][:, sidx:sidx + 1], in_=ge[i][:, sidx:sidx + 1], func=AF.Copy, scale=gat[i][:, sidx:sidx + 1]),
                     reads=[f"g{i}_{sidx}", f"gat{i}"], writes=[f"g{i}_{sidx}"])

            def combine(n):
                tt, sidx = items[n]
                i = tt % 2
                u = n % NS
                d = n % 4
                P.op("act", lambda e: e.activation(out=dg[d][:], in_=identb[:], func=AF.Copy, scale=ge[i][:, sidx:sidx + 1]),
                     reads=["identb", f"g{i}_{sidx}"], writes=[f"dg{d}"])
                for db in range(4):
                    P.op("pe", lambda e, db=db: e.matmul(acc[i][db][:], lhsT=dg[d][:], rhs=uv[u][:, D + db * 512:D + (db + 1) * 512],
                                                         start=(sidx == 0), stop=(sidx == 127)),
                         reads=[f"dg{d}", f"uv{u}"], writes=[f"acc{i}_{db}"])
                if sidx == 127:
                    finalize(tt)

            def finalize(tt):
                r0 = tt * 128
                i = tt % 2
                s = 0 if r0 < TL else 1
                for db in range(4):
                    tq = st["ntmp"] % 2
                    st["ntmp"] += 1
                    P.op("dve", lambda e, tq=tq, db=db: e.tensor_tensor(out=tmp[tq][:], in0=acc[i][db][:], in1=G[s][:, db * 512:(db + 1) * 512], op=ALU.mult),
                         reads=[f"acc{i}_{db}", f"G{s}"], writes=[f"tmp{tq}"])
                    P.op("dve", lambda e, tq=tq, db=db: e.tensor_tensor(out=xt[i][:, db * 512:(db + 1) * 512], in0=tmp[tq][:], in1=xt[i][:, db * 512:(db + 1) * 512], op=ALU.add),
                         reads=[f"tmp{tq}", f"xt{i}"], writes=[f"xt{i}"])
                P.dma("sp", X[r0:r0 + 128, :], xt[i][:], reads=[f"xt{i}"], writes=[], sem=f"xst{i}")

            N = len(items)
            for n in range(min(LOOK, N)):
                gather(n)
            for n in range(N):
                if n + LOOK < N:
                    gather(n + LOOK)
                dot(n)
                if n >= 1:
                    combine(n - 1)
            combine(N - 1)

    def phase_dense(l):
        with P.phase("dense"):
            _, identb = load_consts(need_bf=True)
            ntok = T if l == 0 else TL
            groups = [(0, 768), (768, 768), (1536, ntok - 1536)]
            G = [P.sbuf(f"G{s}", [128, D], F32) for s in range(2)]
            for s in ([0, 1] if l == 0 else [0]):
                load_bc(G[s], f"G{s}", l, s, G_F, f"bcG{s}")
            hT = P.sbuf("hT", [128, 16, 768], BF16)
            acc = P.sbuf("acc", [128, 6, D], F32)
            GA = P.sbuf("GA", [128, 8, 768], BF16)
            Vb = P.sbuf("Vb", [128, 8, D], BF16)
            Ub = [P.sbuf(f"Ub{k}", [128, D], BF16) for k in range(3)]
            UT = [P.sbuf(f"UT{k}", [128, 16, 128], BF16) for k in range(2)]
            gt = [P.sbuf(f"gt{k}", [128, 768], BF16) for k in range(3)]
            gl = [P.sbuf(f"gl{k}", [128, 512], BF16) for k in range(2)]
            xt = [P.sbuf(f"xt{k}", [128, D], F32) for k in range(2)]
            ptu = [P.psum(f"ptu{k}", [128, 16, 128], BF16) for k in range(2)]
            ps1 = [P.psum(f"ps1{k}", [128, 512], F32) for k in range(2)]
            ps2 = [P.psum(f"ps2{k}", [128, 512], F32) for k in range(2)]
            st = {"c": 0, "p1": 0, "p2": 0, "x": 0}

            def chunk(c, ci, g0, gn, nblocks):
                k3 = st["c"] % 3
                k2 = st["c"] % 2
                st["c"] += 1
                row0 = l * NEXP + c * 128
                P.dma("sp", Ub[k3][:], UV[row0:row0 + 128, 0:D], writes=[f"Ub{k3}"], sem=f"Ub{k3}")
                P.dma("sp", Vb[:, ci, :], UV[row0:row0 + 128, D:2 * D], writes=[f"Vb{ci}"], sem=f"Vb{ci}")
                P.dma("sp", gt[k3][:, 0:gn], GTd[c][:, g0:g0 + gn], writes=[f"gt{k3}"], sem=f"gt{k3}")
                for j in range(16):
                    P.op("pe", lambda e, j=j: e.transpose(out=ptu[k2][:, j, :], in_=Ub[k3][:, j * 128:(j + 1) * 128], identity=identb[:]),
                         reads=[f"Ub{k3}", "identb"], writes=[f"ptu{k2}"])
                P.op("pool" if False else "dve", lambda e: e.tensor_copy(out=UT[k2][:], in_=ptu[k2][:]), reads=[f"ptu{k2}"], writes=[f"UT{k2}"])
                for (b0, bn) in nblocks:
                    p1 = st["p1"] % 2
                    st["p1"] += 1
                    for j in range(16):
                        P.op("pe", lambda e, j=j, p1=p1: e.matmul(ps1[p1][:, 0:bn], lhsT=UT[k2][:, j, :], rhs=hT[:, j, b0:b0 + bn], start=(j == 0), stop=(j == 15)),
                             reads=[f"UT{k2}", "hT"], writes=[f"ps1{p1}"])
                    P.op("act", lambda e, p1=p1: e.activation(out=gl[p1][:, 0:bn], in_=ps1[p1][:, 0:bn], func=AF.Gelu), reads=[f"ps1{p1}"], writes=[f"gl{p1}"])
                    P.op("pool", lambda e, p1=p1: e.tensor_tensor(out=GA[:, ci, b0:b0 + bn], in0=gl[p1][:, 0:bn], in1=gt[k3][:, b0:b0 + bn], op=ALU.mult),
                         reads=[f"gl{p1}", f"gt{k3}"], writes=[f"GA{ci}"])

            def combine(cg, ntile):
                for ti in range(ntile):
                    for db in range(4):
                        p2 = st["p2"] % 2
                        st["p2"] += 1
                        for ci in range(8):
                            P.op("pe", lambda e, ci=ci, p2=p2: e.matmul(ps2[p2][:], lhsT=GA[:, ci, ti * 128:(ti + 1) * 128], rhs=Vb[:, ci, db * 512:(db + 1) * 512],
                                                                        start=(ci == 0), stop=(ci == 7)),
                                 reads=[f"GA{ci}", f"Vb{ci}"], writes=[f"ps2{p2}"])
                        if cg == 0:
                            P.op("dve", lambda e, p2=p2: e.tensor_copy(out=acc[:, ti, db * 512:(db + 1) * 512], in_=ps2[p2][:]), reads=[f"ps2{p2}"], writes=[f"acc{ti}_{db}"])
                        else:
                            P.op("dve", lambda e, p2=p2: e.tensor_tensor(out=acc[:, ti, db * 512:(db + 1) * 512], in0=ps2[p2][:], in1=acc[:, ti, db * 512:(db + 1) * 512], op=ALU.add),
                                 reads=[f"ps2{p2}", f"acc{ti}_{db}"], writes=[f"acc{ti}_{db}"])

            def finalize(g0, ti):
                r0 = g0 + ti * 128
                s = 0 if r0 < TL else 1
                k = st["x"] % 2
                st["x"] += 1
                P.dma("sp", xt[k][:], X[r0:r0 + 128, :], writes=[f"xt{k}"], sem=f"xt{k}")
                P.op("pool", lambda e: e.tensor_tensor(out=acc[:, ti, :], in0=acc[:, ti, :], in1=G[s][:], op=ALU.mult),
                     reads=[f"acc{ti}_{db}" for db in range(4)] + [f"G{s}"], writes=[f"acc{ti}_{db}" for db in range(4)])
                P.op("pool", lambda e: e.tensor_tensor(out=xt[k][:], in0=xt[k][:], in1=acc[:, ti, :], op=ALU.add),
                     reads=[f"acc{ti}_{db}" for db in range(4)] + [f"xt{k}"], writes=[f"xt{k}"])
                P.dma("sp", X[r0:r0 + 128, :], xt[k][:], reads=[f"xt{k}"], writes=[], sem=f"xst{k}")

            for (g0, gn) in groups:
                ntile = gn // 128
                nblocks = [(0, 384), (384, 384)] if gn == 768 else [(0, gn)]
                P.dma_split("sp", hT[:, :, 0:gn], HT[:, :, g0:g0 + gn], 2, writes=["hT"], sem="hT")
                for cg in range(16):
                    for ci in range(8):
                        chunk(cg * 8 + ci, ci, g0, gn, nblocks)
                    combine(cg, ntile)
                for ti in range(ntile):
                    finalize(g0, ti)

    def phase_final():
        with P.phase("final"):
            fg = P.sbuf("fg", [128, D], F32)
            P.dma("sp", fg[:], final_gain.partition_broadcast(128), writes=["fg"], sem="const")
            xt = [P.sbuf(f"xt{i}", [128, D], F32) for i in range(3)]
            junk = P.sbuf("junk", [128, D], F32)
            st = [P.sbuf(f"st{i}", [128, 4], F32) for i in range(3)]
            for tt in range(TL // 128):
                r0 = tt * 128
                i = tt % 3
                x_t, s_t = xt[i], st[i]
                P.dma("sp", x_t[:], X[r0:r0 + 128, :], writes=[f"xt{i}"], sem=f"xt{i}")
                P.op("act", lambda e, x_t=x_t, s_t=s_t: e.activation(out=junk[:], in_=x_t[:], func=AF.Square, accum_out=s_t[:, 0:1]),
                     reads=[f"xt{i}"], writes=["junk", f"st{i}"])
                P.op("dve", lambda e, s_t=s_t: e.tensor_scalar(out=s_t[:, 1:2], in0=s_t[:, 0:1], scalar1=1.0 / D, scalar2=EPS, op0=ALU.mult, op1=ALU.add),
                     reads=[f"st{i}"], writes=[f"st{i}"])
                P.op("act", lambda e, s_t=s_t: e.sqrt(out=s_t[:, 2:3], in_=s_t[:, 1:2]), reads=[f"st{i}"], writes=[f"st{i}"])
                P.op("dve", lambda e, s_t=s_t: e.reciprocal(out=s_t[:, 3:4], in_=s_t[:, 2:3]), reads=[f"st{i}"], writes=[f"st{i}"])
                P.op("dve", lambda e, x_t=x_t, s_t=s_t: e.scalar_tensor_tensor(out=x_t[:], in0=x_t[:], scalar=s_t[:, 3:4], in1=fg[:], op0=ALU.mult, op1=ALU.mult),
                     reads=[f"xt{i}", f"st{i}", "fg"], writes=[f"xt{i}"])
                P.dma("pool", out_d[r0:r0 + 128, :], x_t[:], reads=[f"xt{i}"], writes=[], sem=f"ost{i}")

    def stop(l, name):
        return stop_after is not None and stop_after == (l, name)

    done = False
    for l in range(nlayers):
        blocks = BLOCKS_ALL
        steps = [
            ("mod", lambda: phase_mod(l)),
            ("modA", lambda: phase_modulate(l, SH_A, SC_A, False, BLOCKS_ALL, False)),
            ("inproj", lambda: phase_inproj(l)),
            ("pool", lambda: phase_pool(l)),
            ("attn", lambda: phase_attn(l)),
            ("mix", lambda: phase_mix(l)),
            ("wout", lambda: phase_wout(l)),
            ("modF", lambda: phase_modulate(l, SH_F, SC_F, True, BLOCKS_ALL if l == 0 else BLOCKS_LAT, True)),
            ("qpeer", lambda: phase_qpeer(l)),
            ("topk", lambda: phase_topk(l)),
            ("gather", (lambda: phase_dense(l)) if peer_mode == "dense" else (lambda: phase_gather(l))),
        ]
        for name, fn in steps:
            fn()
            if stop(l, name):
                done = True
                break
        if done:
            break
    if not done:
        phase_final()
    P.close()
    return nc, P


def _consts():
    t = np.arange(TL)
    row = (t // 64).astype(np.float32)
    col = (t % 64).astype(np.float32)
    inv = (np.float32(10000.0) ** (-np.arange(32, dtype=np.float32) / np.float32(32))).astype(np.float32)
    ar = (row[:, None] * inv[None, :]).astype(np.float32)
    ac = (col[:, None] * inv[None, :]).astype(np.float32)
    cr, sr, cc, sc = np.cos(ar), np.sin(ar), np.cos(ac), np.sin(ac)
    ropeC = np.concatenate([cr, cr, cc, cc], axis=1).astype(np.float32)
    ropeS = np.concatenate([-sr, sr, -sc, sc], axis=1).astype(np.float32)
    rc = np.zeros((4, T), np.float32)
    for g, w in enumerate((2, 4, 8, 16)):
        for (off, L) in ((0, TL), (TL, TC)):
            tt = np.arange(L)
            lo = np.clip(tt - w // 2, 0, L)
            hi = np.clip(tt + (w - w // 2), 0, L)
            rc[g, off:off + L] = 1.0 / (hi - lo).astype(np.float32)
    identf = np.eye(128, dtype=np.float32)
    iota16 = np.tile(np.arange(16, dtype=np.float32)[None, :], (128, 1))
    iota128 = np.tile(np.arange(128, dtype=np.float32)[None, :], (128, 1))
    return dict(ropeC=ropeC, ropeS=ropeS, rcnt=rc, identf=identf, iota16=iota16, iota128=iota128)


def make_in_map(inputs, b):
    f = lambda a: np.ascontiguousarray(np.asarray(a, dtype=np.float32))
    m = dict(
        x=f(inputs["x"][b]), ctx=f(inputs["ctx"][b]),
        cvec=f(np.stack([np.asarray(inputs["c"][b]), np.asarray(inputs["c_ctx"])], axis=0)),
    )
    for k in ["w_ada", "b_ada", "w_in", "q_gain", "k_gain", "w_br_attn", "w_pool", "pool_scale", "w_out", "w_q_peer",
              "peer_keys", "peer_u", "peer_v", "final_gain"]:
        m[k] = f(inputs[k])
    m.update(_consts())
    return m


def kernel(**inputs):
    nc, _ = build_program()
    shared = None
    in_maps = []
    for b in range(8):
        m = make_in_map(inputs, b) if shared is None else dict(shared)
        if shared is None:
            shared = m
        else:
            m["x"] = np.ascontiguousarray(np.asarray(inputs["x"][b], dtype=np.float32))
            m["ctx"] = np.ascontiguousarray(np.asarray(inputs["ctx"][b], dtype=np.float32))
            m["cvec"] = np.ascontiguousarray(np.stack([np.asarray(inputs["c"][b]), np.asarray(inputs["c_ctx"])], axis=0).astype(np.float32))
        in_maps.append(m)
    res = run_bass_kernel_spmd(nc, in_maps, core_ids=list(range(8)))
    return np.stack([np.asarray(r["out"], dtype=np.float32) for r in res.results], axis=0)
```

```python
from contextlib import ExitStack, contextmanager
import numpy as np
import concourse.bass as bass
import concourse.mybir as mybir
from concourse.bass_utils import run_bass_kernel_spmd

F32 = mybir.dt.float32
BF16 = mybir.dt.bfloat16
U32 = mybir.dt.uint32
AF = mybir.ActivationFunctionType
ALU = mybir.AluOpType
AX = mybir.AxisListType

ENGS = ["pe", "act", "dve", "pool", "sp"]

D = 2048
TL = 2048
TC = 256
T = TL + TC
NEXP = 16384
EPS = 1e-6
SH_A, SC_A, G_A, SH_F, SC_F, G_F = range(6)


class Prog:
    def __init__(self, nc, same_engine_sync=True):
        self.nc = nc
        self.stack = ExitStack()
        self.pstack = None
        self.streams = {e: [] for e in ENGS}
        self.sems = {}
        self.count = {}
        self.waited = {e: {} for e in ENGS}
        self.last_write = {}
        self.reads = {}
        self.same_engine_sync = same_engine_sync
        self.n_ops = 0
        self.phase_sems = {}
        self.persist = set()
        self.uid = 0
        for e in ENGS:
            self._sem("e_" + e)

    def _sem(self, name):
        if name not in self.sems:
            self.sems[name] = self.stack.enter_context(self.nc.semaphore(name))
            self.count[name] = 0
        return self.sems[name]

    def sbuf(self, name, shape, dtype):
        self.uid += 1
        return self.pstack.enter_context(self.nc.sbuf_tensor(f"{name}_s{self.uid}", list(shape), dtype))

    def psum(self, name, shape, dtype):
        self.uid += 1
        return self.pstack.enter_context(self.nc.psum_tensor(f"{name}_p{self.uid}", list(shape), dtype))

    def _wait(self, eng, sem, val):
        if sem == "e_pe" and eng == "pe":
            return
        if sem == "e_" + eng and not self.same_engine_sync:
            return
        if self.waited[eng].get(sem, 0) >= val:
            return
        self.waited[eng][sem] = val
        self.streams[eng].append(("wait", sem, val))

    def _deps(self, eng, reads, writes):
        deps = {}
        for k in reads:
            lw = self.last_write.get(k)
            if lw:
                deps[lw[0]] = max(deps.get(lw[0], 0), lw[1])
        for k in writes:
            lw = self.last_write.get(k)
            if lw:
                deps[lw[0]] = max(deps.get(lw[0], 0), lw[1])
            for s, v in self.reads.get(k, {}).items():
                deps[s] = max(deps.get(s, 0), v)
        for s, v in deps.items():
            self._wait(eng, s, v)

    def _record(self, ev, reads, writes):
        for k in reads:
            d = self.reads.setdefault(k, {})
            d[ev[0]] = max(d.get(ev[0], 0), ev[1])
        for k in writes:
            self.last_write[k] = ev
            self.reads[k] = {}

    def op(self, eng, fn, reads=(), writes=()):
        self._deps(eng, reads, writes)
        sem = "e_" + eng
        self.count[sem] += 1
        ev = (sem, self.count[sem])
        self.streams[eng].append(("op", fn, sem, 1))
        self._record(ev, reads, writes)
        self.n_ops += 1

    def dma(self, queue, out, in_, reads=(), writes=(), sem=None, **kw):
        self.dma_fn(queue, lambda e, o=out, i=in_, kw=kw: e.dma_start(out=o, in_=i, **kw), reads, writes, sem)

    def dma_fn(self, queue, fn, reads=(), writes=(), sem=None):
        self._deps(queue, reads, writes)
        sem = sem or "default"
        if sem.startswith("x_"):
            self.persist.add(sem)
        else:
            if sem not in self.phase_sems:
                self.phase_sems[sem] = "d_%d" % len(self.phase_sems)
            sem = self.phase_sems[sem]
        self._sem(sem)
        self.count[sem] += 16
        ev = (sem, self.count[sem])
        self.streams[queue].append(("op", fn, sem, 16))
        self._record(ev, reads, writes)
        self.n_ops += 1

    def dma_split(self, queue, out, in_, n, reads=(), writes=(), sem=None):
        a = out.shape[1]
        step = (a + n - 1) // n
        for k in range(0, a, step):
            self.dma(queue, out[:, k:min(a, k + step), :], in_[:, k:min(a, k + step), :], reads=reads, writes=writes, sem=sem)

    def wait_persistent(self):
        for e in ENGS:
            for s in sorted(self.persist):
                self._wait(e, s, self.count[s])
        self.persist = set()

    def barrier(self):
        for e in ENGS:
            for s, c in self.count.items():
                if c > 0 and s not in self.persist:
                    self._wait(e, s, c)
        self.last_write = {}
        self.reads = {}

    def emit_block(self):
        nc = self.nc
        streams = self.streams
        self.streams = {e: [] for e in ENGS}
        with nc.Block() as block:
            def replay(name):
                def f(engine):
                    for rec in streams[name]:
                        if rec[0] == "wait":
                            engine.wait_ge(self.sems[rec[1]], rec[2])
                        else:
                            rec[1](engine).then_inc(self.sems[rec[2]], rec[3])
                return f
            block.tensor(replay("pe"))
            block.scalar(replay("act"))
            block.vector(replay("dve"))
            block.gpsimd(replay("pool"))
            block.sync(replay("sp"))

    @contextmanager
    def phase(self, name=""):
        self.pstack = ExitStack()
        self.phase_sems = {}
        try:
            yield
            self.barrier()
            self.emit_block()
        finally:
            self.pstack.close()
            self.pstack = None

    def close(self):
        self.stack.close()


ALL_PHASES = ["mod", "modA", "inproj", "pool", "attn", "mix", "wout", "modF", "qpeer", "topk", "gather"]


def build_program(dbg=(), nlayers=2, stop_after=None, same_engine_sync=True, peer_mode="gather"):
    nc = bass.Bass("TRN2", target_bir_lowering=False)

    def inp(name, shape, dt=F32):
        return nc.dram_tensor(name, list(shape), dt, kind="ExternalInput").ap()

    def scratch(name, shape, dt):
        kind = "ExternalOutput" if name in dbg else "Internal"
        return nc.dram_tensor(name, list(shape), dt, kind=kind).ap()

    x_in = inp("x", [TL, D])
    ctx_in = inp("ctx", [TC, D])
    cvec = inp("cvec", [2, D])
    w_ada = inp("w_ada", [2, D, 6 * D])
    b_ada = inp("b_ada", [2, 6 * D])
    w_in = inp("w_in", [2, D, 8192])
    q_gain = inp("q_gain", [2, 128])
    k_gain = inp("k_gain", [2, 128])
    w_br = inp("w_br_attn", [2, D, D])
    w_pool = inp("w_pool", [2, 4, 256, 512])
    pool_scale = inp("pool_scale", [2, D])
    w_out = inp("w_out", [2, D, D])
    w_qp = inp("w_q_peer", [2, D, D])
    peer_keys = inp("peer_keys", [2, 8, 2, 128, 128])
    peer_u = inp("peer_u", [2, NEXP, D])
    peer_v = inp("peer_v", [2, NEXP, D])
    final_gain = inp("final_gain", [D])
    ropeC = inp("ropeC", [TL, 128])
    ropeS = inp("ropeS", [TL, 128])
    rcnt = inp("rcnt", [4, T])
    identf_d = inp("identf", [128, 128])
    iota16_d = inp("iota16", [128, 16])
    iota128_d = inp("iota128", [128, 128])
    out_d = nc.dram_tensor("out", [TL, D], F32, kind="ExternalOutput").ap()

    MODROW = scratch("MODROW", [2, 2, 6 * D], F32)
    X = scratch("X", [T, D], F32)
    HT = scratch("HT", [128, 16, T], BF16)
    QT = scratch("QT", [16, 128, T], BF16)
    KT = scratch("KT", [4, 128, T], BF16)
    V = scratch("V", [T, 512], BF16)
    PL = scratch("PL", [8, 128, T], F32)
    PD = scratch("PD", [8, 128, T], BF16)
    GAB = scratch("GAB", [32, 128, T], BF16)
    AT = scratch("AT", [16, 128, T], BF16)
    MG = scratch("MG", [16, 128, T], BF16)
    H2 = scratch("H2", [T, D], F32)
    QPT = scratch("QPT", [16, 128, T], F32)
    EIDX = scratch("EIDX", [T, 128], U32)
    GATE = scratch("GATE", [T, 128], F32)
    GTd = scratch("GTd", [128, 128, T], BF16)
    UV = scratch("UV", [2 * NEXP, 2 * D], BF16)

    P = Prog(nc, same_engine_sync=same_engine_sync)

    BLOCKS_ALL = [(0, 512), (512, 512), (1024, 512), (1536, 512), (2048, 256)]
    BLOCKS_LAT = BLOCKS_ALL[:4]

    def xsrc(l, r0, nr, c0=0, ncol=D, after_attn=False):
        if l == 0 and not after_attn:
            if r0 < TL:
                return x_in[r0:r0 + nr, c0:c0 + ncol]
            return ctx_in[r0 - TL:r0 - TL + nr, c0:c0 + ncol]
        return X[r0:r0 + nr, c0:c0 + ncol]

    def load_consts(need_bf=False):
        identf = P.sbuf("identf", [128, 128], F32)
        P.dma("sp", identf[:], identf_d, writes=["identf"], sem="const")
        identb = None
        if need_bf:
            identb = P.sbuf("identb", [128, 128], BF16)
            P.op("dve", lambda e: e.tensor_copy(out=identb[:], in_=identf[:]), reads=["identf"], writes=["identb"])
        return identf, identb

    def emit_convert(k0, k1):
        Uf = peer_u.rearrange("l e d -> (l e) d")
        Vf = peer_v.rearrange("l e d -> (l e) d")
        RB = 1024
        k = 0
        for r0 in range(0, 2 * NEXP, RB):
            for (c0, src) in ((0, Uf), (D, Vf)):
                if k0 <= k < k1:
                    P.dma("pool", UV[r0:r0 + RB, c0:c0 + D], src[r0:r0 + RB, :], sem=f"x_cv{k % 4}")
                k += 1

    def phase_mod(l):
        with P.phase("mod"):
            if l == 0:
                emit_convert(0, 12)
            craw = P.sbuf("craw", [128, 2, 16], F32)
            sc = P.sbuf("sc", [128, 16, 2], F32)
            scb = P.sbuf("scb", [128, 16, 2], BF16)
            bb = P.sbuf("bb", [2, 6 * D], F32)
            wts = [P.sbuf(f"wt{i}", [128, 4, 2048], F32) for i in range(2)]
            wtb = [P.sbuf(f"wtb{i}", [128, 4, 2048], BF16) for i in range(2)]
            mrow = [P.sbuf(f"mrow{i}", [2, 2048], F32) for i in range(2)]
            pm = [[P.psum(f"pm{a_}_{b_}", [128, 512], F32) for b_ in range(4)] for a_ in range(2)]
            P.dma("sp", craw[:], cvec.rearrange("s (p j) -> p s j", j=16), writes=["craw"], sem="const")
            P.dma("sp", bb[:], b_ada[l].partition_broadcast(2), writes=["bb"], sem="const2")
            P.op("act", lambda e: e.activation(out=sc[:].rearrange("p j s -> p s j"), in_=craw[:], func=AF.Silu),
                 reads=["craw"], writes=["sc"])
            P.op("dve", lambda e: e.tensor_copy(out=scb[:], in_=sc[:]), reads=["sc"], writes=["scb"])
            wv = w_ada[l].rearrange("(p j) n -> p j n", j=16)
            k = 0
            for ng in range(6):
                g2 = ng % 2
                for jg in range(4):
                    sl = k % 2
                    k += 1
                    wt, wb = wts[sl], wtb[sl]
                    P.dma_split("sp", wt[:], wv[:, jg * 4:(jg + 1) * 4, ng * 2048:(ng + 1) * 2048], 2, writes=[f"wt{sl}"], sem=f"wt{sl}")
                    P.op("act", lambda e, wt=wt, wb=wb: e.copy(out=wb[:, 0:2, :], in_=wt[:, 0:2, :]), reads=[f"wt{sl}"], writes=[f"wtb{sl}a"])
                    P.op("dve", lambda e, wt=wt, wb=wb: e.tensor_copy(out=wb[:, 2:4, :], in_=wt[:, 2:4, :]), reads=[f"wt{sl}"], writes=[f"wtb{sl}b"])
                    for nb4 in range(4):
                        for j in range(4):
                            P.op("pe", lambda e, wb=wb, j=j, jg=jg, nb4=nb4, g2=g2: e.matmul(pm[g2][nb4][0:2, :], lhsT=scb[:, jg * 4 + j, :], rhs=wb[:, j, nb4 * 512:(nb4 + 1) * 512],
                                                                                         start=(jg == 0 and j == 0), stop=(jg == 3 and j == 3)),
                                 reads=["scb", f"wtb{sl}a", f"wtb{sl}b"], writes=[f"pm{g2}_{nb4}"])
                mr = mrow[g2]
                for nb4 in range(4):
                    nb = ng * 4 + nb4
                    addc = 1.0 if (4 <= nb < 8 or 16 <= nb < 20) else 0.0
                    P.op("dve", lambda e, mr=mr, nb=nb, nb4=nb4, addc=addc, g2=g2: e.scalar_tensor_tensor(
                        out=mr[:, nb4 * 512:(nb4 + 1) * 512], in0=pm[g2][nb4][0:2, :], scalar=addc, in1=bb[:, nb * 512:(nb + 1) * 512], op0=ALU.add, op1=ALU.add),
                        reads=[f"pm{g2}_{nb4}", "bb"], writes=[f"mrow{g2}"])
                P.dma("act", MODROW[l, :, ng * 2048:(ng + 1) * 2048], mr[:], reads=[f"mrow{g2}"], writes=[], sem=f"mrow{g2}")

    def load_bc(tile, key, l, s, which, sem):
        P.dma("sp", tile[:], MODROW[l, s, which * D:(which + 1) * D].partition_broadcast(128), writes=[key], sem=sem)

    def phase_modulate(l, which_sh, which_sc, after_attn, blocks, write_h2):
        with P.phase("modulate"):
            identf, identb = load_consts(need_bf=True)
            A = [P.sbuf(f"A{s}", [128, D], F32) for s in range(2)]
            B = [P.sbuf(f"B{s}", [128, D], F32) for s in range(2)]
            classes = sorted({0 if t0 < TL else 1 for t0, _ in blocks})
            for s in classes:
                load_bc(A[s], f"A{s}", l, s, which_sc, f"bcA{s}")
                load_bc(B[s], f"B{s}", l, s, which_sh, f"bcB{s}")
            xt = [P.sbuf(f"xt{i}", [128, D], F32) for i in range(3)]
            hb = [P.sbuf(f"hb{i}", [128, D], BF16) for i in range(3)]
            junk = P.sbuf("junk", [128, D], F32)
            st = [P.sbuf(f"st{i}", [128, 4], F32) for i in range(3)]
            hT = [P.sbuf(f"hT{i}", [128, 16, 512], BF16) for i in range(2)]
            pT = [P.psum(f"pT{i}", [128, 16, 128], BF16) for i in range(3)]
            k = 0
            for bi, (t0, nt) in enumerate(blocks):
                s = 0 if t0 < TL else 1
                hTb = hT[bi % 2]
                for ti in range(nt // 128):
                    r0 = t0 + ti * 128
                    i = k % 3
                    k += 1
                    x_t, h_b, s_t, p_t = xt[i], hb[i], st[i], pT[i]
                    P.dma("sp", x_t[:], xsrc(l, r0, 128, after_attn=after_attn), writes=[f"xt{i}"], sem=f"xt{i}")
                    P.op("act", lambda e, x_t=x_t, s_t=s_t: e.activation(out=junk[:], in_=x_t[:], func=AF.Square, accum_out=s_t[:, 0:1]),
                         reads=[f"xt{i}"], writes=["junk", f"st{i}"])
                    P.op("dve", lambda e, s_t=s_t: e.tensor_scalar(out=s_t[:, 1:2], in0=s_t[:, 0:1], scalar1=1.0 / D, scalar2=EPS, op0=ALU.mult, op1=ALU.add),
                         reads=[f"st{i}"], writes=[f"st{i}"])
                    P.op("act", lambda e, s_t=s_t: e.sqrt(out=s_t[:, 2:3], in_=s_t[:, 1:2]), reads=[f"st{i}"], writes=[f"st{i}"])
                    P.op("dve", lambda e, s_t=s_t: e.reciprocal(out=s_t[:, 3:4], in_=s_t[:, 2:3]), reads=[f"st{i}"], writes=[f"st{i}"])
                    P.op("dve", lambda e, x_t=x_t, s_t=s_t, s=s: e.scalar_tensor_tensor(out=x_t[:], in0=x_t[:], scalar=s_t[:, 3:4], in1=A[s][:], op0=ALU.mult, op1=ALU.mult),
                         reads=[f"xt{i}", f"st{i}", f"A{s}"], writes=[f"xt{i}"])
                    P.op("dve", lambda e, x_t=x_t, s=s: e.tensor_tensor(out=x_t[:], in0=x_t[:], in1=B[s][:], op=ALU.add),
                         reads=[f"xt{i}", f"B{s}"], writes=[f"xt{i}"])
                    if write_h2:
                        P.dma("act", H2[r0:r0 + 128, :], x_t[:], reads=[f"xt{i}"], writes=[], sem=f"h2st{i}")
                    P.op("act", lambda e, x_t=x_t, h_b=h_b: e.copy(out=h_b[:], in_=x_t[:]), reads=[f"xt{i}"], writes=[f"hb{i}"])
                    for j in range(16):
                        P.op("pe", lambda e, h_b=h_b, p_t=p_t, j=j: e.transpose(out=p_t[:, j, :], in_=h_b[:, j * 128:(j + 1) * 128], identity=identb[:]),
                             reads=[f"hb{i}", "identb"], writes=[f"pT{i}"])
                    P.op("act", lambda e, p_t=p_t, hTb=hTb, ti=ti: e.copy(out=hTb[:, :, ti * 128:(ti + 1) * 128], in_=p_t[:]),
                         reads=[f"pT{i}"], writes=[f"hT{bi%2}"])
                P.dma_split("act", HT[:, :, t0:t0 + nt], hTb[:, :, 0:nt], 2, reads=[f"hT{bi%2}"], writes=[], sem=f"hTst{bi%2}")

    def proj(W, col_blocks, act_src, blocks_for, mode_for, evac, per_block=None, end_block=None, npp=3):
        wst = [P.sbuf(f"wst{i}", [128, 16, 256], F32) for i in range(2)]
        wbf = [P.sbuf(f"wbf{i}", [128, 16, 512], BF16) for i in range(2)]
        ablk = [P.sbuf(f"ablk{i}", [128, 16, 512], BF16) for i in range(2)]
        pp = [P.psum(f"pp{i}", [128, 512], F32) for i in range(npp)]
        Wv = W.rearrange("(j p) n -> p j n", p=128)
        items = []
        for ci, cb in enumerate(col_blocks):
            for (t0, nt) in blocks_for(cb):
                items.append((ci, cb, t0, nt))

        def load_w(ci, cb):
            for hf in range(2):
                P.dma_split("sp", wst[hf][:], Wv[:, :, cb * 512 + hf * 256:cb * 512 + (hf + 1) * 256], 2, writes=[f"wst{hf}"], sem=f"wst{hf}")

        def load_a(n):
            ci, cb, t0, nt = items[n]
            i = n % 2
            P.dma_split("sp", ablk[i][:, :, 0:nt], act_src[:, :, t0:t0 + nt], 2, writes=[f"ablk{i}"], sem=f"ablk{i}")

        load_w(0, col_blocks[0])
        load_a(0)
        q = 0
        last_ci = -1
        for n, (ci, cb, t0, nt) in enumerate(items):
            if ci != last_ci:
                i = ci % 2
                P.op("act", lambda e, i=i: e.copy(out=wbf[i][:, :, 0:256], in_=wst[0][:]), reads=["wst0"], writes=[f"wbf{i}"])
                P.op("pool", lambda e, i=i: e.tensor_copy(out=wbf[i][:, :, 256:512], in_=wst[1][:]), reads=["wst1"], writes=[f"wbf{i}"])
                if ci + 1 < len(col_blocks):
                    load_w(ci + 1, col_blocks[ci + 1])
                last_ci = ci
            if n + 1 < len(items):
                load_a(n + 1)
            wb = wbf[ci % 2]
            ab = ablk[n % 2]
            if per_block:
                per_block(cb, t0, nt)
            if mode_for(cb) == "tok":
                for ti in range(nt // 128):
                    ps = pp[q % npp]
                    pk = f"pp{q % npp}"
                    q += 1
                    for j in range(16):
                        P.op("pe", lambda e, ps=ps, ab=ab, wb=wb, j=j, ti=ti: e.matmul(ps[:], lhsT=ab[:, j, ti * 128:(ti + 1) * 128], rhs=wb[:, j, :],
                                                                                      start=(j == 0), stop=(j == 15)),
                             reads=[f"ablk{n%2}", f"wbf{ci%2}"], writes=[pk])
                    evac(cb, t0, ti, nt, ps, pk)
            else:
                for cc in range(4):
                    ps = pp[q % npp]
                    pk = f"pp{q % npp}"
                    q += 1
                    for j in range(16):
                        P.op("pe", lambda e, ps=ps, ab=ab, wb=wb, j=j, cc=cc, nt=nt: e.matmul(ps[:, 0:nt], lhsT=wb[:, j, cc * 128:(cc + 1) * 128], rhs=ab[:, j, 0:nt],
                                                                                             start=(j == 0), stop=(j == 15)),
                             reads=[f"ablk{n%2}", f"wbf{ci%2}"], writes=[pk])
                    evac(cb, t0, cc, nt, ps, pk)
            if end_block:
                end_block(cb, t0, nt)

    def phase_inproj(l):
        with P.phase("inproj"):
            identf, identb = load_consts(need_bf=True)
            rC = P.sbuf("rC", [128, 16, 128], F32)
            rS = P.sbuf("rS", [128, 16, 128], F32)
            P.dma_split("sp", rC[:], ropeC.rearrange("(t p) d -> p t d", p=128), 2, writes=["rC"], sem="const")
            P.dma_split("sp", rS[:], ropeS.rearrange("(t p) d -> p t d", p=128), 2, writes=["rS"], sem="const2")
            gq = P.sbuf("gq", [128, 128], F32)
            gk = P.sbuf("gk", [128, 128], F32)
            P.dma("sp", gq[:], q_gain[l].partition_broadcast(128), writes=["gq"], sem="const3")
            P.dma("sp", gk[:], k_gain[l].partition_broadcast(128), writes=["gk"], sem="const4")
            NB = 2
            qf = [P.sbuf(f"qf{i}", [128, 512], F32) for i in range(NB)]
            sq = [P.sbuf(f"sq{i}", [128, 512], F32) for i in range(NB)]
            t1 = [P.sbuf(f"t1{i}", [128, 512], F32) for i in range(NB)]
            t2 = [P.sbuf(f"t2{i}", [128, 512], F32) for i in range(NB)]
            qb = [P.sbuf(f"qb{i}", [128, 512], BF16) for i in range(NB)]
            sst = [P.sbuf(f"sst{i}", [128, 16], F32) for i in range(NB)]
            stage = [P.sbuf(f"stage{i}", [128, 4, 512], BF16) for i in range(2)]
            ev = [P.sbuf(f"ev{i}", [128, 512], F32) for i in range(3)]
            evb = [P.sbuf(f"evb{i}", [128, 512], BF16) for i in range(3)]
            pq = [P.psum(f"pq{i}", [128, 4, 128], BF16) for i in range(2)]
            cnt = {"qk": 0, "ev": 0, "blk": 0}

            def blocks_for(cb):
                if l == 1 and cb not in (4, 5):
                    return BLOCKS_LAT
                return BLOCKS_ALL

            def mode_for(cb):
                return "tok" if cb < 6 else "feat"

            pending = []

            def flush():
                while pending:
                    pending.pop(0)()

            def evac(cb, t0, idx, nt, ps, pk):
                flush()
                if cb < 5:
                    ti = idx
                    r0 = t0 + ti * 128
                    latent = r0 < TL
                    i = cnt["qk"] % NB
                    cnt["qk"] += 1
                    gain = gq if cb < 4 else gk
                    gkey = "gq" if cb < 4 else "gk"
                    q_f, s_q, t_1, t_2, q_b, s_t = qf[i], sq[i], t1[i], t2[i], qb[i], sst[i]
                    P.op("act", lambda e: e.copy(out=q_f[:], in_=ps[:]), reads=[pk], writes=[f"qf{i}"])
                    P.op("dve", lambda e: e.tensor_tensor(out=s_q[:], in0=q_f[:], in1=q_f[:], op=ALU.mult), reads=[f"qf{i}"], writes=[f"sq{i}"])
                    P.op("dve", lambda e: e.tensor_reduce(out=s_t[:, 0:4], in_=s_q[:].rearrange("p (h d) -> p h d", h=4), axis=AX.X, op=ALU.add),
                         reads=[f"sq{i}"], writes=[f"sst{i}"])
                    P.op("dve", lambda e: e.tensor_scalar(out=s_t[:, 4:8], in0=s_t[:, 0:4], scalar1=1.0 / 128, scalar2=EPS, op0=ALU.mult, op1=ALU.add),
                         reads=[f"sst{i}"], writes=[f"sst{i}"])
                    P.op("act", lambda e: e.sqrt(out=s_t[:, 8:12], in_=s_t[:, 4:8]), reads=[f"sst{i}"], writes=[f"sst{i}"])
                    P.op("dve", lambda e: e.reciprocal(out=s_t[:, 12:16], in_=s_t[:, 8:12]), reads=[f"sst{i}"], writes=[f"sst{i}"])
                    P.op("dve", lambda e: e.tensor_tensor(out=s_q[:].rearrange("p (h d) -> p h d", h=4), in0=q_f[:].rearrange("p (h d) -> p h d", h=4),
                                                          in1=s_t[:, 12:16].unsqueeze(2).to_broadcast([128, 4, 128]), op=ALU.mult),
                         reads=[f"qf{i}", f"sst{i}"], writes=[f"sq{i}"])
                    P.op("pool", lambda e: e.tensor_tensor(out=q_f[:].rearrange("p (h d) -> p h d", h=4), in0=s_q[:].rearrange("p (h d) -> p h d", h=4),
                                                           in1=gain[:].unsqueeze(1).to_broadcast([128, 4, 128]), op=ALU.mult),
                         reads=[f"sq{i}", gkey], writes=[f"qf{i}"])
                    if latent:
                        tt = r0 // 128
                        P.op("pool", lambda e: e.tensor_tensor(out=t_1[:].rearrange("p (h d) -> p h d", h=4), in0=q_f[:].rearrange("p (h d) -> p h d", h=4),
                                                               in1=rC[:, tt, :].unsqueeze(1).to_broadcast([128, 4, 128]), op=ALU.mult),
                             reads=[f"qf{i}", "rC"], writes=[f"t1{i}"])
                        qv = q_f[:].rearrange("p (h a two d) -> p h a two d", h=4, a=2, two=2)
                        tv = t_2[:].rearrange("p (h a two d) -> p h a two d", h=4, a=2, two=2)
                        sv = rS[:, tt, :].rearrange("p (a two d) -> p a two d", a=2, two=2)
                        for pr in range(2):
                            P.op("dve", lambda e, pr=pr: e.tensor_tensor(out=tv[:, :, :, pr, :], in0=qv[:, :, :, 1 - pr, :],
                                                                         in1=sv[:, :, pr, :].unsqueeze(1).to_broadcast([128, 4, 2, 32]), op=ALU.mult),
                                 reads=[f"qf{i}", "rS"], writes=[f"t2{i}"])
                        P.op("dve", lambda e: e.tensor_tensor(out=q_b[:], in0=t_1[:], in1=t_2[:], op=ALU.add), reads=[f"t1{i}", f"t2{i}"], writes=[f"qb{i}"])
                    else:
                        P.op("act", lambda e: e.copy(out=q_b[:], in_=q_f[:]), reads=[f"qf{i}"], writes=[f"qb{i}"])
                    p_q = pq[i % 2]
                    sg = cnt["blk"] % 2

                    def later():
                        for hh in range(4):
                            P.op("pe", lambda e, hh=hh: e.transpose(out=p_q[:, hh, :], in_=q_b[:, hh * 128:(hh + 1) * 128], identity=identb[:]),
                                 reads=[f"qb{i}", "identb"], writes=[f"pq{i%2}"])
                        P.op("act", lambda e: e.copy(out=stage[sg][:, :, ti * 128:(ti + 1) * 128], in_=p_q[:]), reads=[f"pq{i%2}"], writes=[f"stage{sg}"])
                    pending.append(later)
                elif cb == 5:
                    ti = idx
                    r0 = t0 + ti * 128
                    i = cnt["ev"] % 3
                    cnt["ev"] += 1
                    P.op("act", lambda e: e.copy(out=evb[i][:], in_=ps[:]), reads=[pk], writes=[f"evb{i}"])
                    P.dma("act", V[r0:r0 + 128, :], evb[i][:], reads=[f"evb{i}"], writes=[], sem=f"evb{i}")
                elif cb < 8:
                    cc = idx
                    i = cnt["ev"] % 3
                    cnt["ev"] += 1
                    P.op("act", lambda e: e.copy(out=ev[i][:, 0:nt], in_=ps[:, 0:nt]), reads=[pk], writes=[f"ev{i}"])
                    P.dma("act", PL[(cb - 6) * 4 + cc][:, t0:t0 + nt], ev[i][:, 0:nt], reads=[f"ev{i}"], writes=[], sem=f"ev{i}")
                else:
                    cc = idx
                    i = cnt["ev"] % 3
                    cnt["ev"] += 1
                    P.op("act", lambda e: e.activation(out=evb[i][:, 0:nt], in_=ps[:, 0:nt], func=AF.Sigmoid), reads=[pk], writes=[f"evb{i}"])
                    P.dma("act", GAB[(cb - 8) * 4 + cc][:, t0:t0 + nt], evb[i][:, 0:nt], reads=[f"evb{i}"], writes=[], sem=f"evb{i}")

            def end_block(cb, t0, nt):
                if cb < 5:
                    flush()
                    sg = cnt["blk"] % 2
                    cnt["blk"] += 1
                    dst = QT[cb * 4:(cb + 1) * 4] if cb < 4 else KT[0:4]
                    P.dma("act", dst.rearrange("h p t -> p h t")[:, :, t0:t0 + nt], stage[sg][:, :, 0:nt], reads=[f"stage{sg}"], writes=[], sem=f"stage{sg}")

            proj(w_in[l], list(range(16)), HT, blocks_for, mode_for, evac, end_block=end_block)

    def phase_pool(l):
        with P.phase("pool"):
            classes = [(0, TL)] + ([(TL, TC)] if l == 0 else [])

            def do_class(off, L):
                W = L + 32
                tag = "L" if off == 0 else "C"
                rc = P.sbuf(f"rc{tag}", [128, 4, L], F32)
                for g in range(4):
                    P.dma("sp", rc[:, g, :], rcnt[g, off:off + L].partition_broadcast(128), writes=[f"rc{tag}"], sem=f"rc{tag}")
                u = [P.sbuf(f"u{tag}{i}", [128, W], F32) for i in range(2)]
                sa = P.sbuf(f"sa{tag}", [128, W], F32)
                sb = P.sbuf(f"sb{tag}", [128, W], F32)
                tmp = P.sbuf(f"tmp{tag}", [128, L], F32)
                pd = [P.sbuf(f"pd{tag}{i}", [128, L], BF16) for i in range(2)]
                for i in range(2):
                    P.op("pool", lambda e, i=i: e.memset(u[i][:], 0.0), writes=[f"u{tag}{i}"])
                P.op("pool", lambda e: e.memset(sa[:], 0.0), writes=[f"sa{tag}"])
                P.op("pool", lambda e: e.memset(sb[:], 0.0), writes=[f"sb{tag}"])
                for c in range(8):
                    g = c // 2
                    i = c % 2
                    uu = u[i]
                    uk = f"u{tag}{i}"
                    P.dma("sp", uu[:, 16:16 + L], PL[c][:, off:off + L], writes=[uk], sem=uk)
                    P.op("dve", lambda e, uu=uu: e.tensor_tensor(out=sa[:, 1:W], in0=uu[:, 1:W], in1=uu[:, 0:W - 1], op=ALU.add), reads=[uk], writes=[f"sa{tag}"])
                    cur, curk = sa, f"sa{tag}"
                    if g >= 1:
                        P.op("pool", lambda e: e.tensor_tensor(out=sb[:, 2:W - 1], in0=sa[:, 3:W], in1=sa[:, 1:W - 2], op=ALU.add), reads=[f"sa{tag}"], writes=[f"sb{tag}"])
                        cur, curk = sb, f"sb{tag}"
                    if g >= 2:
                        P.op("dve", lambda e: e.tensor_tensor(out=sa[:, 4:W - 3], in0=sb[:, 6:W - 1], in1=sb[:, 2:W - 5], op=ALU.add), reads=[f"sb{tag}"], writes=[f"sa{tag}"])
                        cur, curk = sa, f"sa{tag}"
                    if g >= 3:
                        P.op("pool", lambda e: e.tensor_tensor(out=sb[:, 8:W - 7], in0=sa[:, 12:W - 3], in1=sa[:, 4:W - 11], op=ALU.add), reads=[f"sa{tag}"], writes=[f"sb{tag}"])
                        cur, curk = sb, f"sb{tag}"
                    P.op("dve", lambda e, cur=cur, g=g: e.tensor_tensor(out=tmp[:], in0=cur[:, 16:16 + L], in1=rc[:, g, :], op=ALU.mult),
                         reads=[curk, f"rc{tag}"], writes=[f"tmp{tag}"])
                    P.op("pool", lambda e, uu=uu, i=i: e.tensor_tensor(out=pd[i][:], in0=tmp[:], in1=uu[:, 16:16 + L], op=ALU.subtract),
                         reads=[f"tmp{tag}", uk], writes=[f"pd{tag}{i}"])
                    P.dma("act", PD[c][:, off:off + L], pd[i][:], reads=[f"pd{tag}{i}"], writes=[], sem=f"pd{tag}{i}")

            for (off, L) in classes:
                do_class(off, L)

    def phase_attn(l):
        with P.phase("attn"):
            if l == 0:
                emit_convert(12, 48)
            ones = P.sbuf("ones", [128, 128], BF16)
            P.op("dve", lambda e: e.memset(ones[:], 1.0), writes=["ones"])
            kT = [P.sbuf(f"kT{i}", [128, T], BF16) for i in range(2)]
            Vg = [P.sbuf(f"Vg{i}", [128, 18, 128], BF16) for i in range(2)]
            qT = [P.sbuf(f"qT{i}", [128, T], BF16) for i in range(2)]
            pt = [P.sbuf(f"pt{i}", [128, 512], BF16) for i in range(6)]
            rden = [P.sbuf(f"rden{i}", [128, 512], F32) for i in range(2)]
            ob = [P.sbuf(f"ob{i}", [128, 512], BF16) for i in range(2)]
            sps = [P.psum(f"sps{i}", [128, 512], F32) for i in range(3)]
            ops_ = [P.psum(f"ops{i}", [128, 512], F32) for i in range(2)]
            dps = [P.psum(f"dps{i}", [128, 512], F32) for i in range(2)]
            scale = 128.0 ** -0.5
            st = {"nq": 0, "npt": 0, "nsp": 0}

            def do_qblock(g, gi, h, qi, c0, nqc, kts, st):
                oi = st["nq"] % 2
                st["nq"] += 1
                o_ps, d_ps = ops_[oi], dps[oi]
                nk = len(kts)

                def S(kt, si):
                    P.op("pe", lambda e: e.matmul(sps[si][:, 0:nqc], lhsT=kT[gi][:, kt * 128:(kt + 1) * 128], rhs=qT[qi][:, c0:c0 + nqc],
                                                  start=True, stop=True),
                         reads=[f"kT{gi}", f"qT{qi}"], writes=[f"sps{si}"])

                def step(ii, kt, si, pi):
                    P.op("act", lambda e: e.activation(out=pt[pi][:, 0:nqc], in_=sps[si][:, 0:nqc], func=AF.Exp, scale=scale),
                         reads=[f"sps{si}"], writes=[f"pt{pi}"])
                    P.op("pe", lambda e: e.matmul(o_ps[:, 0:nqc], lhsT=Vg[gi][:, kt, :], rhs=pt[pi][:, 0:nqc], start=(ii == 0), stop=(ii == nk - 1)),
                         reads=[f"Vg{gi}", f"pt{pi}"], writes=[f"ops{oi}"])
                    P.op("pe", lambda e: e.matmul(d_ps[:, 0:nqc], lhsT=ones[:], rhs=pt[pi][:, 0:nqc], start=(ii == 0), stop=(ii == nk - 1)),
                         reads=["ones", f"pt{pi}"], writes=[f"dps{oi}"])

                base = st["nsp"]
                st["nsp"] += nk
                for pre in range(min(2, nk)):
                    S(kts[pre], (base + pre) % 3)
                for ii, kt in enumerate(kts):
                    si = (base + ii) % 3
                    if ii + 2 < nk:
                        S(kts[ii + 2], (base + ii + 2) % 3)
                    pi = st["npt"] % 6
                    st["npt"] += 1
                    step(ii, kt, si, pi)
                P.op("dve", lambda e: e.reciprocal(out=rden[oi][:, 0:nqc], in_=d_ps[:, 0:nqc]), reads=[f"dps{oi}"], writes=[f"rden{oi}"])
                P.op("dve", lambda e: e.tensor_tensor(out=ob[oi][:, 0:nqc], in0=o_ps[:, 0:nqc], in1=rden[oi][:, 0:nqc], op=ALU.mult),
                     reads=[f"ops{oi}", f"rden{oi}"], writes=[f"ob{oi}"])
                P.dma("sp", AT[h][:, c0:c0 + nqc], ob[oi][:, 0:nqc], reads=[f"ob{oi}"], writes=[], sem=f"ob{oi}")

            ncol = T if l == 0 else TL

            def load_kv(g):
                gi = g % 2
                P.dma("sp", kT[gi][:], KT[g], writes=[f"kT{gi}"], sem=f"kT{gi}")
                P.dma_split("sp", Vg[gi][:], V.rearrange("(kt p) c -> p kt c", p=128)[:, :, g * 128:(g + 1) * 128], 3, writes=[f"Vg{gi}"], sem=f"Vg{gi}")

            def load_q(h):
                qi = h % 2
                P.dma("sp", qT[qi][:, 0:ncol], QT[h][:, 0:ncol], writes=[f"qT{qi}"], sem=f"qT{qi}")

            load_kv(0)
            load_q(0)
            for g in range(4):
                gi = g % 2
                for hh in range(4):
                    h = g * 4 + hh
                    qi = h % 2
                    if h + 1 < 16:
                        load_q(h + 1)
                    if hh == 0 and g + 1 < 4:
                        load_kv(g + 1)
                    qblocks = [(c0, 512, list(range(18))) for c0 in range(0, TL, 512)]
                    if l == 0:
                        qblocks.append((TL, TC, [16, 17]))
                    for (c0, nqc, kts) in qblocks:
                        do_qblock(g, gi, h, qi, c0, nqc, kts, st)

    def phase_mix(l):
        with P.phase("mix"):
            identf, _ = load_consts()
            blocks = BLOCKS_ALL if l == 0 else BLOCKS_LAT
            wpf = P.sbuf("wpf", [128, 8, 512], F32)
            wpb = P.sbuf("wpb", [128, 8, 512], BF16)
            P.dma("sp", wpf[:], w_pool[l].rearrange("g (kc p) d -> p (g kc) d", p=128), writes=["wpf"], sem="const2")
            P.op("dve", lambda e: e.tensor_copy(out=wpb[:], in_=wpf[:]), reads=["wpf"], writes=["wpb"])
            psr = P.sbuf("psr", [16, 128], F32)
            pscT = P.sbuf("pscT", [128, 16], F32)
            P.dma("sp", psr[:], pool_scale[l].rearrange("(j p) -> j p", p=128), writes=["psr"], sem="const3")
            ptp = P.psum("ptp", [128, 16], F32)
            P.op("pe", lambda e: e.transpose(out=ptp[:], in_=psr[:], identity=identf[0:16, 0:16]), reads=["psr", "identf"], writes=["ptp"])
            P.op("dve", lambda e: e.tensor_copy(out=pscT[:], in_=ptp[:]), reads=["ptp"], writes=["pscT"])
            pdb = [P.sbuf(f"pdb{i}", [128, 2, 512], BF16) for i in range(2)]
            gab = [P.sbuf(f"gab{i}", [128, 4, 512], BF16) for i in range(2)]
            gbb = [P.sbuf(f"gbb{i}", [128, 4, 512], BF16) for i in range(2)]
            m1 = [P.sbuf(f"m1{i}", [128, 512], F32) for i in range(2)]
            m2 = [P.sbuf(f"m2{i}", [128, 512], F32) for i in range(2)]
            mg = [P.sbuf(f"mg{i}", [128, 512], BF16) for i in range(3)]
            pb = [P.psum(f"pb{i}", [128, 512], F32) for i in range(2)]
            cnt = {"blk": 0, "ev": 0}
            cur = {}

            def per_block(cb, t0, nt):
                i = cnt["blk"] % 2
                cnt["blk"] += 1
                cur["i"] = i
                P.dma("sp", pdb[i][:, :, 0:nt], PD[2 * cb:2 * cb + 2].rearrange("c p t -> p c t")[:, :, t0:t0 + nt], writes=[f"pdb{i}"], sem=f"pdb{i}")
                P.dma("sp", gab[i][:, :, 0:nt], GAB[4 * cb:4 * cb + 4].rearrange("c p t -> p c t")[:, :, t0:t0 + nt], writes=[f"gab{i}"], sem=f"gab{i}")
                P.dma("sp", gbb[i][:, :, 0:nt], GAB[16 + 4 * cb:16 + 4 * cb + 4].rearrange("c p t -> p c t")[:, :, t0:t0 + nt], writes=[f"gbb{i}"], sem=f"gbb{i}")

            def evac(cb, t0, cc, nt, ps, pk):
                i = cur["i"]
                dc = cb * 4 + cc
                e2 = cnt["ev"] % 2
                e3 = cnt["ev"] % 3
                cnt["ev"] += 1
                p_b = pb[e2]
                for kc in range(2):
                    P.op("pe", lambda e, kc=kc: e.matmul(p_b[:, 0:nt], lhsT=wpb[:, cb * 2 + kc, cc * 128:(cc + 1) * 128], rhs=pdb[i][:, kc, 0:nt],
                                                         start=(kc == 0), stop=(kc == 1)),
                         reads=["wpb", f"pdb{i}"], writes=[f"pb{e2}"])
                P.op("dve", lambda e: e.tensor_tensor(out=m1[e2][:, 0:nt], in0=ps[:, 0:nt], in1=gab[i][:, cc, 0:nt], op=ALU.mult),
                     reads=[pk, f"gab{i}"], writes=[f"m1{e2}"])
                P.op("dve", lambda e: e.scalar_tensor_tensor(out=m2[e2][:, 0:nt], in0=p_b[:, 0:nt], scalar=pscT[:, dc:dc + 1], in1=gbb[i][:, cc, 0:nt],
                                                             op0=ALU.mult, op1=ALU.mult),
                     reads=[f"pb{e2}", "pscT", f"gbb{i}"], writes=[f"m2{e2}"])
                P.op("pool", lambda e: e.tensor_tensor(out=mg[e3][:, 0:nt], in0=m1[e2][:, 0:nt], in1=m2[e2][:, 0:nt], op=ALU.add),
                     reads=[f"m1{e2}", f"m2{e2}"], writes=[f"mg{e3}"])
                P.dma("act", MG[dc][:, t0:t0 + nt], mg[e3][:, 0:nt], reads=[f"mg{e3}"], writes=[], sem=f"mg{e3}")

            proj(w_br[l], list(range(4)), AT.rearrange("h p t -> p h t"), lambda cb: blocks, lambda cb: "feat", evac, per_block=per_block, npp=2)

    def phase_wout(l):
        with P.phase("wout"):
            blocks = BLOCKS_ALL if l == 0 else BLOCKS_LAT
            G = [P.sbuf(f"G{s}", [128, D], F32) for s in range(2)]
            for s in ([0, 1] if l == 0 else [0]):
                load_bc(G[s], f"G{s}", l, s, G_A, f"bcG{s}")
            xb = [P.sbuf(f"xo{i}", [128, 4, 512], F32) for i in range(2)]
            tt = [P.sbuf(f"to{i}", [128, 512], F32) for i in range(3)]
            cnt = {"ev": 0, "blk": 0}
            cur = {}

            def per_block(cb, t0, nt):
                b = cnt["blk"] % 2
                cnt["blk"] += 1
                cur["b"] = b
                P.dma("sp", xb[b][:, 0:nt // 128, :], xsrc(l, t0, nt, cb * 512, 512).rearrange("(t p) c -> p t c", p=128), writes=[f"xo{b}"], sem=f"xo{b}")

            def evac(cb, t0, ti, nt, ps, pk):
                r0 = t0 + ti * 128
                s = 0 if r0 < TL else 1
                i = cnt["ev"] % 3
                cnt["ev"] += 1
                b = cur["b"]
                P.op("dve", lambda e: e.tensor_tensor(out=tt[i][:], in0=ps[:], in1=G[s][:, cb * 512:(cb + 1) * 512], op=ALU.mult),
                     reads=[pk, f"G{s}"], writes=[f"to{i}"])
                P.op("pool", lambda e: e.tensor_tensor(out=tt[i][:], in0=tt[i][:], in1=xb[b][:, ti, :], op=ALU.add), reads=[f"to{i}", f"xo{b}"], writes=[f"to{i}"])
                P.dma("act", X[r0:r0 + 128, cb * 512:(cb + 1) * 512], tt[i][:], reads=[f"to{i}"], writes=[], sem=f"to{i}")

            proj(w_out[l], list(range(4)), MG.rearrange("h p t -> p h t"), lambda cb: blocks, lambda cb: "tok", evac, per_block=per_block)

    def phase_qpeer(l):
        with P.phase("qpeer"):
            blocks = BLOCKS_ALL if l == 0 else BLOCKS_LAT
            ev = [P.sbuf(f"ev{i}", [128, 512], F32) for i in range(3)]
            cnt = {"ev": 0}

            def evac(cb, t0, cc, nt, ps, pk):
                i = cnt["ev"] % 3
                cnt["ev"] += 1
                P.op("act", lambda e: e.copy(out=ev[i][:, 0:nt], in_=ps[:, 0:nt]), reads=[pk], writes=[f"ev{i}"])
                P.dma("act", QPT[cb * 4 + cc][:, t0:t0 + nt], ev[i][:, 0:nt], reads=[f"ev{i}"], writes=[], sem=f"ev{i}")

            proj(w_qp[l], list(range(4)), HT, lambda cb: blocks, lambda cb: "feat", evac)

    def phase_topk(l):
        with P.phase("topk"):
            if l == 0:
                emit_convert(48, 64)
            identf, _ = load_consts()
            ntiles = (T if l == 0 else TL) // 128
            io16 = P.sbuf("io16", [128, 16], F32)
            P.dma("sp", io16[:], iota16_d, writes=["io16"], sem="const2")
            kraw = P.sbuf("kraw", [128, 16, 128], F32)
            keysT = P.sbuf("keysT", [128, 16, 128], F32)
            P.dma_split("sp", kraw[:], peer_keys[l].rearrange("h p k d -> k (h p) d"), 2, writes=["kraw"], sem="const3")
            pk4 = [P.psum(f"pk4{i}", [128, 4, 128], F32) for i in range(4)]
            for grp in range(4):
                for q in range(4):
                    hp = grp * 4 + q
                    P.op("pe", lambda e, hp=hp, q=q, grp=grp: e.transpose(out=pk4[grp][:, q, :], in_=kraw[:, hp, :], identity=identf[:]),
                         reads=["kraw", "identf"], writes=[f"pk4{grp}"])
                P.op("act", lambda e, grp=grp: e.copy(out=keysT[:, grp * 4:(grp + 1) * 4, :], in_=pk4[grp][:]), reads=[f"pk4{grp}"], writes=["keysT"])
            qt = [P.sbuf(f"qt{i}", [128, 16, 128], F32) for i in range(2)]
            Sbuf = [P.sbuf(f"S{k}", [128, 16, 128], F32) for k in range(2)]
            S2 = P.sbuf("S2", [128, 16, 128], F32)
            m = P.sbuf("m", [128, 16, 16], F32)
            ix = P.sbuf("ix", [128, 16, 16], U32)
            ixf = P.sbuf("ixf", [128, 16, 16], F32)
            i1s = P.sbuf("i1s", [128, 8, 16], F32)
            cand = P.sbuf("cand", [128, 8, 256], F32)
            cand2 = P.sbuf("cand2", [128, 8, 256], F32)
            ts = P.sbuf("ts", [128, 8, 16], F32)
            pos = P.sbuf("pos", [128, 8, 16], U32)
            au = P.sbuf("au", [128, 8, 16], U32)
            bu = P.sbuf("bu", [128, 8, 16], U32)
            af_ = P.sbuf("af", [128, 8, 16], F32)
            bf_ = P.sbuf("bf", [128, 8, 16], F32)
            oh = P.sbuf("oh", [128, 8, 16, 16], F32)
            isel = P.sbuf("isel", [128, 8, 16], F32)
            jsel = P.sbuf("jsel", [128, 8, 16], F32)
            ef = P.sbuf("ef", [128, 128], F32)
            eu = [P.sbuf(f"eu{i}", [128, 128], U32) for i in range(2)]
            dd = P.sbuf("dd", [128, 8, 16], F32)
            ee = P.sbuf("ee", [128, 8, 16], F32)
            zz = P.sbuf("zz", [128, 16], F32)
            gg = [P.sbuf(f"gg{i}", [128, 128], F32) for i in range(2)]
            NEG = -1e30
            dense = peer_mode == "dense"
            if dense:
                io128 = P.sbuf("io128", [128, 128], F32)
                P.dma("sp", io128[:], iota128_d, writes=["io128"], sem="const4")
                tp = P.psum("tp", [128, 3, 128], F32)
                ijg = P.sbuf("ijg", [128, 3, 128], F32)
                Aoh = [P.sbuf(f"Aoh{k}", [128, 16, 128], BF16) for k in range(2)]
                Boh = [P.sbuf(f"Boh{k}", [128, 128], BF16) for k in range(8)]
                gp = [P.psum(f"gp{k}", [128, 4, 128], F32) for k in range(2)]
                stg = [P.sbuf(f"stg{k}", [128, 128, 128], BF16) for k in range(2)]
                gst = {"b": 0, "g": 0}

            def gbuild(tt, i):
                r0 = tt * 128
                sg = tt % 2
                srcs = [isel[:].rearrange("p h k -> p (h k)"), jsel[:].rearrange("p h k -> p (h k)"), gg[i][:]]
                keys = ["sel0", "sel1", f"gg{i}"]
                for q in range(3):
                    P.op("pe", lambda e, q=q: e.transpose(out=tp[:, q, :], in_=srcs[q], identity=identf[:]), reads=[keys[q], "identf"], writes=["tp"])
                P.op("act", lambda e: e.copy(out=ijg[:], in_=tp[:]), reads=["tp"], writes=["ijg"])
                for grp in range(8):
                    a = grp % 2
                    for tq in range(16):
                        tl = grp * 16 + tq
                        P.op("pool", lambda e, a=a, tq=tq, tl=tl: e.tensor_scalar(out=Aoh[a][:, tq, :], in0=io128[:], scalar1=ijg[:, 0, tl:tl + 1], scalar2=None, op0=ALU.is_equal),
                             reads=["io128", "ijg"], writes=[f"Aoh{a}"])
                    for q4 in range(4):
                        gk = gst["g"] % 2
                        gst["g"] += 1
                        for q in range(4):
                            tl = grp * 16 + q4 * 4 + q
                            b = gst["b"] % 8
                            gst["b"] += 1
                            P.op("dve", lambda e, b=b, tl=tl: e.tensor_scalar(out=Boh[b][:], in0=io128[:], scalar1=ijg[:, 1, tl:tl + 1], scalar2=ijg[:, 2, tl:tl + 1],
                                                                              op0=ALU.is_equal, op1=ALU.mult),
                                 reads=["io128", "ijg"], writes=[f"Boh{b}"])
                            P.op("pe", lambda e, b=b, a=a, gk=gk, q=q, q4=q4: e.matmul(gp[gk][:, q, :], lhsT=Boh[b][:], rhs=Aoh[a][:, q4 * 4 + q, :], start=True, stop=True),
                                 reads=[f"Boh{b}", f"Aoh{a}"], writes=[f"gp{gk}"])
                        tl0 = grp * 16 + q4 * 4
                        P.op("act", lambda e, gk=gk, tl0=tl0, sg=sg: e.copy(out=stg[sg][:, :, tl0:tl0 + 4].rearrange("p i t -> p t i"), in_=gp[gk][:]),
                             reads=[f"gp{gk}"], writes=[f"stg{sg}"])
                dst = GTd.rearrange("i j t -> j i t")
                for k in range(16):
                    P.dma("sp", dst[:, k * 8:(k + 1) * 8, r0:r0 + 128], stg[sg][:, k * 8:(k + 1) * 8, :], reads=[f"stg{sg}"], writes=[], sem=f"stg{sg}")

            for tt in range(ntiles):
                r0 = tt * 128
                i = tt % 2
                P.dma_split("sp", qt[i][:], QPT.rearrange("c p t -> p c t")[:, :, r0:r0 + 128], 2, writes=[f"qt{i}"], sem=f"qt{i}")
                for grp in range(4):
                    for q in range(4):
                        hp = grp * 4 + q
                        P.op("pe", lambda e, hp=hp, q=q, grp=grp, i=i: e.matmul(pk4[grp][:, q, :], lhsT=qt[i][:, hp, :], rhs=keysT[:, hp, :], start=True, stop=True),
                             reads=[f"qt{i}", "keysT"], writes=[f"pk4{grp}"])
                    P.op("act", lambda e, grp=grp, Sx=Sbuf[i]: e.copy(out=Sx[:, grp * 4:(grp + 1) * 4, :], in_=pk4[grp][:]), reads=[f"pk4{grp}"], writes=[f"S{i}_{grp}"])
                for hp in range(16):
                    sk = f"S{i}_{hp // 4}"
                    P.op("dve", lambda e, hp=hp, Sx=Sbuf[i]: e.max(out=m[:, hp, 0:8], in_=Sx[:, hp, :]), reads=[sk], writes=[f"ma{hp}"])
                for hp in range(16):
                    sk = f"S{i}_{hp // 4}"
                    P.op("dve", lambda e, hp=hp, Sx=Sbuf[i]: e.max_index(out=ix[:, hp, 0:8], in_max=m[:, hp, 0:8], in_values=Sx[:, hp, :]), reads=[sk, f"ma{hp}"], writes=[f"ixa{hp}"])
                for hp in range(16):
                    sk = f"S{i}_{hp // 4}"
                    P.op("dve", lambda e, hp=hp, Sx=Sbuf[i]: e.match_replace(out=S2[:, hp, :], in_to_replace=m[:, hp, 0:8], in_values=Sx[:, hp, :], imm_value=NEG),
                         reads=[sk, f"ma{hp}"], writes=[f"S2_{hp}"])
                for hp in range(16):
                    P.op("dve", lambda e, hp=hp: e.max(out=m[:, hp, 8:16], in_=S2[:, hp, :]), reads=[f"S2_{hp}"], writes=[f"mb{hp}"])
                for hp in range(16):
                    P.op("dve", lambda e, hp=hp: e.max_index(out=ix[:, hp, 8:16], in_max=m[:, hp, 8:16], in_values=S2[:, hp, :]), reads=[f"S2_{hp}", f"mb{hp}"], writes=[f"ixb{hp}"])
                mkeys = [f"ma{hp}" for hp in range(16)] + [f"mb{hp}" for hp in range(16)]
                ixkeys = [f"ixa{hp}" for hp in range(16)] + [f"ixb{hp}" for hp in range(16)]
                P.op("dve", lambda e: e.tensor_copy(out=ixf[:], in_=ix[:]), reads=ixkeys, writes=["ixf"])
                mv = m[:].rearrange("p (h two) k -> p h two k", two=2)
                iv = ixf[:].rearrange("p (h two) k -> p h two k", two=2)
                cv = cand[:].rearrange("p h (a b) -> p h a b", a=16)
                P.op("dve", lambda e: e.tensor_tensor(out=cv, in0=mv[:, :, 0, :].unsqueeze(3).to_broadcast([128, 8, 16, 16]),
                                                      in1=mv[:, :, 1, :].unsqueeze(2).to_broadcast([128, 8, 16, 16]), op=ALU.add),
                     reads=mkeys, writes=["cand"])
                for h in range(8):
                    P.op("dve", lambda e, h=h: e.max(out=ts[:, h, 0:8], in_=cand[:, h, :]), reads=["cand"], writes=[f"tsa{h}"])
                for h in range(8):
                    P.op("dve", lambda e, h=h: e.max_index(out=pos[:, h, 0:8], in_max=ts[:, h, 0:8], in_values=cand[:, h, :]), reads=["cand", f"tsa{h}"], writes=[f"posa{h}"])
                for h in range(8):
                    P.op("dve", lambda e, h=h: e.match_replace(out=cand2[:, h, :], in_to_replace=ts[:, h, 0:8], in_values=cand[:, h, :], imm_value=NEG),
                         reads=["cand", f"tsa{h}"], writes=[f"c2_{h}"])
                for h in range(8):
                    P.op("dve", lambda e, h=h: e.max(out=ts[:, h, 8:16], in_=cand2[:, h, :]), reads=[f"c2_{h}"], writes=[f"tsb{h}"])
                for h in range(8):
                    P.op("dve", lambda e, h=h: e.max_index(out=pos[:, h, 8:16], in_max=ts[:, h, 8:16], in_values=cand2[:, h, :]), reads=[f"c2_{h}", f"tsb{h}"], writes=[f"posb{h}"])
                tskeys = [f"tsa{h}" for h in range(8)] + [f"tsb{h}" for h in range(8)]
                poskeys = [f"posa{h}" for h in range(8)] + [f"posb{h}" for h in range(8)]
                P.op("dve", lambda e: e.tensor_single_scalar(out=au[:], in_=pos[:], scalar=4, op=ALU.logical_shift_right), reads=poskeys, writes=["au"])
                P.op("dve", lambda e: e.tensor_single_scalar(out=bu[:], in_=pos[:], scalar=15, op=ALU.bitwise_and), reads=poskeys, writes=["bu"])
                P.op("dve", lambda e: e.tensor_copy(out=af_[:], in_=au[:]), reads=["au"], writes=["af"])
                P.op("dve", lambda e: e.tensor_copy(out=bf_[:], in_=bu[:]), reads=["bu"], writes=["bf"])
                for (sel, xf, which, key) in ((isel, af_, 0, "af"), (jsel, bf_, 1, "bf")):
                    P.op("dve", lambda e, xf=xf: e.tensor_tensor(out=oh[:], in0=io16[:].unsqueeze(1).unsqueeze(1).to_broadcast([128, 8, 16, 16]),
                                                                  in1=xf[:].unsqueeze(3).to_broadcast([128, 8, 16, 16]), op=ALU.is_equal),
                         reads=["io16", key], writes=["oh"])
                    P.op("dve", lambda e, which=which: e.tensor_tensor(out=oh[:], in0=oh[:], in1=iv[:, :, which, :].unsqueeze(2).to_broadcast([128, 8, 16, 16]), op=ALU.mult),
                         reads=["oh", "ixf"], writes=["oh"])
                    P.op("dve", lambda e, sel=sel: e.tensor_reduce(out=sel[:], in_=oh[:], axis=AX.X, op=ALU.add), reads=["oh"], writes=["sel%d" % which])
                P.op("dve", lambda e: e.scalar_tensor_tensor(out=ef[:], in0=isel[:].rearrange("p h k -> p (h k)"), scalar=128.0, in1=jsel[:].rearrange("p h k -> p (h k)"),
                                                             op0=ALU.mult, op1=ALU.add),
                     reads=["sel0", "sel1"], writes=["ef"])
                if l > 0:
                    P.op("dve", lambda e: e.tensor_scalar(out=ef[:], in0=ef[:], scalar1=float(l * NEXP), scalar2=None, op0=ALU.add), reads=["ef"], writes=["ef"])
                P.op("dve", lambda e, i=i: e.tensor_copy(out=eu[i][:], in_=ef[:]), reads=["ef"], writes=[f"eu{i}"])
                P.dma("sp", EIDX[r0:r0 + 128, :], eu[i][:], reads=[f"eu{i}"], writes=[], sem=f"eu{i}")
                P.op("dve", lambda e: e.tensor_tensor(out=dd[:], in0=ts[:], in1=ts[:, :, 0:1].to_broadcast([128, 8, 16]), op=ALU.subtract), reads=tskeys, writes=["dd"])
                P.op("act", lambda e: e.activation(out=ee[:], in_=dd[:], func=AF.Exp), reads=["dd"], writes=["ee"])
                P.op("dve", lambda e: e.tensor_reduce(out=zz[:, 0:8], in_=ee[:], axis=AX.X, op=ALU.add), reads=["ee"], writes=["zz"])
                P.op("dve", lambda e: e.reciprocal(out=zz[:, 8:16], in_=zz[:, 0:8]), reads=["zz"], writes=["zz"])
                P.op("dve", lambda e, i=i: e.tensor_tensor(out=gg[i][:].rearrange("p (h k) -> p h k", h=8), in0=ee[:], in1=zz[:, 8:16].unsqueeze(2).to_broadcast([128, 8, 16]), op=ALU.mult),
                     reads=["ee", "zz"], writes=[f"gg{i}"])
                P.dma("sp", GATE[r0:r0 + 128, :], gg[i][:], reads=[f"gg{i}"], writes=[], sem=f"gg{i}")
                if dense:
                    gbuild(tt, i)

    def phase_gather(l):
        with P.phase("gather"):
            P.wait_persistent()
            identf, identb = load_consts(need_bf=True)
            ntiles = (T if l == 0 else TL) // 128
            G = [P.sbuf(f"G{s}", [128, D], F32) for s in range(2)]
            for s in ([0, 1] if l == 0 else [0]):
                load_bc(G[s], f"G{s}", l, s, G_F, f"bcG{s}")
            NS = 8
            LOOK = 5
            uv = [P.sbuf(f"uv{i}", [128, 2 * D], BF16) for i in range(NS)]
            h2 = [P.sbuf(f"h2{i}", [128, D], F32) for i in range(2)]
            xt = [P.sbuf(f"xt{i}", [128, D], F32) for i in range(2)]
            junk = P.sbuf("junk", [128, D], F32)
            eix = [P.sbuf(f"eix{i}", [128, 128], U32) for i in range(2)]
            gat = [P.sbuf(f"gat{i}", [128, 128], F32) for i in range(2)]
            act = [P.sbuf(f"act{i}", [128, 128], F32) for i in range(2)]
            ge = [P.sbuf(f"ge{i}", [128, 128], F32) for i in range(2)]
            dg = [P.sbuf(f"dg{i}", [128, 128], BF16) for i in range(4)]
            tmp = [P.sbuf(f"tmp{i}", [128, 512], F32) for i in range(2)]
            acc = [[P.psum(f"acc{a}_{b}", [128, 512], F32) for b in range(4)] for a in range(2)]
            st = {"ntmp": 0}
            items = [(tt, sidx) for tt in range(ntiles) for sidx in range(128)]

            def loads(tt):
                r0 = tt * 128
                i = tt % 2
                P.dma("sp", h2[i][:], H2[r0:r0 + 128, :], writes=[f"h2{i}"], sem=f"h2{i}")
                P.dma("sp", xt[i][:], X[r0:r0 + 128, :], writes=[f"xt{i}"], sem=f"xt{i}")
                P.dma("sp", eix[i][:], EIDX[r0:r0 + 128, :], writes=[f"eix{i}"], sem=f"eix{i}")
                P.dma("sp", gat[i][:], GATE[r0:r0 + 128, :], writes=[f"gat{i}"], sem=f"gat{i}")

            def gather(n):
                tt, sidx = items[n]
                i = tt % 2
                u = n % NS
                if sidx == 0:
                    loads(tt)
                P.dma_fn("pool", lambda e: e.indirect_dma_start(
                    out=uv[u][:], out_offset=None, in_=UV, in_offset=bass.IndirectOffsetOnAxis(ap=eix[i][:, sidx:sidx + 1], axis=0)),
                    reads=[f"eix{i}"], writes=[f"uv{u}"], sem=f"uv{u}")

            def dot(n):
                tt, sidx = items[n]
                i = tt % 2
                u = n % NS
                P.op("dve", lambda e: e.scalar_tensor_tensor(out=junk[:], in0=h2[i][:], scalar=1.0, in1=uv[u][:, 0:D], op0=ALU.mult, op1=ALU.mult,
                                                             accum_out=act[i][:, sidx:sidx + 1]),
                     reads=[f"h2{i}", f"uv{u}"], writes=[f"a{i}_{sidx}"])
                P.op("act", lambda e: e.activation(out=ge[i][:, sidx:sidx + 1], in_=act[i][:, sidx:sidx + 1], func=AF.Gelu),
                     reads=[f"a{i}_{sidx}"], writes=[f"g{i}_{sidx}"])
                P.op("act", lambda e: e.activation(out=ge[i][:, sidx:sidx + 1], in_=ge[i][:, sidx:sidx + 1], func=AF.Copy, scale=gat[i][:, sidx:sidx + 1]),
                     reads=[f"g{i}_{sidx}", f"gat{i}"], writes=[f"g{i}_{sidx}"])

            def combine(n):
                tt, sidx = items[n]
                i = tt % 2
                u = n % NS
                d = n % 4
                P.op("act", lambda e: e.activation(out=dg[d][:], in_=identb[:], func=AF.Copy, scale=ge[i][:, sidx:sidx + 1]),
                     reads=["identb", f"g{i}_{sidx}"], writes=[f"dg{d}"])
                for db in range(4):
                    P.op("pe", lambda e, db=db: e.matmul(acc[i][db][:], lhsT=dg[d][:], rhs=uv[u][:, D + db * 512:D + (db + 1) * 512],
                                                         start=(sidx == 0), stop=(sidx == 127)),
                         reads=[f"dg{d}", f"uv{u}"], writes=[f"acc{i}_{db}"])
                if sidx == 127:
                    finalize(tt)

            def finalize(tt):
                r0 = tt * 128
                i = tt % 2
                s = 0 if r0 < TL else 1
                for db in range(4):
                    tq = st["ntmp"] % 2
                    st["ntmp"] += 1
                    P.op("dve", lambda e, tq=tq, db=db: e.tensor_tensor(out=tmp[tq][:], in0=acc[i][db][:], in1=G[s][:, db * 512:(db + 1) * 512], op=ALU.mult),
                         reads=[f"acc{i}_{db}", f"G{s}"], writes=[f"tmp{tq}"])
                    P.op("dve", lambda e, tq=tq, db=db: e.tensor_tensor(out=xt[i][:, db * 512:(db + 1) * 512], in0=tmp[tq][:], in1=xt[i][:, db * 512:(db + 1) * 512], op=ALU.add),
                         reads=[f"tmp{tq}", f"xt{i}"], writes=[f"xt{i}"])
                P.dma("sp", X[r0:r0 + 128, :], xt[i][:], reads=[f"xt{i}"], writes=[], sem=f"xst{i}")

            N = len(items)
            for n in range(min(LOOK, N)):
                gather(n)
            for n in range(N):
                if n + LOOK < N:
                    gather(n + LOOK)
                dot(n)
                if n >= 1:
                    combine(n - 1)
            combine(N - 1)

    def phase_dense(l):
        with P.phase("dense"):
            _, identb = load_consts(need_bf=True)
            ntok = T if l == 0 else TL
            groups = [(0, 768), (768, 768), (1536, ntok - 1536)]
            G = [P.sbuf(f"G{s}", [128, D], F32) for s in range(2)]
            for s in ([0, 1] if l == 0 else [0]):
                load_bc(G[s], f"G{s}", l, s, G_F, f"bcG{s}")
            hT = P.sbuf("hT", [128, 16, 768], BF16)
            acc = P.sbuf("acc", [128, 6, D], F32)
            GA = P.sbuf("GA", [128, 8, 768], BF16)
            Vb = P.sbuf("Vb", [128, 8, D], BF16)
            Ub = [P.sbuf(f"Ub{k}", [128, D], BF16) for k in range(3)]
            UT = [P.sbuf(f"UT{k}", [128, 16, 128], BF16) for k in range(2)]
            gt = [P.sbuf(f"gt{k}", [128, 768], BF16) for k in range(3)]
            gl = [P.sbuf(f"gl{k}", [128, 512], BF16) for k in range(2)]
            xt = [P.sbuf(f"xt{k}", [128, D], F32) for k in range(2)]
            ptu = [P.psum(f"ptu{k}", [128, 16, 128], BF16) for k in range(2)]
            ps1 = [P.psum(f"ps1{k}", [128, 512], F32) for k in range(2)]
            ps2 = [P.psum(f"ps2{k}", [128, 512], F32) for k in range(2)]
            st = {"c": 0, "p1": 0, "p2": 0, "x": 0}

            def chunk(c, ci, g0, gn, nblocks):
                k3 = st["c"] % 3
                k2 = st["c"] % 2
                st["c"] += 1
                row0 = l * NEXP + c * 128
                P.dma("sp", Ub[k3][:], UV[row0:row0 + 128, 0:D], writes=[f"Ub{k3}"], sem=f"Ub{k3}")
                P.dma("sp", Vb[:, ci, :], UV[row0:row0 + 128, D:2 * D], writes=[f"Vb{ci}"], sem=f"Vb{ci}")
                P.dma("sp", gt[k3][:, 0:gn], GTd[c][:, g0:g0 + gn], writes=[f"gt{k3}"], sem=f"gt{k3}")
                for j in range(16):
                    P.op("pe", lambda e, j=j: e.transpose(out=ptu[k2][:, j, :], in_=Ub[k3][:, j * 128:(j + 1) * 128], identity=identb[:]),
                         reads=[f"Ub{k3}", "identb"], writes=[f"ptu{k2}"])
                P.op("pool" if False else "dve", lambda e: e.tensor_copy(out=UT[k2][:], in_=ptu[k2][:]), reads=[f"ptu{k2}"], writes=[f"UT{k2}"])
                for (b0, bn) in nblocks:
                    p1 = st["p1"] % 2
                    st["p1"] += 1
                    for j in range(16):
                        P.op("pe", lambda e, j=j, p1=p1: e.matmul(ps1[p1][:, 0:bn], lhsT=UT[k2][:, j, :], rhs=hT[:, j, b0:b0 + bn], start=(j == 0), stop=(j == 15)),
                             reads=[f"UT{k2}", "hT"], writes=[f"ps1{p1}"])
                    P.op("act", lambda e, p1=p1: e.activation(out=gl[p1][:, 0:bn], in_=ps1[p1][:, 0:bn], func=AF.Gelu), reads=[f"ps1{p1}"], writes=[f"gl{p1}"])
                    P.op("pool", lambda e, p1=p1: e.tensor_tensor(out=GA[:, ci, b0:b0 + bn], in0=gl[p1][:, 0:bn], in1=gt[k3][:, b0:b0 + bn], op=ALU.mult),
                         reads=[f"gl{p1}", f"gt{k3}"], writes=[f"GA{ci}"])

            def combine(cg, ntile):
                for ti in range(ntile):
                    for db in range(4):
                        p2 = st["p2"] % 2
                        st["p2"] += 1
                        for ci in range(8):
                            P.op("pe", lambda e, ci=ci, p2=p2: e.matmul(ps2[p2][:], lhsT=GA[:, ci, ti * 128:(ti + 1) * 128], rhs=Vb[:, ci, db * 512:(db + 1) * 512],
                                                                        start=(ci == 0), stop=(ci == 7)),
                                 reads=[f"GA{ci}", f"Vb{ci}"], writes=[f"ps2{p2}"])
                        if cg == 0:
                            P.op("dve", lambda e, p2=p2: e.tensor_copy(out=acc[:, ti, db * 512:(db + 1) * 512], in_=ps2[p2][:]), reads=[f"ps2{p2}"], writes=[f"acc{ti}_{db}"])
                        else:
                            P.op("dve", lambda e, p2=p2: e.tensor_tensor(out=acc[:, ti, db * 512:(db + 1) * 512], in0=ps2[p2][:], in1=acc[:, ti, db * 512:(db + 1) * 512], op=ALU.add),
                                 reads=[f"ps2{p2}", f"acc{ti}_{db}"], writes=[f"acc{ti}_{db}"])

            def finalize(g0, ti):
                r0 = g0 + ti * 128
                s = 0 if r0 < TL else 1
                k = st["x"] % 2
                st["x"] += 1
                P.dma("sp", xt[k][:], X[r0:r0 + 128, :], writes=[f"xt{k}"], sem=f"xt{k}")
                P.op("pool", lambda e: e.tensor_tensor(out=acc[:, ti, :], in0=acc[:, ti, :], in1=G[s][:], op=ALU.mult),
                     reads=[f"acc{ti}_{db}" for db in range(4)] + [f"G{s}"], writes=[f"acc{ti}_{db}" for db in range(4)])
                P.op("pool", lambda e: e.tensor_tensor(out=xt[k][:], in0=xt[k][:], in1=acc[:, ti, :], op=ALU.add),
                     reads=[f"acc{ti}_{db}" for db in range(4)] + [f"xt{k}"], writes=[f"xt{k}"])
                P.dma("sp", X[r0:r0 + 128, :], xt[k][:], reads=[f"xt{k}"], writes=[], sem=f"xst{k}")

            for (g0, gn) in groups:
                ntile = gn // 128
                nblocks = [(0, 384), (384, 384)] if gn == 768 else [(0, gn)]
                P.dma_split("sp", hT[:, :, 0:gn], HT[:, :, g0:g0 + gn], 2, writes=["hT"], sem="hT")
                for cg in range(16):
                    for ci in range(8):
                        chunk(cg * 8 + ci, ci, g0, gn, nblocks)
                    combine(cg, ntile)
                for ti in range(ntile):
                    finalize(g0, ti)

    def phase_final():
        with P.phase("final"):
            fg = P.sbuf("fg", [128, D], F32)
            P.dma("sp", fg[:], final_gain.partition_broadcast(128), writes=["fg"], sem="const")
            xt = [P.sbuf(f"xt{i}", [128, D], F32) for i in range(3)]
            junk = P.sbuf("junk", [128, D], F32)
            st = [P.sbuf(f"st{i}", [128, 4], F32) for i in range(3)]
            for tt in range(TL // 128):
                r0 = tt * 128
                i = tt % 3
                x_t, s_t = xt[i], st[i]
                P.dma("sp", x_t[:], X[r0:r0 + 128, :], writes=[f"xt{i}"], sem=f"xt{i}")
                P.op("act", lambda e, x_t=x_t, s_t=s_t: e.activation(out=junk[:], in_=x_t[:], func=AF.Square, accum_out=s_t[:, 0:1]),
                     reads=[f"xt{i}"], writes=["junk", f"st{i}"])
                P.op("dve", lambda e, s_t=s_t: e.tensor_scalar(out=s_t[:, 1:2], in0=s_t[:, 0:1], scalar1=1.0 / D, scalar2=EPS, op0=ALU.mult, op1=ALU.add),
                     reads=[f"st{i}"], writes=[f"st{i}"])
                P.op("act", lambda e, s_t=s_t: e.sqrt(out=s_t[:, 2:3], in_=s_t[:, 1:2]), reads=[f"st{i}"], writes=[f"st{i}"])
                P.op("dve", lambda e, s_t=s_t: e.reciprocal(out=s_t[:, 3:4], in_=s_t[:, 2:3]), reads=[f"st{i}"], writes=[f"st{i}"])
                P.op("dve", lambda e, x_t=x_t, s_t=s_t: e.scalar_tensor_tensor(out=x_t[:], in0=x_t[:], scalar=s_t[:, 3:4], in1=fg[:], op0=ALU.mult, op1=ALU.mult),
                     reads=[f"xt{i}", f"st{i}", "fg"], writes=[f"xt{i}"])
                P.dma("pool", out_d[r0:r0 + 128, :], x_t[:], reads=[f"xt{i}"], writes=[], sem=f"ost{i}")

    def stop(l, name):
        return stop_after is not None and stop_after == (l, name)

    done = False
    for l in range(nlayers):
        blocks = BLOCKS_ALL
        steps = [
            ("mod", lambda: phase_mod(l)),
            ("modA", lambda: phase_modulate(l, SH_A, SC_A, False, BLOCKS_ALL, False)),
            ("inproj", lambda: phase_inproj(l)),
            ("pool", lambda: phase_pool(l)),
            ("attn", lambda: phase_attn(l)),
            ("mix", lambda: phase_mix(l)),
            ("wout", lambda: phase_wout(l)),
            ("modF", lambda: phase_modulate(l, SH_F, SC_F, True, BLOCKS_ALL if l == 0 else BLOCKS_LAT, True)),
            ("qpeer", lambda: phase_qpeer(l)),
            ("topk", lambda: phase_topk(l)),
            ("gather", (lambda: phase_dense(l)) if peer_mode == "dense" else (lambda: phase_gather(l))),
        ]
        for name, fn in steps:
            fn()
            if stop(l, name):
                done = True
                break
        if done:
            break
    if not done:
        phase_final()
    P.close()
    return nc, P


def _consts():
    t = np.arange(TL)
    row = (t // 64).astype(np.float32)
    col = (t % 64).astype(np.float32)
    inv = (np.float32(10000.0) ** (-np.arange(32, dtype=np.float32) / np.float32(32))).astype(np.float32)
    ar = (row[:, None] * inv[None, :]).astype(np.float32)
    ac = (col[:, None] * inv[None, :]).astype(np.float32)
    cr, sr, cc, sc = np.cos(ar), np.sin(ar), np.cos(ac), np.sin(ac)
    ropeC = np.concatenate([cr, cr, cc, cc], axis=1).astype(np.float32)
    ropeS = np.concatenate([-sr, sr, -sc, sc], axis=1).astype(np.float32)
    rc = np.zeros((4, T), np.float32)
    for g, w in enumerate((2, 4, 8, 16)):
        for (off, L) in ((0, TL), (TL, TC)):
            tt = np.arange(L)
            lo = np.clip(tt - w // 2, 0, L)
            hi = np.clip(tt + (w - w // 2), 0, L)
            rc[g, off:off + L] = 1.0 / (hi - lo).astype(np.float32)
    identf = np.eye(128, dtype=np.float32)
    iota16 = np.tile(np.arange(16, dtype=np.float32)[None, :], (128, 1))
    iota128 = np.tile(np.arange(128, dtype=np.float32)[None, :], (128, 1))
    return dict(ropeC=ropeC, ropeS=ropeS, rcnt=rc, identf=identf, iota16=iota16, iota128=iota128)


def make_in_map(inputs, b):
    f = lambda a: np.ascontiguousarray(np.asarray(a, dtype=np.float32))
    m = dict(
        x=f(inputs["x"][b]), ctx=f(inputs["ctx"][b]),
        cvec=f(np.stack([np.asarray(inputs["c"][b]), np.asarray(inputs["c_ctx"])], axis=0)),
    )
    for k in ["w_ada", "b_ada", "w_in", "q_gain", "k_gain", "w_br_attn", "w_pool", "pool_scale", "w_out", "w_q_peer",
              "peer_keys", "peer_u", "peer_v", "final_gain"]:
        m[k] = f(inputs[k])
    m.update(_consts())
    return m


def kernel(**inputs):
    nc, _ = build_program()
    shared = None
    in_maps = []
    for b in range(8):
        m = make_in_map(inputs, b) if shared is None else dict(shared)
        if shared is None:
            shared = m
        else:
            m["x"] = np.ascontiguousarray(np.asarray(inputs["x"][b], dtype=np.float32))
            m["ctx"] = np.ascontiguousarray(np.asarray(inputs["ctx"][b], dtype=np.float32))
            m["cvec"] = np.ascontiguousarray(np.stack([np.asarray(inputs["c"][b]), np.asarray(inputs["c_ctx"])], axis=0).astype(np.float32))
        in_maps.append(m)
    res = run_bass_kernel_spmd(nc, in_maps, core_ids=list(range(8)))
    return np.stack([np.asarray(r["out"], dtype=np.float32) for r in res.results], axis=0)
```

```python
from contextlib import ExitStack, contextmanager
import numpy as np
import concourse.bass as bass
import concourse.mybir as mybir
from concourse.bass_utils import run_bass_kernel_spmd

F32 = mybir.dt.float32
BF16 = mybir.dt.bfloat16
U32 = mybir.dt.uint32
AF = mybir.ActivationFunctionType
ALU = mybir.AluOpType
AX = mybir.AxisListType

ENGS = ["pe", "act", "dve", "pool", "sp"]

D = 2048
TL = 2048
TC = 256
T = TL + TC
NEXP = 16384
EPS = 1e-6
SH_A, SC_A, G_A, SH_F, SC_F, G_F = range(6)


class Prog:
    def __init__(self, nc, same_engine_sync=True):
        self.nc = nc
        self.stack = ExitStack()
        self.pstack = None
        self.streams = {e: [] for e in ENGS}
        self.sems = {}
        self.count = {}
        self.waited = {e: {} for e in ENGS}
        self.last_write = {}
        self.reads = {}
        self.same_engine_sync = same_engine_sync
        self.n_ops = 0
        self.phase_sems = {}
        self.persist = set()
        self.uid = 0
        for e in ENGS:
            self._sem("e_" + e)

    def _sem(self, name):
        if name not in self.sems:
            self.sems[name] = self.stack.enter_context(self.nc.semaphore(name))
            self.count[name] = 0
        return self.sems[name]

    def sbuf(self, name, shape, dtype):
        self.uid += 1
        return self.pstack.enter_context(self.nc.sbuf_tensor(f"{name}_s{self.uid}", list(shape), dtype))

    def psum(self, name, shape, dtype):
        self.uid += 1
        return self.pstack.enter_context(self.nc.psum_tensor(f"{name}_p{self.uid}", list(shape), dtype))

    def _wait(self, eng, sem, val):
        if sem == "e_pe" and eng == "pe":
            return
        if sem == "e_" + eng and not self.same_engine_sync:
            return
        if self.waited[eng].get(sem, 0) >= val:
            return
        self.waited[eng][sem] = val
        self.streams[eng].append(("wait", sem, val))

    def _deps(self, eng, reads, writes):
        deps = {}
        for k in reads:
            lw = self.last_write.get(k)
            if lw:
                deps[lw[0]] = max(deps.get(lw[0], 0), lw[1])
        for k in writes:
            lw = self.last_write.get(k)
            if lw:
                deps[lw[0]] = max(deps.get(lw[0], 0), lw[1])
            for s, v in self.reads.get(k, {}).items():
                deps[s] = max(deps.get(s, 0), v)
        for s, v in deps.items():
            self._wait(eng, s, v)

    def _record(self, ev, reads, writes):
        for k in reads:
            d = self.reads.setdefault(k, {})
            d[ev[0]] = max(d.get(ev[0], 0), ev[1])
        for k in writes:
            self.last_write[k] = ev
            self.reads[k] = {}

    def op(self, eng, fn, reads=(), writes=()):
        self._deps(eng, reads, writes)
        sem = "e_" + eng
        self.count[sem] += 1
        ev = (sem, self.count[sem])
        self.streams[eng].append(("op", fn, sem, 1))
        self._record(ev, reads, writes)
        self.n_ops += 1

    def dma(self, queue, out, in_, reads=(), writes=(), sem=None, **kw):
        self.dma_fn(queue, lambda e, o=out, i=in_, kw=kw: e.dma_start(out=o, in_=i, **kw), reads, writes, sem)

    def dma_fn(self, queue, fn, reads=(), writes=(), sem=None):
        self._deps(queue, reads, writes)
        sem = sem or "default"
        if sem.startswith("x_"):
            self.persist.add(sem)
        else:
            if sem not in self.phase_sems:
                self.phase_sems[sem] = "d_%d" % len(self.phase_sems)
            sem = self.phase_sems[sem]
        self._sem(sem)
        self.count[sem] += 16
        ev = (sem, self.count[sem])
        self.streams[queue].append(("op", fn, sem, 16))
        self._record(ev, reads, writes)
        self.n_ops += 1

    def dma_split(self, queue, out, in_, n, reads=(), writes=(), sem=None):
        a = out.shape[1]
        step = (a + n - 1) // n
        for k in range(0, a, step):
            self.dma(queue, out[:, k:min(a, k + step), :], in_[:, k:min(a, k + step), :], reads=reads, writes=writes, sem=sem)

    def wait_persistent(self):
        for e in ENGS:
            for s in sorted(self.persist):
                self._wait(e, s, self.count[s])
        self.persist = set()

    def barrier(self):
        for e in ENGS:
            for s, c in self.count.items():
                if c > 0 and s not in self.persist:
                    self._wait(e, s, c)
        self.last_write = {}
        self.reads = {}

    def emit_block(self):
        nc = self.nc
        streams = self.streams
        self.streams = {e: [] for e in ENGS}
        with nc.Block() as block:
            def replay(name):
                def f(engine):
                    for rec in streams[name]:
                        if rec[0] == "wait":
                            engine.wait_ge(self.sems[rec[1]], rec[2])
                        else:
                            rec[1](engine).then_inc(self.sems[rec[2]], rec[3])
                return f
            block.tensor(replay("pe"))
            block.scalar(replay("act"))
            block.vector(replay("dve"))
            block.gpsimd(replay("pool"))
            block.sync(replay("sp"))

    @contextmanager
    def phase(self, name=""):
        self.pstack = ExitStack()
        self.phase_sems = {}
        try:
            yield
            self.barrier()
            self.emit_block()
        finally:
            self.pstack.close()
            self.pstack = None

    def close(self):
        self.stack.close()


ALL_PHASES = ["mod", "modA", "inproj", "pool", "attn", "mix", "wout", "modF", "qpeer", "topk", "gather"]


def build_program(dbg=(), nlayers=2, stop_after=None, same_engine_sync=True, peer_mode="gather"):
    nc = bass.Bass("TRN2", target_bir_lowering=False)

    def inp(name, shape, dt=F32):
        return nc.dram_tensor(name, list(shape), dt, kind="ExternalInput").ap()

    def scratch(name, shape, dt):
        kind = "ExternalOutput" if name in dbg else "Internal"
        return nc.dram_tensor(name, list(shape), dt, kind=kind).ap()

    x_in = inp("x", [TL, D])
    ctx_in = inp("ctx", [TC, D])
    cvec = inp("cvec", [2, D])
    w_ada = inp("w_ada", [2, D, 6 * D])
    b_ada = inp("b_ada", [2, 6 * D])
    w_in = inp("w_in", [2, D, 8192])
    q_gain = inp("q_gain", [2, 128])
    k_gain = inp("k_gain", [2, 128])
    w_br = inp("w_br_attn", [2, D, D])
    w_pool = inp("w_pool", [2, 4, 256, 512])
    pool_scale = inp("pool_scale", [2, D])
    w_out = inp("w_out", [2, D, D])
    w_qp = inp("w_q_peer", [2, D, D])
    peer_keys = inp("peer_keys", [2, 8, 2, 128, 128])
    peer_u = inp("peer_u", [2, NEXP, D])
    peer_v = inp("peer_v", [2, NEXP, D])
    final_gain = inp("final_gain", [D])
    ropeC = inp("ropeC", [TL, 128])
    ropeS = inp("ropeS", [TL, 128])
    rcnt = inp("rcnt", [4, T])
    identf_d = inp("identf", [128, 128])
    iota16_d = inp("iota16", [128, 16])
    iota128_d = inp("iota128", [128, 128])
    out_d = nc.dram_tensor("out", [TL, D], F32, kind="ExternalOutput").ap()

    MODROW = scratch("MODROW", [2, 2, 6 * D], F32)
    X = scratch("X", [T, D], F32)
    HT = scratch("HT", [128, 16, T], BF16)
    QT = scratch("QT", [16, 128, T], BF16)
    KT = scratch("KT", [4, 128, T], BF16)
    V = scratch("V", [T, 512], BF16)
    PL = scratch("PL", [8, 128, T], F32)
    PD = scratch("PD", [8, 128, T], BF16)
    GAB = scratch("GAB", [32, 128, T], BF16)
    AT = scratch("AT", [16, 128, T], BF16)
    MG = scratch("MG", [16, 128, T], BF16)
    H2 = scratch("H2", [T, D], F32)
    QPT = scratch("QPT", [16, 128, T], F32)
    EIDX = scratch("EIDX", [T, 128], U32)
    GATE = scratch("GATE", [T, 128], F32)
    GTd = scratch("GTd", [128, 128, T], BF16)
    UV = scratch("UV", [2 * NEXP, 2 * D], BF16)

    P = Prog(nc, same_engine_sync=same_engine_sync)

    BLOCKS_ALL = [(0, 512), (512, 512), (1024, 512), (1536, 512), (2048, 256)]
    BLOCKS_LAT = BLOCKS_ALL[:4]

    def xsrc(l, r0, nr, c0=0, ncol=D, after_attn=False):
        if l == 0 and not after_attn:
            if r0 < TL:
                return x_in[r0:r0 + nr, c0:c0 + ncol]
            return ctx_in[r0 - TL:r0 - TL + nr, c0:c0 + ncol]
        return X[r0:r0 + nr, c0:c0 + ncol]

    def load_consts(need_bf=False):
        identf = P.sbuf("identf", [128, 128], F32)
        P.dma("sp", identf[:], identf_d, writes=["identf"], sem="const")
        identb = None
        if need_bf:
            identb = P.sbuf("identb", [128, 128], BF16)
            P.op("dve", lambda e: e.tensor_copy(out=identb[:], in_=identf[:]), reads=["identf"], writes=["identb"])
        return identf, identb

    def emit_convert(k0, k1):
        Uf = peer_u.rearrange("l e d -> (l e) d")
        Vf = peer_v.rearrange("l e d -> (l e) d")
        RB = 1024
        k = 0
        for r0 in range(0, 2 * NEXP, RB):
            for (c0, src) in ((0, Uf), (D, Vf)):
                if k0 <= k < k1:
                    P.dma("pool", UV[r0:r0 + RB, c0:c0 + D], src[r0:r0 + RB, :], sem=f"x_cv{k % 4}")
                k += 1

    def phase_mod(l):
        with P.phase("mod"):
            if l == 0:
                emit_convert(0, 12)
            craw = P.sbuf("craw", [128, 2, 16], F32)
            sc = P.sbuf("sc", [128, 16, 2], F32)
            scb = P.sbuf("scb", [128, 16, 2], BF16)
            bb = P.sbuf("bb", [2, 6 * D], F32)
            wts = [P.sbuf(f"wt{i}", [128, 4, 2048], F32) for i in range(2)]
            wtb = [P.sbuf(f"wtb{i}", [128, 4, 2048], BF16) for i in range(2)]
            mrow = [P.sbuf(f"mrow{i}", [2, 2048], F32) for i in range(2)]
            pm = [[P.psum(f"pm{a_}_{b_}", [128, 512], F32) for b_ in range(4)] for a_ in range(2)]
            P.dma("sp", craw[:], cvec.rearrange("s (p j) -> p s j", j=16), writes=["craw"], sem="const")
            P.dma("sp", bb[:], b_ada[l].partition_broadcast(2), writes=["bb"], sem="const2")
            P.op("act", lambda e: e.activation(out=sc[:].rearrange("p j s -> p s j"), in_=craw[:], func=AF.Silu),
                 reads=["craw"], writes=["sc"])
            P.op("dve", lambda e: e.tensor_copy(out=scb[:], in_=sc[:]), reads=["sc"], writes=["scb"])
            wv = w_ada[l].rearrange("(p j) n -> p j n", j=16)
            k = 0
            for ng in range(6):
                g2 = ng % 2
                for jg in range(4):
                    sl = k % 2
                    k += 1
                    wt, wb = wts[sl], wtb[sl]
                    P.dma_split("sp", wt[:], wv[:, jg * 4:(jg + 1) * 4, ng * 2048:(ng + 1) * 2048], 2, writes=[f"wt{sl}"], sem=f"wt{sl}")
                    P.op("act", lambda e, wt=wt, wb=wb: e.copy(out=wb[:, 0:2, :], in_=wt[:, 0:2, :]), reads=[f"wt{sl}"], writes=[f"wtb{sl}a"])
                    P.op("dve", lambda e, wt=wt, wb=wb: e.tensor_copy(out=wb[:, 2:4, :], in_=wt[:, 2:4, :]), reads=[f"wt{sl}"], writes=[f"wtb{sl}b"])
                    for nb4 in range(4):
                        for j in range(4):
                            P.op("pe", lambda e, wb=wb, j=j, jg=jg, nb4=nb4, g2=g2: e.matmul(pm[g2][nb4][0:2, :], lhsT=scb[:, jg * 4 + j, :], rhs=wb[:, j, nb4 * 512:(nb4 + 1) * 512],
                                                                                         start=(jg == 0 and j == 0), stop=(jg == 3 and j == 3)),
                                 reads=["scb", f"wtb{sl}a", f"wtb{sl}b"], writes=[f"pm{g2}_{nb4}"])
                mr = mrow[g2]
                for nb4 in range(4):
                    nb = ng * 4 + nb4
                    addc = 1.0 if (4 <= nb < 8 or 16 <= nb < 20) else 0.0
                    P.op("dve", lambda e, mr=mr, nb=nb, nb4=nb4, addc=addc, g2=g2: e.scalar_tensor_tensor(
                        out=mr[:, nb4 * 512:(nb4 + 1) * 512], in0=pm[g2][nb4][0:2, :], scalar=addc, in1=bb[:, nb * 512:(nb + 1) * 512], op0=ALU.add, op1=ALU.add),
                        reads=[f"pm{g2}_{nb4}", "bb"], writes=[f"mrow{g2}"])
                P.dma("act", MODROW[l, :, ng * 2048:(ng + 1) * 2048], mr[:], reads=[f"mrow{g2}"], writes=[], sem=f"mrow{g2}")

    def load_bc(tile, key, l, s, which, sem):
        P.dma("sp", tile[:], MODROW[l, s, which * D:(which + 1) * D].partition_broadcast(128), writes=[key], sem=sem)

    def phase_modulate(l, which_sh, which_sc, after_attn, blocks, write_h2):
        with P.phase("modulate"):
            identf, identb = load_consts(need_bf=True)
            A = [P.sbuf(f"A{s}", [128, D], F32) for s in range(2)]
            B = [P.sbuf(f"B{s}", [128, D], F32) for s in range(2)]
            classes = sorted({0 if t0 < TL else 1 for t0, _ in blocks})
            for s in classes:
                load_bc(A[s], f"A{s}", l, s, which_sc, f"bcA{s}")
                load_bc(B[s], f"B{s}", l, s, which_sh, f"bcB{s}")
            xt = [P.sbuf(f"xt{i}", [128, D], F32) for i in range(3)]
            hb = [P.sbuf(f"hb{i}", [128, D], BF16) for i in range(3)]
            junk = P.sbuf("junk", [128, D], F32)
            st = [P.sbuf(f"st{i}", [128, 4], F32) for i in range(3)]
            hT = [P.sbuf(f"hT{i}", [128, 16, 512], BF16) for i in range(2)]
            pT = [P.psum(f"pT{i}", [128, 16, 128], BF16) for i in range(3)]
            k = 0
            for bi, (t0, nt) in enumerate(blocks):
                s = 0 if t0 < TL else 1
                hTb = hT[bi % 2]
                for ti in range(nt // 128):
                    r0 = t0 + ti * 128
                    i = k % 3
                    k += 1
                    x_t, h_b, s_t, p_t = xt[i], hb[i], st[i], pT[i]
                    P.dma("sp", x_t[:], xsrc(l, r0, 128, after_attn=after_attn), writes=[f"xt{i}"], sem=f"xt{i}")
                    P.op("act", lambda e, x_t=x_t, s_t=s_t: e.activation(out=junk[:], in_=x_t[:], func=AF.Square, accum_out=s_t[:, 0:1]),
                         reads=[f"xt{i}"], writes=["junk", f"st{i}"])
                    P.op("dve", lambda e, s_t=s_t: e.tensor_scalar(out=s_t[:, 1:2], in0=s_t[:, 0:1], scalar1=1.0 / D, scalar2=EPS, op0=ALU.mult, op1=ALU.add),
                         reads=[f"st{i}"], writes=[f"st{i}"])
                    P.op("act", lambda e, s_t=s_t: e.sqrt(out=s_t[:, 2:3], in_=s_t[:, 1:2]), reads=[f"st{i}"], writes=[f"st{i}"])
                    P.op("dve", lambda e, s_t=s_t: e.reciprocal(out=s_t[:, 3:4], in_=s_t[:, 2:3]), reads=[f"st{i}"], writes=[f"st{i}"])
                    P.op("dve", lambda e, x_t=x_t, s_t=s_t, s=s: e.scalar_tensor_tensor(out=x_t[:], in0=x_t[:], scalar=s_t[:, 3:4], in1=A[s][:], op0=ALU.mult, op1=ALU.mult),
                         reads=[f"xt{i}", f"st{i}", f"A{s}"], writes=[f"xt{i}"])
                    P.op("dve", lambda e, x_t=x_t, s=s: e.tensor_tensor(out=x_t[:], in0=x_t[:], in1=B[s][:], op=ALU.add),
                         reads=[f"xt{i}", f"B{s}"], writes=[f"xt{i}"])
                    if write_h2:
                        P.dma("act", H2[r0:r0 + 128, :], x_t[:], reads=[f"xt{i}"], writes=[], sem=f"h2st{i}")
                    P.op("act", lambda e, x_t=x_t, h_b=h_b: e.copy(out=h_b[:], in_=x_t[:]), reads=[f"xt{i}"], writes=[f"hb{i}"])
                    for j in range(16):
                        P.op("pe", lambda e, h_b=h_b, p_t=p_t, j=j: e.transpose(out=p_t[:, j, :], in_=h_b[:, j * 128:(j + 1) * 128], identity=identb[:]),
                             reads=[f"hb{i}", "identb"], writes=[f"pT{i}"])
                    P.op("act", lambda e, p_t=p_t, hTb=hTb, ti=ti: e.copy(out=hTb[:, :, ti * 128:(ti + 1) * 128], in_=p_t[:]),
                         reads=[f"pT{i}"], writes=[f"hT{bi%2}"])
                P.dma_split("act", HT[:, :, t0:t0 + nt], hTb[:, :, 0:nt], 2, reads=[f"hT{bi%2}"], writes=[], sem=f"hTst{bi%2}")

    def proj(W, col_blocks, act_src, blocks_for, mode_for, evac, per_block=None, end_block=None, npp=3):
        wst = [P.sbuf(f"wst{i}", [128, 16, 256], F32) for i in range(2)]
        wbf = [P.sbuf(f"wbf{i}", [128, 16, 512], BF16) for i in range(2)]
        ablk = [P.sbuf(f"ablk{i}", [128, 16, 512], BF16) for i in range(2)]
        pp = [P.psum(f"pp{i}", [128, 512], F32) for i in range(npp)]
        Wv = W.rearrange("(j p) n -> p j n", p=128)
        items = []
        for ci, cb in enumerate(col_blocks):
            for (t0, nt) in blocks_for(cb):
                items.append((ci, cb, t0, nt))

        def load_w(ci, cb):
            for hf in range(2):
                P.dma_split("sp", wst[hf][:], Wv[:, :, cb * 512 + hf * 256:cb * 512 + (hf + 1) * 256], 2, writes=[f"wst{hf}"], sem=f"wst{hf}")

        def load_a(n):
            ci, cb, t0, nt = items[n]
            i = n % 2
            P.dma_split("sp", ablk[i][:, :, 0:nt], act_src[:, :, t0:t0 + nt], 2, writes=[f"ablk{i}"], sem=f"ablk{i}")

        load_w(0, col_blocks[0])
        load_a(0)
        q = 0
        last_ci = -1
        for n, (ci, cb, t0, nt) in enumerate(items):
            if ci != last_ci:
                i = ci % 2
                P.op("act", lambda e, i=i: e.copy(out=wbf[i][:, :, 0:256], in_=wst[0][:]), reads=["wst0"], writes=[f"wbf{i}"])
                P.op("pool", lambda e, i=i: e.tensor_copy(out=wbf[i][:, :, 256:512], in_=wst[1][:]), reads=["wst1"], writes=[f"wbf{i}"])
                if ci + 1 < len(col_blocks):
                    load_w(ci + 1, col_blocks[ci + 1])
                last_ci = ci
            if n + 1 < len(items):
                load_a(n + 1)
            wb = wbf[ci % 2]
            ab = ablk[n % 2]
            if per_block:
                per_block(cb, t0, nt)
            if mode_for(cb) == "tok":
                for ti in range(nt // 128):
                    ps = pp[q % npp]
                    pk = f"pp{q % npp}"
                    q += 1
                    for j in range(16):
                        P.op("pe", lambda e, ps=ps, ab=ab, wb=wb, j=j, ti=ti: e.matmul(ps[:], lhsT=ab[:, j, ti * 128:(ti + 1) * 128], rhs=wb[:, j, :],
                                                                                      start=(j == 0), stop=(j == 15)),
                             reads=[f"ablk{n%2}", f"wbf{ci%2}"], writes=[pk])
                    evac(cb, t0, ti, nt, ps, pk)
            else:
                for cc in range(4):
                    ps = pp[q % npp]
                    pk = f"pp{q % npp}"
                    q += 1
                    for j in range(16):
                        P.op("pe", lambda e, ps=ps, ab=ab, wb=wb, j=j, cc=cc, nt=nt: e.matmul(ps[:, 0:nt], lhsT=wb[:, j, cc * 128:(cc + 1) * 128], rhs=ab[:, j, 0:nt],
                                                                                             start=(j == 0), stop=(j == 15)),
                             reads=[f"ablk{n%2}", f"wbf{ci%2}"], writes=[pk])
                    evac(cb, t0, cc, nt, ps, pk)
            if end_block:
                end_block(cb, t0, nt)

    def phase_inproj(l):
        with P.phase("inproj"):
            identf, identb = load_consts(need_bf=True)
            rC = P.sbuf("rC", [128, 16, 128], F32)
            rS = P.sbuf("rS", [128, 16, 128], F32)
            P.dma_split("sp", rC[:], ropeC.rearrange("(t p) d -> p t d", p=128), 2, writes=["rC"], sem="const")
            P.dma_split("sp", rS[:], ropeS.rearrange("(t p) d -> p t d", p=128), 2, writes=["rS"], sem="const2")
            gq = P.sbuf("gq", [128, 128], F32)
            gk = P.sbuf("gk", [128, 128], F32)
            P.dma("sp", gq[:], q_gain[l].partition_broadcast(128), writes=["gq"], sem="const3")
            P.dma("sp", gk[:], k_gain[l].partition_broadcast(128), writes=["gk"], sem="const4")
            NB = 2
            qf = [P.sbuf(f"qf{i}", [128, 512], F32) for i in range(NB)]
            sq = [P.sbuf(f"sq{i}", [128, 512], F32) for i in range(NB)]
            t1 = [P.sbuf(f"t1{i}", [128, 512], F32) for i in range(NB)]
            t2 = [P.sbuf(f"t2{i}", [128, 512], F32) for i in range(NB)]
            qb = [P.sbuf(f"qb{i}", [128, 512], BF16) for i in range(NB)]
            sst = [P.sbuf(f"sst{i}", [128, 16], F32) for i in range(NB)]
            stage = [P.sbuf(f"stage{i}", [128, 4, 512], BF16) for i in range(2)]
            ev = [P.sbuf(f"ev{i}", [128, 512], F32) for i in range(3)]
            evb = [P.sbuf(f"evb{i}", [128, 512], BF16) for i in range(3)]
            pq = [P.psum(f"pq{i}", [128, 4, 128], BF16) for i in range(2)]
            cnt = {"qk": 0, "ev": 0, "blk": 0}

            def blocks_for(cb):
                if l == 1 and cb not in (4, 5):
                    return BLOCKS_LAT
                return BLOCKS_ALL

            def mode_for(cb):
                return "tok" if cb < 6 else "feat"

            pending = []

            def flush():
                while pending:
                    pending.pop(0)()

            def evac(cb, t0, idx, nt, ps, pk):
                flush()
                if cb < 5:
                    ti = idx
                    r0 = t0 + ti * 128
                    latent = r0 < TL
                    i = cnt["qk"] % NB
                    cnt["qk"] += 1
                    gain = gq if cb < 4 else gk
                    gkey = "gq" if cb < 4 else "gk"
                    q_f, s_q, t_1, t_2, q_b, s_t = qf[i], sq[i], t1[i], t2[i], qb[i], sst[i]
                    P.op("act", lambda e: e.copy(out=q_f[:], in_=ps[:]), reads=[pk], writes=[f"qf{i}"])
                    P.op("dve", lambda e: e.tensor_tensor(out=s_q[:], in0=q_f[:], in1=q_f[:], op=ALU.mult), reads=[f"qf{i}"], writes=[f"sq{i}"])
                    P.op("dve", lambda e: e.tensor_reduce(out=s_t[:, 0:4], in_=s_q[:].rearrange("p (h d) -> p h d", h=4), axis=AX.X, op=ALU.add),
                         reads=[f"sq{i}"], writes=[f"sst{i}"])
                    P.op("dve", lambda e: e.tensor_scalar(out=s_t[:, 4:8], in0=s_t[:, 0:4], scalar1=1.0 / 128, scalar2=EPS, op0=ALU.mult, op1=ALU.add),
                         reads=[f"sst{i}"], writes=[f"sst{i}"])
                    P.op("act", lambda e: e.sqrt(out=s_t[:, 8:12], in_=s_t[:, 4:8]), reads=[f"sst{i}"], writes=[f"sst{i}"])
                    P.op("dve", lambda e: e.reciprocal(out=s_t[:, 12:16], in_=s_t[:, 8:12]), reads=[f"sst{i}"], writes=[f"sst{i}"])
                    P.op("dve", lambda e: e.tensor_tensor(out=s_q[:].rearrange("p (h d) -> p h d", h=4), in0=q_f[:].rearrange("p (h d) -> p h d", h=4),
                                                          in1=s_t[:, 12:16].unsqueeze(2).to_broadcast([128, 4, 128]), op=ALU.mult),
                         reads=[f"qf{i}", f"sst{i}"], writes=[f"sq{i}"])
                    P.op("pool", lambda e: e.tensor_tensor(out=q_f[:].rearrange("p (h d) -> p h d", h=4), in0=s_q[:].rearrange("p (h d) -> p h d", h=4),
                                                           in1=gain[:].unsqueeze(1).to_broadcast([128, 4, 128]), op=ALU.mult),
                         reads=[f"sq{i}", gkey], writes=[f"qf{i}"])
                    if latent:
                        tt = r0 // 128
                        P.op("pool", lambda e: e.tensor_tensor(out=t_1[:].rearrange("p (h d) -> p h d", h=4), in0=q_f[:].rearrange("p (h d) -> p h d", h=4),
                                                               in1=rC[:, tt, :].unsqueeze(1).to_broadcast([128, 4, 128]), op=ALU.mult),
                             reads=[f"qf{i}", "rC"], writes=[f"t1{i}"])
                        qv = q_f[:].rearrange("p (h a two d) -> p h a two d", h=4, a=2, two=2)
                        tv = t_2[:].rearrange("p (h a two d) -> p h a two d", h=4, a=2, two=2)
                        sv = rS[:, tt, :].rearrange("p (a two d) -> p a two d", a=2, two=2)
                        for pr in range(2):
                            P.op("dve", lambda e, pr=pr: e.tensor_tensor(out=tv[:, :, :, pr, :], in0=qv[:, :, :, 1 - pr, :],
                                                                         in1=sv[:, :, pr, :].unsqueeze(1).to_broadcast([128, 4, 2, 32]), op=ALU.mult),
                                 reads=[f"qf{i}", "rS"], writes=[f"t2{i}"])
                        P.op("dve", lambda e: e.tensor_tensor(out=q_b[:], in0=t_1[:], in1=t_2[:], op=ALU.add), reads=[f"t1{i}", f"t2{i}"], writes=[f"qb{i}"])
                    else:
                        P.op("act", lambda e: e.copy(out=q_b[:], in_=q_f[:]), reads=[f"qf{i}"], writes=[f"qb{i}"])
                    p_q = pq[i % 2]
                    sg = cnt["blk"] % 2

                    def later():
                        for hh in range(4):
                            P.op("pe", lambda e, hh=hh: e.transpose(out=p_q[:, hh, :], in_=q_b[:, hh * 128:(hh + 1) * 128], identity=identb[:]),
                                 reads=[f"qb{i}", "identb"], writes=[f"pq{i%2}"])
                        P.op("act", lambda e: e.copy(out=stage[sg][:, :, ti * 128:(ti + 1) * 128], in_=p_q[:]), reads=[f"pq{i%2}"], writes=[f"stage{sg}"])
                    pending.append(later)
                elif cb == 5:
                    ti = idx
                    r0 = t0 + ti * 128
                    i = cnt["ev"] % 3
                    cnt["ev"] += 1
                    P.op("act", lambda e: e.copy(out=evb[i][:], in_=ps[:]), reads=[pk], writes=[f"evb{i}"])
                    P.dma("act", V[r0:r0 + 128, :], evb[i][:], reads=[f"evb{i}"], writes=[], sem=f"evb{i}")
                elif cb < 8:
                    cc = idx
                    i = cnt["ev"] % 3
                    cnt["ev"] += 1
                    P.op("act", lambda e: e.copy(out=ev[i][:, 0:nt], in_=ps[:, 0:nt]), reads=[pk], writes=[f"ev{i}"])
                    P.dma("act", PL[(cb - 6) * 4 + cc][:, t0:t0 + nt], ev[i][:, 0:nt], reads=[f"ev{i}"], writes=[], sem=f"ev{i}")
                else:
                    cc = idx
                    i = cnt["ev"] % 3
                    cnt["ev"] += 1
                    P.op("act", lambda e: e.activation(out=evb[i][:, 0:nt], in_=ps[:, 0:nt], func=AF.Sigmoid), reads=[pk], writes=[f"evb{i}"])
                    P.dma("act", GAB[(cb - 8) * 4 + cc][:, t0:t0 + nt], evb[i][:, 0:nt], reads=[f"evb{i}"], writes=[], sem=f"evb{i}")

            def end_block(cb, t0, nt):
                if cb < 5:
                    flush()
                    sg = cnt["blk"] % 2
                    cnt["blk"] += 1
                    dst = QT[cb * 4:(cb + 1) * 4] if cb < 4 else KT[0:4]
                    P.dma("act", dst.rearrange("h p t -> p h t")[:, :, t0:t0 + nt], stage[sg][:, :, 0:nt], reads=[f"stage{sg}"], writes=[], sem=f"stage{sg}")

            proj(w_in[l], list(range(16)), HT, blocks_for, mode_for, evac, end_block=end_block)

    def phase_pool(l):
        with P.phase("pool"):
            classes = [(0, TL)] + ([(TL, TC)] if l == 0 else [])

            def do_class(off, L):
                W = L + 32
                tag = "L" if off == 0 else "C"
                rc = P.sbuf(f"rc{tag}", [128, 4, L], F32)
                for g in range(4):
                    P.dma("sp", rc[:, g, :], rcnt[g, off:off + L].partition_broadcast(128), writes=[f"rc{tag}"], sem=f"rc{tag}")
                u = [P.sbuf(f"u{tag}{i}", [128, W], F32) for i in range(2)]
                sa = P.sbuf(f"sa{tag}", [128, W], F32)
                sb = P.sbuf(f"sb{tag}", [128, W], F32)
                tmp = P.sbuf(f"tmp{tag}", [128, L], F32)
                pd = [P.sbuf(f"pd{tag}{i}", [128, L], BF16) for i in range(2)]
                for i in range(2):
                    P.op("pool", lambda e, i=i: e.memset(u[i][:], 0.0), writes=[f"u{tag}{i}"])
                P.op("pool", lambda e: e.memset(sa[:], 0.0), writes=[f"sa{tag}"])
                P.op("pool", lambda e: e.memset(sb[:], 0.0), writes=[f"sb{tag}"])
                for c in range(8):
                    g = c // 2
                    i = c % 2
                    uu = u[i]
                    uk = f"u{tag}{i}"
                    P.dma("sp", uu[:, 16:16 + L], PL[c][:, off:off + L], writes=[uk], sem=uk)
                    P.op("dve", lambda e, uu=uu: e.tensor_tensor(out=sa[:, 1:W], in0=uu[:, 1:W], in1=uu[:, 0:W - 1], op=ALU.add), reads=[uk], writes=[f"sa{tag}"])
                    cur, curk = sa, f"sa{tag}"
                    if g >= 1:
                        P.op("pool", lambda e: e.tensor_tensor(out=sb[:, 2:W - 1], in0=sa[:, 3:W], in1=sa[:, 1:W - 2], op=ALU.add), reads=[f"sa{tag}"], writes=[f"sb{tag}"])
                        cur, curk = sb, f"sb{tag}"
                    if g >= 2:
                        P.op("dve", lambda e: e.tensor_tensor(out=sa[:, 4:W - 3], in0=sb[:, 6:W - 1], in1=sb[:, 2:W - 5], op=ALU.add), reads=[f"sb{tag}"], writes=[f"sa{tag}"])
                        cur, curk = sa, f"sa{tag}"
                    if g >= 3:
                        P.op("pool", lambda e: e.tensor_tensor(out=sb[:, 8:W - 7], in0=sa[:, 12:W - 3], in1=sa[:, 4:W - 11], op=ALU.add), reads=[f"sa{tag}"], writes=[f"sb{tag}"])
                        cur, curk = sb, f"sb{tag}"
                    P.op("dve", lambda e, cur=cur, g=g: e.tensor_tensor(out=tmp[:], in0=cur[:, 16:16 + L], in1=rc[:, g, :], op=ALU.mult),
                         reads=[curk, f"rc{tag}"], writes=[f"tmp{tag}"])
                    P.op("pool", lambda e, uu=uu, i=i: e.tensor_tensor(out=pd[i][:], in0=tmp[:], in1=uu[:, 16:16 + L], op=ALU.subtract),
                         reads=[f"tmp{tag}", uk], writes=[f"pd{tag}{i}"])
                    P.dma("act", PD[c][:, off:off + L], pd[i][:], reads=[f"pd{tag}{i}"], writes=[], sem=f"pd{tag}{i}")

            for (off, L) in classes:
                do_class(off, L)

    def phase_attn(l):
        with P.phase("attn"):
            if l == 0:
                emit_convert(12, 48)
            ones = P.sbuf("ones", [128, 128], BF16)
            P.op("dve", lambda e: e.memset(ones[:], 1.0), writes=["ones"])
            kT = [P.sbuf(f"kT{i}", [128, T], BF16) for i in range(2)]
            Vg = [P.sbuf(f"Vg{i}", [128, 18, 128], BF16) for i in range(2)]
            qT = [P.sbuf(f"qT{i}", [128, T], BF16) for i in range(2)]
            pt = [P.sbuf(f"pt{i}", [128, 512], BF16) for i in range(6)]
            rden = [P.sbuf(f"rden{i}", [128, 512], F32) for i in range(2)]
            ob = [P.sbuf(f"ob{i}", [128, 512], BF16) for i in range(2)]
            sps = [P.psum(f"sps{i}", [128, 512], F32) for i in range(3)]
            ops_ = [P.psum(f"ops{i}", [128, 512], F32) for i in range(2)]
            dps = [P.psum(f"dps{i}", [128, 512], F32) for i in range(2)]
            scale = 128.0 ** -0.5
            st = {"nq": 0, "npt": 0, "nsp": 0}

            def do_qblock(g, gi, h, qi, c0, nqc, kts, st):
                oi = st["nq"] % 2
                st["nq"] += 1
                o_ps, d_ps = ops_[oi], dps[oi]
                nk = len(kts)

                def S(kt, si):
                    P.op("pe", lambda e: e.matmul(sps[si][:, 0:nqc], lhsT=kT[gi][:, kt * 128:(kt + 1) * 128], rhs=qT[qi][:, c0:c0 + nqc],
                                                  start=True, stop=True),
                         reads=[f"kT{gi}", f"qT{qi}"], writes=[f"sps{si}"])

                def step(ii, kt, si, pi):
                    P.op("act", lambda e: e.activation(out=pt[pi][:, 0:nqc], in_=sps[si][:, 0:nqc], func=AF.Exp, scale=scale),
                         reads=[f"sps{si}"], writes=[f"pt{pi}"])
                    P.op("pe", lambda e: e.matmul(o_ps[:, 0:nqc], lhsT=Vg[gi][:, kt, :], rhs=pt[pi][:, 0:nqc], start=(ii == 0), stop=(ii == nk - 1)),
                         reads=[f"Vg{gi}", f"pt{pi}"], writes=[f"ops{oi}"])
                    P.op("pe", lambda e: e.matmul(d_ps[:, 0:nqc], lhsT=ones[:], rhs=pt[pi][:, 0:nqc], start=(ii == 0), stop=(ii == nk - 1)),
                         reads=["ones", f"pt{pi}"], writes=[f"dps{oi}"])

                base = st["nsp"]
                st["nsp"] += nk
                for pre in range(min(2, nk)):
                    S(kts[pre], (base + pre) % 3)
                for ii, kt in enumerate(kts):
                    si = (base + ii) % 3
                    if ii + 2 < nk:
                        S(kts[ii + 2], (base + ii + 2) % 3)
                    pi = st["npt"] % 6
                    st["npt"] += 1
                    step(ii, kt, si, pi)
                P.op("dve", lambda e: e.reciprocal(out=rden[oi][:, 0:nqc], in_=d_ps[:, 0:nqc]), reads=[f"dps{oi}"], writes=[f"rden{oi}"])
                P.op("dve", lambda e: e.tensor_tensor(out=ob[oi][:, 0:nqc], in0=o_ps[:, 0:nqc], in1=rden[oi][:, 0:nqc], op=ALU.mult),
                     reads=[f"ops{oi}", f"rden{oi}"], writes=[f"ob{oi}"])
                P.dma("sp", AT[h][:, c0:c0 + nqc], ob[oi][:, 0:nqc], reads=[f"ob{oi}"], writes=[], sem=f"ob{oi}")

            ncol = T if l == 0 else TL

            def load_kv(g):
                gi = g % 2
                P.dma("sp", kT[gi][:], KT[g], writes=[f"kT{gi}"], sem=f"kT{gi}")
                P.dma_split("sp", Vg[gi][:], V.rearrange("(kt p) c -> p kt c", p=128)[:, :, g * 128:(g + 1) * 128], 3, writes=[f"Vg{gi}"], sem=f"Vg{gi}")

            def load_q(h):
                qi = h % 2
                P.dma("sp", qT[qi][:, 0:ncol], QT[h][:, 0:ncol], writes=[f"qT{qi}"], sem=f"qT{qi}")

            load_kv(0)
            load_q(0)
            for g in range(4):
                gi = g % 2
                for hh in range(4):
                    h = g * 4 + hh
                    qi = h % 2
                    if h + 1 < 16:
                        load_q(h + 1)
                    if hh == 0 and g + 1 < 4:
                        load_kv(g + 1)
                    qblocks = [(c0, 512, list(range(18))) for c0 in range(0, TL, 512)]
                    if l == 0:
                        qblocks.append((TL, TC, [16, 17]))
                    for (c0, nqc, kts) in qblocks:
                        do_qblock(g, gi, h, qi, c0, nqc, kts, st)

    def phase_mix(l):
        with P.phase("mix"):
            identf, _ = load_consts()
            blocks = BLOCKS_ALL if l == 0 else BLOCKS_LAT
            wpf = P.sbuf("wpf", [128, 8, 512], F32)
            wpb = P.sbuf("wpb", [128, 8, 512], BF16)
            P.dma("sp", wpf[:], w_pool[l].rearrange("g (kc p) d -> p (g kc) d", p=128), writes=["wpf"], sem="const2")
            P.op("dve", lambda e: e.tensor_copy(out=wpb[:], in_=wpf[:]), reads=["wpf"], writes=["wpb"])
            psr = P.sbuf("psr", [16, 128], F32)
            pscT = P.sbuf("pscT", [128, 16], F32)
            P.dma("sp", psr[:], pool_scale[l].rearrange("(j p) -> j p", p=128), writes=["psr"], sem="const3")
            ptp = P.psum("ptp", [128, 16], F32)
            P.op("pe", lambda e: e.transpose(out=ptp[:], in_=psr[:], identity=identf[0:16, 0:16]), reads=["psr", "identf"], writes=["ptp"])
            P.op("dve", lambda e: e.tensor_copy(out=pscT[:], in_=ptp[:]), reads=["ptp"], writes=["pscT"])
            pdb = [P.sbuf(f"pdb{i}", [128, 2, 512], BF16) for i in range(2)]
            gab = [P.sbuf(f"gab{i}", [128, 4, 512], BF16) for i in range(2)]
            gbb = [P.sbuf(f"gbb{i}", [128, 4, 512], BF16) for i in range(2)]
            m1 = [P.sbuf(f"m1{i}", [128, 512], F32) for i in range(2)]
            m2 = [P.sbuf(f"m2{i}", [128, 512], F32) for i in range(2)]
            mg = [P.sbuf(f"mg{i}", [128, 512], BF16) for i in range(3)]
            pb = [P.psum(f"pb{i}", [128, 512], F32) for i in range(2)]
            cnt = {"blk": 0, "ev": 0}
            cur = {}

            def per_block(cb, t0, nt):
                i = cnt["blk"] % 2
                cnt["blk"] += 1
                cur["i"] = i
                P.dma("sp", pdb[i][:, :, 0:nt], PD[2 * cb:2 * cb + 2].rearrange("c p t -> p c t")[:, :, t0:t0 + nt], writes=[f"pdb{i}"], sem=f"pdb{i}")
                P.dma("sp", gab[i][:, :, 0:nt], GAB[4 * cb:4 * cb + 4].rearrange("c p t -> p c t")[:, :, t0:t0 + nt], writes=[f"gab{i}"], sem=f"gab{i}")
                P.dma("sp", gbb[i][:, :, 0:nt], GAB[16 + 4 * cb:16 + 4 * cb + 4].rearrange("c p t -> p c t")[:, :, t0:t0 + nt], writes=[f"gbb{i}"], sem=f"gbb{i}")

            def evac(cb, t0, cc, nt, ps, pk):
                i = cur["i"]
                dc = cb * 4 + cc
                e2 = cnt["ev"] % 2
                e3 = cnt["ev"] % 3
                cnt["ev"] += 1
                p_b = pb[e2]
                for kc in range(2):
                    P.op("pe", lambda e, kc=kc: e.matmul(p_b[:, 0:nt], lhsT=wpb[:, cb * 2 + kc, cc * 128:(cc + 1) * 128], rhs=pdb[i][:, kc, 0:nt],
                                                         start=(kc == 0), stop=(kc == 1)),
                         reads=["wpb", f"pdb{i}"], writes=[f"pb{e2}"])
                P.op("dve", lambda e: e.tensor_tensor(out=m1[e2][:, 0:nt], in0=ps[:, 0:nt], in1=gab[i][:, cc, 0:nt], op=ALU.mult),
                     reads=[pk, f"gab{i}"], writes=[f"m1{e2}"])
                P.op("dve", lambda e: e.scalar_tensor_tensor(out=m2[e2][:, 0:nt], in0=p_b[:, 0:nt], scalar=pscT[:, dc:dc + 1], in1=gbb[i][:, cc, 0:nt],
                                                             op0=ALU.mult, op1=ALU.mult),
                     reads=[f"pb{e2}", "pscT", f"gbb{i}"], writes=[f"m2{e2}"])
                P.op("pool", lambda e: e.tensor_tensor(out=mg[e3][:, 0:nt], in0=m1[e2][:, 0:nt], in1=m2[e2][:, 0:nt], op=ALU.add),
                     reads=[f"m1{e2}", f"m2{e2}"], writes=[f"mg{e3}"])
                P.dma("act", MG[dc][:, t0:t0 + nt], mg[e3][:, 0:nt], reads=[f"mg{e3}"], writes=[], sem=f"mg{e3}")

            proj(w_br[l], list(range(4)), AT.rearrange("h p t -> p h t"), lambda cb: blocks, lambda cb: "feat", evac, per_block=per_block, npp=2)

    def phase_wout(l):
        with P.phase("wout"):
            blocks = BLOCKS_ALL if l == 0 else BLOCKS_LAT
            G = [P.sbuf(f"G{s}", [128, D], F32) for s in range(2)]
            for s in ([0, 1] if l == 0 else [0]):
                load_bc(G[s], f"G{s}", l, s, G_A, f"bcG{s}")
            xb = [P.sbuf(f"xo{i}", [128, 4, 512], F32) for i in range(2)]
            tt = [P.sbuf(f"to{i}", [128, 512], F32) for i in range(3)]
            cnt = {"ev": 0, "blk": 0}
            cur = {}

            def per_block(cb, t0, nt):
                b = cnt["blk"] % 2
                cnt["blk"] += 1
                cur["b"] = b
                P.dma("sp", xb[b][:, 0:nt // 128, :], xsrc(l, t0, nt, cb * 512, 512).rearrange("(t p) c -> p t c", p=128), writes=[f"xo{b}"], sem=f"xo{b}")

            def evac(cb, t0, ti, nt, ps, pk):
                r0 = t0 + ti * 128
                s = 0 if r0 < TL else 1
                i = cnt["ev"] % 3
                cnt["ev"] += 1
                b = cur["b"]
                P.op("dve", lambda e: e.tensor_tensor(out=tt[i][:], in0=ps[:], in1=G[s][:, cb * 512:(cb + 1) * 512], op=ALU.mult),
                     reads=[pk, f"G{s}"], writes=[f"to{i}"])
                P.op("pool", lambda e: e.tensor_tensor(out=tt[i][:], in0=tt[i][:], in1=xb[b][:, ti, :], op=ALU.add), reads=[f"to{i}", f"xo{b}"], writes=[f"to{i}"])
                P.dma("act", X[r0:r0 + 128, cb * 512:(cb + 1) * 512], tt[i][:], reads=[f"to{i}"], writes=[], sem=f"to{i}")

            proj(w_out[l], list(range(4)), MG.rearrange("h p t -> p h t"), lambda cb: blocks, lambda cb: "tok", evac, per_block=per_block)

    def phase_qpeer(l):
        with P.phase("qpeer"):
            blocks = BLOCKS_ALL if l == 0 else BLOCKS_LAT
            ev = [P.sbuf(f"ev{i}", [128, 512], F32) for i in range(3)]
            cnt = {"ev": 0}

            def evac(cb, t0, cc, nt, ps, pk):
                i = cnt["ev"] % 3
                cnt["ev"] += 1
                P.op("act", lambda e: e.copy(out=ev[i][:, 0:nt], in_=ps[:, 0:nt]), reads=[pk], writes=[f"ev{i}"])
                P.dma("act", QPT[cb * 4 + cc][:, t0:t0 + nt], ev[i][:, 0:nt], reads=[f"ev{i}"], writes=[], sem=f"ev{i}")

            proj(w_qp[l], list(range(4)), HT, lambda cb: blocks, lambda cb: "feat", evac)

    def phase_topk(l):
        with P.phase("topk"):
            if l == 0:
                emit_convert(48, 64)
            identf, _ = load_consts()
            ntiles = (T if l == 0 else TL) // 128
            io16 = P.sbuf("io16", [128, 16], F32)
            P.dma("sp", io16[:], iota16_d, writes=["io16"], sem="const2")
            kraw = P.sbuf("kraw", [128, 16, 128], F32)
            keysT = P.sbuf("keysT", [128, 16, 128], F32)
            P.dma_split("sp", kraw[:], peer_keys[l].rearrange("h p k d -> k (h p) d"), 2, writes=["kraw"], sem="const3")
            pk4 = [P.psum(f"pk4{i}", [128, 4, 128], F32) for i in range(4)]
            for grp in range(4):
                for q in range(4):
                    hp = grp * 4 + q
                    P.op("pe", lambda e, hp=hp, q=q, grp=grp: e.transpose(out=pk4[grp][:, q, :], in_=kraw[:, hp, :], identity=identf[:]),
                         reads=["kraw", "identf"], writes=[f"pk4{grp}"])
                P.op("act", lambda e, grp=grp: e.copy(out=keysT[:, grp * 4:(grp + 1) * 4, :], in_=pk4[grp][:]), reads=[f"pk4{grp}"], writes=["keysT"])
            qt = [P.sbuf(f"qt{i}", [128, 16, 128], F32) for i in range(2)]
            Sbuf = [P.sbuf(f"S{k}", [128, 16, 128], F32) for k in range(2)]
            S2 = P.sbuf("S2", [128, 16, 128], F32)
            m = P.sbuf("m", [128, 16, 16], F32)
            ix = P.sbuf("ix", [128, 16, 16], U32)
            ixf = P.sbuf("ixf", [128, 16, 16], F32)
            i1s = P.sbuf("i1s", [128, 8, 16], F32)
            cand = P.sbuf("cand", [128, 8, 256], F32)
            cand2 = P.sbuf("cand2", [128, 8, 256], F32)
            ts = P.sbuf("ts", [128, 8, 16], F32)
            pos = P.sbuf("pos", [128, 8, 16], U32)
            au = P.sbuf("au", [128, 8, 16], U32)
            bu = P.sbuf("bu", [128, 8, 16], U32)
            af_ = P.sbuf("af", [128, 8, 16], F32)
            bf_ = P.sbuf("bf", [128, 8, 16], F32)
            oh = P.sbuf("oh", [128, 8, 16, 16], F32)
            isel = P.sbuf("isel", [128, 8, 16], F32)
            jsel = P.sbuf("jsel", [128, 8, 16], F32)
            ef = P.sbuf("ef", [128, 128], F32)
            eu = [P.sbuf(f"eu{i}", [128, 128], U32) for i in range(2)]
            dd = P.sbuf("dd", [128, 8, 16], F32)
            ee = P.sbuf("ee", [128, 8, 16], F32)
            zz = P.sbuf("zz", [128, 16], F32)
            gg = [P.sbuf(f"gg{i}", [128, 128], F32) for i in range(2)]
            NEG = -1e30
            dense = peer_mode == "dense"
            if dense:
                io128 = P.sbuf("io128", [128, 128], F32)
                P.dma("sp", io128[:], iota128_d, writes=["io128"], sem="const4")
                tp = P.psum("tp", [128, 3, 128], F32)
                ijg = P.sbuf("ijg", [128, 3, 128], F32)
                Aoh = [P.sbuf(f"Aoh{k}", [128, 16, 128], BF16) for k in range(2)]
                Boh = [P.sbuf(f"Boh{k}", [128, 128], BF16) for k in range(8)]
                gp = [P.psum(f"gp{k}", [128, 4, 128], F32) for k in range(2)]
                stg = [P.sbuf(f"stg{k}", [128, 128, 128], BF16) for k in range(2)]
                gst = {"b": 0, "g": 0}

            def gbuild(tt, i):
                r0 = tt * 128
                sg = tt % 2
                srcs = [isel[:].rearrange("p h k -> p (h k)"), jsel[:].rearrange("p h k -> p (h k)"), gg[i][:]]
                keys = ["sel0", "sel1", f"gg{i}"]
                for q in range(3):
                    P.op("pe", lambda e, q=q: e.transpose(out=tp[:, q, :], in_=srcs[q], identity=identf[:]), reads=[keys[q], "identf"], writes=["tp"])
                P.op("act", lambda e: e.copy(out=ijg[:], in_=tp[:]), reads=["tp"], writes=["ijg"])
                for grp in range(8):
                    a = grp % 2
                    for tq in range(16):
                        tl = grp * 16 + tq
                        P.op("pool", lambda e, a=a, tq=tq, tl=tl: e.tensor_scalar(out=Aoh[a][:, tq, :], in0=io128[:], scalar1=ijg[:, 0, tl:tl + 1], scalar2=None, op0=ALU.is_equal),
                             reads=["io128", "ijg"], writes=[f"Aoh{a}"])
                    for q4 in range(4):
                        gk = gst["g"] % 2
                        gst["g"] += 1
                        for q in range(4):
                            tl = grp * 16 + q4 * 4 + q
                            b = gst["b"] % 8
                            gst["b"] += 1
                            P.op("dve", lambda e, b=b, tl=tl: e.tensor_scalar(out=Boh[b][:], in0=io128[:], scalar1=ijg[:, 1, tl:tl + 1], scalar2=ijg[:, 2, tl:tl + 1],
                                                                              op0=ALU.is_equal, op1=ALU.mult),
                                 reads=["io128", "ijg"], writes=[f"Boh{b}"])
                            P.op("pe", lambda e, b=b, a=a, gk=gk, q=q, q4=q4: e.matmul(gp[gk][:, q, :], lhsT=Boh[b][:], rhs=Aoh[a][:, q4 * 4 + q, :], start=True, stop=True),
                                 reads=[f"Boh{b}", f"Aoh{a}"], writes=[f"gp{gk}"])
                        tl0 = grp * 16 + q4 * 4
                        P.op("act", lambda e, gk=gk, tl0=tl0, sg=sg: e.copy(out=stg[sg][:, :, tl0:tl0 + 4].rearrange("p i t -> p t i"), in_=gp[gk][:]),
                             reads=[f"gp{gk}"], writes=[f"stg{sg}"])
                dst = GTd.rearrange("i j t -> j i t")
                for k in range(16):
                    P.dma("sp", dst[:, k * 8:(k + 1) * 8, r0:r0 + 128], stg[sg][:, k * 8:(k + 1) * 8, :], reads=[f"stg{sg}"], writes=[], sem=f"stg{sg}")

            for tt in range(ntiles):
                r0 = tt * 128
                i = tt % 2
                P.dma_split("sp", qt[i][:], QPT.rearrange("c p t -> p c t")[:, :, r0:r0 + 128], 2, writes=[f"qt{i}"], sem=f"qt{i}")
                for grp in range(4):
                    for q in range(4):
                        hp = grp * 4 + q
                        P.op("pe", lambda e, hp=hp, q=q, grp=grp, i=i: e.matmul(pk4[grp][:, q, :], lhsT=qt[i][:, hp, :], rhs=keysT[:, hp, :], start=True, stop=True),
                             reads=[f"qt{i}", "keysT"], writes=[f"pk4{grp}"])
                    P.op("act", lambda e, grp=grp, Sx=Sbuf[i]: e.copy(out=Sx[:, grp * 4:(grp + 1) * 4, :], in_=pk4[grp][:]), reads=[f"pk4{grp}"], writes=[f"S{i}_{grp}"])
                for hp in range(16):
                    sk = f"S{i}_{hp // 4}"
                    P.op("dve", lambda e, hp=hp, Sx=Sbuf[i]: e.max(out=m[:, hp, 0:8], in_=Sx[:, hp, :]), reads=[sk], writes=[f"ma{hp}"])
                for hp in range(16):
                    sk = f"S{i}_{hp // 4}"
                    P.op("dve", lambda e, hp=hp, Sx=Sbuf[i]: e.max_index(out=ix[:, hp, 0:8], in_max=m[:, hp, 0:8], in_values=Sx[:, hp, :]), reads=[sk, f"ma{hp}"], writes=[f"ixa{hp}"])
                for hp in range(16):
                    sk = f"S{i}_{hp // 4}"
                    P.op("dve", lambda e, hp=hp, Sx=Sbuf[i]: e.match_replace(out=S2[:, hp, :], in_to_replace=m[:, hp, 0:8], in_values=Sx[:, hp, :], imm_value=NEG),
                         reads=[sk, f"ma{hp}"], writes=[f"S2_{hp}"])
                for hp in range(16):
                    P.op("dve", lambda e, hp=hp: e.max(out=m[:, hp, 8:16], in_=S2[:, hp, :]), reads=[f"S2_{hp}"], writes=[f"mb{hp}"])
                for hp in range(16):
                    P.op("dve", lambda e, hp=hp: e.max_index(out=ix[:, hp, 8:16], in_max=m[:, hp, 8:16], in_values=S2[:, hp, :]), reads=[f"S2_{hp}", f"mb{hp}"], writes=[f"ixb{hp}"])
                mkeys = [f"ma{hp}" for hp in range(16)] + [f"mb{hp}" for hp in range(16)]
                ixkeys = [f"ixa{hp}" for hp in range(16)] + [f"ixb{hp}" for hp in range(16)]
                P.op("dve", lambda e: e.tensor_copy(out=ixf[:], in_=ix[:]), reads=ixkeys, writes=["ixf"])
                mv = m[:].rearrange("p (h two) k -> p h two k", two=2)
                iv = ixf[:].rearrange("p (h two) k -> p h two k", two=2)
                cv = cand[:].rearrange("p h (a b) -> p h a b", a=16)
                P.op("dve", lambda e: e.tensor_tensor(out=cv, in0=mv[:, :, 0, :].unsqueeze(3).to_broadcast([128, 8, 16, 16]),
                                                      in1=mv[:, :, 1, :].unsqueeze(2).to_broadcast([128, 8, 16, 16]), op=ALU.add),
                     reads=mkeys, writes=["cand"])
                for h in range(8):
                    P.op("dve", lambda e, h=h: e.max(out=ts[:, h, 0:8], in_=cand[:, h, :]), reads=["cand"], writes=[f"tsa{h}"])
                for h in range(8):
                    P.op("dve", lambda e, h=h: e.max_index(out=pos[:, h, 0:8], in_max=ts[:, h, 0:8], in_values=cand[:, h, :]), reads=["cand", f"tsa{h}"], writes=[f"posa{h}"])
                for h in range(8):
                    P.op("dve", lambda e, h=h: e.match_replace(out=cand2[:, h, :], in_to_replace=ts[:, h, 0:8], in_values=cand[:, h, :], imm_value=NEG),
                         reads=["cand", f"tsa{h}"], writes=[f"c2_{h}"])
                for h in range(8):
                    P.op("dve", lambda e, h=h: e.max(out=ts[:, h, 8:16], in_=cand2[:, h, :]), reads=[f"c2_{h}"], writes=[f"tsb{h}"])
                for h in range(8):
                    P.op("dve", lambda e, h=h: e.max_index(out=pos[:, h, 8:16], in_max=ts[:, h, 8:16], in_values=cand2[:, h, :]), reads=[f"c2_{h}", f"tsb{h}"], writes=[f"posb{h}"])
                tskeys = [f"tsa{h}" for h in range(8)] + [f"tsb{h}" for h in range(8)]
                poskeys = [f"posa{h}" for h in range(8)] + [f"posb{h}" for h in range(8)]
                P.op("dve", lambda e: e.tensor_single_scalar(out=au[:], in_=pos[:], scalar=4, op=ALU.logical_shift_right), reads=poskeys, writes=["au"])
                P.op("dve", lambda e: e.tensor_single_scalar(out=bu[:], in_=pos[:], scalar=15, op=ALU.bitwise_and), reads=poskeys, writes=["bu"])
                P.op("dve", lambda e: e.tensor_copy(out=af_[:], in_=au[:]), reads=["au"], writes=["af"])
                P.op("dve", lambda e: e.tensor_copy(out=bf_[:], in_=bu[:]), reads=["bu"], writes=["bf"])
                for (sel, xf, which, key) in ((isel, af_, 0, "af"), (jsel, bf_, 1, "bf")):
                    P.op("dve", lambda e, xf=xf: e.tensor_tensor(out=oh[:], in0=io16[:].unsqueeze(1).unsqueeze(1).to_broadcast([128, 8, 16, 16]),
                                                                  in1=xf[:].unsqueeze(3).to_broadcast([128, 8, 16, 16]), op=ALU.is_equal),
                         reads=["io16", key], writes=["oh"])
                    P.op("dve", lambda e, which=which: e.tensor_tensor(out=oh[:], in0=oh[:], in1=iv[:, :, which, :].unsqueeze(2).to_broadcast([128, 8, 16, 16]), op=ALU.mult),
                         reads=["oh", "ixf"], writes=["oh"])
                    P.op("dve", lambda e, sel=sel: e.tensor_reduce(out=sel[:], in_=oh[:], axis=AX.X, op=ALU.add), reads=["oh"], writes=["sel%d" % which])
                P.op("dve", lambda e: e.scalar_tensor_tensor(out=ef[:], in0=isel[:].rearrange("p h k -> p (h k)"), scalar=128.0, in1=jsel[:].rearrange("p h k -> p (h k)"),
                                                             op0=ALU.mult, op1=ALU.add),
                     reads=["sel0", "sel1"], writes=["ef"])
                if l > 0:
                    P.op("dve", lambda e: e.tensor_scalar(out=ef[:], in0=ef[:], scalar1=float(l * NEXP), scalar2=None, op0=ALU.add), reads=["ef"], writes=["ef"])
                P.op("dve", lambda e, i=i: e.tensor_copy(out=eu[i][:], in_=ef[:]), reads=["ef"], writes=[f"eu{i}"])
                P.dma("sp", EIDX[r0:r0 + 128, :], eu[i][:], reads=[f"eu{i}"], writes=[], sem=f"eu{i}")
                P.op("dve", lambda e: e.tensor_tensor(out=dd[:], in0=ts[:], in1=ts[:, :, 0:1].to_broadcast([128, 8, 16]), op=ALU.subtract), reads=tskeys, writes=["dd"])
                P.op("act", lambda e: e.activation(out=ee[:], in_=dd[:], func=AF.Exp), reads=["dd"], writes=["ee"])
                P.op("dve", lambda e: e.tensor_reduce(out=zz[:, 0:8], in_=ee[:], axis=AX.X, op=ALU.add), reads=["ee"], writes=["zz"])
                P.op("dve", lambda e: e.reciprocal(out=zz[:, 8:16], in_=zz[:, 0:8]), reads=["zz"], writes=["zz"])
                P.op("dve", lambda e, i=i: e.tensor_tensor(out=gg[i][:].rearrange("p (h k) -> p h k", h=8), in0=ee[:], in1=zz[:, 8:16].unsqueeze(2).to_broadcast([128, 8, 16]), op=ALU.mult),
                     reads=["ee", "zz"], writes=[f"gg{i}"])
                P.dma("sp", GATE[r0:r0 + 128, :], gg[i][:], reads=[f"gg{i}"], writes=[], sem=f"gg{i}")
                if dense:
                    gbuild(tt, i)

    def phase_gather(l):
        with P.phase("gather"):
            P.wait_persistent()
            identf, identb = load_consts(need_bf=True)
            ntiles = (T if l == 0 else TL) // 128
            G = [P.sbuf(f"G{s}", [128, D], F32) for s in range(2)]
            for s in ([0, 1] if l == 0 else [0]):
                load_bc(G[s], f"G{s}", l, s, G_F, f"bcG{s}")
            NS = 10
            LOOK = 7
            uv = [P.sbuf(f"uv{i}", [128, 2 * D], BF16) for i in range(NS)]
            h2 = [P.sbuf(f"h2{i}", [128, D], F32) for i in range(2)]
            xt = [P.sbuf(f"xt{i}", [128, D], F32) for i in range(2)]
            junk = P.sbuf("junk", [128, D], F32)
            eix = [P.sbuf(f"eix{i}", [128, 128], U32) for i in range(2)]
            gat = [P.sbuf(f"gat{i}", [128, 128], F32) for i in range(2)]
            act = [P.sbuf(f"act{i}", [128, 128], F32) for i in range(2)]
            ge = [P.sbuf(f"ge{i}", [128, 128], F32) for i in range(2)]
            dg = [P.sbuf(f"dg{i}", [128, 128], BF16) for i in range(4)]
            tmp = [P.sbuf(f"tmp{i}", [128, 512], F32) for i in range(2)]
            acc = [[P.psum(f"acc{a}_{b}", [128, 512], F32) for b in range(4)] for a in range(2)]
            st = {"ntmp": 0}
            items = [(tt, sidx) for tt in range(ntiles) for sidx in range(128)]

            def loads(tt):
                r0 = tt * 128
                i = tt % 2
                P.dma("sp", h2[i][:], H2[r0:r0 + 128, :], writes=[f"h2{i}"], sem=f"h2{i}")
                P.dma("sp", xt[i][:], X[r0:r0 + 128, :], writes=[f"xt{i}"], sem=f"xt{i}")
                P.dma("sp", eix[i][:], EIDX[r0:r0 + 128, :], writes=[f"eix{i}"], sem=f"eix{i}")
                P.dma("sp", gat[i][:], GATE[r0:r0 + 128, :], writes=[f"gat{i}"], sem=f"gat{i}")

            def gather(n):
                tt, sidx = items[n]
                i = tt % 2
                u = n % NS
                if sidx == 0:
                    loads(tt)
                P.dma_fn("pool", lambda e: e.indirect_dma_start(
                    out=uv[u][:], out_offset=None, in_=UV, in_offset=bass.IndirectOffsetOnAxis(ap=eix[i][:, sidx:sidx + 1], axis=0)),
                    reads=[f"eix{i}"], writes=[f"uv{u}"], sem=f"uv{u}")

            def dot(n):
                tt, sidx = items[n]
                i = tt % 2
                u = n % NS
                P.op("dve", lambda e: e.scalar_tensor_tensor(out=junk[:], in0=h2[i][:], scalar=1.0, in1=uv[u][:, 0:D], op0=ALU.mult, op1=ALU.mult,
                                                             accum_out=act[i][:, sidx:sidx + 1]),
                     reads=[f"h2{i}", f"uv{u}"], writes=[f"a{i}_{sidx}"])
                P.op("act", lambda e: e.activation(out=ge[i][:, sidx:sidx + 1], in_=act[i][:, sidx:sidx + 1], func=AF.Gelu),
                     reads=[f"a{i}_{sidx}"], writes=[f"g{i}_{sidx}"])
                P.op("act", lambda e: e.activation(out=ge[i][:, sidx:sidx + 1], in_=ge[i][:, sidx:sidx + 1], func=AF.Copy, scale=gat[i][:, sidx:sidx + 1]),
                     reads=[f"g{i}_{sidx}", f"gat{i}"], writes=[f"g{i}_{sidx}"])

            def combine(n):
                tt, sidx = items[n]
                i = tt % 2
                u = n % NS
                d = n % 4
                P.op("act", lambda e: e.activation(out=dg[d][:], in_=identb[:], func=AF.Copy, scale=ge[i][:, sidx:sidx + 1]),
                     reads=["identb", f"g{i}_{sidx}"], writes=[f"dg{d}"])
                for db in range(4):
                    P.op("pe", lambda e, db=db: e.matmul(acc[i][db][:], lhsT=dg[d][:], rhs=uv[u][:, D + db * 512:D + (db + 1) * 512],
                                                         start=(sidx == 0), stop=(sidx == 127)),
                         reads=[f"dg{d}", f"uv{u}"], writes=[f"acc{i}_{db}"])
                if sidx == 127:
                    finalize(tt)

            def finalize(tt):
                r0 = tt * 128
                i = tt % 2
                s = 0 if r0 < TL else 1
                for db in range(4):
                    tq = st["ntmp"] % 2
                    st["ntmp"] += 1
                    P.op("dve", lambda e, tq=tq, db=db: e.tensor_tensor(out=tmp[tq][:], in0=acc[i][db][:], in1=G[s][:, db * 512:(db + 1) * 512], op=ALU.mult),
                         reads=[f"acc{i}_{db}", f"G{s}"], writes=[f"tmp{tq}"])
                    P.op("dve", lambda e, tq=tq, db=db: e.tensor_tensor(out=xt[i][:, db * 512:(db + 1) * 512], in0=tmp[tq][:], in1=xt[i][:, db * 512:(db + 1) * 512], op=ALU.add),
                         reads=[f"tmp{tq}", f"xt{i}"], writes=[f"xt{i}"])
                P.dma("sp", X[r0:r0 + 128, :], xt[i][:], reads=[f"xt{i}"], writes=[], sem=f"xst{i}")

            N = len(items)
            for n in range(min(LOOK, N)):
                gather(n)
            for n in range(N):
                if n + LOOK < N:
                    gather(n + LOOK)
                dot(n)
                if n >= 1:
                    combine(n - 1)
            combine(N - 1)

    def phase_dense(l):
        with P.phase("dense"):
            _, identb = load_consts(need_bf=True)
            ntok = T if l == 0 else TL
            groups = [(0, 768), (768, 768), (1536, ntok - 1536)]
            G = [P.sbuf(f"G{s}", [128, D], F32) for s in range(2)]
            for s in ([0, 1] if l == 0 else [0]):
                load_bc(G[s], f"G{s}", l, s, G_F, f"bcG{s}")
            hT = P.sbuf("hT", [128, 16, 768], BF16)
            acc = P.sbuf("acc", [128, 6, D], F32)
            GA = P.sbuf("GA", [128, 8, 768], BF16)
            Vb = P.sbuf("Vb", [128, 8, D], BF16)
            Ub = [P.sbuf(f"Ub{k}", [128, D], BF16) for k in range(3)]
            UT = [P.sbuf(f"UT{k}", [128, 16, 128], BF16) for k in range(2)]
            gt = [P.sbuf(f"gt{k}", [128, 768], BF16) for k in range(3)]
            gl = [P.sbuf(f"gl{k}", [128, 512], BF16) for k in range(2)]
            xt = [P.sbuf(f"xt{k}", [128, D], F32) for k in range(2)]
            ptu = [P.psum(f"ptu{k}", [128, 16, 128], BF16) for k in range(2)]
            ps1 = [P.psum(f"ps1{k}", [128, 512], F32) for k in range(2)]
            ps2 = [P.psum(f"ps2{k}", [128, 512], F32) for k in range(2)]
            st = {"c": 0, "p1": 0, "p2": 0, "x": 0}

            def chunk(c, ci, g0, gn, nblocks):
                k3 = st["c"] % 3
                k2 = st["c"] % 2
                st["c"] += 1
                row0 = l * NEXP + c * 128
                P.dma("sp", Ub[k3][:], UV[row0:row0 + 128, 0:D], writes=[f"Ub{k3}"], sem=f"Ub{k3}")
                P.dma("sp", Vb[:, ci, :], UV[row0:row0 + 128, D:2 * D], writes=[f"Vb{ci}"], sem=f"Vb{ci}")
                P.dma("sp", gt[k3][:, 0:gn], GTd[c][:, g0:g0 + gn], writes=[f"gt{k3}"], sem=f"gt{k3}")
                for j in range(16):
                    P.op("pe", lambda e, j=j: e.transpose(out=ptu[k2][:, j, :], in_=Ub[k3][:, j * 128:(j + 1) * 128], identity=identb[:]),
                         reads=[f"Ub{k3}", "identb"], writes=[f"ptu{k2}"])
                P.op("pool" if False else "dve", lambda e: e.tensor_copy(out=UT[k2][:], in_=ptu[k2][:]), reads=[f"ptu{k2}"], writes=[f"UT{k2}"])
                for (b0, bn) in nblocks:
                    p1 = st["p1"] % 2
                    st["p1"] += 1
                    for j in range(16):
                        P.op("pe", lambda e, j=j, p1=p1: e.matmul(ps1[p1][:, 0:bn], lhsT=UT[k2][:, j, :], rhs=hT[:, j, b0:b0 + bn], start=(j == 0), stop=(j == 15)),
                             reads=[f"UT{k2}", "hT"], writes=[f"ps1{p1}"])
                    P.op("act", lambda e, p1=p1: e.activation(out=gl[p1][:, 0:bn], in_=ps1[p1][:, 0:bn], func=AF.Gelu), reads=[f"ps1{p1}"], writes=[f"gl{p1}"])
                    P.op("pool", lambda e, p1=p1: e.tensor_tensor(out=GA[:, ci, b0:b0 + bn], in0=gl[p1][:, 0:bn], in1=gt[k3][:, b0:b0 + bn], op=ALU.mult),
                         reads=[f"gl{p1}", f"gt{k3}"], writes=[f"GA{ci}"])

            def combine(cg, ntile):
                for ti in range(ntile):
                    for db in range(4):
                        p2 = st["p2"] % 2
                        st["p2"] += 1
                        for ci in range(8):
                            P.op("pe", lambda e, ci=ci, p2=p2: e.matmul(ps2[p2][:], lhsT=GA[:, ci, ti * 128:(ti + 1) * 128], rhs=Vb[:, ci, db * 512:(db + 1) * 512],
                                                                        start=(ci == 0), stop=(ci == 7)),
                                 reads=[f"GA{ci}", f"Vb{ci}"], writes=[f"ps2{p2}"])
                        if cg == 0:
                            P.op("dve", lambda e, p2=p2: e.tensor_copy(out=acc[:, ti, db * 512:(db + 1) * 512], in_=ps2[p2][:]), reads=[f"ps2{p2}"], writes=[f"acc{ti}_{db}"])
                        else:
                            P.op("dve", lambda e, p2=p2: e.tensor_tensor(out=acc[:, ti, db * 512:(db + 1) * 512], in0=ps2[p2][:], in1=acc[:, ti, db * 512:(db + 1) * 512], op=ALU.add),
                                 reads=[f"ps2{p2}", f"acc{ti}_{db}"], writes=[f"acc{ti}_{db}"])

            def finalize(g0, ti):
                r0 = g0 + ti * 128
                s = 0 if r0 < TL else 1
                k = st["x"] % 2
                st["x"] += 1
                P.dma("sp", xt[k][:], X[r0:r0 + 128, :], writes=[f"xt{k}"], sem=f"xt{k}")
                P.op("pool", lambda e: e.tensor_tensor(out=acc[:, ti, :], in0=acc[:, ti, :], in1=G[s][:], op=ALU.mult),
                     reads=[f"acc{ti}_{db}" for db in range(4)] + [f"G{s}"], writes=[f"acc{ti}_{db}" for db in range(4)])
                P.op("pool", lambda e: e.tensor_tensor(out=xt[k][:], in0=xt[k][:], in1=acc[:, ti, :], op=ALU.add),
                     reads=[f"acc{ti}_{db}" for db in range(4)] + [f"xt{k}"], writes=[f"xt{k}"])
                P.dma("sp", X[r0:r0 + 128, :], xt[k][:], reads=[f"xt{k}"], writes=[], sem=f"xst{k}")

            for (g0, gn) in groups:
                ntile = gn // 128
                nblocks = [(0, 384), (384, 384)] if gn == 768 else [(0, gn)]
                P.dma_split("sp", hT[:, :, 0:gn], HT[:, :, g0:g0 + gn], 2, writes=["hT"], sem="hT")
                for cg in range(16):
                    for ci in range(8):
                        chunk(cg * 8 + ci, ci, g0, gn, nblocks)
                    combine(cg, ntile)
                for ti in range(ntile):
                    finalize(g0, ti)

    def phase_final():
        with P.phase("final"):
            fg = P.sbuf("fg", [128, D], F32)
            P.dma("sp", fg[:], final_gain.partition_broadcast(128), writes=["fg"], sem="const")
            xt = [P.sbuf(f"xt{i}", [128, D], F32) for i in range(3)]
            junk = P.sbuf("junk", [128, D], F32)
            st = [P.sbuf(f"st{i}", [128, 4], F32) for i in range(3)]
            for tt in range(TL // 128):
                r0 = tt * 128
                i = tt % 3
                x_t, s_t = xt[i], st[i]
                P.dma("sp", x_t[:], X[r0:r0 + 128, :], writes=[f"xt{i}"], sem=f"xt{i}")
                P.op("act", lambda e, x_t=x_t, s_t=s_t: e.activation(out=junk[:], in_=x_t[:], func=AF.Square, accum_out=s_t[:, 0:1]),
                     reads=[f"xt{i}"], writes=["junk", f"st{i}"])
                P.op("dve", lambda e, s_t=s_t: e.tensor_scalar(out=s_t[:, 1:2], in0=s_t[:, 0:1], scalar1=1.0 / D, scalar2=EPS, op0=ALU.mult, op1=ALU.add),
                     reads=[f"st{i}"], writes=[f"st{i}"])
                P.op("act", lambda e, s_t=s_t: e.sqrt(out=s_t[:, 2:3], in_=s_t[:, 1:2]), reads=[f"st{i}"], writes=[f"st{i}"])
                P.op("dve", lambda e, s_t=s_t: e.reciprocal(out=s_t[:, 3:4], in_=s_t[:, 2:3]), reads=[f"st{i}"], writes=[f"st{i}"])
                P.op("dve", lambda e, x_t=x_t, s_t=s_t: e.scalar_tensor_tensor(out=x_t[:], in0=x_t[:], scalar=s_t[:, 3:4], in1=fg[:], op0=ALU.mult, op1=ALU.mult),
                     reads=[f"xt{i}", f"st{i}", "fg"], writes=[f"xt{i}"])
                P.dma("pool", out_d[r0:r0 + 128, :], x_t[:], reads=[f"xt{i}"], writes=[], sem=f"ost{i}")

    def stop(l, name):
        return stop_after is not None and stop_after == (l, name)

    done = False
    for l in range(nlayers):
        blocks = BLOCKS_ALL
        steps = [
            ("mod", lambda: phase_mod(l)),
            ("modA", lambda: phase_modulate(l, SH_A, SC_A, False, BLOCKS_ALL, False)),
            ("inproj", lambda: phase_inproj(l)),
            ("pool", lambda: phase_pool(l)),
            ("attn", lambda: phase_attn(l)),
            ("mix", lambda: phase_mix(l)),
            ("wout", lambda: phase_wout(l)),
            ("modF", lambda: phase_modulate(l, SH_F, SC_F, True, BLOCKS_ALL if l == 0 else BLOCKS_LAT, True)),
            ("qpeer", lambda: phase_qpeer(l)),
            ("topk", lambda: phase_topk(l)),
            ("gather", (lambda: phase_dense(l)) if peer_mode == "dense" else (lambda: phase_gather(l))),
        ]
        for name, fn in steps:
            fn()
            if stop(l, name):
                done = True
                break
        if done:
            break
    if not done:
        phase_final()
    P.close()
    return nc, P


def _consts():
    t = np.arange(TL)
    row = (t // 64).astype(np.float32)
    col = (t % 64).astype(np.float32)
    inv = (np.float32(10000.0) ** (-np.arange(32, dtype=np.float32) / np.float32(32))).astype(np.float32)
    ar = (row[:, None] * inv[None, :]).astype(np.float32)
    ac = (col[:, None] * inv[None, :]).astype(np.float32)
    cr, sr, cc, sc = np.cos(ar), np.sin(ar), np.cos(ac), np.sin(ac)
    ropeC = np.concatenate([cr, cr, cc, cc], axis=1).astype(np.float32)
    ropeS = np.concatenate([-sr, sr, -sc, sc], axis=1).astype(np.float32)
    rc = np.zeros((4, T), np.float32)
    for g, w in enumerate((2, 4, 8, 16)):
        for (off, L) in ((0, TL), (TL, TC)):
            tt = np.arange(L)
            lo = np.clip(tt - w // 2, 0, L)
            hi = np.clip(tt + (w - w // 2), 0, L)
            rc[g, off:off + L] = 1.0 / (hi - lo).astype(np.float32)
    identf = np.eye(128, dtype=np.float32)
    iota16 = np.tile(np.arange(16, dtype=np.float32)[None, :], (128, 1))
    iota128 = np.tile(np.arange(128, dtype=np.float32)[None, :], (128, 1))
    return dict(ropeC=ropeC, ropeS=ropeS, rcnt=rc, identf=identf, iota16=iota16, iota128=iota128)


def make_in_map(inputs, b):
    f = lambda a: np.ascontiguousarray(np.asarray(a, dtype=np.float32))
    m = dict(
        x=f(inputs["x"][b]), ctx=f(inputs["ctx"][b]),
        cvec=f(np.stack([np.asarray(inputs["c"][b]), np.asarray(inputs["c_ctx"])], axis=0)),
    )
    for k in ["w_ada", "b_ada", "w_in", "q_gain", "k_gain", "w_br_attn", "w_pool", "pool_scale", "w_out", "w_q_peer",
              "peer_keys", "peer_u", "peer_v", "final_gain"]:
        m[k] = f(inputs[k])
    m.update(_consts())
    return m


def kernel(**inputs):
    nc, _ = build_program()
    shared = None
    in_maps = []
    for b in range(8):
        m = make_in_map(inputs, b) if shared is None else dict(shared)
        if shared is None:
            shared = m
        else:
            m["x"] = np.ascontiguousarray(np.asarray(inputs["x"][b], dtype=np.float32))
            m["ctx"] = np.ascontiguousarray(np.asarray(inputs["ctx"][b], dtype=np.float32))
            m["cvec"] = np.ascontiguousarray(np.stack([np.asarray(inputs["c"][b]), np.asarray(inputs["c_ctx"])], axis=0).astype(np.float32))
        in_maps.append(m)
    res = run_bass_kernel_spmd(nc, in_maps, core_ids=list(range(8)))
    return np.stack([np.asarray(r["out"], dtype=np.float32) for r in res.results], axis=0)
```
